# Optimizing a Trainium2 kernel written in Bass

```python
import jax, jax.numpy as jnp
from jax import lax
import numpy as np

D_MODEL = 1024
BATCH = 16
SEQ = 2048
DEPTH = 1

RWKV_HEADS = 8
RWKV_HEAD_DIM = 64
RWKV_WIDTH = RWKV_HEADS * RWKV_HEAD_DIM
DECAY_LORA = 64
ICLR_LORA = 64
GATE_LORA = 160
RWKV_GN_EPS = 64e-5
L2_EPS = 1e-12
ATT_HEADS = 8
ATT_HEAD_DIM = 64
ATT_WIDTH = ATT_HEADS * ATT_HEAD_DIM
IDX_HEADS = 8
IDX_DIM = 64
TOPK_MAX = 256
Q_BLOCK = 128
ROPE_THETA = 10000.0
N_BRANCH = 2
BRANCH_WIDTH = RWKV_WIDTH
FFN_HIDDEN = 4 * D_MODEL
NORM_EPS = 1e-6

RWKV_IN_SIZES = (RWKV_WIDTH, RWKV_WIDTH, RWKV_WIDTH, DECAY_LORA, ICLR_LORA, GATE_LORA)
RWKV_IN = 3 * RWKV_WIDTH + DECAY_LORA + ICLR_LORA + GATE_LORA
DSA_IN_SIZES = (ATT_WIDTH, ATT_HEAD_DIM, ATT_HEAD_DIM, IDX_HEADS * IDX_DIM, IDX_DIM, IDX_HEADS)
DSA_IN = ATT_WIDTH + 2 * ATT_HEAD_DIM + IDX_HEADS * IDX_DIM + IDX_DIM + IDX_HEADS
GATE_IN = N_BRANCH * D_MODEL
N_IN = RWKV_IN + DSA_IN + GATE_IN

kernel_name = "hybrid_rwkv7_dsa_gated_block"


def split_cols(z, sizes):
    idx = np.cumsum(sizes)[:-1].tolist()
    return jnp.split(z, idx, axis=-1)


def rms_norm(x, g):
    xf = x.astype(jnp.float32)
    y = xf * lax.rsqrt(jnp.mean(xf * xf, axis=-1, keepdims=True) + NORM_EPS)
    return (y * g.astype(jnp.float32)).astype(x.dtype)


def rope_tables(T, d):
    inv = 1.0 / (ROPE_THETA ** (jnp.arange(0, d, 2, dtype=jnp.float32) / d))
    ang = jnp.arange(T, dtype=jnp.float32)[:, None] * inv[None, :]
    return jnp.cos(ang), jnp.sin(ang)


def apply_rope(x, cos, sin):
    d2 = x.shape[-1] // 2
    shape = (1, x.shape[1]) + (1,) * (x.ndim - 3) + (d2,)
    c = cos.reshape(shape)
    s = sin.reshape(shape)
    xf = x.astype(jnp.float32)
    x1, x2 = xf[..., :d2], xf[..., d2:]
    return jnp.concatenate([x1 * c - x2 * s, x2 * c + x1 * s], axis=-1).astype(x.dtype)


def token_shift(z, mu):
    prev = jnp.pad(z[:, :-1], ((0, 0), (1, 0), (0, 0)))
    return z + (prev - z) * mu


def rwkv7_mix(z, mu, decay_bias, w_decay_up, iclr_bias, w_iclr_up, w_gate_up,
              k_k, k_a, r_k, gn_w, gn_b):
    B, T, _ = z.shape
    H, N = RWKV_HEADS, RWKV_HEAD_DIM
    f32 = jnp.float32
    z = token_shift(z, mu)
    r, k, v, wd, ad, gd = split_cols(z, RWKV_IN_SIZES)
    w_log = -jax.nn.softplus(-(decay_bias + jnp.tanh(wd) @ w_decay_up)) - 0.5
    decay = jnp.exp(-jnp.exp(w_log.astype(f32)))
    iclr = jax.nn.sigmoid(iclr_bias + ad @ w_iclr_up)
    gate = jax.nn.sigmoid(gd) @ w_gate_up
    heads = lambda t: t.reshape(B, T, H, N).astype(f32)
    kk = heads(k * k_k)
    kk = kk / jnp.maximum(jnp.sqrt(jnp.sum(kk * kk, axis=-1, keepdims=True)), L2_EPS)
    k = k * (1.0 + (iclr - 1.0) * k_a)
    r_h, k_h, v_h, w_h, a_h = heads(r), heads(k), heads(v), heads(decay), heads(iclr)
    a_vec = -kk
    b_vec = kk * a_h

    def step(S, inp):
        r_t, w_t, k_t, v_t, a_t, b_t = inp
        sa = jnp.einsum('bhvk,bhk->bhv', S, a_t)
        S = (S * w_t[:, :, None, :] + sa[..., None] * b_t[:, :, None, :]
             + v_t[..., None] * k_t[:, :, None, :])
        return S, jnp.einsum('bhvk,bhk->bhv', S, r_t)

    tm = lambda t: jnp.moveaxis(t, 1, 0)
    S0 = jnp.zeros((B, H, N, N), f32)
    _, y = lax.scan(step, S0, (tm(r_h), tm(w_h), tm(k_h), tm(v_h), tm(a_vec), tm(b_vec)))
    y = jnp.moveaxis(y, 0, 1)
    mean = jnp.mean(y, axis=-1, keepdims=True)
    var = jnp.mean(jnp.square(y - mean), axis=-1, keepdims=True)
    y = ((y - mean) * lax.rsqrt(var + RWKV_GN_EPS) * gn_w.astype(f32).reshape(H, N)
         + gn_b.astype(f32).reshape(H, N))
    bonus = jnp.sum(r_h * k_h * r_k.astype(f32).reshape(H, N), axis=-1, keepdims=True) * v_h
    y = (y + bonus).reshape(B, T, RWKV_WIDTH)
    return (y * gate).astype(z.dtype)


def dsa_mix(z, cos, sin):
    B, T, _ = z.shape
    f32 = jnp.float32
    q, k, v, qi, ki, wi = split_cols(z, DSA_IN_SIZES)
    q = apply_rope(q.reshape(B, T, ATT_HEADS, ATT_HEAD_DIM), cos, sin)
    k = apply_rope(k, cos, sin)
    qi = apply_rope(qi.reshape(B, T, IDX_HEADS, IDX_DIM), cos, sin)
    ki = apply_rope(ki, cos, sin)
    wi = wi * (IDX_HEADS ** -0.5 * IDX_DIM ** -0.5)
    topk = min(TOPK_MAX, T // 4)
    nb = T // Q_BLOCK
    blocks = lambda t: jnp.moveaxis(t.reshape((B, nb, Q_BLOCK) + t.shape[2:]), 1, 0)
    key_pos = jnp.arange(T)
    gather = jax.vmap(lambda tab, i: tab[i])

    def one_block(args):
        q_b, qi_b, wi_b, qpos = args
        logits = jnp.einsum('bqhd,bsd->bqhs', qi_b, ki)
        score = jnp.einsum('bqh,bqhs->bqs', wi_b, jax.nn.relu(logits)).astype(f32)
        causal = key_pos[None, :] <= qpos[:, None]
        score = jnp.where(causal[None], score, -jnp.inf)
        _, idx = lax.top_k(score, topk)
        k_sel = gather(k, idx)
        v_sel = gather(v, idx)
        valid = idx <= qpos[None, :, None]
        s = jnp.einsum('bqhd,bqkd->bqhk', q_b, k_sel).astype(f32) * (ATT_HEAD_DIM ** -0.5)
        s = jnp.where(valid[:, :, None, :], s, -jnp.inf)
        p = jax.nn.softmax(s, axis=-1).astype(v_sel.dtype)
        return jnp.einsum('bqhk,bqkd->bqhd', p, v_sel)

    pos_blocks = jnp.arange(T).reshape(nb, Q_BLOCK)
    o = lax.map(one_block, (blocks(q), blocks(qi), blocks(wi), pos_blocks))
    return jnp.moveaxis(o, 0, 1).reshape(B, T, ATT_WIDTH)


def setup_inputs(seed: int = 0) -> dict:
    key = jax.random.key(seed)
    ks = jax.random.split(key, 20)
    f32 = jnp.float32
    L = DEPTH
    nrm = lambda k, shape, scale: jax.random.normal(k, shape, f32) * scale
    return {
        "x": nrm(ks[0], (BATCH, SEQ, D_MODEL), 1.0),
        "g_mix": 1.0 + nrm(ks[1], (L, D_MODEL), 0.02),
        "w_in": nrm(ks[2], (L, D_MODEL, N_IN), D_MODEL ** -0.5),
        "mu_shift": jax.random.uniform(ks[3], (L, RWKV_IN), f32),
        "decay_bias": jax.random.uniform(ks[4], (L, RWKV_WIDTH), f32, -6.0, 1.0),
        "w_decay_up": nrm(ks[5], (L, DECAY_LORA, RWKV_WIDTH), 0.5 * DECAY_LORA ** -0.5),
        "iclr_bias": nrm(ks[6], (L, RWKV_WIDTH), 0.5),
        "w_iclr_up": nrm(ks[7], (L, ICLR_LORA, RWKV_WIDTH), ICLR_LORA ** -0.5),
        "w_gate_up": nrm(ks[8], (L, GATE_LORA, RWKV_WIDTH), GATE_LORA ** -0.5),
        "k_k": 0.85 + nrm(ks[9], (L, RWKV_WIDTH), 0.02),
        "k_a": 1.0 + nrm(ks[10], (L, RWKV_WIDTH), 0.02),
        "r_k": nrm(ks[11], (L, RWKV_WIDTH), 0.1),
        "gn_w": 1.0 + nrm(ks[12], (L, RWKV_WIDTH), 0.02),
        "gn_b": nrm(ks[13], (L, RWKV_WIDTH), 0.01),
        "w_branch": nrm(ks[14], (L, N_BRANCH, BRANCH_WIDTH, D_MODEL), BRANCH_WIDTH ** -0.5),
        "w_out": nrm(ks[15], (L, D_MODEL, D_MODEL), D_MODEL ** -0.5),
        "g_ffn": 1.0 + nrm(ks[16], (L, D_MODEL), 0.02),
        "w_ffn_up": nrm(ks[17], (L, D_MODEL, FFN_HIDDEN), D_MODEL ** -0.5),
        "w_ffn_down": nrm(ks[18], (L, FFN_HIDDEN, D_MODEL), FFN_HIDDEN ** -0.5),
        "g_final": 1.0 + nrm(ks[19], (D_MODEL,), 0.02),
    }


def reference(x, g_mix, w_in, mu_shift, decay_bias, w_decay_up, iclr_bias, w_iclr_up,
              w_gate_up, k_k, k_a, r_k, gn_w, gn_b, w_branch, w_out, g_ffn,
              w_ffn_up, w_ffn_down, g_final):
    B, T, D = x.shape
    cos, sin = rope_tables(T, ATT_HEAD_DIM)
    for l in range(DEPTH):
        h = rms_norm(x, g_mix[l])
        z = h @ w_in[l]
        z_rwkv, z_dsa, z_gate = split_cols(z, (RWKV_IN, DSA_IN, GATE_IN))
        y_a = rwkv7_mix(z_rwkv, mu_shift[l], decay_bias[l], w_decay_up[l], iclr_bias[l],
                        w_iclr_up[l], w_gate_up[l], k_k[l], k_a[l], r_k[l], gn_w[l], gn_b[l])
        y_b = dsa_mix(z_dsa, cos, sin)
        branch = jnp.stack([y_a, y_b], axis=2)
        proj = jnp.einsum('btnc,ncd->btnd', branch, w_branch[l])
        gates = jax.nn.sigmoid(z_gate.reshape(B, T, N_BRANCH, D))
        merged = jnp.sum(gates * proj, axis=2)
        x = x + merged @ w_out[l]
        h2 = rms_norm(x, g_ffn[l])
        x = x + jnp.square(jax.nn.relu(h2 @ w_ffn_up[l])) @ w_ffn_down[l]
    return rms_norm(x, g_final)
```

```python
import os
from contextlib import ExitStack

import numpy as np
import concourse.bass as bass
import concourse.mybir as mybir
from concourse.bass_utils import run_bass_kernel_spmd

F32 = mybir.dt.float32
BF16 = mybir.dt.bfloat16
ALU = mybir.AluOpType
AF = mybir.ActivationFunctionType
AX = mybir.AxisListType

NCORES = 8
T = 2048
D = 1024
NSEQ = 2
NTOK = NSEQ * T
C = 64
C0 = float(np.exp(-0.5))
NEG = -1.0e30
RW_NB = 256
DS_NB = 512
MG_NB = 512
FF_NB = 256
FFH = 4096


class Res:
    __slots__ = ("name", "w", "rd", "rd_dma")

    def __init__(self, name):
        self.name = name
        self.w = None
        self.rd = {}
        self.rd_dma = []


class DmaSem:
    def __init__(self, sem):
        self.sem = sem
        self.count = 0


class _Op:
    __slots__ = ("id", "eng", "fn", "deps", "dsem", "val", "signal")


class Sched:
    ENGS = ("pe", "act", "dve", "pool", "sp")

    def __init__(self, nc, es):
        self.nc = nc
        self.es = es
        self.ops = []
        self.per = {e: [] for e in self.ENGS}
        self.sem = {e: es.enter_context(nc.semaphore("s_" + e)) for e in self.ENGS}
        self.n_dsem = 0
        self.last = {e: None for e in self.ENGS}
        self.dma_since_barrier = []

    def new_dsem(self):
        self.n_dsem += 1
        return DmaSem(self.es.enter_context(self.nc.semaphore("d%d" % self.n_dsem)))

    def op(self, eng, fn, reads=(), writes=(), dsem=None):
        o = _Op()
        o.id = len(self.ops)
        o.eng = eng
        o.fn = fn
        o.dsem = dsem
        o.signal = False
        o.val = None
        deps = {}

        def add(d, kind):
            if d is None:
                return
            if kind == "raw" or d not in deps:
                deps[d] = kind

        for r in reads:
            add(r.w, "raw")
        for w in writes:
            add(w.w, "waw")
            for d in w.rd.values():
                add(d, "war")
            for d in w.rd_dma:
                add(d, "war")
        o.deps = deps
        for r in reads:
            if dsem is not None:
                r.rd_dma.append(o.id)
            else:
                r.rd[eng] = o.id
        for w in writes:
            w.w = o.id
            w.rd = {}
            w.rd_dma = []
        if dsem is not None:
            dsem.count += 16
            o.val = dsem.count
            self.dma_since_barrier.append(o.id)
        self.ops.append(o)
        self.per[eng].append(o)
        self.last[eng] = o.id
        return o

    def barrier(self):
        lasts = [v for v in self.last.values() if v is not None]
        dmas = list(self.dma_since_barrier)
        self.dma_since_barrier = []
        for e in self.ENGS:
            o = self.op(e, lambda en: en.nop())
            for d in lasts + dmas:
                if d != o.id:
                    o.deps[d] = "raw"

    def finish(self, dma_ops):
        o = self.op("sp", lambda en: en.nop())
        for d in dma_ops:
            o.deps[d.id] = "raw"

    def emit(self):
        ops = self.ops
        for o in ops:
            for d, kind in o.deps.items():
                p = ops[d]
                if p.dsem is not None:
                    continue
                if p.eng == o.eng and o.dsem is None and o.eng in ("pe", "sp"):
                    continue
                p.signal = True
        cnt = {e: 0 for e in self.ENGS}
        for o in ops:
            if o.dsem is None and o.signal:
                cnt[o.eng] += 1
                o.val = cnt[o.eng]
        sem = self.sem

        def run(eng, en):
            known = {}
            for o in self.per[eng]:
                need = {}
                for d, kind in o.deps.items():
                    p = ops[d]
                    if p.dsem is not None:
                        key, s, v = ("d", id(p.dsem)), p.dsem.sem, p.val
                    else:
                        if not p.signal:
                            continue
                        if p.eng == eng and o.dsem is None and eng in ("pe", "sp"):
                            continue
                        key, s, v = ("e", p.eng), sem[p.eng], p.val
                    if known.get(key, 0) >= v:
                        continue
                    if key not in need or need[key][1] < v:
                        need[key] = (s, v)
                for key, (s, v) in need.items():
                    en.wait_ge(s, v)
                    known[key] = v
                ins = o.fn(en)
                if o.dsem is not None:
                    ins.then_inc(o.dsem.sem, 16)
                elif o.signal:
                    ins.then_inc(sem[eng], 1)

        with self.nc.Block() as block:
            @block.tensor
            def _(en):
                run("pe", en)

            @block.scalar
            def _(en):
                run("act", en)

            @block.vector
            def _(en):
                run("dve", en)

            @block.gpsimd
            def _(en):
                run("pool", en)

            @block.sync
            def _(en):
                run("sp", en)


class Tl:
    def __init__(self, h, name, nres=1):
        self.h = h
        self.name = name
        self.rs = [Res("%s.%d" % (name, i)) for i in range(nres)]

    @property
    def r(self):
        return self.rs[0]

    def __getitem__(self, k):
        return self.h[k]


class Pool:
    def __init__(self, tiles):
        self.tiles = tiles
        self.i = 0

    def next(self):
        t = self.tiles[self.i % len(self.tiles)]
        self.i += 1
        return t


class _CB:
    def __init__(self):
        self.cols = {}
        self.parts = []
        self.off = 0

    def put(self, name, arr):
        a = np.zeros((128, arr.shape[1]), np.float32)
        a[: arr.shape[0]] = arr
        self.cols[name] = (self.off, arr.shape[1], arr.shape[0])
        self.parts.append(a)
        self.off += arr.shape[1]

    def arr(self):
        return np.ascontiguousarray(np.concatenate(self.parts, axis=1))


def _const_f32():
    A, B1, B2 = _CB(), _CB(), _CB()
    A.put("ident", np.eye(128, dtype=np.float32))
    s = np.arange(64)[:, None]
    t = np.arange(64)[None, :]
    m1 = np.concatenate([(s < t), (s <= t)], axis=1).astype(np.float32)
    B1.put("mask1", np.tile(m1, (1, 8)))
    B1.put("maskL", np.tile((s > t).astype(np.float32), (1, 8)))
    B1.put("eye8", np.tile(np.eye(64, dtype=np.float32), (1, 8)))
    rm = np.ones((128, RW_NB), np.float32)
    rm[:, ::C] = 0.0
    B1.put("reset", rm)
    bo = np.zeros((128, 128), np.float32)
    bo[:64, :64] = 1.0
    bo[64:, 64:] = 1.0
    A.put("blockones", bo)
    hi = np.zeros((128, 2), np.float32)
    hi[:64, 0] = 1.0
    hi[64:, 1] = 1.0
    A.put("headind", hi)
    tq = np.arange(128)[:, None]
    kk = np.arange(128)[None, :]
    A.put("causal_bias", np.where(kk <= tq, 0.0, NEG).astype(np.float32))
    A.put("causalT", (tq <= kk).astype(np.float32))
    inv = (1.0 / (10000.0 ** (np.arange(0, 64, 2, dtype=np.float32) / np.float32(64)))).astype(np.float32)
    ang = (np.arange(T, dtype=np.float32)[:, None] * inv[None, :]).astype(np.float32)
    cs = np.cos(ang).astype(np.float32).T
    sn = np.sin(ang).astype(np.float32).T
    d = np.arange(128) % 64
    B2.put("ropeC", cs[d % 32])
    sg = np.where(d < 32, -1.0, 1.0).astype(np.float32)[:, None]
    B2.put("ropeS", sn[d % 32] * sg)
    return A, B1, B2


_CA, _C1, _C2 = _const_f32()
_CSTA, _CST1, _CST2 = _CA.arr(), _C1.arr(), _C2.arr()


def build_nc(debug=None):
    nc = bass.Bass("TRN2", target_bir_lowering=False)
    dt_in = lambda name, shape: nc.dram_tensor(name, list(shape), F32, kind="ExternalInput").ap()
    x_d = dt_in("x", (NTOK, D))
    wA_d = dt_in("wA", (D, 1312))
    wV_d = dt_in("wV", (D, 512))
    muA_d = dt_in("muA", (1312,))
    muV_d = dt_in("muV", (512,))
    wD_d = dt_in("wD", (D, 2560))
    wT_d = dt_in("wT", (D, 72))
    wG_d = dt_in("wG", (D, 2048))
    gcol_d = dt_in("gcols", (128, 16))
    gz_d = dt_in("gfin", (D,))
    wlora_d = dt_in("wlora", (128, 512))
    wgate_d = dt_in("wgate", (160, 512))
    pp_d = dt_in("pp", (128, 20))
    gnw_d = dt_in("gnw", (512,))
    gnb_d = dt_in("gnb", (512,))
    wbr_d = dt_in("wbr", (1024, 1024))
    wout_d = dt_in("wout", (D, D))
    wup_d = dt_in("wup", (D, FFH))
    wdn_d = dt_in("wdn", (FFH, D))
    cstA_d = dt_in("cstA", _CSTA.shape)
    cst1_d = dt_in("cst1", _CST1.shape)
    cst2_d = dt_in("cst2", _CST2.shape)
    out_d = nc.dram_tensor("out", [NTOK, D], F32, kind="ExternalOutput").ap()
    yaT_d = nc.dram_tensor("yaT_scr", [128, 4, NTOK], BF16, kind="Internal").ap()
    ybT_d = nc.dram_tensor("ybT_scr", [128, 4, NTOK], BF16, kind="Internal").ap()
    x1_d = nc.dram_tensor("x1_scr", [NTOK, D], F32, kind="Internal").ap()
    dbg_d = {}
    dbg_sem = {}
    if debug:
        for name, shape in debug.items():
            dbg_d[name] = nc.dram_tensor("dbg_" + name, list(shape), F32, kind="ExternalOutput").ap()

    top = ExitStack()
    with top:
        S = Sched(nc, top)
        out_dmas = []
        yaT_res = Res("yaT_scr")
        ybT_res = Res("ybT_scr")
        x1_res = [Res("x1_scr%d" % i) for i in range(NTOK // 512)]

        uid = [0]

        def sb(es, name, shape, dt=F32, nres=1):
            uid[0] += 1
            return Tl(es.enter_context(nc.sbuf_tensor("sb%d_%s" % (uid[0], name), list(shape), dt)), name, nres)

        def ps(es, name, shape, dt=F32):
            uid[0] += 1
            return Tl(es.enter_context(nc.psum_tensor("ps%d_%s" % (uid[0], name), list(shape), dt)), name)

        def rr(*xs):
            out = []
            for x in xs:
                if isinstance(x, Tl):
                    out.extend(x.rs)
                elif isinstance(x, Res):
                    out.append(x)
                else:
                    out.extend(x)
            return out

        def dma(out_ap, in_ap, reads, writes, dsem):
            return S.op("sp", lambda en: en.dma_start(out=out_ap, in_=in_ap),
                        reads=rr(*reads), writes=rr(*writes), dsem=dsem)

        def mm(out_ap, lhsT, rhs, start, stop, reads, writes, skip=False):
            return S.op("pe", lambda en: en.matmul(out_ap, lhsT, rhs, start=start, stop=stop, skip_group_check=skip),
                        reads=rr(*reads), writes=rr(*writes))

        def tr(out_ap, in_ap, ident, reads, writes):
            return S.op("pe", lambda en: en.transpose(out_ap, in_ap, ident),
                        reads=rr(*reads), writes=rr(*writes))

        def act(out_ap, in_ap, func, reads, writes, bias=0.0, scale=1.0, accum_out=None):
            return S.op("act", lambda en: en.activation(out_ap, in_ap, func, bias=bias, scale=scale,
                                                        accum_out=accum_out),
                        reads=rr(*reads), writes=rr(*writes))

        def tt(eng, out_ap, a, b, op, reads, writes):
            return S.op(eng, lambda en: en.tensor_tensor(out_ap, a, b, op), reads=rr(*reads), writes=rr(*writes))

        def ts(eng, out_ap, a, s1, s2, op0, op1, reads, writes):
            if op1 is None:
                return S.op(eng, lambda en: en.tensor_scalar(out_ap, a, s1, None, op0),
                            reads=rr(*reads), writes=rr(*writes))
            return S.op(eng, lambda en: en.tensor_scalar(out_ap, a, s1, s2, op0, op1),
                        reads=rr(*reads), writes=rr(*writes))

        def stt(out_ap, a, sc, b, op0, op1, reads, writes):
            return S.op("dve", lambda en: en.scalar_tensor_tensor(out_ap, a, sc, b, op0, op1),
                        reads=rr(*reads), writes=rr(*writes))

        def cp(eng, out_ap, in_ap, reads, writes):
            if eng == "act":
                return S.op("act", lambda en: en.copy(out_ap, in_ap), reads=rr(*reads), writes=rr(*writes))
            return S.op(eng, lambda en: en.tensor_copy(out_ap, in_ap), reads=rr(*reads), writes=rr(*writes))

        def memset(eng, ap, val, writes):
            return S.op(eng, lambda en: en.memset(ap, val), writes=rr(*writes))

        def dbg_dump(name, src_ap, reads, dst_slice=None):
            if name not in dbg_d:
                return
            dst = dbg_d[name] if dst_slice is None else dbg_d[name][dst_slice]
            if name not in dbg_sem:
                dbg_sem[name] = S.new_dsem()
            out_dmas.append(dma(dst, src_ap, reads, [], dbg_sem[name]))

        cst = Tl(None, "cstgroup", 0)
        ctiles = {}

        def load_const(es_, key, arr, src_d, cb):
            t_ = sb(es_, "cs_sb" + key, arr.shape)
            dma(t_[:], src_d, [], [t_], S.new_dsem())
            cst.rs.extend(t_.rs)
            for nm in cb.cols:
                ctiles[nm] = (t_, cb.cols[nm])

        load_const(top, "A", _CSTA, cstA_d, _CA)

        def cc(name, rows=None):
            t_, (o, n, r0) = ctiles[name]
            return t_[: (rows or r0), o:o + n]

        def d0():
            return S.new_dsem()

        nhalf = sb(top, "nhalf", (128, 256))
        memset("pool", nhalf[:], -0.5, [nhalf])
        identb = sb(top, "identb", (128, 128), BF16)
        cp("dve", identb[:], cc("ident"), [cst], [identb])
        gcol = sb(top, "gcol", (128, 2, 8))
        dma(gcol[:].rearrange("p a k -> p (a k)"), gcol_d, [], [gcol], d0())
        pp = sb(top, "pp", (128, 20))
        dma(pp[:], pp_d, [], [pp], d0())

        WSTN = 1312
        wst = [sb(top, "wst%d" % i, (128, WSTN)) for i in range(2)]
        wst_sem = [S.new_dsem() for _ in range(2)]
        wst_i = [0]

        def load_weight(src_ap, ncols, consume):
            i = wst_i[0] % 2
            wst_i[0] += 1
            dma(wst[i][:, :ncols], src_ap, [], [wst[i]], wst_sem[i])
            consume(wst[i])

        xt_sem = [S.new_dsem() for _ in range(2)]
        xt_i = [0]
        xs_bf = [sb(top, "xsbf%d" % i, (128, D), BF16) for i in range(2)]
        stat = [sb(top, "stat%d" % i, (128, 4)) for i in range(2)]

        def make_hT(es_ps, tok0, ntile, hT, col0, src_d, xt=None, keep=None, src_res=()):
            for i in range(ntile):
                k = xt_i[0] % 2
                xt_i[0] += 1
                if keep is not None:
                    xin = keep
                    xap = keep[:, i, :]
                    dma(xap, src_d[tok0 + i * 128: tok0 + (i + 1) * 128, :], src_res, [keep], xt_sem[k])
                else:
                    xin = xt[k]
                    xap = xt[k][:]
                    dma(xap, src_d[tok0 + i * 128: tok0 + (i + 1) * 128, :], src_res, [xt[k]], xt_sem[k])
                st = stat[k]
                act(xs_bf[k][:], xap, AF.Square, [xin], [xs_bf[k], st], accum_out=st[:, 0:1])
                ts("dve", st[:, 1:2], st[:, 0:1], 1.0 / D, 1e-6, ALU.mult, ALU.add, [st], [st])
                tt("pool", st[:, 2:3], st[:, 1:2], nhalf[:, 0:1], ALU.pow, [st, nhalf], [st])
                ts("dve", xs_bf[k][:], xap, st[:, 2:3], None, ALU.mult, None, [xin, st], [xs_bf[k]])
                pt = es_ps.next()
                for kc in range(8):
                    tr(pt[:, kc, :], xs_bf[k][:, kc * 128:(kc + 1) * 128], identb[:], [xs_bf[k], identb], [pt])
                cp("act" if i % 2 else "dve", hT[:, :, col0 + i * 128: col0 + (i + 1) * 128], pt[:, :, :], [pt], [hT])

        with ExitStack() as es:
            W1A = sb(es, "W1A", (128, 8, 1312), BF16)
            W2A = sb(es, "W2A", (128, 8, 1312), BF16)
            W1V = sb(es, "W1V", (128, 8, 512), BF16)
            W2V = sb(es, "W2V", (128, 8, 512), BF16)
            with ExitStack() as es_w:
                mub = sb(es_w, "mub", (128, 1824))
                omb = sb(es_w, "omb", (128, 1824))
                dma(mub[:, 0:1312], muA_d.partition_broadcast(128), [], [mub], d0())
                dma(mub[:, 1312:1824], muV_d.partition_broadcast(128), [], [mub], d0())
                ts("pool", omb[:], mub[:], -1.0, 1.0, ALU.mult, ALU.add, [mub], [omb])
                for kc in range(8):
                    def consA(st_, kc=kc):
                        stt(W1A[:, kc, :], st_[:, :1312], gcol[:, 0, kc:kc + 1], omb[:, 0:1312], ALU.mult, ALU.mult,
                            [st_, gcol, omb], [W1A])
                        stt(W2A[:, kc, :], st_[:, :1312], gcol[:, 0, kc:kc + 1], mub[:, 0:1312], ALU.mult, ALU.mult,
                            [st_, gcol, mub], [W2A])
                    load_weight(wA_d[kc * 128:(kc + 1) * 128, :], 1312, consA)

                    def consV(st_, kc=kc):
                        stt(W1V[:, kc, :], st_[:, :512], gcol[:, 0, kc:kc + 1], omb[:, 1312:1824], ALU.mult, ALU.mult,
                            [st_, gcol, omb], [W1V])
                        stt(W2V[:, kc, :], st_[:, :512], gcol[:, 0, kc:kc + 1], mub[:, 1312:1824], ALU.mult, ALU.mult,
                            [st_, gcol, mub], [W2V])
                    load_weight(wV_d[kc * 128:(kc + 1) * 128, :], 512, consV)
            S.barrier()
            load_const(es, "1", _CST1, cst1_d, _C1)
            xt = [sb(es, "xt%d" % i, (128, D)) for i in range(2)]
            wlora = sb(es, "wlora", (128, 512))
            wg0 = sb(es, "wg0", (128, 512))
            wg1 = sb(es, "wg1", (32, 512))
            gnwb = sb(es, "gnwb", (64, 512))
            gnbb = sb(es, "gnbb", (64, 512))
            dma(wlora[:], wlora_d, [], [wlora], d0())
            dma(wg0[:], wgate_d[0:128, :], [], [wg0], d0())
            dma(wg1[:], wgate_d[128:160, :], [], [wg1], d0())
            dma(gnwb[:], gnw_d.partition_broadcast(64), [], [gnwb], d0())
            dma(gnbb[:], gnb_d.partition_broadcast(64), [], [gnbb], d0())

            NB = RW_NB
            NCH = NB // C
            hT = sb(es, "hT", (128, 8, NB + 2), BF16)
            pTr = Pool([ps(es, "pTr%d" % i, (128, 8, 128), BF16) for i in range(1)])
            pj = Pool([ps(es, "pj%d" % i, (128, 512)) for i in range(2)])
            pW = Pool([ps(es, "pW%d" % i, (128, 2, 512)) for i in range(2)])
            pbon = ps(es, "pbon", (128, 512))
            r_sb = sb(es, "r_sb", (128, 4, NB))
            k_sb = sb(es, "k_sb", (128, 4, NB))
            wa_sb = sb(es, "wa_sb", (128, NB))
            sg0 = sb(es, "sg0", (128, NB))
            sg1 = sb(es, "sg1", (32, NB))
            v_sb = sb(es, "v_sb", (64, NCH, 512))
            AR = [sb(es, "AR%d" % h, (128, NCH, 2, C)) for h in range(4)]
            Bt = [sb(es, "Bt%d" % h, (128, NCH, C)) for h in range(4)]
            Kt = [sb(es, "Kt%d" % h, (128, NCH, C)) for h in range(4)]
            BKh = [sb(es, "BKh%d" % h, (128, NCH, 2, C)) for h in range(4)]
            wC = sb(es, "wC", (128, 4, NCH))
            bon = sb(es, "bon", (64, NCH, 8))
            tmp = [sb(es, "rt%d" % i, (128, NB)) for i in range(10)]
            ST = sb(es, "ST", (128, 4, C))
            MA = sb(es, "MA", (64, 8, 2 * C))
            KA = sb(es, "KA", (64, 8, 2 * C))
            ML = [sb(es, "ML%d" % i, (64, 8, 2, C)) for i in range(2)]
            TT = [sb(es, "TT%d" % i, (64, 8, C)) for i in range(2)]
            X_sb = sb(es, "X_sb", (64, 8, C))
            U_sb = sb(es, "U_sb", (64, 512))
            BKtok = sb(es, "BKtok", (64, 4, 2, 128))
            y_sb = sb(es, "y_sb", (64, 512))
            ysq = sb(es, "ysq", (64, 512))
            ytmp = sb(es, "ytmp", (64, 512))
            gst = sb(es, "gst", (64, 6, 8))
            yaT = [sb(es, "yaT%d" % i, (128, 4, NB), BF16) for i in range(2)]
            yaT_sem = [S.new_dsem() for _ in range(2)]
            ident = cc("ident")

            blk_i = 0
            dbg_blocks = int(os.environ.get("MK_BLOCKS", "999")) if debug else 999
            dbg_sub = int(os.environ.get("MK_SUB", "10")) if debug else 10
            dbg_s6 = int(os.environ.get("MK_S6", "3"))
            dbg_x = int(os.environ.get("MK_X", "3"))
            for b in range(NSEQ):
                memset("dve", ST[:], 0.0, [ST])
                for blk in range(T // NB):
                    if blk_i >= dbg_blocks:
                        continue
                    t0 = b * T + blk * NB
                    ya = yaT[blk_i % 2]
                    if blk == 0:
                        memset("pool", hT[:, :, 0:2], 0.0, [hT])
                    else:
                        cp("pool", hT[:, :, 1:2], hT[:, :, NB + 1:NB + 2], [hT], [hT])
                    make_hT(pTr, t0, NB // 128, hT, 2, x_d, xt=xt)

                    for ct in range(11):
                        rows = 32 if ct == 10 else 128
                        c0 = ct * 128
                        p_ = pj.next()
                        for kc in range(8):
                            mm(p_[:rows, :NB], W1A[:, kc, c0:c0 + rows], hT[:, kc, 2:NB + 2], kc == 0, False,
                               [W1A, hT], [p_])
                            mm(p_[:rows, :NB], W2A[:, kc, c0:c0 + rows], hT[:, kc, 1:NB + 1], False, kc == 7,
                               [W2A, hT], [p_])
                        if ct < 4:
                            cp("act", r_sb[:, ct, :], p_[:, :NB], [p_], [r_sb])
                        elif ct < 8:
                            cp("dve", k_sb[:, ct - 4, :], p_[:, :NB], [p_], [k_sb])
                        elif ct == 8:
                            act(wa_sb[0:64, :], p_[0:64, :NB], AF.Tanh, [p_], [wa_sb])
                            cp("dve", wa_sb[64:128, :], p_[64:128, :NB], [p_], [wa_sb])
                        elif ct == 9:
                            act(sg0[:], p_[:, :NB], AF.Sigmoid, [p_], [sg0])
                        else:
                            act(sg1[:], p_[0:32, :NB], AF.Sigmoid, [p_], [sg1])
                    for c in range(NCH):
                        p_ = pj.next()
                        for kc in range(8):
                            mm(p_[0:64, :], hT[:, kc, 2 + c * C:2 + (c + 1) * C], W1V[:, kc, :], kc == 0, False,
                               [W1V, hT], [p_])
                            mm(p_[0:64, :], hT[:, kc, 1 + c * C:1 + (c + 1) * C], W2V[:, kc, :], False, kc == 7,
                               [W2V, hT], [p_])
                        cp("act", v_sb[:, c, :], p_[0:64, :], [p_], [v_sb])

                    if dbg_sub < 2:
                        blk_i += 1
                        continue
                    for hp in range(4):
                        cs_ = slice(hp * 128, (hp + 1) * 128)
                        ppc = lambda j: pp[:, j * 4 + hp: j * 4 + hp + 1]
                        sgd, icl, cum, e_in, e_ng, e_ex, e_rm, kkn, kmod, t9 = tmp
                        p_ = pj.next()
                        mm(p_[:, :NB], wlora[0:64, cs_], wa_sb[0:64, :], True, True, [wlora, wa_sb], [p_])
                        act(sgd[:], p_[:, :NB], AF.Sigmoid, [p_, pp], [sgd], bias=ppc(0))
                        p_ = pj.next()
                        mm(p_[:, :NB], wlora[64:128, cs_], wa_sb[64:128, :], True, True, [wlora, wa_sb], [p_])
                        act(icl[:], p_[:, :NB], AF.Sigmoid, [p_, pp], [icl], bias=ppc(1))
                        S.op("dve", lambda en, cum=cum, sgd=sgd: en.tensor_tensor_scan(
                            cum[:], cc("reset"), sgd[:], 0.0, ALU.mult, ALU.add), reads=rr(cst, sgd), writes=rr(cum))
                        act(e_in[:], cum[:], AF.Exp, [cum], [e_in], scale=-C0)
                        act(e_ng[:], cum[:], AF.Exp, [cum], [e_ng], scale=C0)
                        tt("pool", t9[:], cum[:], sgd[:], ALU.subtract, [cum, sgd], [t9])
                        act(e_ex[:], t9[:], AF.Exp, [t9], [e_ex], scale=-C0)
                        cum3 = cum[:].rearrange("p (c t) -> p c t", t=C)
                        tt("pool", t9[:].rearrange("p (c t) -> p c t", t=C),
                           cum3[:, :, C - 1:C].to_broadcast([128, NCH, C]), cum3, ALU.subtract, [cum], [t9])
                        act(e_rm[:], t9[:], AF.Exp, [t9], [e_rm], scale=-C0)
                        cp("pool", wC[:, hp, :], e_in[:].rearrange("p (c t) -> p c t", t=C)[:, :, C - 1], [e_in], [wC])
                        kx = k_sb[:, hp, :]
                        ts("dve", kkn[:], kx, ppc(2), None, ALU.mult, None, [k_sb, pp], [kkn])
                        tt("pool", t9[:], kkn[:], kkn[:], ALU.mult, [kkn], [t9])
                        p_ = pj.next()
                        mm(p_[:, :NB], cc("blockones"), t9[:], True, True, [cst, t9], [p_])
                        ts("dve", t9[:], p_[:, :NB], 1e-24, None, ALU.max, None, [p_], [t9])
                        tt("pool", t9[:], t9[:], nhalf[:, :NB], ALU.pow, [t9, nhalf], [t9])
                        tt("dve", kkn[:], kkn[:], t9[:], ALU.mult, [kkn, t9], [kkn])
                        ts("dve", t9[:], icl[:], -1.0, ppc(3), ALU.add, ALU.mult, [icl, pp], [t9])
                        stt(kmod[:], t9[:], 1.0, kx, ALU.add, ALU.mult, [t9, k_sb], [kmod])
                        tt("pool", icl[:], icl[:], kkn[:], ALU.mult, [icl, kkn], [icl])
                        v4 = lambda a: a[:].rearrange("p (c t) -> p c t", t=C)
                        stt(AR[hp][:, :, 0, :], v4(kkn), -1.0, v4(e_ex), ALU.mult, ALU.mult, [kkn, e_ex], [AR[hp]])
                        tt("pool", AR[hp][:, :, 1, :], r_sb[:, hp, :].rearrange("p (c t) -> p c t", t=C), v4(e_in),
                           ALU.mult, [r_sb, e_in], [AR[hp]])
                        tt("dve", Bt[hp][:], v4(icl), v4(e_ng), ALU.mult, [icl, e_ng], [Bt[hp]])
                        tt("pool", Kt[hp][:], v4(kmod), v4(e_ng), ALU.mult, [kmod, e_ng], [Kt[hp]])
                        tt("dve", BKh[hp][:, :, 0, :], v4(icl), v4(e_rm), ALU.mult, [icl, e_rm], [BKh[hp]])
                        tt("pool", BKh[hp][:, :, 1, :], v4(kmod), v4(e_rm), ALU.mult, [kmod, e_rm], [BKh[hp]])
                        stt(t9[:], r_sb[:, hp, :], ppc(4), kmod[:], ALU.mult, ALU.mult, [r_sb, pp, kmod], [t9])
                        for c in range(NCH):
                            mm(pbon[0:64, c * 8 + hp * 2: c * 8 + hp * 2 + 2], t9[:, c * C:(c + 1) * C],
                               cc("headind"), True, True, [t9, cst], [pbon])
                    cp("act", bon[:].rearrange("p c h -> p (c h)"), pbon[0:64, 0:NCH * 8], [pbon], [bon])

                    if dbg_sub < 3:
                        blk_i += 1
                        continue
                    for c in range(NCH):
                        hrow = lambda h: slice((h % 2) * 64, (h % 2) * 64 + 64)
                        p1 = pW.next()
                        p2 = pW.next()
                        p3 = pj.next()
                        for h in range(8):
                            hp = h // 2
                            rs_ = hrow(h)
                            arh = AR[hp][rs_, c, :, :].rearrange("p a t -> p (a t)")
                            o1 = p1[0:64, h // 4, (h % 4) * 128:(h % 4) * 128 + 128]
                            o2 = p2[0:64, h // 4, (h % 4) * 128:(h % 4) * 128 + 128]
                            mm(o1, Bt[hp][rs_, c, :], arh, True, True, [Bt[hp], AR[hp]], [p1])
                            mm(o2, Kt[hp][rs_, c, :], arh, True, True, [Kt[hp], AR[hp]], [p2])
                            mm(p3[0:64, h * C:(h + 1) * C], AR[hp][rs_, c, 0, :], Bt[hp][rs_, c, :], True, True,
                               [AR[hp], Bt[hp]], [p3])
                        m1v = cc("mask1").rearrange("p (a b) -> p a b", a=2)
                        tt("dve", MA[:].rearrange("p (a h) m -> p a (h m)", a=2), p1[0:64, :, :], m1v, ALU.mult,
                           [p1, cst], [MA])
                        tt("dve", KA[:].rearrange("p (a h) m -> p a (h m)", a=2), p2[0:64, :, :], m1v, ALU.mult,
                           [p2, cst], [KA])
                        if dbg_sub < 4:
                            continue
                        mlc = ML[0]
                        cp("pool", mlc[:, :, 0, :], MA[:, :, 0:C], [MA], [mlc])
                        tt("dve", mlc[:, :, 1, :], p3[0:64, :].rearrange("p (h s) -> p h s", h=8),
                           cc("maskL").rearrange("p (h s) -> p h s", h=8), ALU.mult, [p3, cst], [mlc])
                        tcur = TT[0]
                        tt("pool", tcur[:], MA[:, :, 0:C], cc("eye8").rearrange("p (h s) -> p h s", h=8), ALU.add,
                           [MA, cst], [tcur])
                        for lev in range(1, 6):
                            mln = ML[lev % 2]
                            pm = pW.next()
                            for h in range(8):
                                if lev < 5:
                                    mm(pm[0:64, h // 4, (h % 4) * 128:(h % 4) * 128 + C], mlc[:, h, 1, :], mlc[:, h, 0, :],
                                       True, True, [mlc], [pm])
                                mm(pm[0:64, h // 4, (h % 4) * 128 + C:(h % 4) * 128 + 2 * C], mlc[:, h, 0, :],
                                   mlc[:, h, 1, :], True, True, [mlc], [pm])
                            if lev < 5:
                                cp("act", mln[:].rearrange("p (a h) x s -> p a (h x s)", a=2), pm[0:64, :, :], [pm], [mln])
                            else:
                                cp("act", mln[:, :, 1, :].rearrange("p (a h) s -> p a h s", a=2),
                                   pm[0:64, :, :].rearrange("p a (h x s) -> p a h x s", h=4, x=2)[:, :, :, 1, :],
                                   [pm], [mln])
                            pt_ = pj.next()
                            for h in range(8):
                                mm(pt_[0:64, h * C:(h + 1) * C], mln[:, h, 1, :], tcur[:, h, :], True, True,
                                   [mln, tcur], [pt_])
                            tnew = TT[lev % 2]
                            tt("dve", tnew[:], pt_[0:64, :].rearrange("p (h s) -> p h s", h=8), tcur[:], ALU.add,
                               [pt_, tcur], [tnew])
                            mlc = mln
                            tcur = tnew
                        if dbg_sub < 5:
                            continue
                        pbk = pW.next()
                        for hp in range(4):
                            for a in range(2):
                                tr(pbk[0:64, hp // 2, (hp % 2) * 256 + a * 128:(hp % 2) * 256 + a * 128 + 128],
                                   BKh[hp][:, c, a, :], ident, [BKh[hp], cst], [pbk])
                        cp("act", BKtok[:].rearrange("p (a h) x m -> p a (h x m)", a=2), pbk[0:64, :, :], [pbk], [BKtok])
                        if dbg_sub < 6:
                            continue
                        px = pW.next()
                        px2 = pj.next()
                        for h in range(8):
                            hp = h // 2
                            rs_ = hrow(h)
                            mm(px[0:64, h % 2, hp * C:(hp + 1) * C], AR[hp][rs_, c, 0, :], ST[rs_, hp, :], True, True,
                               [AR[hp], ST], [px])
                            mm(px2[0:64, h * C:(h + 1) * C], KA[:, h, 0:C], v_sb[:, c, h * C:(h + 1) * C], True, True,
                               [KA, v_sb], [px2])
                        cp("act", X_sb[:].rearrange("p (hp par) s -> p par hp s", par=2),
                           px[0:64, :, 0:4 * C].rearrange("p par (hp s) -> p par hp s", s=C), [px], [X_sb])
                        tt("dve", X_sb[:].rearrange("p h s -> p (h s)"), X_sb[:].rearrange("p h s -> p (h s)"), px2[0:64, :],
                           ALU.add, [X_sb, px2], [X_sb])
                        if dbg_sub == 6 and dbg_s6 < 2:
                            continue
                        pu = pj.next()
                        for h in range(8):
                            mm(pu[0:64, h * C:(h + 1) * C], tcur[:, h, :], X_sb[:, h, :], True, True, [tcur, X_sb], [pu])
                        cp("dve", U_sb[:], pu[0:64, :], [pu], [U_sb])
                        if dbg_sub == 6 and dbg_s6 < 3:
                            continue
                        py = pW.next()
                        py2 = pj.next()
                        for h in range(8):
                            hp = h // 2
                            rs_ = hrow(h)
                            mm(py[0:64, h % 2, hp * C:(hp + 1) * C], AR[hp][rs_, c, 1, :], ST[rs_, hp, :], True, True,
                               [AR[hp], ST], [py])
                            mm(py2[0:64, h * C:(h + 1) * C], MA[:, h, C:2 * C], U_sb[:, h * C:(h + 1) * C], True, False,
                               [MA, U_sb], [py2])
                            mm(py2[0:64, h * C:(h + 1) * C], KA[:, h, C:2 * C], v_sb[:, c, h * C:(h + 1) * C], False, True,
                               [KA, v_sb], [py2])
                        cp("act", y_sb[:].rearrange("p (hp par s) -> p par hp s", par=2, s=C),
                           py[0:64, :, 0:4 * C].rearrange("p par (hp s) -> p par hp s", s=C), [py], [y_sb])
                        tt("dve", y_sb[:], y_sb[:], py2[0:64, :], ALU.add, [y_sb, py2], [y_sb])
                        if dbg_sub < 7:
                            continue
                        pS = pj.next()
                        for hp in range(4):
                            mm(pS[:, hp * 128:(hp + 1) * 128], BKtok[:, hp, 0, :], U_sb[:, hp * 128:(hp + 1) * 128],
                               True, False, [BKtok, U_sb], [pS])
                            mm(pS[:, hp * 128:(hp + 1) * 128], BKtok[:, hp, 1, :], v_sb[:, c, hp * 128:(hp + 1) * 128],
                               False, True, [BKtok, v_sb], [pS])
                        for hp in range(4):
                            for hh in range(2):
                                rs_ = slice(hh * 64, hh * 64 + 64)
                                stt(ST[rs_, hp, :], ST[rs_, hp, :], wC[rs_, hp, c:c + 1],
                                    pS[rs_, hp * 128 + hh * 64: hp * 128 + hh * 64 + 64], ALU.mult, ALU.add,
                                    [ST, wC, pS], [ST])
                        if dbg_sub < 8:
                            continue
                        y3 = y_sb[:].rearrange("p (h i) -> p h i", h=8)
                        S.op("dve", lambda en, y3=y3: en.tensor_reduce(gst[:, 0, :], y3, AX.X, ALU.add),
                             reads=rr(y_sb), writes=rr(gst))
                        act(ysq[:], y_sb[:], AF.Square, [y_sb], [ysq])
                        S.op("dve", lambda en: en.tensor_reduce(gst[:, 1, :], ysq[:].rearrange("p (h i) -> p h i", h=8),
                                                                AX.X, ALU.add), reads=rr(ysq), writes=rr(gst))
                        ts("dve", gst[:, 2, :], gst[:, 0, :], 1.0 / 64, None, ALU.mult, None, [gst], [gst])
                        tt("dve", gst[:, 3, :], gst[:, 2, :], gst[:, 2, :], ALU.mult, [gst], [gst])
                        stt(gst[:, 4, :], gst[:, 1, :], 1.0 / 64, gst[:, 3, :], ALU.mult, ALU.subtract, [gst], [gst])
                        ts("dve", gst[:, 4, :], gst[:, 4, :], 64e-5, None, ALU.add, None, [gst], [gst])
                        tt("pool", gst[:, 5, :], gst[:, 4, :], nhalf[0:64, 0:8], ALU.pow, [gst, nhalf], [gst])
                        bc = lambda a: a.unsqueeze(2).to_broadcast([64, 8, 64])
                        yt3 = ytmp[:].rearrange("p (h i) -> p h i", h=8)
                        tt("pool", yt3, y3, bc(gst[:, 2, :]), ALU.subtract, [y_sb, gst], [ytmp])
                        tt("pool", yt3, yt3, bc(gst[:, 5, :]), ALU.mult, [ytmp, gst], [ytmp])
                        tt("pool", ytmp[:], ytmp[:], gnwb[:], ALU.mult, [ytmp, gnwb], [ytmp])
                        tt("pool", ytmp[:], ytmp[:], gnbb[:], ALU.add, [ytmp, gnbb], [ytmp])
                        ys3 = ysq[:].rearrange("p (h i) -> p h i", h=8)
                        tt("dve", ys3, v_sb[:, c, :].rearrange("p (h i) -> p h i", h=8), bc(bon[:, c, :]), ALU.mult,
                           [v_sb, bon], [ysq])
                        tt("pool", ytmp[:], ytmp[:], ysq[:], ALU.add, [ytmp, ysq], [ytmp])
                        pg = pj.next()
                        mm(pg[0:64, :], sg0[:, c * C:(c + 1) * C], wg0[:], True, False, [sg0, wg0], [pg])
                        mm(pg[0:64, :], sg1[:, c * C:(c + 1) * C], wg1[:], False, True, [sg1, wg1], [pg])
                        tt("dve", ytmp[:], ytmp[:], pg[0:64, :], ALU.mult, [ytmp, pg], [ytmp])
                        dbg_dump("ya", ytmp[:], [ytmp], (slice(t0 + c * C, t0 + (c + 1) * C), slice(None))) if b == 0 else None
                        if dbg_sub < 9:
                            continue
                        pq = pj.next()
                        for kc in range(4):
                            tr(pq[:, kc * C:(kc + 1) * C], ytmp[:, kc * 128:(kc + 1) * 128], ident[0:64, 0:64],
                               [ytmp, cst], [pq])
                        cp("act", ya[:, :, c * C:(c + 1) * C], pq[:, 0:4 * C].rearrange("p (k t) -> p k t", k=4),
                           [pq], [ya])
                    if dbg_sub >= 10:
                        dma(yaT_d[:, :, t0:t0 + NB], ya[:], [ya], [yaT_res], yaT_sem[blk_i % 2])
                    blk_i += 1
        S.barrier()
        stop_after = int(os.environ.get("MK_STOP", "99")) if debug else 99

        if stop_after >= 2:
          with ExitStack() as es:
            WD = sb(es, "WD", (128, 8, 2560), BF16)
            WT = sb(es, "WT", (128, 8, 72), BF16)
            load_const(es, "2", _CST2, cst2_d, _C2)
            xt = [sb(es, "xt2_%d" % i, (128, D)) for i in range(2)]
            for kc in range(8):
                for hf in range(2):
                    def consD(st_, kc=kc, hf=hf):
                        ts("pool" if hf else "dve", WD[:, kc, hf * 1280:(hf + 1) * 1280], st_[:, :1280], gcol[:, 0, kc:kc + 1], None,
                           ALU.mult, None, [st_, gcol], [WD])
                    load_weight(wD_d[kc * 128:(kc + 1) * 128, hf * 1280:(hf + 1) * 1280], 1280, consD)

                def consT(st_, kc=kc):
                    ts("dve", WT[:, kc, :], st_[:, :72], gcol[:, 0, kc:kc + 1], None, ALU.mult, None, [st_, gcol], [WT])
                load_weight(wT_d[kc * 128:(kc + 1) * 128, :], 72, consT)
            NB = DS_NB
            hT = sb(es, "hTd", (128, 8, NB), BF16)
            pTr = Pool([ps(es, "pTr2_%d" % i, (128, 8, 128), BF16) for i in range(1)])
            pj = Pool([ps(es, "pj2_%d" % i, (128, 512)) for i in range(3)])
            pW = Pool([ps(es, "pW2_%d" % i, (128, 2, 512)) for i in range(1)])
            po = ps(es, "po2", (128, 2, 512))
            qT = sb(es, "qT", (128, 4, NB), BF16)
            qiT = sb(es, "qiT", (128, 4, NB), BF16)
            kT_all = sb(es, "kT_all", (128, T), BF16)
            kiT_all = sb(es, "kiT_all", (128, T), BF16)
            vones = sb(es, "vones", (128, 16, 65), BF16)
            wi_sb = sb(es, "wi_sb", (128, 4, 8))
            rt1 = sb(es, "rt1", (128, NB))
            rt2 = sb(es, "rt2", (128, NB))
            acc = sb(es, "acc", (128, T))
            work = sb(es, "work", (128, T))
            relu_t = [sb(es, "relu%d" % i, (128, 512)) for i in range(2)]
            mx8 = sb(es, "mx8", (128, 8))
            maskb = sb(es, "maskb", (128, T), BF16)
            maskT = sb(es, "maskT", (128, 16, 128), BF16)
            causT = sb(es, "causT", (128, 128), BF16)
            eT = [sb(es, "eT%d" % i, (128, 8, 128), BF16) for i in range(2)]
            pT = [sb(es, "pT%d" % i, (128, 8, 128), BF16) for i in range(2)]
            rcp = sb(es, "rcp", (128, 8))
            yb = sb(es, "yb", (128, 512), BF16)
            ybT = [sb(es, "ybT%d" % i, (128, 4, NB), BF16) for i in range(2)]
            ybT_sem = [S.new_dsem() for _ in range(2)]
            cp("dve", causT[:], cc("causalT"), [cst], [causT])
            memset("pool", vones[:, :, 64:65], 1.0, [vones])
            WI_SCALE = float(8 ** -0.5 * 64 ** -0.5)
            blk_i = 0
            for b in range(NSEQ):
                for blk in range(T // NB):
                    tl0 = blk * NB
                    t0 = b * T + tl0
                    ybt = ybT[blk_i % 2]
                    make_hT(pTr, t0, NB // 128, hT, 0, x_d, xt=xt)
                    ropeC = cc("ropeC")[:, tl0:tl0 + NB]
                    ropeS = cc("ropeS")[:, tl0:tl0 + NB]

                    def proj(ct):
                        p_ = pj.next()
                        for kc in range(8):
                            mm(p_[:, :NB], WD[:, kc, ct * 128:(ct + 1) * 128], hT[:, kc, :], kc == 0, kc == 7, [WD, hT], [p_])
                        return p_

                    def rope(ct_a, ct_b, dst_ap, dst_tl):
                        pa = proj(ct_a)
                        tt("dve", rt1[:], pa[:, :NB], ropeC, ALU.mult, [pa, cst], [rt1])
                        pb = proj(ct_b)
                        tt("dve", rt2[:], pb[:, :NB], ropeS, ALU.mult, [pb, cst], [rt2])
                        tt("pool", dst_ap, rt1[:], rt2[:], ALU.add, [rt1, rt2], [dst_tl])

                    for i in range(4):
                        rope(i, 4 + i, qT[:, i, :], qT)
                    rope(8, 9, kT_all[:, tl0:tl0 + NB], kT_all)
                    for i in range(4):
                        rope(10 + i, 14 + i, qiT[:, i, :], qiT)
                    rope(18, 19, kiT_all[:, tl0:tl0 + NB], kiT_all)
                    for i in range(NB // 128):
                        p_ = pj.next()
                        for kc in range(8):
                            mm(p_[:, 0:72], hT[:, kc, i * 128:(i + 1) * 128], WT[:, kc, :], kc == 0, kc == 7, [WT, hT], [p_])
                        cp("dve", vones[:, blk * 4 + i, 0:64], p_[:, 0:64], [p_], [vones])
                        ts("dve", wi_sb[:, i, :], p_[:, 64:72], WI_SCALE, None, ALU.mult, None, [p_], [wi_sb])

                    for i in range(NB // 128):
                        qt = blk * 4 + i
                        if debug and (b * 16 + qt >= int(os.environ.get("MK_QT", "999"))):
                            continue
                        dsub = int(os.environ.get("MK_DSUB", "9")) if debug else 9
                        Sk = (qt + 1) * 128
                        tq = slice(i * 128, (i + 1) * 128)
                        if qt >= 2:
                            nseg = (Sk + 511) // 512
                            for sg in range(nseg):
                                s0 = sg * 512
                                sn = min(512, Sk - s0)
                                for h in range(8):
                                    rs_ = slice((h % 2) * 64, (h % 2) * 64 + 64)
                                    p_ = pj.next()
                                    mm(p_[:, :sn], qiT[rs_, h // 2, tq], kiT_all[rs_, s0:s0 + sn], True, True,
                                       [qiT, kiT_all], [p_])
                                    rl = relu_t[h % 2]
                                    act(rl[:, :sn], p_[:, :sn], AF.Relu, [p_], [rl])
                                    if h == 0:
                                        ts("dve", acc[:, s0:s0 + sn], rl[:, :sn], wi_sb[:, i, 0:1], None, ALU.mult, None,
                                           [rl, wi_sb], [acc])
                                    else:
                                        stt(acc[:, s0:s0 + sn], rl[:, :sn], wi_sb[:, i, h:h + 1], acc[:, s0:s0 + sn],
                                            ALU.mult, ALU.add, [rl, wi_sb, acc], [acc])
                            tt("pool", acc[:, Sk - 128:Sk], acc[:, Sk - 128:Sk], cc("causal_bias"), ALU.add, [acc, cst], [acc])
                            src = acc
                            for rnd in range(32):
                                S.op("dve", lambda en, src=src, Sk=Sk: en.max(out=mx8[:], in_=src[:, :Sk]),
                                     reads=rr(src), writes=rr(mx8))
                                if rnd < 31:
                                    S.op("dve", lambda en, src=src, Sk=Sk: en.match_replace(
                                        out=work[:, :Sk], in_to_replace=mx8[:], in_values=src[:, :Sk], imm_value=NEG),
                                        reads=rr(src, mx8), writes=rr(work))
                                    src = work
                            if dsub < 2:
                                continue
                            ts("dve", maskb[:, :Sk], acc[:, :Sk], mx8[:, 7:8], None, ALU.is_ge, None, [acc, mx8], [maskb])
                            for g in range((qt + 1 + 3) // 4):
                                pm = pj.next()
                                pmb = pm[:].bitcast(BF16)
                                nk = min(4, qt + 1 - g * 4)
                                for j in range(nk):
                                    kt = g * 4 + j
                                    tr(pmb[:, j * 128:(j + 1) * 128], maskb[:, kt * 128:(kt + 1) * 128], identb[:],
                                       [maskb, identb], [pm])
                                cp("act", maskT[:, g * 4:g * 4 + nk, :],
                                   pmb[:, 0:nk * 128].rearrange("p (k t) -> p k t", t=128), [pm], [maskT])
                        if dsub < 3:
                            continue
                        for kt in range(qt + 1):
                            psc = pW.next()
                            for h in range(8):
                                rs_ = slice((h % 2) * 64, (h % 2) * 64 + 64)
                                mm(psc[:, h % 2, (h // 2) * 128:(h // 2) * 128 + 128], kT_all[rs_, kt * 128:(kt + 1) * 128],
                                   qT[rs_, h // 2, tq], True, True, [kT_all, qT], [psc])
                            e_ = eT[kt % 2]
                            act(e_[:].rearrange("p (a h) t -> p a (h t)", a=2), psc[:, :, :], AF.Exp, [psc], [e_], scale=0.125)
                            if qt >= 2:
                                p__ = pT[kt % 2]
                                tt("pool" if kt % 2 else "dve", p__[:], e_[:],
                                   maskT[:, kt, :].unsqueeze(1).to_broadcast([128, 8, 128]), ALU.mult, [e_, maskT], [p__])
                            elif kt == qt:
                                p__ = pT[kt % 2]
                                tt("dve", p__[:], e_[:], causT[:].unsqueeze(1).to_broadcast([128, 8, 128]), ALU.mult,
                                   [e_, causT], [p__])
                            else:
                                p__ = e_
                            for h in range(8):
                                mm(po[:, h // 4, (h % 4) * 65:(h % 4) * 65 + 65], p__[:, (h % 2) * 4 + h // 2, :], vones[:, kt, :],
                                   kt == 0 and h % 4 == 0, kt == qt, [p__, vones], [po], skip=True)
                        pov = po[:, :, 0:260].rearrange("p a (h e) -> p a h e", e=65)
                        S.op("dve", lambda en, pov=pov: en.reciprocal(rcp[:].rearrange("p (a h) -> p a h", a=2), pov[:, :, :, 64]),
                             reads=rr(po), writes=rr(rcp))
                        tt("dve", yb[:].rearrange("p (a h e) -> p a h e", a=2, h=4), pov[:, :, :, 0:64],
                           rcp[:].rearrange("p (a h) -> p a h", a=2).unsqueeze(3).to_broadcast([128, 2, 4, 64]), ALU.mult,
                           [po, rcp], [yb])
                        if "yb" in dbg_d and b == 0:
                            cp("dve", acc[:, 0:512], yb[:], [yb], [acc])
                            dbg_dump("yb", acc[:, 0:512], [acc], (slice(t0 + i * 128, t0 + (i + 1) * 128), slice(None)))
                        pm = pj.next()
                        pmb = pm[:].bitcast(BF16)
                        for kc in range(4):
                            tr(pmb[:, kc * 128:(kc + 1) * 128], yb[:, kc * 128:(kc + 1) * 128], identb[:], [yb, identb], [pm])
                        cp("act", ybt[:, :, tq], pmb[:, 0:512].rearrange("p (k t) -> p k t", t=128), [pm], [ybt])
                    dma(ybT_d[:, :, t0:t0 + NB], ybt[:], [ybt], [ybT_res], ybT_sem[blk_i % 2])
                    blk_i += 1
          S.barrier()

        if stop_after >= 3:
          with ExitStack() as es:
            WG = sb(es, "WG", (128, 8, 2048), BF16)
            wbr = sb(es, "wbr", (128, 8, 1024), BF16)
            wout = sb(es, "wout", (128, 8, 1024), BF16)
            for kc in range(8):
                for hf in range(2):
                    def consG(st_, kc=kc, hf=hf):
                        ts("pool" if hf else "dve", WG[:, kc, hf * 1024:(hf + 1) * 1024], st_[:, :1024], gcol[:, 0, kc:kc + 1], None,
                           ALU.mult, None, [st_, gcol], [WG])
                    load_weight(wG_d[kc * 128:(kc + 1) * 128, hf * 1024:(hf + 1) * 1024], 1024, consG)

                def consB(st_, kc=kc):
                    cp("pool", wbr[:, kc, :], st_[:, :1024], [st_], [wbr])
                load_weight(wbr_d[kc * 128:(kc + 1) * 128, :], 1024, consB)

                def consO(st_, kc=kc):
                    cp("dve", wout[:, kc, :], st_[:, :1024], [st_], [wout])
                load_weight(wout_d[kc * 128:(kc + 1) * 128, :], 1024, consO)
            NB = MG_NB
            hT = sb(es, "hTm", (128, 8, NB), BF16)
            xk = sb(es, "xk", (128, 4, D))
            pTr = Pool([ps(es, "pTr3_%d" % i, (128, 8, 128), BF16) for i in range(1)])
            pj = Pool([ps(es, "pj3_%d" % i, (128, 512)) for i in range(6)])
            yaL = sb(es, "yaL", (128, 4, NB), BF16)
            ybL = sb(es, "ybL", (128, 4, NB), BF16)
            yl_sem = S.new_dsem()
            yl_sem2 = S.new_dsem()
            sgA = sb(es, "sgA", (128, NB))
            sgB = sb(es, "sgB", (128, NB))
            mA = sb(es, "mA", (128, NB))
            mB = sb(es, "mB", (128, NB))
            mgT = sb(es, "mgT", (128, 8, NB), BF16)
            x1t = [sb(es, "x1t%d" % i, (128, D)) for i in range(2)]
            x1_sem = [S.new_dsem() for _ in range(2)]
            n1 = 0
            for bi in range(NTOK // NB):
                t0 = bi * NB
                make_hT(pTr, t0, 4, hT, 0, x_d, keep=xk)
                dma(yaL[:], yaT_d[:, :, t0:t0 + NB], [yaT_res], [yaL], yl_sem)
                dma(ybL[:], ybT_d[:, :, t0:t0 + NB], [ybT_res], [ybL], yl_sem2)
                for dt_ in range(8):
                    ds_ = slice(dt_ * 128, (dt_ + 1) * 128)
                    pa = pj.next()
                    for kc in range(4):
                        mm(pa[:, :NB], wbr[:, kc, ds_], yaL[:, kc, :], kc == 0, kc == 3, [wbr, yaL], [pa])
                    pb = pj.next()
                    for kc in range(4):
                        mm(pb[:, :NB], wbr[:, 4 + kc, ds_], ybL[:, kc, :], kc == 0, kc == 3, [wbr, ybL], [pb])
                    g0 = pj.next()
                    for kc in range(8):
                        mm(g0[:, :NB], WG[:, kc, dt_ * 128:(dt_ + 1) * 128], hT[:, kc, :], kc == 0, kc == 7, [WG, hT], [g0])
                    g1 = pj.next()
                    for kc in range(8):
                        mm(g1[:, :NB], WG[:, kc, 1024 + dt_ * 128:1024 + (dt_ + 1) * 128], hT[:, kc, :], kc == 0, kc == 7,
                           [WG, hT], [g1])
                    act(sgA[:], g0[:, :NB], AF.Sigmoid, [g0], [sgA])
                    act(sgB[:], g1[:, :NB], AF.Sigmoid, [g1], [sgB])
                    tt("dve", mA[:], pa[:, :NB], sgA[:], ALU.mult, [pa, sgA], [mA])
                    tt("dve", mB[:], pb[:, :NB], sgB[:], ALU.mult, [pb, sgB], [mB])
                    tt("pool", mgT[:, dt_, :], mA[:], mB[:], ALU.add, [mA, mB], [mgT])
                for tt_ in range(4):
                    x1 = x1t[n1 % 2]
                    for hf in range(2):
                        po = pj.next()
                        for kc in range(8):
                            mm(po[:, :], mgT[:, kc, tt_ * 128:(tt_ + 1) * 128], wout[:, kc, hf * 512:(hf + 1) * 512],
                               kc == 0, kc == 7, [mgT, wout], [po])
                        tt("dve", x1[:, hf * 512:(hf + 1) * 512], po[:, :], xk[:, tt_, hf * 512:(hf + 1) * 512], ALU.add,
                           [po, xk], [x1])
                    dma(x1_d[t0 + tt_ * 128:t0 + (tt_ + 1) * 128, :], x1[:], [x1], [x1_res[bi]], x1_sem[n1 % 2])
                    if t0 < T:
                        dbg_dump("x1", x1[:], [x1], (slice(t0 + tt_ * 128, t0 + (tt_ + 1) * 128), slice(None)))
                    n1 += 1
          S.barrier()

        if stop_after >= 4:
          with ExitStack() as es:
            wup = sb(es, "wup", (128, 8, FFH), BF16)
            wdn = sb(es, "wdn", (128, 32, D), BF16)
            gzb = sb(es, "gzb", (128, D))
            dma(gzb[:], gz_d.partition_broadcast(128), [], [gzb], d0())
            for kc in range(8):
                for q4 in range(4):
                    def consU(st_, kc=kc, q4=q4):
                        if True:
                            ts("pool" if q4 % 2 else "dve", wup[:, kc, q4 * 1024:(q4 + 1) * 1024], st_[:, 0:1024], gcol[:, 1, kc:kc + 1], None,
                               ALU.mult, None, [st_, gcol], [wup])
                    load_weight(wup_d[kc * 128:(kc + 1) * 128, q4 * 1024:(q4 + 1) * 1024], 1024, consU)
            for g in range(32):
                def consDn(st_, g=g):
                    cp("pool" if g % 2 else "dve", wdn[:, g, :], st_[:, 0:1024], [st_], [wdn])
                load_weight(wdn_d[g * 128:(g + 1) * 128, :], 1024, consDn)
            NB = FF_NB
            NT4 = NB // 128
            hT = sb(es, "hTf", (128, 8, NB), BF16)
            xk = sb(es, "xkf", (128, NT4, D))
            pTr = Pool([ps(es, "pTr4_%d" % i, (128, 8, 128), BF16) for i in range(1)])
            pj = Pool([ps(es, "pj4_%d" % i, (128, 512)) for i in range(6)])
            aT = sb(es, "aT", (128, 32, NB), BF16)
            rl = [sb(es, "rl%d" % i, (128, NB), BF16) for i in range(2)]
            xx = sb(es, "x2", (128, D))
            ot = [sb(es, "ot%d" % i, (128, D)) for i in range(2)]
            o_sem = [S.new_dsem() for _ in range(2)]
            st2 = [sb(es, "st2_%d" % i, (128, 4)) for i in range(2)]
            n2 = 0
            for bi in range(NTOK // NB):
                t0 = bi * NB
                make_hT(pTr, t0, NT4, hT, 0, x1_d, keep=xk, src_res=[x1_res[t0 // 512]])
                for ht in range(32):
                    pu = pj.next()
                    for kc in range(8):
                        mm(pu[:, :NB], wup[:, kc, ht * 128:(ht + 1) * 128], hT[:, kc, :], kc == 0, kc == 7, [wup, hT], [pu])
                    r_ = rl[ht % 2]
                    act(r_[:], pu[:, :NB], AF.Relu, [pu], [r_])
                    tt("pool" if ht % 2 else "dve", aT[:, ht, :], r_[:], r_[:], ALU.mult, [r_], [aT])
                for tt_ in range(NT4):
                    oo = ot[n2 % 2]
                    s2 = st2[n2 % 2]
                    for hf in range(2):
                        pd = pj.next()
                        for ht in range(32):
                            mm(pd[:, :], aT[:, ht, tt_ * 128:(tt_ + 1) * 128], wdn[:, ht, hf * 512:(hf + 1) * 512],
                               ht == 0, ht == 31, [aT, wdn], [pd])
                        tt("dve", xx[:, hf * 512:(hf + 1) * 512], pd[:, :], xk[:, tt_, hf * 512:(hf + 1) * 512], ALU.add,
                           [pd, xk], [xx])
                    act(oo[:], xx[:], AF.Square, [xx], [oo, s2], accum_out=s2[:, 0:1])
                    ts("dve", s2[:, 1:2], s2[:, 0:1], 1.0 / D, 1e-6, ALU.mult, ALU.add, [s2], [s2])
                    tt("pool", s2[:, 2:3], s2[:, 1:2], nhalf[:, 0:1], ALU.pow, [s2, nhalf], [s2])
                    stt(oo[:], xx[:], s2[:, 2:3], gzb[:], ALU.mult, ALU.mult, [xx, s2, gzb], [oo])
                    out_dmas.append(dma(out_d[t0 + tt_ * 128:t0 + (tt_ + 1) * 128, :], oo[:], [oo], [], o_sem[n2 % 2]))
                    n2 += 1

        S.finish(out_dmas)
        S.emit()
    return nc


def _swap_halves(cols):
    c = np.asarray(cols).reshape(-1, 2, 32)
    return c[:, ::-1, :].reshape(-1)


def _layout_inputs(inp):
    f = lambda a: np.ascontiguousarray(np.asarray(a, dtype=np.float32))
    w_in = f(inp["w_in"])[0]
    mu = f(inp["mu_shift"])[0]
    colsA = np.concatenate([np.arange(0, 1024), np.arange(1536, 1824)])
    colsV = np.arange(1024, 1536)
    base = 1824
    q = base + np.arange(512)
    k = base + 512 + np.arange(64)
    v = base + 576 + np.arange(64)
    qi = base + 640 + np.arange(512)
    ki = base + 1152 + np.arange(64)
    wi = base + 1216 + np.arange(8)
    colsD = np.concatenate([q, _swap_halves(q), k, k, _swap_halves(k), _swap_halves(k),
                            qi, _swap_halves(qi), ki, ki, _swap_halves(ki), _swap_halves(ki)])
    colsT = np.concatenate([v, wi])
    colsG = 1824 + 1224 + np.arange(2048)
    per_ch = lambda a: f(a)[0].reshape(4, 128).T
    pp = np.concatenate([per_ch(inp["decay_bias"]), per_ch(inp["iclr_bias"]), per_ch(inp["k_k"]),
                         per_ch(inp["k_a"]), per_ch(inp["r_k"])], axis=1)
    shared = {
        "wA": f(w_in[:, colsA]), "wV": f(w_in[:, colsV]), "muA": f(mu[colsA]), "muV": f(mu[colsV]),
        "wD": f(w_in[:, colsD]), "wT": f(w_in[:, colsT]), "wG": f(w_in[:, colsG]),
        "gcols": f(np.concatenate([f(inp["g_mix"])[0].reshape(8, 128).T, f(inp["g_ffn"])[0].reshape(8, 128).T], axis=1)),
        "gfin": f(inp["g_final"]),
        "wlora": f(np.concatenate([f(inp["w_decay_up"])[0], f(inp["w_iclr_up"])[0]], axis=0)),
        "wgate": f(inp["w_gate_up"])[0], "pp": f(pp),
        "gnw": f(inp["gn_w"])[0], "gnb": f(inp["gn_b"])[0],
        "wbr": f(f(inp["w_branch"])[0].reshape(1024, 1024)), "wout": f(inp["w_out"])[0],
        "wup": f(inp["w_ffn_up"])[0], "wdn": f(inp["w_ffn_down"])[0], "cstA": _CSTA, "cst1": _CST1, "cst2": _CST2,
    }
    x = f(inp["x"])
    maps = []
    for c in range(NCORES):
        m = dict(shared)
        m["x"] = np.ascontiguousarray(x[c * NSEQ:(c + 1) * NSEQ].reshape(NTOK, D))
        maps.append(m)
    return maps


def kernel(**inputs):
    maps = _layout_inputs(inputs)
    nc = build_nc()
    res = run_bass_kernel_spmd(nc, maps, core_ids=list(range(NCORES)))
    outs = [np.asarray(r["out"], dtype=np.float32).reshape(NSEQ, T, D) for r in res.results]
    return np.concatenate(outs, axis=0)
```

```python
import os
from contextlib import ExitStack

import numpy as np
import concourse.bass as bass
import concourse.mybir as mybir
from concourse.bass_utils import run_bass_kernel_spmd

F32 = mybir.dt.float32
BF16 = mybir.dt.bfloat16
ALU = mybir.AluOpType
AF = mybir.ActivationFunctionType
AX = mybir.AxisListType

NCORES = 8
T = 2048
D = 1024
NSEQ = 2
NTOK = NSEQ * T
C = 64
C0 = float(np.exp(-0.5))
NEG = -1.0e30
RW_NB = 128
DS_NB = 512
MG_NB = 512
FF_NB = 256
FFH = 4096


class Res:
    __slots__ = ("name", "w", "rd", "rd_dma")

    def __init__(self, name):
        self.name = name
        self.w = None
        self.rd = {}
        self.rd_dma = []


class DmaSem:
    def __init__(self, sem):
        self.sem = sem
        self.count = 0


class _Op:
    __slots__ = ("id", "eng", "fn", "deps", "dsem", "val", "signal")


class Sched:
    ENGS = ("pe", "act", "dve", "pool", "sp")

    def __init__(self, nc, es):
        self.nc = nc
        self.es = es
        self.ops = []
        self.per = {e: [] for e in self.ENGS}
        self.sem = {e: es.enter_context(nc.semaphore("s_" + e)) for e in self.ENGS}
        self.n_dsem = 0
        self.last = {e: None for e in self.ENGS}
        self.dma_since_barrier = []

    def new_dsem(self):
        self.n_dsem += 1
        return DmaSem(self.es.enter_context(self.nc.semaphore("d%d" % self.n_dsem)))

    def op(self, eng, fn, reads=(), writes=(), dsem=None):
        o = _Op()
        o.id = len(self.ops)
        o.eng = eng
        o.fn = fn
        o.dsem = dsem
        o.signal = False
        o.val = None
        deps = {}

        def add(d, kind):
            if d is None:
                return
            if kind == "raw" or d not in deps:
                deps[d] = kind

        for r in reads:
            add(r.w, "raw")
        for w in writes:
            add(w.w, "waw")
            for d in w.rd.values():
                add(d, "war")
            for d in w.rd_dma:
                add(d, "war")
        o.deps = deps
        for r in reads:
            if dsem is not None:
                r.rd_dma.append(o.id)
            else:
                r.rd[eng] = o.id
        for w in writes:
            w.w = o.id
            w.rd = {}
            w.rd_dma = []
        if dsem is not None:
            dsem.count += 16
            o.val = dsem.count
            self.dma_since_barrier.append(o.id)
        self.ops.append(o)
        self.per[eng].append(o)
        self.last[eng] = o.id
        return o

    def barrier(self):
        lasts = [v for v in self.last.values() if v is not None]
        dmas = list(self.dma_since_barrier)
        self.dma_since_barrier = []
        for e in self.ENGS:
            o = self.op(e, lambda en: en.nop())
            for d in lasts + dmas:
                if d != o.id:
                    o.deps[d] = "raw"

    def finish(self, dma_ops):
        o = self.op("sp", lambda en: en.nop())
        for d in dma_ops:
            o.deps[d.id] = "raw"

    def emit(self):
        ops = self.ops
        for o in ops:
            for d, kind in o.deps.items():
                p = ops[d]
                if p.dsem is not None:
                    continue
                if p.eng == o.eng and o.dsem is None and o.eng in ("pe", "sp"):
                    continue
                p.signal = True
        cnt = {e: 0 for e in self.ENGS}
        for o in ops:
            if o.dsem is None and o.signal:
                cnt[o.eng] += 1
                o.val = cnt[o.eng]
        sem = self.sem

        def run(eng, en):
            known = {}
            for o in self.per[eng]:
                need = {}
                for d, kind in o.deps.items():
                    p = ops[d]
                    if p.dsem is not None:
                        key, s, v = ("d", id(p.dsem)), p.dsem.sem, p.val
                    else:
                        if not p.signal:
                            continue
                        if p.eng == eng and o.dsem is None and eng in ("pe", "sp"):
                            continue
                        key, s, v = ("e", p.eng), sem[p.eng], p.val
                    if known.get(key, 0) >= v:
                        continue
                    if key not in need or need[key][1] < v:
                        need[key] = (s, v)
                for key, (s, v) in need.items():
                    en.wait_ge(s, v)
                    known[key] = v
                ins = o.fn(en)
                if o.dsem is not None:
                    ins.then_inc(o.dsem.sem, 16)
                elif o.signal:
                    ins.then_inc(sem[eng], 1)

        with self.nc.Block() as block:
            @block.tensor
            def _(en):
                run("pe", en)

            @block.scalar
            def _(en):
                run("act", en)

            @block.vector
            def _(en):
                run("dve", en)

            @block.gpsimd
            def _(en):
                run("pool", en)

            @block.sync
            def _(en):
                run("sp", en)


class Tl:
    def __init__(self, h, name, nres=1):
        self.h = h
        self.name = name
        self.rs = [Res("%s.%d" % (name, i)) for i in range(nres)]

    @property
    def r(self):
        return self.rs[0]

    def __getitem__(self, k):
        return self.h[k]


class Pool:
    def __init__(self, tiles):
        self.tiles = tiles
        self.i = 0

    def next(self):
        t = self.tiles[self.i % len(self.tiles)]
        self.i += 1
        return t


class _CB:
    def __init__(self):
        self.cols = {}
        self.parts = []
        self.off = 0

    def put(self, name, arr):
        a = np.zeros((128, arr.shape[1]), np.float32)
        a[: arr.shape[0]] = arr
        self.cols[name] = (self.off, arr.shape[1], arr.shape[0])
        self.parts.append(a)
        self.off += arr.shape[1]

    def arr(self):
        return np.ascontiguousarray(np.concatenate(self.parts, axis=1))


def _const_f32():
    A, B1, B2 = _CB(), _CB(), _CB()
    A.put("ident", np.eye(128, dtype=np.float32))
    s = np.arange(64)[:, None]
    t = np.arange(64)[None, :]
    m1 = np.concatenate([(s < t), (s <= t)], axis=1).astype(np.float32)
    B1.put("mask1", m1)
    B1.put("maskL", (s > t).astype(np.float32))
    B1.put("eye8", np.eye(64, dtype=np.float32))
    rm = np.ones((128, RW_NB), np.float32)
    rm[:, ::C] = 0.0
    B1.put("reset", rm)
    bo = np.zeros((128, 128), np.float32)
    bo[:64, :64] = 1.0
    bo[64:, 64:] = 1.0
    A.put("blockones", bo)
    hi = np.zeros((128, 2), np.float32)
    hi[:64, 0] = 1.0
    hi[64:, 1] = 1.0
    A.put("headind", hi)
    tq = np.arange(128)[:, None]
    kk = np.arange(128)[None, :]
    A.put("causal_bias", np.where(kk <= tq, 0.0, NEG).astype(np.float32))
    A.put("causalT", (tq <= kk).astype(np.float32))
    inv = (1.0 / (10000.0 ** (np.arange(0, 64, 2, dtype=np.float32) / np.float32(64)))).astype(np.float32)
    ang = (np.arange(T, dtype=np.float32)[:, None] * inv[None, :]).astype(np.float32)
    cs = np.cos(ang).astype(np.float32).T
    sn = np.sin(ang).astype(np.float32).T
    d = np.arange(128) % 64
    B2.put("ropeC", cs[d % 32])
    sg = np.where(d < 32, -1.0, 1.0).astype(np.float32)[:, None]
    B2.put("ropeS", sn[d % 32] * sg)
    return A, B1, B2


_CA, _C1, _C2 = _const_f32()
_CSTA, _CST1, _CST2 = _CA.arr(), _C1.arr(), _C2.arr()


def build_nc(debug=None):
    nc = bass.Bass("TRN2", target_bir_lowering=False)
    dt_in = lambda name, shape: nc.dram_tensor(name, list(shape), F32, kind="ExternalInput").ap()
    x_d = dt_in("x", (NTOK, D))
    wA_d = dt_in("wA", (D, 1312))
    wV_d = dt_in("wV", (D, 512))
    muA_d = dt_in("muA", (1312,))
    muV_d = dt_in("muV", (512,))
    wD_d = dt_in("wD", (D, 2560))
    wT_d = dt_in("wT", (D, 72))
    wG_d = dt_in("wG", (D, 2048))
    gcol_d = dt_in("gcols", (128, 16))
    gz_d = dt_in("gfin", (D,))
    wlora_d = dt_in("wlora", (128, 512))
    wgate_d = dt_in("wgate", (160, 512))
    pp_d = dt_in("pp", (128, 20))
    gnw_d = dt_in("gnw", (512,))
    gnb_d = dt_in("gnb", (512,))
    wbr_d = dt_in("wbr", (1024, 1024))
    wout_d = dt_in("wout", (D, D))
    wup_d = dt_in("wup", (D, FFH))
    wdn_d = dt_in("wdn", (FFH, D))
    cstA_d = dt_in("cstA", _CSTA.shape)
    cst1_d = dt_in("cst1", _CST1.shape)
    cst2_d = dt_in("cst2", _CST2.shape)
    out_d = nc.dram_tensor("out", [NTOK, D], F32, kind="ExternalOutput").ap()
    yaT_d = nc.dram_tensor("yaT_scr", [128, 4, NTOK], BF16, kind="Internal").ap()
    ybT_d = nc.dram_tensor("ybT_scr", [128, 4, NTOK], BF16, kind="Internal").ap()
    x1_d = nc.dram_tensor("x1_scr", [NTOK, D], F32, kind="Internal").ap()
    dbg_d = {}
    dbg_sem = {}
    if debug:
        for name, shape in debug.items():
            dbg_d[name] = nc.dram_tensor("dbg_" + name, list(shape), F32, kind="ExternalOutput").ap()

    top = ExitStack()
    with top:
        S = Sched(nc, top)
        out_dmas = []
        yaT_res = Res("yaT_scr")
        ybT_res = Res("ybT_scr")
        x1_res = [Res("x1_scr%d" % i) for i in range(NTOK // 512)]

        uid = [0]

        def sb(es, name, shape, dt=F32, nres=1):
            uid[0] += 1
            return Tl(es.enter_context(nc.sbuf_tensor("sb%d_%s" % (uid[0], name), list(shape), dt)), name, nres)

        def ps(es, name, shape, dt=F32):
            uid[0] += 1
            return Tl(es.enter_context(nc.psum_tensor("ps%d_%s" % (uid[0], name), list(shape), dt)), name)

        def rr(*xs):
            out = []
            for x in xs:
                if isinstance(x, Tl):
                    out.extend(x.rs)
                elif isinstance(x, Res):
                    out.append(x)
                else:
                    out.extend(x)
            return out

        def dma(out_ap, in_ap, reads, writes, dsem):
            return S.op("sp", lambda en: en.dma_start(out=out_ap, in_=in_ap),
                        reads=rr(*reads), writes=rr(*writes), dsem=dsem)

        def mm(out_ap, lhsT, rhs, start, stop, reads, writes, skip=False):
            return S.op("pe", lambda en: en.matmul(out_ap, lhsT, rhs, start=start, stop=stop, skip_group_check=skip),
                        reads=rr(*reads), writes=rr(*writes))

        def tr(out_ap, in_ap, ident, reads, writes):
            return S.op("pe", lambda en: en.transpose(out_ap, in_ap, ident),
                        reads=rr(*reads), writes=rr(*writes))

        def act(out_ap, in_ap, func, reads, writes, bias=0.0, scale=1.0, accum_out=None):
            return S.op("act", lambda en: en.activation(out_ap, in_ap, func, bias=bias, scale=scale,
                                                        accum_out=accum_out),
                        reads=rr(*reads), writes=rr(*writes))

        def tt(eng, out_ap, a, b, op, reads, writes):
            return S.op(eng, lambda en: en.tensor_tensor(out_ap, a, b, op), reads=rr(*reads), writes=rr(*writes))

        def ts(eng, out_ap, a, s1, s2, op0, op1, reads, writes):
            if op1 is None:
                return S.op(eng, lambda en: en.tensor_scalar(out_ap, a, s1, None, op0),
                            reads=rr(*reads), writes=rr(*writes))
            return S.op(eng, lambda en: en.tensor_scalar(out_ap, a, s1, s2, op0, op1),
                        reads=rr(*reads), writes=rr(*writes))

        def stt(out_ap, a, sc, b, op0, op1, reads, writes):
            return S.op("dve", lambda en: en.scalar_tensor_tensor(out_ap, a, sc, b, op0, op1),
                        reads=rr(*reads), writes=rr(*writes))

        def rsqrt(out_ap, in_ap, reads, writes):
            act(out_ap, in_ap, AF.Sqrt, reads, writes)
            S.op("dve", lambda en: en.reciprocal(out_ap, out_ap), reads=rr(*writes), writes=rr(*writes))

        def cp(eng, out_ap, in_ap, reads, writes):
            if eng == "act":
                return S.op("act", lambda en: en.copy(out_ap, in_ap), reads=rr(*reads), writes=rr(*writes))
            return S.op(eng, lambda en: en.tensor_copy(out_ap, in_ap), reads=rr(*reads), writes=rr(*writes))

        def memset(eng, ap, val, writes):
            return S.op(eng, lambda en: en.memset(ap, val), writes=rr(*writes))

        def dbg_dump(name, src_ap, reads, dst_slice=None):
            if name not in dbg_d:
                return
            dst = dbg_d[name] if dst_slice is None else dbg_d[name][dst_slice]
            if name not in dbg_sem:
                dbg_sem[name] = S.new_dsem()
            out_dmas.append(dma(dst, src_ap, reads, [], dbg_sem[name]))

        cst = Tl(None, "cstgroup", 0)
        ctiles = {}

        def load_const(es_, key, arr, src_d, cb):
            t_ = sb(es_, "cs_sb" + key, arr.shape)
            dma(t_[:], src_d, [], [t_], S.new_dsem())
            cst.rs.extend(t_.rs)
            for nm in cb.cols:
                ctiles[nm] = (t_, cb.cols[nm])

        load_const(top, "A", _CSTA, cstA_d, _CA)

        def cc(name, rows=None):
            t_, (o, n, r0) = ctiles[name]
            return t_[: (rows or r0), o:o + n]

        def d0():
            return S.new_dsem()

        nhalf = sb(top, "nhalf", (128, 256))
        memset("pool", nhalf[:], -0.5, [nhalf])
        identb = sb(top, "identb", (128, 128), BF16)
        cp("dve", identb[:], cc("ident"), [cst], [identb])
        gcol = sb(top, "gcol", (128, 2, 8))
        dma(gcol[:].rearrange("p a k -> p (a k)"), gcol_d, [], [gcol], d0())
        pp = sb(top, "pp", (128, 20))
        dma(pp[:], pp_d, [], [pp], d0())

        WSTN = 1312
        wst = None
        wst_sem = [S.new_dsem() for _ in range(2)]
        wst_i = [0]

        def load_weight(src_ap, ncols, consume):
            i = wst_i[0] % 2
            wst_i[0] += 1
            dma(wst[i][:, :ncols], src_ap, [], [wst[i]], wst_sem[i])
            consume(wst[i])

        xt_sem = [S.new_dsem() for _ in range(2)]
        xt_i = [0]
        xs_bf = [sb(top, "xsbf%d" % i, (128, D), BF16) for i in range(2)]
        stat = [sb(top, "stat%d" % i, (128, 4)) for i in range(2)]

        def make_hT(es_ps, tok0, ntile, hT, col0, src_d, xt=None, keep=None, src_res=()):
            for i in range(ntile):
                k = xt_i[0] % 2
                xt_i[0] += 1
                if keep is not None:
                    xin = keep
                    xap = keep[:, i, :]
                    dma(xap, src_d[tok0 + i * 128: tok0 + (i + 1) * 128, :], src_res, [keep], xt_sem[k])
                else:
                    xin = xt[k % len(xt)]
                    xap = xin[:]
                    dma(xap, src_d[tok0 + i * 128: tok0 + (i + 1) * 128, :], src_res, [xin], xt_sem[k % len(xt)])
                st = stat[k]
                act(xs_bf[k][:], xap, AF.Square, [xin], [xs_bf[k], st], accum_out=st[:, 0:1])
                ts("dve", st[:, 1:2], st[:, 0:1], 1.0 / D, 1e-6, ALU.mult, ALU.add, [st], [st])
                rsqrt(st[:, 2:3], st[:, 1:2], [st], [st])
                ts("dve", xs_bf[k][:], xap, st[:, 2:3], None, ALU.mult, None, [xin, st], [xs_bf[k]])
                pt = es_ps.next()
                for kc in range(8):
                    tr(pt[:, kc, :], xs_bf[k][:, kc * 128:(kc + 1) * 128], identb[:], [xs_bf[k], identb], [pt])
                cp("act" if i % 2 else "dve", hT[:, :, col0 + i * 128: col0 + (i + 1) * 128], pt[:, :, :], [pt], [hT])

        with ExitStack() as es:
            W1A = sb(es, "W1A", (128, 8, 1312), BF16)
            W2A = sb(es, "W2A", (128, 8, 1312), BF16)
            W1V = sb(es, "W1V", (128, 8, 512), BF16)
            W2V = sb(es, "W2V", (128, 8, 512), BF16)
            with ExitStack() as es_w:
                wst = [sb(es_w, "wst1_%d" % i, (128, WSTN)) for i in range(2)]
                mub = sb(es_w, "mub", (128, 1824))
                omb = sb(es_w, "omb", (128, 1824))
                dma(mub[:, 0:1312], muA_d.partition_broadcast(128), [], [mub], d0())
                dma(mub[:, 1312:1824], muV_d.partition_broadcast(128), [], [mub], d0())
                ts("pool", omb[:], mub[:], -1.0, 1.0, ALU.mult, ALU.add, [mub], [omb])
                for kc in range(8):
                    def consA(st_, kc=kc):
                        stt(W1A[:, kc, :], st_[:, :1312], gcol[:, 0, kc:kc + 1], omb[:, 0:1312], ALU.mult, ALU.mult,
                            [st_, gcol, omb], [W1A])
                        stt(W2A[:, kc, :], st_[:, :1312], gcol[:, 0, kc:kc + 1], mub[:, 0:1312], ALU.mult, ALU.mult,
                            [st_, gcol, mub], [W2A])
                    load_weight(wA_d[kc * 128:(kc + 1) * 128, :], 1312, consA)

                    def consV(st_, kc=kc):
                        stt(W1V[:, kc, :], st_[:, :512], gcol[:, 0, kc:kc + 1], omb[:, 1312:1824], ALU.mult, ALU.mult,
                            [st_, gcol, omb], [W1V])
                        stt(W2V[:, kc, :], st_[:, :512], gcol[:, 0, kc:kc + 1], mub[:, 1312:1824], ALU.mult, ALU.mult,
                            [st_, gcol, mub], [W2V])
                    load_weight(wV_d[kc * 128:(kc + 1) * 128, :], 512, consV)
            S.barrier()
            load_const(es, "1", _CST1, cst1_d, _C1)
            xt = [sb(es, "xt%d" % i, (128, D)) for i in range(1)]
            wlora = sb(es, "wlora", (128, 512))
            wg0 = sb(es, "wg0", (128, 512))
            wg1 = sb(es, "wg1", (32, 512))
            gnwb = sb(es, "gnwb", (64, 512))
            gnbb = sb(es, "gnbb", (64, 512))
            dma(wlora[:], wlora_d, [], [wlora], d0())
            dma(wg0[:], wgate_d[0:128, :], [], [wg0], d0())
            dma(wg1[:], wgate_d[128:160, :], [], [wg1], d0())
            dma(gnwb[:], gnw_d.partition_broadcast(64), [], [gnwb], d0())
            dma(gnbb[:], gnb_d.partition_broadcast(64), [], [gnbb], d0())

            NB = RW_NB
            NCH = NB // C
            G = T // C
            hT = sb(es, "hT", (128, 8, NB + 2), BF16)
            pTr = Pool([ps(es, "pTr", (128, 8, 128), BF16)])
            pjp = ps(es, "pjp", (128, 512))
            pbon = ps(es, "pbon", (128, 512))
            pI = ps(es, "pI", (128, 2, 512))
            pI3 = ps(es, "pI3", (128, 512))
            pSA = ps(es, "pSA", (128, 512))
            pSB = ps(es, "pSB", (128, 512))
            r_sb = sb(es, "r_sb", (128, 4, NB))
            k_sb = sb(es, "k_sb", (128, 4, NB))
            wa_sb = sb(es, "wa_sb", (128, NB))
            tmp = [sb(es, "rt%d" % i, (128, NB)) for i in range(10)]

            class PSet:
                pass
            psets = []
            for i in range(3):
                P_ = PSet()
                P_.sg0 = sb(es, "sg0_%d" % i, (128, NB))
                P_.sg1 = sb(es, "sg1_%d" % i, (32, NB))
                P_.v = sb(es, "v_sb%d" % i, (64, NCH, 512))
                P_.AR = [sb(es, "AR%d_%d" % (h, i), (128, NCH, 2, C)) for h in range(4)]
                P_.Bt = [sb(es, "Bt%d_%d" % (h, i), (128, NCH, C)) for h in range(4)]
                P_.Kt = [sb(es, "Kt%d_%d" % (h, i), (128, NCH, C)) for h in range(4)]
                P_.BKh = [sb(es, "BKh%d_%d" % (h, i), (128, NCH, 2, C)) for h in range(4)]
                P_.wC = sb(es, "wC%d" % i, (128, 4, NCH))
                P_.bon = sb(es, "bon%d" % i, (64, NCH, 8))
                P_.ya = sb(es, "yaT%d" % i, (128, 4, NB), BF16)
                P_.ya_sem = S.new_dsem()
                psets.append(P_)
            csets = []
            for i in range(2):
                Q_ = PSet()
                Q_.MA = sb(es, "MA%d" % i, (64, 8, 2 * C))
                Q_.KA = sb(es, "KA%d" % i, (64, 8, 2 * C))
                Q_.Tf = sb(es, "Tf%d" % i, (64, 8, C))
                Q_.BKtok = sb(es, "BKtok%d" % i, (64, 4, 2, 128))
                Q_.y = sb(es, "y_sb%d" % i, (64, 512))
                csets.append(Q_)
            ML = [sb(es, "ML%d" % i, (64, 8, 2, C)) for i in range(2)]
            TT = [sb(es, "TT%d" % i, (64, 8, C)) for i in range(2)]
            ST = sb(es, "ST", (128, 4, C))
            X_sb = sb(es, "X_sb", (64, 8, C))
            U_sb = sb(es, "U_sb", (64, 512))
            ysq = sb(es, "ysq", (64, 512))
            ytmp = sb(es, "ytmp", (64, 512))
            gst = sb(es, "gst", (64, 6, 8))
            ident = cc("ident")
            m1b = cc("mask1").unsqueeze(1).to_broadcast([64, 8, 2 * C])
            mLb = cc("maskL").unsqueeze(1).to_broadcast([64, 8, C])
            eyb = cc("eye8").unsqueeze(1).to_broadcast([64, 8, C])
            hrow = lambda h: slice((h % 2) * 64, (h % 2) * 64 + 64)
            HORD = [0, 2, 4, 6, 1, 3, 5, 7]
            dbg_chunks = int(os.environ.get("MK_CHUNKS", "999")) if debug else 999

            def prep_task(b, blk, P_):
                t0 = b * T + blk * NB
                if blk == 0:
                    memset("pool", hT[:, :, 0:2], 0.0, [hT])
                else:
                    cp("dve", hT[:, :, 1:2], hT[:, :, NB + 1:NB + 2], [hT], [hT])
                make_hT(pTr, t0, NB // 128, hT, 2, x_d, xt=xt)
                yield
                for ct in range(11):
                    rows = 32 if ct == 10 else 128
                    c0 = ct * 128
                    p_ = pjp
                    for kc in range(8):
                        mm(p_[:rows, :NB], W1A[:, kc, c0:c0 + rows], hT[:, kc, 2:NB + 2], kc == 0, False, [W1A, hT], [p_])
                        mm(p_[:rows, :NB], W2A[:, kc, c0:c0 + rows], hT[:, kc, 1:NB + 1], False, kc == 7, [W2A, hT], [p_])
                    if ct < 4:
                        cp("act", r_sb[:, ct, :], p_[:, :NB], [p_], [r_sb])
                    elif ct < 8:
                        cp("dve", k_sb[:, ct - 4, :], p_[:, :NB], [p_], [k_sb])
                    elif ct == 8:
                        act(wa_sb[0:64, :], p_[0:64, :NB], AF.Tanh, [p_], [wa_sb])
                        cp("dve", wa_sb[64:128, :], p_[64:128, :NB], [p_], [wa_sb])
                    elif ct == 9:
                        act(P_.sg0[:], p_[:, :NB], AF.Sigmoid, [p_], [P_.sg0])
                    else:
                        act(P_.sg1[:], p_[0:32, :NB], AF.Sigmoid, [p_], [P_.sg1])
                    if ct % 3 == 2:
                        yield
                for c in range(NCH):
                    p_ = pjp
                    for kc in range(8):
                        mm(p_[0:64, :], hT[:, kc, 2 + c * C:2 + (c + 1) * C], W1V[:, kc, :], kc == 0, False, [W1V, hT], [p_])
                        mm(p_[0:64, :], hT[:, kc, 1 + c * C:1 + (c + 1) * C], W2V[:, kc, :], False, kc == 7, [W2V, hT], [p_])
                    cp("act", P_.v[:, c, :], p_[0:64, :], [p_], [P_.v])
                yield
                v4 = lambda a: a[:].rearrange("p (c t) -> p c t", t=C)
                for hp in range(4):
                    cs_ = slice(hp * 128, (hp + 1) * 128)
                    ppc = lambda j, hp=hp: pp[:, j * 4 + hp: j * 4 + hp + 1]
                    sgd, icl, cum, e_in, e_ng, e_ex, e_rm, kkn, kmod, t9 = tmp
                    AR, Bt, Kt, BKh = P_.AR, P_.Bt, P_.Kt, P_.BKh
                    p_ = pjp
                    mm(p_[:, :NB], wlora[0:64, cs_], wa_sb[0:64, :], True, True, [wlora, wa_sb], [p_])
                    act(sgd[:], p_[:, :NB], AF.Sigmoid, [p_, pp], [sgd], bias=ppc(0))
                    p_ = pbon
                    mm(p_[:, 256:256 + NB], wlora[64:128, cs_], wa_sb[64:128, :], True, True, [wlora, wa_sb], [p_])
                    act(icl[:], p_[:, 256:256 + NB], AF.Sigmoid, [p_, pp], [icl], bias=ppc(1))
                    S.op("dve", lambda en, cum=cum, sgd=sgd: en.tensor_tensor_scan(
                        cum[:], cc("reset"), sgd[:], 0.0, ALU.mult, ALU.add), reads=rr(cst, sgd), writes=rr(cum))
                    yield
                    act(e_in[:], cum[:], AF.Exp, [cum], [e_in], scale=-C0)
                    act(e_ng[:], cum[:], AF.Exp, [cum], [e_ng], scale=C0)
                    tt("dve", t9[:], cum[:], sgd[:], ALU.subtract, [cum, sgd], [t9])
                    act(e_ex[:], t9[:], AF.Exp, [t9], [e_ex], scale=-C0)
                    cum3 = cum[:].rearrange("p (c t) -> p c t", t=C)
                    tt("dve", t9[:].rearrange("p (c t) -> p c t", t=C),
                       cum3[:, :, C - 1:C].to_broadcast([128, NCH, C]), cum3, ALU.subtract, [cum], [t9])
                    act(e_rm[:], t9[:], AF.Exp, [t9], [e_rm], scale=-C0)
                    cp("dve", P_.wC[:, hp, :], e_in[:].rearrange("p (c t) -> p c t", t=C)[:, :, C - 1], [e_in], [P_.wC])
                    kx = k_sb[:, hp, :]
                    ts("dve", kkn[:], kx, ppc(2), None, ALU.mult, None, [k_sb, pp], [kkn])
                    tt("dve", t9[:], kkn[:], kkn[:], ALU.mult, [kkn], [t9])
                    p_ = pjp
                    mm(p_[:, :NB], cc("blockones"), t9[:], True, True, [cst, t9], [p_])
                    ts("dve", t9[:], p_[:, :NB], 1e-24, None, ALU.max, None, [p_], [t9])
                    yield
                    rsqrt(t9[:], t9[:], [t9], [t9])
                    tt("dve", kkn[:], kkn[:], t9[:], ALU.mult, [kkn, t9], [kkn])
                    ts("dve", t9[:], icl[:], -1.0, ppc(3), ALU.add, ALU.mult, [icl, pp], [t9])
                    stt(kmod[:], t9[:], 1.0, kx, ALU.add, ALU.mult, [t9, k_sb], [kmod])
                    tt("dve", icl[:], icl[:], kkn[:], ALU.mult, [icl, kkn], [icl])
                    stt(AR[hp][:, :, 0, :], v4(kkn), -1.0, v4(e_ex), ALU.mult, ALU.mult, [kkn, e_ex], [AR[hp]])
                    tt("dve", AR[hp][:, :, 1, :], r_sb[:, hp, :].rearrange("p (c t) -> p c t", t=C), v4(e_in),
                       ALU.mult, [r_sb, e_in], [AR[hp]])
                    yield
                    tt("dve", Bt[hp][:], v4(icl), v4(e_ng), ALU.mult, [icl, e_ng], [Bt[hp]])
                    tt("dve", Kt[hp][:], v4(kmod), v4(e_ng), ALU.mult, [kmod, e_ng], [Kt[hp]])
                    tt("dve", BKh[hp][:, :, 0, :], v4(icl), v4(e_rm), ALU.mult, [icl, e_rm], [BKh[hp]])
                    tt("dve", BKh[hp][:, :, 1, :], v4(kmod), v4(e_rm), ALU.mult, [kmod, e_rm], [BKh[hp]])
                    stt(t9[:], r_sb[:, hp, :], ppc(4), kmod[:], ALU.mult, ALU.mult, [r_sb, pp, kmod], [t9])
                    for c in range(NCH):
                        mm(pbon[0:64, c * 8 + hp * 2: c * 8 + hp * 2 + 2], t9[:, c * C:(c + 1) * C],
                           cc("headind"), True, True, [t9, cst], [pbon])
                    yield
                cp("act", P_.bon[:].rearrange("p c h -> p (c h)"), pbon[0:64, 0:NCH * 8], [pbon], [P_.bon])

            def inv_task(c, P_, Q_):
                AR, Bt, Kt, BKh = P_.AR, P_.Bt, P_.Kt, P_.BKh
                MA, KA = Q_.MA, Q_.KA
                for h in HORD:
                    hp, rs_ = h // 2, hrow(h)
                    arh = AR[hp][rs_, c, :, :].rearrange("p a t -> p (a t)")
                    mm(pI[0:64, h % 2, hp * 128:hp * 128 + 128], Bt[hp][rs_, c, :], arh, True, True,
                       [Bt[hp], AR[hp]], [pI])
                m1p = cc("mask1").unsqueeze(1).unsqueeze(1).to_broadcast([64, 2, 4, 2 * C])
                tt("dve", MA[:].rearrange("p (hp par) m -> p par hp m", par=2),
                   pI[0:64, :, :].rearrange("p par (hp m) -> p par hp m", m=2 * C), m1p, ALU.mult, [pI, cst], [MA])
                mlc = ML[0]
                cp("dve", mlc[:, :, 0, :], MA[:, :, 0:C], [MA], [mlc])
                tcur = TT[0]
                tt("dve", tcur[:], MA[:, :, 0:C], eyb, ALU.add, [MA, cst], [tcur])
                yield
                for h in HORD:
                    hp, rs_ = h // 2, hrow(h)
                    mm(pI[0:64, h % 2, hp * C:(hp + 1) * C], AR[hp][rs_, c, 0, :], Bt[hp][rs_, c, :], True, True,
                       [AR[hp], Bt[hp]], [pI])
                tt("dve", mlc[:, :, 1, :].rearrange("p (hp par) s -> p par hp s", par=2),
                   pI[0:64, :, 0:4 * C].rearrange("p par (hp s) -> p par hp s", s=C),
                   cc("maskL").unsqueeze(1).unsqueeze(1).to_broadcast([64, 2, 4, C]), ALU.mult, [pI, cst], [mlc])
                yield
                for h in HORD:
                    hp, rs_ = h // 2, hrow(h)
                    arh = AR[hp][rs_, c, :, :].rearrange("p a t -> p (a t)")
                    mm(pI[0:64, h % 2, hp * 128:hp * 128 + 128], Kt[hp][rs_, c, :], arh, True, True,
                       [Kt[hp], AR[hp]], [pI])
                tt("dve", KA[:].rearrange("p (hp par) m -> p par hp m", par=2),
                   pI[0:64, :, :].rearrange("p par (hp m) -> p par hp m", m=2 * C), m1p, ALU.mult, [pI, cst], [KA])
                yield

                def squares(mlc, mln, lev):
                    for h in range(8):
                        if lev < 5:
                            mm(pI[0:64, h // 4, (h % 4) * 128:(h % 4) * 128 + C], mlc[:, h, 1, :], mlc[:, h, 0, :],
                               True, True, [mlc], [pI])
                        mm(pI[0:64, h // 4, (h % 4) * 128 + C:(h % 4) * 128 + 2 * C], mlc[:, h, 0, :],
                           mlc[:, h, 1, :], True, True, [mlc], [pI])
                    if lev < 5:
                        cp("act", mln[:].rearrange("p (a h) x s -> p a (h x s)", a=2), pI[0:64, :, :], [pI], [mln])
                    else:
                        cp("act", mln[:, :, 1, :].rearrange("p (a h) s -> p a h s", a=2),
                           pI[0:64, :, :].rearrange("p a (h x s) -> p a h x s", h=4, x=2)[:, :, :, 1, :], [pI], [mln])

                def tupdate(mln, tcur, tnew):
                    for h in range(8):
                        mm(pI3[0:64, h * C:(h + 1) * C], mln[:, h, 1, :], tcur[:, h, :], True, True, [mln, tcur], [pI3])
                    tt("dve", tnew[:], pI3[0:64, :].rearrange("p (h s) -> p h s", h=8), tcur[:], ALU.add,
                       [pI3, tcur], [tnew])

                for lev in range(1, 6):
                    mln = ML[lev % 2]
                    squares(mlc, mln, lev)
                    tnew = Q_.Tf if lev == 5 else TT[lev % 2]
                    tupdate(mln, tcur, tnew)
                    mlc, tcur = mln, tnew
                    yield
                for hp in range(4):
                    for a in range(2):
                        tr(pI[0:64, hp // 2, (hp % 2) * 256 + a * 128:(hp % 2) * 256 + a * 128 + 128],
                           BKh[hp][:, c, a, :], ident, [BKh[hp], cst], [pI])
                cp("act", Q_.BKtok[:].rearrange("p (a h) x m -> p a (h x m)", a=2), pI[0:64, :, :], [pI], [Q_.BKtok])
                yield

            def state_task(c, P_, Q_):
                AR, v_sb = P_.AR, P_.v
                MA, KA, tcur, BKtok, y_sb = Q_.MA, Q_.KA, Q_.Tf, Q_.BKtok, Q_.y
                bank = lambda h: (pSA if h % 2 == 0 else pSB)
                bank2 = lambda h: (pSA if h < 4 else pSB)
                for h in HORD:
                    hp, rs_ = h // 2, hrow(h)
                    mm(bank(h)[0:64, hp * C:(hp + 1) * C], AR[hp][rs_, c, 0, :], ST[rs_, hp, :], True, True,
                       [AR[hp], ST], [bank(h)])
                for h in range(8):
                    mm(bank2(h)[0:64, 256 + (h % 4) * C:256 + (h % 4 + 1) * C], KA[:, h, 0:C], v_sb[:, c, h * C:(h + 1) * C],
                       True, True, [KA, v_sb], [bank2(h)])
                X4 = X_sb[:].rearrange("p (hp par) s -> p par hp s", par=2)
                cp("act", X4[:, 0, :, :], pSA[0:64, 0:4 * C].rearrange("p (hp s) -> p hp s", s=C), [pSA], [X_sb])
                cp("act", X4[:, 1, :, :], pSB[0:64, 0:4 * C].rearrange("p (hp s) -> p hp s", s=C), [pSB], [X_sb])
                tt("dve", X_sb[:, 0:4, :], X_sb[:, 0:4, :], pSA[0:64, 256:512].rearrange("p (h s) -> p h s", s=C), ALU.add,
                   [X_sb, pSA], [X_sb])
                tt("dve", X_sb[:, 4:8, :], X_sb[:, 4:8, :], pSB[0:64, 256:512].rearrange("p (h s) -> p h s", s=C), ALU.add,
                   [X_sb, pSB], [X_sb])
                yield
                for h in range(8):
                    mm(pSA[0:64, h * C:(h + 1) * C], tcur[:, h, :], X_sb[:, h, :], True, True, [tcur, X_sb], [pSA])
                cp("dve", U_sb[:], pSA[0:64, :], [pSA], [U_sb])
                yield
                for h in HORD:
                    hp, rs_ = h // 2, hrow(h)
                    mm(bank(h)[0:64, hp * C:(hp + 1) * C], AR[hp][rs_, c, 1, :], ST[rs_, hp, :], True, True,
                       [AR[hp], ST], [bank(h)])
                for h in range(8):
                    o_ = bank2(h)[0:64, 256 + (h % 4) * C:256 + (h % 4 + 1) * C]
                    mm(o_, MA[:, h, C:2 * C], U_sb[:, h * C:(h + 1) * C], True, False, [MA, U_sb], [bank2(h)])
                    mm(o_, KA[:, h, C:2 * C], v_sb[:, c, h * C:(h + 1) * C], False, True, [KA, v_sb], [bank2(h)])
                Y4 = y_sb[:].rearrange("p (hp par s) -> p par hp s", par=2, s=C)
                cp("act", Y4[:, 0, :, :], pSA[0:64, 0:4 * C].rearrange("p (hp s) -> p hp s", s=C), [pSA], [y_sb])
                cp("act", Y4[:, 1, :, :], pSB[0:64, 0:4 * C].rearrange("p (hp s) -> p hp s", s=C), [pSB], [y_sb])
                tt("dve", y_sb[:, 0:256], y_sb[:, 0:256], pSA[0:64, 256:512], ALU.add, [y_sb, pSA], [y_sb])
                tt("dve", y_sb[:, 256:512], y_sb[:, 256:512], pSB[0:64, 256:512], ALU.add, [y_sb, pSB], [y_sb])
                yield
                pS = pSB
                for hp in range(4):
                    mm(pS[:, hp * 128:(hp + 1) * 128], BKtok[:, hp, 0, :], U_sb[:, hp * 128:(hp + 1) * 128],
                       True, False, [BKtok, U_sb], [pS])
                    mm(pS[:, hp * 128:(hp + 1) * 128], BKtok[:, hp, 1, :], v_sb[:, c, hp * 128:(hp + 1) * 128],
                       False, True, [BKtok, v_sb], [pS])
                for hp in range(4):
                    for hh in range(2):
                        rs_ = slice(hh * 64, hh * 64 + 64)
                        stt(ST[rs_, hp, :], ST[rs_, hp, :], P_.wC[rs_, hp, c:c + 1],
                            pS[rs_, hp * 128 + hh * 64: hp * 128 + hh * 64 + 64], ALU.mult, ALU.add, [ST, P_.wC, pS], [ST])
                yield

            def ypost_task(b, g, c, P_, Q_):
                y_sb, v_sb = Q_.y, P_.v
                t0c = b * T + g * C
                y3 = y_sb[:].rearrange("p (h i) -> p h i", h=8)
                S.op("dve", lambda en: en.tensor_reduce(gst[:, 0, :], y3, AX.X, ALU.add), reads=rr(y_sb), writes=rr(gst))
                act(ysq[:], y_sb[:], AF.Square, [y_sb], [ysq])
                S.op("dve", lambda en: en.tensor_reduce(gst[:, 1, :], ysq[:].rearrange("p (h i) -> p h i", h=8),
                                                        AX.X, ALU.add), reads=rr(ysq), writes=rr(gst))
                ts("dve", gst[:, 2, :], gst[:, 0, :], 1.0 / 64, None, ALU.mult, None, [gst], [gst])
                tt("dve", gst[:, 3, :], gst[:, 2, :], gst[:, 2, :], ALU.mult, [gst], [gst])
                stt(gst[:, 4, :], gst[:, 1, :], 1.0 / 64, gst[:, 3, :], ALU.mult, ALU.subtract, [gst], [gst])
                ts("dve", gst[:, 4, :], gst[:, 4, :], 64e-5, None, ALU.add, None, [gst], [gst])
                rsqrt(gst[:, 5, :], gst[:, 4, :], [gst], [gst])
                yield
                bc = lambda a: a.unsqueeze(2).to_broadcast([64, 8, 64])
                yt3 = ytmp[:].rearrange("p (h i) -> p h i", h=8)
                tt("dve", yt3, y3, bc(gst[:, 2, :]), ALU.subtract, [y_sb, gst], [ytmp])
                tt("dve", yt3, yt3, bc(gst[:, 5, :]), ALU.mult, [ytmp, gst], [ytmp])
                tt("dve", ytmp[:], ytmp[:], gnwb[:], ALU.mult, [ytmp, gnwb], [ytmp])
                tt("dve", ytmp[:], ytmp[:], gnbb[:], ALU.add, [ytmp, gnbb], [ytmp])
                ys3 = ysq[:].rearrange("p (h i) -> p h i", h=8)
                tt("dve", ys3, v_sb[:, c, :].rearrange("p (h i) -> p h i", h=8), bc(P_.bon[:, c, :]), ALU.mult,
                   [v_sb, P_.bon], [ysq])
                tt("dve", ytmp[:], ytmp[:], ysq[:], ALU.add, [ytmp, ysq], [ytmp])
                pg = pjp
                mm(pg[0:64, :], P_.sg0[:, c * C:(c + 1) * C], wg0[:], True, False, [P_.sg0, wg0], [pg])
                mm(pg[0:64, :], P_.sg1[:, c * C:(c + 1) * C], wg1[:], False, True, [P_.sg1, wg1], [pg])
                tt("dve", ytmp[:], ytmp[:], pg[0:64, :], ALU.mult, [ytmp, pg], [ytmp])
                if b == 0:
                    dbg_dump("ya", ytmp[:], [ytmp], (slice(t0c, t0c + C), slice(None)))
                yield
                pq = pjp
                for kc in range(4):
                    tr(pq[:, kc * C:(kc + 1) * C], ytmp[:, kc * 128:(kc + 1) * 128], ident[0:64, 0:64], [ytmp, cst], [pq])
                cp("act", P_.ya[:, :, c * C:(c + 1) * C], pq[:, 0:4 * C].rearrange("p (k t) -> p k t", k=4), [pq], [P_.ya])
                if c == NCH - 1:
                    tb = b * T + (g // NCH) * NB
                    dma(yaT_d[:, :, tb:tb + NB], P_.ya[:], [P_.ya], [yaT_res], P_.ya_sem)
                yield

            def pgen_slice(pg_, j, n):
                cnt = 0
                while True:
                    if j < n - 1 and cnt >= (24 // n):
                        return
                    try:
                        next(pg_)
                    except StopIteration:
                        return
                    cnt += 1
                    yield

            def run_rr(gens):
                gens = list(gens)
                while gens:
                    for g_ in list(gens):
                        try:
                            next(g_)
                        except StopIteration:
                            gens.remove(g_)

            for b in range(NSEQ):
                memset("dve", ST[:], 0.0, [ST])
                Gn = min(G, dbg_chunks)
                run_rr([prep_task(b, 0, psets[0])])
                for k in range(Gn + 2):
                    gens = []
                    if k < Gn:
                        gens.append(inv_task(k % NCH, psets[(k // NCH) % 3], csets[k % 2]))
                    if 1 <= k <= Gn:
                        g = k - 1
                        gens.append(state_task(g % NCH, psets[(g // NCH) % 3], csets[g % 2]))
                    if 2 <= k <= Gn + 1:
                        g = k - 2
                        gens.append(ypost_task(b, g, g % NCH, psets[(g // NCH) % 3], csets[g % 2]))
                    if k % NCH == 0:
                        nb_ = k // NCH + 1
                        pgen = prep_task(b, nb_, psets[nb_ % 3]) if nb_ * NCH < Gn else None
                    if pgen is not None:
                        gens.append(pgen_slice(pgen, k % NCH, NCH))
                    run_rr(gens)

        S.barrier()
        stop_after = int(os.environ.get("MK_STOP", "99")) if debug else 99

        if stop_after >= 2:
          with ExitStack() as es:
            WD = sb(es, "WD", (128, 8, 2560), BF16)
            WT = sb(es, "WT", (128, 8, 72), BF16)
            load_const(es, "2", _CST2, cst2_d, _C2)
            xt = [sb(es, "xt2_%d" % i, (128, D)) for i in range(2)]
            wst = [sb(es, "wst2_%d" % i, (128, WSTN)) for i in range(2)]
            for kc in range(8):
                for hf in range(2):
                    def consD(st_, kc=kc, hf=hf):
                        ts("pool" if hf else "dve", WD[:, kc, hf * 1280:(hf + 1) * 1280], st_[:, :1280], gcol[:, 0, kc:kc + 1], None,
                           ALU.mult, None, [st_, gcol], [WD])
                    load_weight(wD_d[kc * 128:(kc + 1) * 128, hf * 1280:(hf + 1) * 1280], 1280, consD)

                def consT(st_, kc=kc):
                    ts("dve", WT[:, kc, :], st_[:, :72], gcol[:, 0, kc:kc + 1], None, ALU.mult, None, [st_, gcol], [WT])
                load_weight(wT_d[kc * 128:(kc + 1) * 128, :], 72, consT)
            NB = DS_NB
            hT = sb(es, "hTd", (128, 8, NB), BF16)
            pTr = Pool([ps(es, "pTr2_%d" % i, (128, 8, 128), BF16) for i in range(1)])
            pj = Pool([ps(es, "pj2_%d" % i, (128, 512)) for i in range(3)])
            pW = Pool([ps(es, "pW2_%d" % i, (128, 2, 512)) for i in range(1)])
            po = ps(es, "po2", (128, 2, 512))
            qT = sb(es, "qT", (128, 4, NB), BF16)
            qiT = sb(es, "qiT", (128, 4, NB), BF16)
            kT_all = sb(es, "kT_all", (128, T), BF16)
            kiT_all = sb(es, "kiT_all", (128, T), BF16)
            vones = sb(es, "vones", (128, 16, 65), BF16)
            wi_sb = sb(es, "wi_sb", (128, 4, 8))
            rt1 = sb(es, "rt1", (128, NB))
            rt2 = sb(es, "rt2", (128, NB))
            acc = sb(es, "acc", (128, T))
            work = sb(es, "work", (128, T))
            relu_t = [sb(es, "relu%d" % i, (128, 512)) for i in range(2)]
            mx8 = sb(es, "mx8", (128, 8))
            maskb = sb(es, "maskb", (128, T), BF16)
            maskT = sb(es, "maskT", (128, 16, 128), BF16)
            causT = sb(es, "causT", (128, 128), BF16)
            eT = [sb(es, "eT%d" % i, (128, 8, 128), BF16) for i in range(2)]
            pT = [sb(es, "pT%d" % i, (128, 8, 128), BF16) for i in range(2)]
            rcp = sb(es, "rcp", (128, 8))
            yb = sb(es, "yb", (128, 512), BF16)
            ybT = [sb(es, "ybT%d" % i, (128, 4, NB), BF16) for i in range(2)]
            ybT_sem = [S.new_dsem() for _ in range(2)]
            cp("dve", causT[:], cc("causalT"), [cst], [causT])
            memset("pool", vones[:, :, 64:65], 1.0, [vones])
            WI_SCALE = float(8 ** -0.5 * 64 ** -0.5)
            blk_i = 0
            for b in range(NSEQ):
                for blk in range(T // NB):
                    tl0 = blk * NB
                    t0 = b * T + tl0
                    ybt = ybT[blk_i % 2]
                    make_hT(pTr, t0, NB // 128, hT, 0, x_d, xt=xt)
                    ropeC = cc("ropeC")[:, tl0:tl0 + NB]
                    ropeS = cc("ropeS")[:, tl0:tl0 + NB]

                    def proj(ct):
                        p_ = pj.next()
                        for kc in range(8):
                            mm(p_[:, :NB], WD[:, kc, ct * 128:(ct + 1) * 128], hT[:, kc, :], kc == 0, kc == 7, [WD, hT], [p_])
                        return p_

                    def rope(ct_a, ct_b, dst_ap, dst_tl):
                        pa = proj(ct_a)
                        tt("dve", rt1[:], pa[:, :NB], ropeC, ALU.mult, [pa, cst], [rt1])
                        pb = proj(ct_b)
                        tt("dve", rt2[:], pb[:, :NB], ropeS, ALU.mult, [pb, cst], [rt2])
                        tt("pool", dst_ap, rt1[:], rt2[:], ALU.add, [rt1, rt2], [dst_tl])

                    for i in range(4):
                        rope(i, 4 + i, qT[:, i, :], qT)
                    rope(8, 9, kT_all[:, tl0:tl0 + NB], kT_all)
                    for i in range(4):
                        rope(10 + i, 14 + i, qiT[:, i, :], qiT)
                    rope(18, 19, kiT_all[:, tl0:tl0 + NB], kiT_all)
                    for i in range(NB // 128):
                        p_ = pj.next()
                        for kc in range(8):
                            mm(p_[:, 0:72], hT[:, kc, i * 128:(i + 1) * 128], WT[:, kc, :], kc == 0, kc == 7, [WT, hT], [p_])
                        cp("dve", vones[:, blk * 4 + i, 0:64], p_[:, 0:64], [p_], [vones])
                        ts("dve", wi_sb[:, i, :], p_[:, 64:72], WI_SCALE, None, ALU.mult, None, [p_], [wi_sb])

                    for i in range(NB // 128):
                        qt = blk * 4 + i
                        if debug and (b * 16 + qt >= int(os.environ.get("MK_QT", "999"))):
                            continue
                        dsub = int(os.environ.get("MK_DSUB", "9")) if debug else 9
                        Sk = (qt + 1) * 128
                        tq = slice(i * 128, (i + 1) * 128)
                        if qt >= 2:
                            nseg = (Sk + 511) // 512
                            for sg in range(nseg):
                                s0 = sg * 512
                                sn = min(512, Sk - s0)
                                for h in range(8):
                                    rs_ = slice((h % 2) * 64, (h % 2) * 64 + 64)
                                    p_ = pj.next()
                                    mm(p_[:, :sn], qiT[rs_, h // 2, tq], kiT_all[rs_, s0:s0 + sn], True, True,
                                       [qiT, kiT_all], [p_])
                                    rl = relu_t[h % 2]
                                    act(rl[:, :sn], p_[:, :sn], AF.Relu, [p_], [rl])
                                    if h == 0:
                                        ts("dve", acc[:, s0:s0 + sn], rl[:, :sn], wi_sb[:, i, 0:1], None, ALU.mult, None,
                                           [rl, wi_sb], [acc])
                                    else:
                                        stt(acc[:, s0:s0 + sn], rl[:, :sn], wi_sb[:, i, h:h + 1], acc[:, s0:s0 + sn],
                                            ALU.mult, ALU.add, [rl, wi_sb, acc], [acc])
                            tt("pool", acc[:, Sk - 128:Sk], acc[:, Sk - 128:Sk], cc("causal_bias"), ALU.add, [acc, cst], [acc])
                            src = acc
                            for rnd in range(32):
                                S.op("dve", lambda en, src=src, Sk=Sk: en.max(out=mx8[:], in_=src[:, :Sk]),
                                     reads=rr(src), writes=rr(mx8))
                                if rnd < 31:
                                    S.op("dve", lambda en, src=src, Sk=Sk: en.match_replace(
                                        out=work[:, :Sk], in_to_replace=mx8[:], in_values=src[:, :Sk], imm_value=NEG),
                                        reads=rr(src, mx8), writes=rr(work))
                                    src = work
                            if dsub < 2:
                                continue
                            ts("dve", maskb[:, :Sk], acc[:, :Sk], mx8[:, 7:8], None, ALU.is_ge, None, [acc, mx8], [maskb])
                            for g in range((qt + 1 + 3) // 4):
                                pm = pj.next()
                                pmb = pm[:].bitcast(BF16)
                                nk = min(4, qt + 1 - g * 4)
                                for j in range(nk):
                                    kt = g * 4 + j
                                    tr(pmb[:, j * 128:(j + 1) * 128], maskb[:, kt * 128:(kt + 1) * 128], identb[:],
                                       [maskb, identb], [pm])
                                cp("act", maskT[:, g * 4:g * 4 + nk, :],
                                   pmb[:, 0:nk * 128].rearrange("p (k t) -> p k t", t=128), [pm], [maskT])
                        if dsub < 3:
                            continue
                        for kt in range(qt + 1):
                            psc = pW.next()
                            for h in range(8):
                                rs_ = slice((h % 2) * 64, (h % 2) * 64 + 64)
                                mm(psc[:, h % 2, (h // 2) * 128:(h // 2) * 128 + 128], kT_all[rs_, kt * 128:(kt + 1) * 128],
                                   qT[rs_, h // 2, tq], True, True, [kT_all, qT], [psc])
                            e_ = eT[kt % 2]
                            act(e_[:].rearrange("p (a h) t -> p a (h t)", a=2), psc[:, :, :], AF.Exp, [psc], [e_], scale=0.125)
                            if qt >= 2:
                                p__ = pT[kt % 2]
                                tt("pool" if kt % 2 else "dve", p__[:], e_[:],
                                   maskT[:, kt, :].unsqueeze(1).to_broadcast([128, 8, 128]), ALU.mult, [e_, maskT], [p__])
                            elif kt == qt:
                                p__ = pT[kt % 2]
                                tt("dve", p__[:], e_[:], causT[:].unsqueeze(1).to_broadcast([128, 8, 128]), ALU.mult,
                                   [e_, causT], [p__])
                            else:
                                p__ = e_
                            for h in range(8):
                                mm(po[:, h // 4, (h % 4) * 65:(h % 4) * 65 + 65], p__[:, (h % 2) * 4 + h // 2, :], vones[:, kt, :],
                                   kt == 0 and h % 4 == 0, kt == qt, [p__, vones], [po], skip=True)
                        pov = po[:, :, 0:260].rearrange("p a (h e) -> p a h e", e=65)
                        S.op("dve", lambda en, pov=pov: en.reciprocal(rcp[:].rearrange("p (a h) -> p a h", a=2), pov[:, :, :, 64]),
                             reads=rr(po), writes=rr(rcp))
                        tt("dve", yb[:].rearrange("p (a h e) -> p a h e", a=2, h=4), pov[:, :, :, 0:64],
                           rcp[:].rearrange("p (a h) -> p a h", a=2).unsqueeze(3).to_broadcast([128, 2, 4, 64]), ALU.mult,
                           [po, rcp], [yb])
                        if "yb" in dbg_d and b == 0:
                            cp("dve", acc[:, 0:512], yb[:], [yb], [acc])
                            dbg_dump("yb", acc[:, 0:512], [acc], (slice(t0 + i * 128, t0 + (i + 1) * 128), slice(None)))
                        pm = pj.next()
                        pmb = pm[:].bitcast(BF16)
                        for kc in range(4):
                            tr(pmb[:, kc * 128:(kc + 1) * 128], yb[:, kc * 128:(kc + 1) * 128], identb[:], [yb, identb], [pm])
                        cp("act", ybt[:, :, tq], pmb[:, 0:512].rearrange("p (k t) -> p k t", t=128), [pm], [ybt])
                    dma(ybT_d[:, :, t0:t0 + NB], ybt[:], [ybt], [ybT_res], ybT_sem[blk_i % 2])
                    blk_i += 1
          S.barrier()

        if stop_after >= 3:
          with ExitStack() as es:
            WG = sb(es, "WG", (128, 8, 2048), BF16)
            wbr = sb(es, "wbr", (128, 8, 1024), BF16)
            wout = sb(es, "wout", (128, 8, 1024), BF16)
            wst = [sb(es, "wst3_%d" % i, (128, WSTN)) for i in range(2)]
            for kc in range(8):
                for hf in range(2):
                    def consG(st_, kc=kc, hf=hf):
                        ts("pool" if hf else "dve", WG[:, kc, hf * 1024:(hf + 1) * 1024], st_[:, :1024], gcol[:, 0, kc:kc + 1], None,
                           ALU.mult, None, [st_, gcol], [WG])
                    load_weight(wG_d[kc * 128:(kc + 1) * 128, hf * 1024:(hf + 1) * 1024], 1024, consG)

                def consB(st_, kc=kc):
                    cp("pool", wbr[:, kc, :], st_[:, :1024], [st_], [wbr])
                load_weight(wbr_d[kc * 128:(kc + 1) * 128, :], 1024, consB)

                def consO(st_, kc=kc):
                    cp("dve", wout[:, kc, :], st_[:, :1024], [st_], [wout])
                load_weight(wout_d[kc * 128:(kc + 1) * 128, :], 1024, consO)
            NB = MG_NB
            hT = sb(es, "hTm", (128, 8, NB), BF16)
            xk = sb(es, "xk", (128, 4, D))
            pTr = Pool([ps(es, "pTr3_%d" % i, (128, 8, 128), BF16) for i in range(1)])
            pj = Pool([ps(es, "pj3_%d" % i, (128, 512)) for i in range(6)])
            yaL = sb(es, "yaL", (128, 4, NB), BF16)
            ybL = sb(es, "ybL", (128, 4, NB), BF16)
            yl_sem = S.new_dsem()
            yl_sem2 = S.new_dsem()
            sgA = sb(es, "sgA", (128, NB))
            sgB = sb(es, "sgB", (128, NB))
            mA = sb(es, "mA", (128, NB))
            mB = sb(es, "mB", (128, NB))
            mgT = sb(es, "mgT", (128, 8, NB), BF16)
            x1t = [sb(es, "x1t%d" % i, (128, D)) for i in range(2)]
            x1_sem = [S.new_dsem() for _ in range(2)]
            n1 = 0
            for bi in range(NTOK // NB):
                t0 = bi * NB
                make_hT(pTr, t0, 4, hT, 0, x_d, keep=xk)
                dma(yaL[:], yaT_d[:, :, t0:t0 + NB], [yaT_res], [yaL], yl_sem)
                dma(ybL[:], ybT_d[:, :, t0:t0 + NB], [ybT_res], [ybL], yl_sem2)
                for dt_ in range(8):
                    ds_ = slice(dt_ * 128, (dt_ + 1) * 128)
                    pa = pj.next()
                    for kc in range(4):
                        mm(pa[:, :NB], wbr[:, kc, ds_], yaL[:, kc, :], kc == 0, kc == 3, [wbr, yaL], [pa])
                    pb = pj.next()
                    for kc in range(4):
                        mm(pb[:, :NB], wbr[:, 4 + kc, ds_], ybL[:, kc, :], kc == 0, kc == 3, [wbr, ybL], [pb])
                    g0 = pj.next()
                    for kc in range(8):
                        mm(g0[:, :NB], WG[:, kc, dt_ * 128:(dt_ + 1) * 128], hT[:, kc, :], kc == 0, kc == 7, [WG, hT], [g0])
                    g1 = pj.next()
                    for kc in range(8):
                        mm(g1[:, :NB], WG[:, kc, 1024 + dt_ * 128:1024 + (dt_ + 1) * 128], hT[:, kc, :], kc == 0, kc == 7,
                           [WG, hT], [g1])
                    act(sgA[:], g0[:, :NB], AF.Sigmoid, [g0], [sgA])
                    act(sgB[:], g1[:, :NB], AF.Sigmoid, [g1], [sgB])
                    tt("dve", mA[:], pa[:, :NB], sgA[:], ALU.mult, [pa, sgA], [mA])
                    tt("dve", mB[:], pb[:, :NB], sgB[:], ALU.mult, [pb, sgB], [mB])
                    tt("pool", mgT[:, dt_, :], mA[:], mB[:], ALU.add, [mA, mB], [mgT])
                for tt_ in range(4):
                    x1 = x1t[n1 % 2]
                    for hf in range(2):
                        po = pj.next()
                        for kc in range(8):
                            mm(po[:, :], mgT[:, kc, tt_ * 128:(tt_ + 1) * 128], wout[:, kc, hf * 512:(hf + 1) * 512],
                               kc == 0, kc == 7, [mgT, wout], [po])
                        tt("dve", x1[:, hf * 512:(hf + 1) * 512], po[:, :], xk[:, tt_, hf * 512:(hf + 1) * 512], ALU.add,
                           [po, xk], [x1])
                    dma(x1_d[t0 + tt_ * 128:t0 + (tt_ + 1) * 128, :], x1[:], [x1], [x1_res[bi]], x1_sem[n1 % 2])
                    if t0 < T:
                        dbg_dump("x1", x1[:], [x1], (slice(t0 + tt_ * 128, t0 + (tt_ + 1) * 128), slice(None)))
                    n1 += 1
          S.barrier()

        if stop_after >= 4:
          with ExitStack() as es:
            wup = sb(es, "wup", (128, 8, FFH), BF16)
            wdn = sb(es, "wdn", (128, 32, D), BF16)
            gzb = sb(es, "gzb", (128, D))
            wst = [sb(es, "wst4_%d" % i, (128, 1024)) for i in range(2)]
            dma(gzb[:], gz_d.partition_broadcast(128), [], [gzb], d0())
            for kc in range(8):
                for q4 in range(4):
                    def consU(st_, kc=kc, q4=q4):
                        if True:
                            ts("pool" if q4 % 2 else "dve", wup[:, kc, q4 * 1024:(q4 + 1) * 1024], st_[:, 0:1024], gcol[:, 1, kc:kc + 1], None,
                               ALU.mult, None, [st_, gcol], [wup])
                    load_weight(wup_d[kc * 128:(kc + 1) * 128, q4 * 1024:(q4 + 1) * 1024], 1024, consU)
            for g in range(32):
                def consDn(st_, g=g):
                    cp("pool" if g % 2 else "dve", wdn[:, g, :], st_[:, 0:1024], [st_], [wdn])
                load_weight(wdn_d[g * 128:(g + 1) * 128, :], 1024, consDn)
            NB = FF_NB
            NT4 = NB // 128
            hT = sb(es, "hTf", (128, 8, NB), BF16)
            xk = sb(es, "xkf", (128, NT4, D))
            pTr = Pool([ps(es, "pTr4_%d" % i, (128, 8, 128), BF16) for i in range(1)])
            pj = Pool([ps(es, "pj4_%d" % i, (128, 512)) for i in range(6)])
            aT = sb(es, "aT", (128, 32, NB), BF16)
            rl = [sb(es, "rl%d" % i, (128, NB), BF16) for i in range(2)]
            xx = sb(es, "x2", (128, D))
            ot = [sb(es, "ot%d" % i, (128, D)) for i in range(2)]
            o_sem = [S.new_dsem() for _ in range(2)]
            st2 = [sb(es, "st2_%d" % i, (128, 4)) for i in range(2)]
            n2 = 0
            for bi in range(NTOK // NB):
                t0 = bi * NB
                make_hT(pTr, t0, NT4, hT, 0, x1_d, keep=xk, src_res=[x1_res[t0 // 512]])
                for ht in range(32):
                    pu = pj.next()
                    for kc in range(8):
                        mm(pu[:, :NB], wup[:, kc, ht * 128:(ht + 1) * 128], hT[:, kc, :], kc == 0, kc == 7, [wup, hT], [pu])
                    r_ = rl[ht % 2]
                    act(r_[:], pu[:, :NB], AF.Relu, [pu], [r_])
                    tt("pool" if ht % 2 else "dve", aT[:, ht, :], r_[:], r_[:], ALU.mult, [r_], [aT])
                for tt_ in range(NT4):
                    oo = ot[n2 % 2]
                    s2 = st2[n2 % 2]
                    for hf in range(2):
                        pd = pj.next()
                        for ht in range(32):
                            mm(pd[:, :], aT[:, ht, tt_ * 128:(tt_ + 1) * 128], wdn[:, ht, hf * 512:(hf + 1) * 512],
                               ht == 0, ht == 31, [aT, wdn], [pd])
                        tt("dve", xx[:, hf * 512:(hf + 1) * 512], pd[:, :], xk[:, tt_, hf * 512:(hf + 1) * 512], ALU.add,
                           [pd, xk], [xx])
                    act(oo[:], xx[:], AF.Square, [xx], [oo, s2], accum_out=s2[:, 0:1])
                    ts("dve", s2[:, 1:2], s2[:, 0:1], 1.0 / D, 1e-6, ALU.mult, ALU.add, [s2], [s2])
                    rsqrt(s2[:, 2:3], s2[:, 1:2], [s2], [s2])
                    stt(oo[:], xx[:], s2[:, 2:3], gzb[:], ALU.mult, ALU.mult, [xx, s2, gzb], [oo])
                    out_dmas.append(dma(out_d[t0 + tt_ * 128:t0 + (tt_ + 1) * 128, :], oo[:], [oo], [], o_sem[n2 % 2]))
                    n2 += 1

        S.finish(out_dmas)
        S.emit()
    return nc


def _swap_halves(cols):
    c = np.asarray(cols).reshape(-1, 2, 32)
    return c[:, ::-1, :].reshape(-1)


def _layout_inputs(inp):
    f = lambda a: np.ascontiguousarray(np.asarray(a, dtype=np.float32))
    w_in = f(inp["w_in"])[0]
    mu = f(inp["mu_shift"])[0]
    colsA = np.concatenate([np.arange(0, 1024), np.arange(1536, 1824)])
    colsV = np.arange(1024, 1536)
    base = 1824
    q = base + np.arange(512)
    k = base + 512 + np.arange(64)
    v = base + 576 + np.arange(64)
    qi = base + 640 + np.arange(512)
    ki = base + 1152 + np.arange(64)
    wi = base + 1216 + np.arange(8)
    colsD = np.concatenate([q, _swap_halves(q), k, k, _swap_halves(k), _swap_halves(k),
                            qi, _swap_halves(qi), ki, ki, _swap_halves(ki), _swap_halves(ki)])
    colsT = np.concatenate([v, wi])
    colsG = 1824 + 1224 + np.arange(2048)
    per_ch = lambda a: f(a)[0].reshape(4, 128).T
    pp = np.concatenate([per_ch(inp["decay_bias"]), per_ch(inp["iclr_bias"]), per_ch(inp["k_k"]),
                         per_ch(inp["k_a"]), per_ch(inp["r_k"])], axis=1)
    shared = {
        "wA": f(w_in[:, colsA]), "wV": f(w_in[:, colsV]), "muA": f(mu[colsA]), "muV": f(mu[colsV]),
        "wD": f(w_in[:, colsD]), "wT": f(w_in[:, colsT]), "wG": f(w_in[:, colsG]),
        "gcols": f(np.concatenate([f(inp["g_mix"])[0].reshape(8, 128).T, f(inp["g_ffn"])[0].reshape(8, 128).T], axis=1)),
        "gfin": f(inp["g_final"]),
        "wlora": f(np.concatenate([f(inp["w_decay_up"])[0], f(inp["w_iclr_up"])[0]], axis=0)),
        "wgate": f(inp["w_gate_up"])[0], "pp": f(pp),
        "gnw": f(inp["gn_w"])[0], "gnb": f(inp["gn_b"])[0],
        "wbr": f(f(inp["w_branch"])[0].reshape(1024, 1024)), "wout": f(inp["w_out"])[0],
        "wup": f(inp["w_ffn_up"])[0], "wdn": f(inp["w_ffn_down"])[0], "cstA": _CSTA, "cst1": _CST1, "cst2": _CST2,
    }
    x = f(inp["x"])
    maps = []
    for c in range(NCORES):
        m = dict(shared)
        m["x"] = np.ascontiguousarray(x[c * NSEQ:(c + 1) * NSEQ].reshape(NTOK, D))
        maps.append(m)
    return maps


def kernel(**inputs):
    maps = _layout_inputs(inputs)
    nc = build_nc()
    res = run_bass_kernel_spmd(nc, maps, core_ids=list(range(NCORES)))
    outs = [np.asarray(r["out"], dtype=np.float32).reshape(NSEQ, T, D) for r in res.results]
    return np.concatenate(outs, axis=0)
```

```python
import os
from contextlib import ExitStack

import numpy as np
import concourse.bass as bass
import concourse.mybir as mybir
from concourse.bass_utils import run_bass_kernel_spmd

F32 = mybir.dt.float32
BF16 = mybir.dt.bfloat16
ALU = mybir.AluOpType
AF = mybir.ActivationFunctionType
AX = mybir.AxisListType

NCORES = 8
T = 2048
D = 1024
NSEQ = 2
NTOK = NSEQ * T
C = 64
C0 = float(np.exp(-0.5))
NEG = -1.0e30
RW_NB = 128
DS_NB = 512
MG_NB = 512
FF_NB = 256
FFH = 4096


class Res:
    __slots__ = ("name", "w", "rd", "rd_dma")

    def __init__(self, name):
        self.name = name
        self.w = None
        self.rd = {}
        self.rd_dma = []


class DmaSem:
    def __init__(self, sem):
        self.sem = sem
        self.count = 0


class _Op:
    __slots__ = ("id", "eng", "fn", "deps", "dsem", "val", "signal")


class Sched:
    ENGS = ("pe", "act", "dve", "pool", "sp")

    def __init__(self, nc, es):
        self.nc = nc
        self.es = es
        self.ops = []
        self.per = {e: [] for e in self.ENGS}
        self.sem = {e: es.enter_context(nc.semaphore("s_" + e)) for e in self.ENGS}
        self.n_dsem = 0
        self.last = {e: None for e in self.ENGS}
        self.dma_since_barrier = []

    def new_dsem(self):
        self.n_dsem += 1
        return DmaSem(self.es.enter_context(self.nc.semaphore("d%d" % self.n_dsem)))

    def op(self, eng, fn, reads=(), writes=(), dsem=None):
        o = _Op()
        o.id = len(self.ops)
        o.eng = eng
        o.fn = fn
        o.dsem = dsem
        o.signal = False
        o.val = None
        deps = {}

        def add(d, kind):
            if d is None:
                return
            if kind == "raw" or d not in deps:
                deps[d] = kind

        for r in reads:
            add(r.w, "raw")
        for w in writes:
            add(w.w, "waw")
            for d in w.rd.values():
                add(d, "war")
            for d in w.rd_dma:
                add(d, "war")
        o.deps = deps
        for r in reads:
            if dsem is not None:
                r.rd_dma.append(o.id)
            else:
                r.rd[eng] = o.id
        for w in writes:
            w.w = o.id
            w.rd = {}
            w.rd_dma = []
        if dsem is not None:
            dsem.count += 16
            o.val = dsem.count
            self.dma_since_barrier.append(o.id)
        self.ops.append(o)
        self.per[eng].append(o)
        self.last[eng] = o.id
        return o

    def barrier(self):
        lasts = [v for v in self.last.values() if v is not None]
        dmas = list(self.dma_since_barrier)
        self.dma_since_barrier = []
        for e in self.ENGS:
            o = self.op(e, lambda en: en.nop())
            for d in lasts + dmas:
                if d != o.id:
                    o.deps[d] = "raw"

    def finish(self, dma_ops):
        o = self.op("sp", lambda en: en.nop())
        for d in dma_ops:
            o.deps[d.id] = "raw"

    def emit(self):
        ops = self.ops
        for o in ops:
            for d, kind in o.deps.items():
                p = ops[d]
                if p.dsem is not None:
                    continue
                if p.eng == o.eng and o.dsem is None and o.eng in ("pe", "sp"):
                    continue
                p.signal = True
        cnt = {e: 0 for e in self.ENGS}
        for o in ops:
            if o.dsem is None and o.signal:
                cnt[o.eng] += 1
                o.val = cnt[o.eng]
        sem = self.sem

        def run(eng, en):
            known = {}
            for o in self.per[eng]:
                need = {}
                for d, kind in o.deps.items():
                    p = ops[d]
                    if p.dsem is not None:
                        key, s, v = ("d", id(p.dsem)), p.dsem.sem, p.val
                    else:
                        if not p.signal:
                            continue
                        if p.eng == eng and o.dsem is None and eng in ("pe", "sp"):
                            continue
                        key, s, v = ("e", p.eng), sem[p.eng], p.val
                    if known.get(key, 0) >= v:
                        continue
                    if key not in need or need[key][1] < v:
                        need[key] = (s, v)
                for key, (s, v) in need.items():
                    en.wait_ge(s, v)
                    known[key] = v
                ins = o.fn(en)
                if o.dsem is not None:
                    ins.then_inc(o.dsem.sem, 16)
                elif o.signal:
                    ins.then_inc(sem[eng], 1)

        with self.nc.Block() as block:
            @block.tensor
            def _(en):
                run("pe", en)

            @block.scalar
            def _(en):
                run("act", en)

            @block.vector
            def _(en):
                run("dve", en)

            @block.gpsimd
            def _(en):
                run("pool", en)

            @block.sync
            def _(en):
                run("sp", en)


class Tl:
    def __init__(self, h, name, nres=1):
        self.h = h
        self.name = name
        self.rs = [Res("%s.%d" % (name, i)) for i in range(nres)]

    @property
    def r(self):
        return self.rs[0]

    def __getitem__(self, k):
        return self.h[k]


class Pool:
    def __init__(self, tiles):
        self.tiles = tiles
        self.i = 0

    def next(self):
        t = self.tiles[self.i % len(self.tiles)]
        self.i += 1
        return t


class _CB:
    def __init__(self):
        self.cols = {}
        self.parts = []
        self.off = 0

    def put(self, name, arr):
        a = np.zeros((128, arr.shape[1]), np.float32)
        a[: arr.shape[0]] = arr
        self.cols[name] = (self.off, arr.shape[1], arr.shape[0])
        self.parts.append(a)
        self.off += arr.shape[1]

    def arr(self):
        return np.ascontiguousarray(np.concatenate(self.parts, axis=1))


def _const_f32():
    A, B1, B2 = _CB(), _CB(), _CB()
    A.put("ident", np.eye(128, dtype=np.float32))
    s = np.arange(64)[:, None]
    t = np.arange(64)[None, :]
    m1 = np.concatenate([(s < t), (s <= t)], axis=1).astype(np.float32)
    B1.put("mask1", m1)
    B1.put("maskL", (s > t).astype(np.float32))
    B1.put("eye8", np.eye(64, dtype=np.float32))
    rm = np.ones((128, RW_NB), np.float32)
    rm[:, ::C] = 0.0
    B1.put("reset", rm)
    bo = np.zeros((128, 128), np.float32)
    bo[:64, :64] = 1.0
    bo[64:, 64:] = 1.0
    A.put("blockones", bo)
    hi = np.zeros((128, 2), np.float32)
    hi[:64, 0] = 1.0
    hi[64:, 1] = 1.0
    A.put("headind", hi)
    tq = np.arange(128)[:, None]
    kk = np.arange(128)[None, :]
    A.put("causal_bias", np.where(kk <= tq, 0.0, NEG).astype(np.float32))
    A.put("causalT", (tq <= kk).astype(np.float32))
    inv = (1.0 / (10000.0 ** (np.arange(0, 64, 2, dtype=np.float32) / np.float32(64)))).astype(np.float32)
    ang = (np.arange(T, dtype=np.float32)[:, None] * inv[None, :]).astype(np.float32)
    cs = np.cos(ang).astype(np.float32).T
    sn = np.sin(ang).astype(np.float32).T
    d = np.arange(128) % 64
    B2.put("ropeC", cs[d % 32])
    sg = np.where(d < 32, -1.0, 1.0).astype(np.float32)[:, None]
    B2.put("ropeS", sn[d % 32] * sg)
    return A, B1, B2


_CA, _C1, _C2 = _const_f32()
_CSTA, _CST1, _CST2 = _CA.arr(), _C1.arr(), _C2.arr()


def build_nc(debug=None):
    nc = bass.Bass("TRN2", target_bir_lowering=False)
    dt_in = lambda name, shape: nc.dram_tensor(name, list(shape), F32, kind="ExternalInput").ap()
    x_d = dt_in("x", (NTOK, D))
    wA_d = dt_in("wA", (D, 1312))
    wV_d = dt_in("wV", (D, 512))
    muA_d = dt_in("muA", (1312,))
    muV_d = dt_in("muV", (512,))
    wD_d = dt_in("wD", (D, 2560))
    wT_d = dt_in("wT", (D, 72))
    wG_d = dt_in("wG", (D, 2048))
    gcol_d = dt_in("gcols", (128, 16))
    gz_d = dt_in("gfin", (D,))
    wlora_d = dt_in("wlora", (128, 512))
    wgate_d = dt_in("wgate", (160, 512))
    pp_d = dt_in("pp", (128, 20))
    gnw_d = dt_in("gnw", (512,))
    gnb_d = dt_in("gnb", (512,))
    wbr_d = dt_in("wbr", (1024, 1024))
    wout_d = dt_in("wout", (D, D))
    wup_d = dt_in("wup", (D, FFH))
    wdn_d = dt_in("wdn", (FFH, D))
    cstA_d = dt_in("cstA", _CSTA.shape)
    cst1_d = dt_in("cst1", _CST1.shape)
    cst2_d = dt_in("cst2", _CST2.shape)
    out_d = nc.dram_tensor("out", [NTOK, D], F32, kind="ExternalOutput").ap()
    yaT_d = nc.dram_tensor("yaT_scr", [128, 4, NTOK], BF16, kind="Internal").ap()
    ybT_d = nc.dram_tensor("ybT_scr", [128, 4, NTOK], BF16, kind="Internal").ap()
    x1_d = nc.dram_tensor("x1_scr", [NTOK, D], F32, kind="Internal").ap()
    dbg_d = {}
    dbg_sem = {}
    if debug:
        for name, shape in debug.items():
            dbg_d[name] = nc.dram_tensor("dbg_" + name, list(shape), F32, kind="ExternalOutput").ap()

    top = ExitStack()
    with top:
        S = Sched(nc, top)
        out_dmas = []
        yaT_res = Res("yaT_scr")
        ybT_res = Res("ybT_scr")
        x1_res = [Res("x1_scr%d" % i) for i in range(NTOK // 512)]

        uid = [0]

        def sb(es, name, shape, dt=F32, nres=1):
            uid[0] += 1
            return Tl(es.enter_context(nc.sbuf_tensor("sb%d_%s" % (uid[0], name), list(shape), dt)), name, nres)

        def ps(es, name, shape, dt=F32):
            uid[0] += 1
            return Tl(es.enter_context(nc.psum_tensor("ps%d_%s" % (uid[0], name), list(shape), dt)), name)

        def rr(*xs):
            out = []
            for x in xs:
                if isinstance(x, Tl):
                    out.extend(x.rs)
                elif isinstance(x, Res):
                    out.append(x)
                else:
                    out.extend(x)
            return out

        def dma(out_ap, in_ap, reads, writes, dsem):
            return S.op("sp", lambda en: en.dma_start(out=out_ap, in_=in_ap),
                        reads=rr(*reads), writes=rr(*writes), dsem=dsem)

        def mm(out_ap, lhsT, rhs, start, stop, reads, writes, skip=False):
            return S.op("pe", lambda en: en.matmul(out_ap, lhsT, rhs, start=start, stop=stop, skip_group_check=skip),
                        reads=rr(*reads), writes=rr(*writes))

        def tr(out_ap, in_ap, ident, reads, writes):
            return S.op("pe", lambda en: en.transpose(out_ap, in_ap, ident),
                        reads=rr(*reads), writes=rr(*writes))

        def act(out_ap, in_ap, func, reads, writes, bias=0.0, scale=1.0, accum_out=None):
            return S.op("act", lambda en: en.activation(out_ap, in_ap, func, bias=bias, scale=scale,
                                                        accum_out=accum_out),
                        reads=rr(*reads), writes=rr(*writes))

        def tt(eng, out_ap, a, b, op, reads, writes):
            return S.op(eng, lambda en: en.tensor_tensor(out_ap, a, b, op), reads=rr(*reads), writes=rr(*writes))

        def ts(eng, out_ap, a, s1, s2, op0, op1, reads, writes):
            if op1 is None:
                return S.op(eng, lambda en: en.tensor_scalar(out_ap, a, s1, None, op0),
                            reads=rr(*reads), writes=rr(*writes))
            return S.op(eng, lambda en: en.tensor_scalar(out_ap, a, s1, s2, op0, op1),
                        reads=rr(*reads), writes=rr(*writes))

        def stt(out_ap, a, sc, b, op0, op1, reads, writes):
            return S.op("dve", lambda en: en.scalar_tensor_tensor(out_ap, a, sc, b, op0, op1),
                        reads=rr(*reads), writes=rr(*writes))

        def rsqrt(out_ap, in_ap, reads, writes):
            act(out_ap, in_ap, AF.Sqrt, reads, writes)
            S.op("dve", lambda en: en.reciprocal(out_ap, out_ap), reads=rr(*writes), writes=rr(*writes))

        def cp(eng, out_ap, in_ap, reads, writes):
            if eng == "act":
                return S.op("act", lambda en: en.copy(out_ap, in_ap), reads=rr(*reads), writes=rr(*writes))
            return S.op(eng, lambda en: en.tensor_copy(out_ap, in_ap), reads=rr(*reads), writes=rr(*writes))

        def memset(eng, ap, val, writes):
            return S.op(eng, lambda en: en.memset(ap, val), writes=rr(*writes))

        def dbg_dump(name, src_ap, reads, dst_slice=None):
            if name not in dbg_d:
                return
            dst = dbg_d[name] if dst_slice is None else dbg_d[name][dst_slice]
            if name not in dbg_sem:
                dbg_sem[name] = S.new_dsem()
            out_dmas.append(dma(dst, src_ap, reads, [], dbg_sem[name]))

        cst = Tl(None, "cstgroup", 0)
        ctiles = {}

        def load_const(es_, key, arr, src_d, cb):
            t_ = sb(es_, "cs_sb" + key, arr.shape)
            dma(t_[:], src_d, [], [t_], S.new_dsem())
            cst.rs.extend(t_.rs)
            for nm in cb.cols:
                ctiles[nm] = (t_, cb.cols[nm])

        load_const(top, "A", _CSTA, cstA_d, _CA)

        def cc(name, rows=None):
            t_, (o, n, r0) = ctiles[name]
            return t_[: (rows or r0), o:o + n]

        def d0():
            return S.new_dsem()

        nhalf = sb(top, "nhalf", (128, 256))
        memset("pool", nhalf[:], -0.5, [nhalf])
        identb = sb(top, "identb", (128, 128), BF16)
        cp("dve", identb[:], cc("ident"), [cst], [identb])
        gcol = sb(top, "gcol", (128, 2, 8))
        dma(gcol[:].rearrange("p a k -> p (a k)"), gcol_d, [], [gcol], d0())
        pp = sb(top, "pp", (128, 20))
        dma(pp[:], pp_d, [], [pp], d0())

        WSTN = 1312
        wst = None
        wst_sem = [S.new_dsem() for _ in range(2)]
        wst_i = [0]

        def load_weight(src_ap, ncols, consume):
            i = wst_i[0] % 2
            wst_i[0] += 1
            dma(wst[i][:, :ncols], src_ap, [], [wst[i]], wst_sem[i])
            consume(wst[i])

        xt_sem = [S.new_dsem() for _ in range(2)]
        xt_i = [0]
        xs_bf = [sb(top, "xsbf%d" % i, (128, D), BF16) for i in range(2)]
        stat = [sb(top, "stat%d" % i, (128, 4)) for i in range(2)]

        def make_hT(es_ps, tok0, ntile, hT, col0, src_d, xt=None, keep=None, src_res=()):
            for i in range(ntile):
                k = xt_i[0] % 2
                xt_i[0] += 1
                if keep is not None:
                    xin = keep
                    xap = keep[:, i, :]
                    dma(xap, src_d[tok0 + i * 128: tok0 + (i + 1) * 128, :], src_res, [keep], xt_sem[k])
                else:
                    xin = xt[k % len(xt)]
                    xap = xin[:]
                    dma(xap, src_d[tok0 + i * 128: tok0 + (i + 1) * 128, :], src_res, [xin], xt_sem[k % len(xt)])
                st = stat[k]
                act(xs_bf[k][:], xap, AF.Square, [xin], [xs_bf[k], st], accum_out=st[:, 0:1])
                ts("dve", st[:, 1:2], st[:, 0:1], 1.0 / D, 1e-6, ALU.mult, ALU.add, [st], [st])
                rsqrt(st[:, 2:3], st[:, 1:2], [st], [st])
                ts("dve", xs_bf[k][:], xap, st[:, 2:3], None, ALU.mult, None, [xin, st], [xs_bf[k]])
                pt = es_ps.next()
                for kc in range(8):
                    tr(pt[:, kc, :], xs_bf[k][:, kc * 128:(kc + 1) * 128], identb[:], [xs_bf[k], identb], [pt])
                cp("act" if i % 2 else "dve", hT[:, :, col0 + i * 128: col0 + (i + 1) * 128], pt[:, :, :], [pt], [hT])

        with ExitStack() as es:
            W1A = sb(es, "W1A", (128, 8, 1312), BF16)
            W2A = sb(es, "W2A", (128, 8, 1312), BF16)
            W1V = sb(es, "W1V", (128, 8, 512), BF16)
            W2V = sb(es, "W2V", (128, 8, 512), BF16)
            with ExitStack() as es_w:
                wst = [sb(es_w, "wst1_%d" % i, (128, WSTN)) for i in range(2)]
                mub = sb(es_w, "mub", (128, 1824))
                omb = sb(es_w, "omb", (128, 1824))
                dma(mub[:, 0:1312], muA_d.partition_broadcast(128), [], [mub], d0())
                dma(mub[:, 1312:1824], muV_d.partition_broadcast(128), [], [mub], d0())
                ts("pool", omb[:], mub[:], -1.0, 1.0, ALU.mult, ALU.add, [mub], [omb])
                for kc in range(8):
                    def consA(st_, kc=kc):
                        stt(W1A[:, kc, :], st_[:, :1312], gcol[:, 0, kc:kc + 1], omb[:, 0:1312], ALU.mult, ALU.mult,
                            [st_, gcol, omb], [W1A])
                        stt(W2A[:, kc, :], st_[:, :1312], gcol[:, 0, kc:kc + 1], mub[:, 0:1312], ALU.mult, ALU.mult,
                            [st_, gcol, mub], [W2A])
                    load_weight(wA_d[kc * 128:(kc + 1) * 128, :], 1312, consA)

                    def consV(st_, kc=kc):
                        stt(W1V[:, kc, :], st_[:, :512], gcol[:, 0, kc:kc + 1], omb[:, 1312:1824], ALU.mult, ALU.mult,
                            [st_, gcol, omb], [W1V])
                        stt(W2V[:, kc, :], st_[:, :512], gcol[:, 0, kc:kc + 1], mub[:, 1312:1824], ALU.mult, ALU.mult,
                            [st_, gcol, mub], [W2V])
                    load_weight(wV_d[kc * 128:(kc + 1) * 128, :], 512, consV)
            S.barrier()
            load_const(es, "1", _CST1, cst1_d, _C1)
            xt = [sb(es, "xt%d" % i, (128, D)) for i in range(1)]
            wlora = sb(es, "wlora", (128, 512))
            wg0f = sb(es, "wg0f", (128, 512))
            wg1f = sb(es, "wg1f", (32, 512))
            wg0 = sb(es, "wg0", (128, 512), BF16)
            wg1 = sb(es, "wg1", (32, 512), BF16)
            gnwb = sb(es, "gnwb", (64, 512))
            gnbb = sb(es, "gnbb", (64, 512))
            dma(wlora[:], wlora_d, [], [wlora], d0())
            dma(wg0f[:], wgate_d[0:128, :], [], [wg0f], d0())
            dma(wg1f[:], wgate_d[128:160, :], [], [wg1f], d0())
            cp("dve", wg0[:], wg0f[:], [wg0f], [wg0])
            cp("dve", wg1[:], wg1f[:], [wg1f], [wg1])
            dma(gnwb[:], gnw_d.partition_broadcast(64), [], [gnwb], d0())
            dma(gnbb[:], gnb_d.partition_broadcast(64), [], [gnbb], d0())

            NB = RW_NB
            NCH = NB // C
            G = T // C
            hT = sb(es, "hT", (128, 8, NB + 2), BF16)
            pTr = Pool([ps(es, "pTr", (128, 8, 128), BF16)])
            pjp = ps(es, "pjp", (128, 512))
            pbon = ps(es, "pbon", (128, 512))
            pI = ps(es, "pI", (128, 2, 512))
            pI3 = ps(es, "pI3", (128, 512))
            pSA = ps(es, "pSA", (128, 512))
            pSB = ps(es, "pSB", (128, 512))
            r_sb = sb(es, "r_sb", (128, 4, NB))
            k_sb = sb(es, "k_sb", (128, 4, NB))
            wa_sb = sb(es, "wa_sb", (128, NB))
            tmp = [sb(es, "rt%d" % i, (128, NB)) for i in range(10)]

            class PSet:
                pass
            psets = []
            for i in range(3):
                P_ = PSet()
                P_.sg0 = sb(es, "sg0_%d" % i, (128, NB), BF16)
                P_.sg1 = sb(es, "sg1_%d" % i, (32, NB), BF16)
                P_.v = sb(es, "v_sb%d" % i, (64, NCH, 512), BF16)
                P_.AR = [sb(es, "AR%d_%d" % (h, i), (128, NCH, 2, C), BF16) for h in range(4)]
                P_.Bt = [sb(es, "Bt%d_%d" % (h, i), (128, NCH, C), BF16) for h in range(4)]
                P_.Kt = [sb(es, "Kt%d_%d" % (h, i), (128, NCH, C), BF16) for h in range(4)]
                P_.BKh = [sb(es, "BKh%d_%d" % (h, i), (128, NCH, 2, C), BF16) for h in range(4)]
                P_.wC = sb(es, "wC%d" % i, (128, 4, NCH))
                P_.bon = sb(es, "bon%d" % i, (64, NCH, 8))
                P_.ya = sb(es, "yaT%d" % i, (128, 4, NB), BF16)
                P_.ya_sem = S.new_dsem()
                psets.append(P_)
            csets = []
            for i in range(2):
                Q_ = PSet()
                Q_.MA = sb(es, "MA%d" % i, (64, 8, 2 * C), BF16)
                Q_.KA = sb(es, "KA%d" % i, (64, 8, 2 * C), BF16)
                Q_.Tf = sb(es, "Tf%d" % i, (64, 8, C), BF16)
                Q_.BKtok = sb(es, "BKtok%d" % i, (64, 4, 2, 128), BF16)
                Q_.y = sb(es, "y_sb%d" % i, (64, 512))
                csets.append(Q_)
            ML = [sb(es, "ML%d" % i, (64, 8, 2, C), BF16) for i in range(2)]
            TT = [sb(es, "TT%d" % i, (64, 8, C), BF16) for i in range(2)]
            ST = sb(es, "ST", (128, 4, C))
            X_sb = sb(es, "X_sb", (64, 8, C), BF16)
            X32 = sb(es, "X32", (64, 8, C))
            STb = sb(es, "STb", (128, 4, C), BF16)
            U_sb = sb(es, "U_sb", (64, 512), BF16)
            ysq = sb(es, "ysq", (64, 512))
            ytmp = sb(es, "ytmp", (64, 512))
            gst = sb(es, "gst", (64, 6, 8))
            ident = cc("ident")
            m1b = cc("mask1").unsqueeze(1).to_broadcast([64, 8, 2 * C])
            mLb = cc("maskL").unsqueeze(1).to_broadcast([64, 8, C])
            eyb = cc("eye8").unsqueeze(1).to_broadcast([64, 8, C])
            hrow = lambda h: slice((h % 2) * 64, (h % 2) * 64 + 64)
            HORD = [0, 2, 4, 6, 1, 3, 5, 7]
            dbg_chunks = int(os.environ.get("MK_CHUNKS", "999")) if debug else 999

            def prep_task(b, blk, P_):
                t0 = b * T + blk * NB
                if blk == 0:
                    memset("pool", hT[:, :, 0:2], 0.0, [hT])
                else:
                    cp("dve", hT[:, :, 1:2], hT[:, :, NB + 1:NB + 2], [hT], [hT])
                make_hT(pTr, t0, NB // 128, hT, 2, x_d, xt=xt)
                yield
                for ct in range(11):
                    rows = 32 if ct == 10 else 128
                    c0 = ct * 128
                    p_ = pjp
                    for kc in range(8):
                        mm(p_[:rows, :NB], W1A[:, kc, c0:c0 + rows], hT[:, kc, 2:NB + 2], kc == 0, False, [W1A, hT], [p_])
                        mm(p_[:rows, :NB], W2A[:, kc, c0:c0 + rows], hT[:, kc, 1:NB + 1], False, kc == 7, [W2A, hT], [p_])
                    if ct < 4:
                        cp("act", r_sb[:, ct, :], p_[:, :NB], [p_], [r_sb])
                    elif ct < 8:
                        cp("dve", k_sb[:, ct - 4, :], p_[:, :NB], [p_], [k_sb])
                    elif ct == 8:
                        act(wa_sb[0:64, :], p_[0:64, :NB], AF.Tanh, [p_], [wa_sb])
                        cp("dve", wa_sb[64:128, :], p_[64:128, :NB], [p_], [wa_sb])
                    elif ct == 9:
                        act(P_.sg0[:], p_[:, :NB], AF.Sigmoid, [p_], [P_.sg0])
                    else:
                        act(P_.sg1[:], p_[0:32, :NB], AF.Sigmoid, [p_], [P_.sg1])
                    if ct % 3 == 2:
                        yield
                for c in range(NCH):
                    p_ = pjp
                    for kc in range(8):
                        mm(p_[0:64, :], hT[:, kc, 2 + c * C:2 + (c + 1) * C], W1V[:, kc, :], kc == 0, False, [W1V, hT], [p_])
                        mm(p_[0:64, :], hT[:, kc, 1 + c * C:1 + (c + 1) * C], W2V[:, kc, :], False, kc == 7, [W2V, hT], [p_])
                    cp("act", P_.v[:, c, :], p_[0:64, :], [p_], [P_.v])
                yield
                v4 = lambda a: a[:].rearrange("p (c t) -> p c t", t=C)
                for hp in range(4):
                    cs_ = slice(hp * 128, (hp + 1) * 128)
                    ppc = lambda j, hp=hp: pp[:, j * 4 + hp: j * 4 + hp + 1]
                    sgd, icl, cum, e_in, e_ng, e_ex, e_rm, kkn, kmod, t9 = tmp
                    AR, Bt, Kt, BKh = P_.AR, P_.Bt, P_.Kt, P_.BKh
                    p_ = pjp
                    mm(p_[:, :NB], wlora[0:64, cs_], wa_sb[0:64, :], True, True, [wlora, wa_sb], [p_])
                    act(sgd[:], p_[:, :NB], AF.Sigmoid, [p_, pp], [sgd], bias=ppc(0))
                    p_ = pbon
                    mm(p_[:, 256:256 + NB], wlora[64:128, cs_], wa_sb[64:128, :], True, True, [wlora, wa_sb], [p_])
                    act(icl[:], p_[:, 256:256 + NB], AF.Sigmoid, [p_, pp], [icl], bias=ppc(1))
                    S.op("dve", lambda en, cum=cum, sgd=sgd: en.tensor_tensor_scan(
                        cum[:], cc("reset"), sgd[:], 0.0, ALU.mult, ALU.add), reads=rr(cst, sgd), writes=rr(cum))
                    yield
                    act(e_in[:], cum[:], AF.Exp, [cum], [e_in], scale=-C0)
                    act(e_ng[:], cum[:], AF.Exp, [cum], [e_ng], scale=C0)
                    tt("dve", t9[:], cum[:], sgd[:], ALU.subtract, [cum, sgd], [t9])
                    act(e_ex[:], t9[:], AF.Exp, [t9], [e_ex], scale=-C0)
                    cum3 = cum[:].rearrange("p (c t) -> p c t", t=C)
                    tt("dve", t9[:].rearrange("p (c t) -> p c t", t=C),
                       cum3[:, :, C - 1:C].to_broadcast([128, NCH, C]), cum3, ALU.subtract, [cum], [t9])
                    act(e_rm[:], t9[:], AF.Exp, [t9], [e_rm], scale=-C0)
                    cp("dve", P_.wC[:, hp, :], e_in[:].rearrange("p (c t) -> p c t", t=C)[:, :, C - 1], [e_in], [P_.wC])
                    kx = k_sb[:, hp, :]
                    ts("dve", kkn[:], kx, ppc(2), None, ALU.mult, None, [k_sb, pp], [kkn])
                    tt("dve", t9[:], kkn[:], kkn[:], ALU.mult, [kkn], [t9])
                    p_ = pjp
                    mm(p_[:, :NB], cc("blockones"), t9[:], True, True, [cst, t9], [p_])
                    ts("dve", t9[:], p_[:, :NB], 1e-24, None, ALU.max, None, [p_], [t9])
                    yield
                    rsqrt(t9[:], t9[:], [t9], [t9])
                    tt("dve", kkn[:], kkn[:], t9[:], ALU.mult, [kkn, t9], [kkn])
                    ts("dve", t9[:], icl[:], -1.0, ppc(3), ALU.add, ALU.mult, [icl, pp], [t9])
                    stt(kmod[:], t9[:], 1.0, kx, ALU.add, ALU.mult, [t9, k_sb], [kmod])
                    tt("dve", icl[:], icl[:], kkn[:], ALU.mult, [icl, kkn], [icl])
                    stt(AR[hp][:, :, 0, :], v4(kkn), -1.0, v4(e_ex), ALU.mult, ALU.mult, [kkn, e_ex], [AR[hp]])
                    tt("dve", AR[hp][:, :, 1, :], r_sb[:, hp, :].rearrange("p (c t) -> p c t", t=C), v4(e_in),
                       ALU.mult, [r_sb, e_in], [AR[hp]])
                    yield
                    tt("dve", Bt[hp][:], v4(icl), v4(e_ng), ALU.mult, [icl, e_ng], [Bt[hp]])
                    tt("dve", Kt[hp][:], v4(kmod), v4(e_ng), ALU.mult, [kmod, e_ng], [Kt[hp]])
                    tt("dve", BKh[hp][:, :, 0, :], v4(icl), v4(e_rm), ALU.mult, [icl, e_rm], [BKh[hp]])
                    tt("dve", BKh[hp][:, :, 1, :], v4(kmod), v4(e_rm), ALU.mult, [kmod, e_rm], [BKh[hp]])
                    stt(t9[:], r_sb[:, hp, :], ppc(4), kmod[:], ALU.mult, ALU.mult, [r_sb, pp, kmod], [t9])
                    for c in range(NCH):
                        mm(pbon[0:64, c * 8 + hp * 2: c * 8 + hp * 2 + 2], t9[:, c * C:(c + 1) * C],
                           cc("headind"), True, True, [t9, cst], [pbon])
                    yield
                cp("act", P_.bon[:].rearrange("p c h -> p (c h)"), pbon[0:64, 0:NCH * 8], [pbon], [P_.bon])

            def inv_task(c, P_, Q_):
                AR, Bt, Kt, BKh = P_.AR, P_.Bt, P_.Kt, P_.BKh
                MA, KA = Q_.MA, Q_.KA
                for h in HORD:
                    hp, rs_ = h // 2, hrow(h)
                    arh = AR[hp][rs_, c, :, :].rearrange("p a t -> p (a t)")
                    mm(pI[0:64, h % 2, hp * 128:hp * 128 + 128], Bt[hp][rs_, c, :], arh, True, True,
                       [Bt[hp], AR[hp]], [pI])
                m1p = cc("mask1").unsqueeze(1).unsqueeze(1).to_broadcast([64, 2, 4, 2 * C])
                tt("dve", MA[:].rearrange("p (hp par) m -> p par hp m", par=2),
                   pI[0:64, :, :].rearrange("p par (hp m) -> p par hp m", m=2 * C), m1p, ALU.mult, [pI, cst], [MA])
                mlc = ML[0]
                cp("dve", mlc[:, :, 0, :], MA[:, :, 0:C], [MA], [mlc])
                tcur = TT[0]
                tt("dve", tcur[:], MA[:, :, 0:C], eyb, ALU.add, [MA, cst], [tcur])
                yield
                for h in HORD:
                    hp, rs_ = h // 2, hrow(h)
                    mm(pI[0:64, h % 2, hp * C:(hp + 1) * C], AR[hp][rs_, c, 0, :], Bt[hp][rs_, c, :], True, True,
                       [AR[hp], Bt[hp]], [pI])
                tt("dve", mlc[:, :, 1, :].rearrange("p (hp par) s -> p par hp s", par=2),
                   pI[0:64, :, 0:4 * C].rearrange("p par (hp s) -> p par hp s", s=C),
                   cc("maskL").unsqueeze(1).unsqueeze(1).to_broadcast([64, 2, 4, C]), ALU.mult, [pI, cst], [mlc])
                yield
                for h in HORD:
                    hp, rs_ = h // 2, hrow(h)
                    arh = AR[hp][rs_, c, :, :].rearrange("p a t -> p (a t)")
                    mm(pI[0:64, h % 2, hp * 128:hp * 128 + 128], Kt[hp][rs_, c, :], arh, True, True,
                       [Kt[hp], AR[hp]], [pI])
                tt("dve", KA[:].rearrange("p (hp par) m -> p par hp m", par=2),
                   pI[0:64, :, :].rearrange("p par (hp m) -> p par hp m", m=2 * C), m1p, ALU.mult, [pI, cst], [KA])
                yield

                def squares(mlc, mln, lev):
                    for h in range(8):
                        if lev < 5:
                            mm(pI[0:64, h // 4, (h % 4) * 128:(h % 4) * 128 + C], mlc[:, h, 1, :], mlc[:, h, 0, :],
                               True, True, [mlc], [pI])
                        mm(pI[0:64, h // 4, (h % 4) * 128 + C:(h % 4) * 128 + 2 * C], mlc[:, h, 0, :],
                           mlc[:, h, 1, :], True, True, [mlc], [pI])
                    if lev < 5:
                        cp("act", mln[:].rearrange("p (a h) x s -> p a (h x s)", a=2), pI[0:64, :, :], [pI], [mln])
                    else:
                        cp("act", mln[:, :, 1, :].rearrange("p (a h) s -> p a h s", a=2),
                           pI[0:64, :, :].rearrange("p a (h x s) -> p a h x s", h=4, x=2)[:, :, :, 1, :], [pI], [mln])

                def tupdate(mln, tcur, tnew):
                    for h in range(8):
                        mm(pI3[0:64, h * C:(h + 1) * C], mln[:, h, 1, :], tcur[:, h, :], True, True, [mln, tcur], [pI3])
                    tt("dve", tnew[:], pI3[0:64, :].rearrange("p (h s) -> p h s", h=8), tcur[:], ALU.add,
                       [pI3, tcur], [tnew])

                for lev in range(1, 6):
                    mln = ML[lev % 2]
                    squares(mlc, mln, lev)
                    tnew = Q_.Tf if lev == 5 else TT[lev % 2]
                    tupdate(mln, tcur, tnew)
                    mlc, tcur = mln, tnew
                    yield
                pIb = pI[:, 0, :].bitcast(BF16)
                for hp in range(4):
                    for a in range(2):
                        tr(pIb[0:64, hp * 256 + a * 128:hp * 256 + a * 128 + 128], BKh[hp][:, c, a, :], identb[:],
                           [BKh[hp], identb], [pI])
                cp("act", Q_.BKtok[:].rearrange("p h x m -> p (h x m)"), pIb[0:64, :], [pI], [Q_.BKtok])
                yield

            def state_task(c, P_, Q_):
                AR, v_sb = P_.AR, P_.v
                MA, KA, tcur, BKtok, y_sb = Q_.MA, Q_.KA, Q_.Tf, Q_.BKtok, Q_.y
                bank = lambda h: (pSA if h % 2 == 0 else pSB)
                bank2 = lambda h: (pSA if h < 4 else pSB)
                for h in HORD:
                    hp, rs_ = h // 2, hrow(h)
                    mm(bank(h)[0:64, hp * C:(hp + 1) * C], AR[hp][rs_, c, 0, :], STb[rs_, hp, :], True, True,
                       [AR[hp], STb], [bank(h)])
                for h in range(8):
                    mm(bank2(h)[0:64, 256 + (h % 4) * C:256 + (h % 4 + 1) * C], KA[:, h, 0:C], v_sb[:, c, h * C:(h + 1) * C],
                       True, True, [KA, v_sb], [bank2(h)])
                X4 = X32[:].rearrange("p (hp par) s -> p par hp s", par=2)
                cp("act", X4[:, 0, :, :], pSA[0:64, 0:4 * C].rearrange("p (hp s) -> p hp s", s=C), [pSA], [X32])
                cp("act", X4[:, 1, :, :], pSB[0:64, 0:4 * C].rearrange("p (hp s) -> p hp s", s=C), [pSB], [X32])
                tt("dve", X_sb[:, 0:4, :], X32[:, 0:4, :], pSA[0:64, 256:512].rearrange("p (h s) -> p h s", s=C), ALU.add,
                   [X32, pSA], [X_sb])
                tt("dve", X_sb[:, 4:8, :], X32[:, 4:8, :], pSB[0:64, 256:512].rearrange("p (h s) -> p h s", s=C), ALU.add,
                   [X32, pSB], [X_sb])
                yield
                for h in range(8):
                    mm(pSA[0:64, h * C:(h + 1) * C], tcur[:, h, :], X_sb[:, h, :], True, True, [tcur, X_sb], [pSA])
                cp("dve", U_sb[:], pSA[0:64, :], [pSA], [U_sb])
                yield
                for h in HORD:
                    hp, rs_ = h // 2, hrow(h)
                    mm(bank(h)[0:64, hp * C:(hp + 1) * C], AR[hp][rs_, c, 1, :], STb[rs_, hp, :], True, True,
                       [AR[hp], STb], [bank(h)])
                for h in range(8):
                    o_ = bank2(h)[0:64, 256 + (h % 4) * C:256 + (h % 4 + 1) * C]
                    mm(o_, MA[:, h, C:2 * C], U_sb[:, h * C:(h + 1) * C], True, False, [MA, U_sb], [bank2(h)])
                    mm(o_, KA[:, h, C:2 * C], v_sb[:, c, h * C:(h + 1) * C], False, True, [KA, v_sb], [bank2(h)])
                Y4 = y_sb[:].rearrange("p (hp par s) -> p par hp s", par=2, s=C)
                cp("act", Y4[:, 0, :, :], pSA[0:64, 0:4 * C].rearrange("p (hp s) -> p hp s", s=C), [pSA], [y_sb])
                cp("act", Y4[:, 1, :, :], pSB[0:64, 0:4 * C].rearrange("p (hp s) -> p hp s", s=C), [pSB], [y_sb])
                tt("dve", y_sb[:, 0:256], y_sb[:, 0:256], pSA[0:64, 256:512], ALU.add, [y_sb, pSA], [y_sb])
                tt("dve", y_sb[:, 256:512], y_sb[:, 256:512], pSB[0:64, 256:512], ALU.add, [y_sb, pSB], [y_sb])
                yield
                pS = pSB
                for hp in range(4):
                    mm(pS[:, hp * 128:(hp + 1) * 128], BKtok[:, hp, 0, :], U_sb[:, hp * 128:(hp + 1) * 128],
                       True, False, [BKtok, U_sb], [pS])
                    mm(pS[:, hp * 128:(hp + 1) * 128], BKtok[:, hp, 1, :], v_sb[:, c, hp * 128:(hp + 1) * 128],
                       False, True, [BKtok, v_sb], [pS])
                for hp in range(4):
                    for hh in range(2):
                        rs_ = slice(hh * 64, hh * 64 + 64)
                        stt(ST[rs_, hp, :], ST[rs_, hp, :], P_.wC[rs_, hp, c:c + 1],
                            pS[rs_, hp * 128 + hh * 64: hp * 128 + hh * 64 + 64], ALU.mult, ALU.add, [ST, P_.wC, pS], [ST])
                cp("act", STb[:], ST[:], [ST], [STb])
                yield

            def ypost_task(b, g, c, P_, Q_):
                y_sb, v_sb = Q_.y, P_.v
                t0c = b * T + g * C
                y3 = y_sb[:].rearrange("p (h i) -> p h i", h=8)
                S.op("dve", lambda en: en.tensor_reduce(gst[:, 0, :], y3, AX.X, ALU.add), reads=rr(y_sb), writes=rr(gst))
                act(ysq[:], y_sb[:], AF.Square, [y_sb], [ysq])
                S.op("dve", lambda en: en.tensor_reduce(gst[:, 1, :], ysq[:].rearrange("p (h i) -> p h i", h=8),
                                                        AX.X, ALU.add), reads=rr(ysq), writes=rr(gst))
                ts("dve", gst[:, 2, :], gst[:, 0, :], 1.0 / 64, None, ALU.mult, None, [gst], [gst])
                tt("dve", gst[:, 3, :], gst[:, 2, :], gst[:, 2, :], ALU.mult, [gst], [gst])
                stt(gst[:, 4, :], gst[:, 1, :], 1.0 / 64, gst[:, 3, :], ALU.mult, ALU.subtract, [gst], [gst])
                ts("dve", gst[:, 4, :], gst[:, 4, :], 64e-5, None, ALU.add, None, [gst], [gst])
                rsqrt(gst[:, 5, :], gst[:, 4, :], [gst], [gst])
                yield
                bc = lambda a: a.unsqueeze(2).to_broadcast([64, 8, 64])
                yt3 = ytmp[:].rearrange("p (h i) -> p h i", h=8)
                tt("dve", yt3, y3, bc(gst[:, 2, :]), ALU.subtract, [y_sb, gst], [ytmp])
                tt("dve", yt3, yt3, bc(gst[:, 5, :]), ALU.mult, [ytmp, gst], [ytmp])
                tt("dve", ytmp[:], ytmp[:], gnwb[:], ALU.mult, [ytmp, gnwb], [ytmp])
                tt("dve", ytmp[:], ytmp[:], gnbb[:], ALU.add, [ytmp, gnbb], [ytmp])
                ys3 = ysq[:].rearrange("p (h i) -> p h i", h=8)
                tt("dve", ys3, v_sb[:, c, :].rearrange("p (h i) -> p h i", h=8), bc(P_.bon[:, c, :]), ALU.mult,
                   [v_sb, P_.bon], [ysq])
                tt("dve", ytmp[:], ytmp[:], ysq[:], ALU.add, [ytmp, ysq], [ytmp])
                pg = pjp
                mm(pg[0:64, :], P_.sg0[:, c * C:(c + 1) * C], wg0[:], True, False, [P_.sg0, wg0], [pg])
                mm(pg[0:64, :], P_.sg1[:, c * C:(c + 1) * C], wg1[:], False, True, [P_.sg1, wg1], [pg])
                tt("dve", ytmp[:], ytmp[:], pg[0:64, :], ALU.mult, [ytmp, pg], [ytmp])
                if b == 0:
                    dbg_dump("ya", ytmp[:], [ytmp], (slice(t0c, t0c + C), slice(None)))
                yield
                pq = pjp
                for kc in range(4):
                    tr(pq[:, kc * C:(kc + 1) * C], ytmp[:, kc * 128:(kc + 1) * 128], ident[0:64, 0:64], [ytmp, cst], [pq])
                cp("act", P_.ya[:, :, c * C:(c + 1) * C], pq[:, 0:4 * C].rearrange("p (k t) -> p k t", k=4), [pq], [P_.ya])
                if c == NCH - 1:
                    tb = b * T + (g // NCH) * NB
                    dma(yaT_d[:, :, tb:tb + NB], P_.ya[:], [P_.ya], [yaT_res], P_.ya_sem)
                yield

            def pgen_slice(pg_, j, n):
                cnt = 0
                while True:
                    if j < n - 1 and cnt >= (24 // n):
                        return
                    try:
                        next(pg_)
                    except StopIteration:
                        return
                    cnt += 1
                    yield

            def run_rr(gens):
                gens = list(gens)
                while gens:
                    for g_ in list(gens):
                        try:
                            next(g_)
                        except StopIteration:
                            gens.remove(g_)

            for b in range(NSEQ):
                memset("dve", ST[:], 0.0, [ST])
                memset("dve", STb[:], 0.0, [STb])
                Gn = min(G, dbg_chunks)
                run_rr([prep_task(b, 0, psets[0])])
                for k in range(Gn + 2):
                    gens = []
                    if k < Gn:
                        gens.append(inv_task(k % NCH, psets[(k // NCH) % 3], csets[k % 2]))
                    if 1 <= k <= Gn:
                        g = k - 1
                        gens.append(state_task(g % NCH, psets[(g // NCH) % 3], csets[g % 2]))
                    if 2 <= k <= Gn + 1:
                        g = k - 2
                        gens.append(ypost_task(b, g, g % NCH, psets[(g // NCH) % 3], csets[g % 2]))
                    if k % NCH == 0:
                        nb_ = k // NCH + 1
                        pgen = prep_task(b, nb_, psets[nb_ % 3]) if nb_ * NCH < Gn else None
                    if pgen is not None:
                        gens.append(pgen_slice(pgen, k % NCH, NCH))
                    run_rr(gens)

        S.barrier()
        stop_after = int(os.environ.get("MK_STOP", "99")) if debug else 99

        if stop_after >= 2:
          with ExitStack() as es:
            WD = sb(es, "WD", (128, 8, 2560), BF16)
            WT = sb(es, "WT", (128, 8, 72), BF16)
            load_const(es, "2", _CST2, cst2_d, _C2)
            xt = [sb(es, "xt2_%d" % i, (128, D)) for i in range(2)]
            wst = [sb(es, "wst2_%d" % i, (128, WSTN)) for i in range(2)]
            for kc in range(8):
                for hf in range(2):
                    def consD(st_, kc=kc, hf=hf):
                        ts("pool" if hf else "dve", WD[:, kc, hf * 1280:(hf + 1) * 1280], st_[:, :1280], gcol[:, 0, kc:kc + 1], None,
                           ALU.mult, None, [st_, gcol], [WD])
                    load_weight(wD_d[kc * 128:(kc + 1) * 128, hf * 1280:(hf + 1) * 1280], 1280, consD)

                def consT(st_, kc=kc):
                    ts("dve", WT[:, kc, :], st_[:, :72], gcol[:, 0, kc:kc + 1], None, ALU.mult, None, [st_, gcol], [WT])
                load_weight(wT_d[kc * 128:(kc + 1) * 128, :], 72, consT)
            NB = DS_NB
            hT = sb(es, "hTd", (128, 8, NB), BF16)
            pTr = Pool([ps(es, "pTr2_%d" % i, (128, 8, 128), BF16) for i in range(1)])
            pj = Pool([ps(es, "pj2_%d" % i, (128, 512)) for i in range(3)])
            pW = Pool([ps(es, "pW2_%d" % i, (128, 2, 512)) for i in range(1)])
            po = ps(es, "po2", (128, 2, 512))
            qT = sb(es, "qT", (128, 4, NB), BF16)
            qiT = sb(es, "qiT", (128, 4, NB), BF16)
            kT_all = sb(es, "kT_all", (128, T), BF16)
            kiT_all = sb(es, "kiT_all", (128, T), BF16)
            vones = sb(es, "vones", (128, 16, 65), BF16)
            wi_sb = sb(es, "wi_sb", (128, 4, 8))
            rt1 = sb(es, "rt1", (128, NB))
            rt2 = sb(es, "rt2", (128, NB))
            acc = sb(es, "acc", (128, T))
            work = sb(es, "work", (128, T))
            relu_t = [sb(es, "relu%d" % i, (128, 512)) for i in range(2)]
            mx8 = sb(es, "mx8", (128, 8))
            maskb = sb(es, "maskb", (128, T), BF16)
            maskT = sb(es, "maskT", (128, 16, 128), BF16)
            causT = sb(es, "causT", (128, 128), BF16)
            eT = [sb(es, "eT%d" % i, (128, 8, 128), BF16) for i in range(2)]
            pT = [sb(es, "pT%d" % i, (128, 8, 128), BF16) for i in range(2)]
            rcp = sb(es, "rcp", (128, 8))
            yb = sb(es, "yb", (128, 512), BF16)
            ybT = [sb(es, "ybT%d" % i, (128, 4, NB), BF16) for i in range(2)]
            ybT_sem = [S.new_dsem() for _ in range(2)]
            cp("dve", causT[:], cc("causalT"), [cst], [causT])
            memset("pool", vones[:, :, 64:65], 1.0, [vones])
            WI_SCALE = float(8 ** -0.5 * 64 ** -0.5)
            blk_i = 0
            for b in range(NSEQ):
                for blk in range(T // NB):
                    tl0 = blk * NB
                    t0 = b * T + tl0
                    ybt = ybT[blk_i % 2]
                    make_hT(pTr, t0, NB // 128, hT, 0, x_d, xt=xt)
                    ropeC = cc("ropeC")[:, tl0:tl0 + NB]
                    ropeS = cc("ropeS")[:, tl0:tl0 + NB]

                    def proj(ct):
                        p_ = pj.next()
                        for kc in range(8):
                            mm(p_[:, :NB], WD[:, kc, ct * 128:(ct + 1) * 128], hT[:, kc, :], kc == 0, kc == 7, [WD, hT], [p_])
                        return p_

                    def rope(ct_a, ct_b, dst_ap, dst_tl):
                        pa = proj(ct_a)
                        tt("dve", rt1[:], pa[:, :NB], ropeC, ALU.mult, [pa, cst], [rt1])
                        pb = proj(ct_b)
                        tt("dve", rt2[:], pb[:, :NB], ropeS, ALU.mult, [pb, cst], [rt2])
                        tt("pool", dst_ap, rt1[:], rt2[:], ALU.add, [rt1, rt2], [dst_tl])

                    for i in range(4):
                        rope(i, 4 + i, qT[:, i, :], qT)
                    rope(8, 9, kT_all[:, tl0:tl0 + NB], kT_all)
                    for i in range(4):
                        rope(10 + i, 14 + i, qiT[:, i, :], qiT)
                    rope(18, 19, kiT_all[:, tl0:tl0 + NB], kiT_all)
                    for i in range(NB // 128):
                        p_ = pj.next()
                        for kc in range(8):
                            mm(p_[:, 0:72], hT[:, kc, i * 128:(i + 1) * 128], WT[:, kc, :], kc == 0, kc == 7, [WT, hT], [p_])
                        cp("dve", vones[:, blk * 4 + i, 0:64], p_[:, 0:64], [p_], [vones])
                        ts("dve", wi_sb[:, i, :], p_[:, 64:72], WI_SCALE, None, ALU.mult, None, [p_], [wi_sb])

                    for i in range(NB // 128):
                        qt = blk * 4 + i
                        if debug and (b * 16 + qt >= int(os.environ.get("MK_QT", "999"))):
                            continue
                        dsub = int(os.environ.get("MK_DSUB", "9")) if debug else 9
                        Sk = (qt + 1) * 128
                        tq = slice(i * 128, (i + 1) * 128)
                        if qt >= 2:
                            nseg = (Sk + 511) // 512
                            for sg in range(nseg):
                                s0 = sg * 512
                                sn = min(512, Sk - s0)
                                for h in range(8):
                                    rs_ = slice((h % 2) * 64, (h % 2) * 64 + 64)
                                    p_ = pj.next()
                                    mm(p_[:, :sn], qiT[rs_, h // 2, tq], kiT_all[rs_, s0:s0 + sn], True, True,
                                       [qiT, kiT_all], [p_])
                                    rl = relu_t[h % 2]
                                    act(rl[:, :sn], p_[:, :sn], AF.Relu, [p_], [rl])
                                    if h == 0:
                                        ts("dve", acc[:, s0:s0 + sn], rl[:, :sn], wi_sb[:, i, 0:1], None, ALU.mult, None,
                                           [rl, wi_sb], [acc])
                                    else:
                                        stt(acc[:, s0:s0 + sn], rl[:, :sn], wi_sb[:, i, h:h + 1], acc[:, s0:s0 + sn],
                                            ALU.mult, ALU.add, [rl, wi_sb, acc], [acc])
                            tt("pool", acc[:, Sk - 128:Sk], acc[:, Sk - 128:Sk], cc("causal_bias"), ALU.add, [acc, cst], [acc])
                            src = acc
                            for rnd in range(32):
                                S.op("dve", lambda en, src=src, Sk=Sk: en.max(out=mx8[:], in_=src[:, :Sk]),
                                     reads=rr(src), writes=rr(mx8))
                                if rnd < 31:
                                    S.op("dve", lambda en, src=src, Sk=Sk: en.match_replace(
                                        out=work[:, :Sk], in_to_replace=mx8[:], in_values=src[:, :Sk], imm_value=NEG),
                                        reads=rr(src, mx8), writes=rr(work))
                                    src = work
                            if dsub < 2:
                                continue
                            ts("dve", maskb[:, :Sk], acc[:, :Sk], mx8[:, 7:8], None, ALU.is_ge, None, [acc, mx8], [maskb])
                            for g in range((qt + 1 + 3) // 4):
                                pm = pj.next()
                                pmb = pm[:].bitcast(BF16)
                                nk = min(4, qt + 1 - g * 4)
                                for j in range(nk):
                                    kt = g * 4 + j
                                    tr(pmb[:, j * 128:(j + 1) * 128], maskb[:, kt * 128:(kt + 1) * 128], identb[:],
                                       [maskb, identb], [pm])
                                cp("act", maskT[:, g * 4:g * 4 + nk, :],
                                   pmb[:, 0:nk * 128].rearrange("p (k t) -> p k t", t=128), [pm], [maskT])
                        if dsub < 3:
                            continue
                        for kt in range(qt + 1):
                            psc = pW.next()
                            for h in range(8):
                                rs_ = slice((h % 2) * 64, (h % 2) * 64 + 64)
                                mm(psc[:, h % 2, (h // 2) * 128:(h // 2) * 128 + 128], kT_all[rs_, kt * 128:(kt + 1) * 128],
                                   qT[rs_, h // 2, tq], True, True, [kT_all, qT], [psc])
                            e_ = eT[kt % 2]
                            act(e_[:].rearrange("p (a h) t -> p a (h t)", a=2), psc[:, :, :], AF.Exp, [psc], [e_], scale=0.125)
                            if qt >= 2:
                                p__ = pT[kt % 2]
                                tt("pool" if kt % 2 else "dve", p__[:], e_[:],
                                   maskT[:, kt, :].unsqueeze(1).to_broadcast([128, 8, 128]), ALU.mult, [e_, maskT], [p__])
                            elif kt == qt:
                                p__ = pT[kt % 2]
                                tt("dve", p__[:], e_[:], causT[:].unsqueeze(1).to_broadcast([128, 8, 128]), ALU.mult,
                                   [e_, causT], [p__])
                            else:
                                p__ = e_
                            for h in range(8):
                                mm(po[:, h // 4, (h % 4) * 65:(h % 4) * 65 + 65], p__[:, (h % 2) * 4 + h // 2, :], vones[:, kt, :],
                                   kt == 0 and h % 4 == 0, kt == qt, [p__, vones], [po], skip=True)
                        pov = po[:, :, 0:260].rearrange("p a (h e) -> p a h e", e=65)
                        S.op("dve", lambda en, pov=pov: en.reciprocal(rcp[:].rearrange("p (a h) -> p a h", a=2), pov[:, :, :, 64]),
                             reads=rr(po), writes=rr(rcp))
                        tt("dve", yb[:].rearrange("p (a h e) -> p a h e", a=2, h=4), pov[:, :, :, 0:64],
                           rcp[:].rearrange("p (a h) -> p a h", a=2).unsqueeze(3).to_broadcast([128, 2, 4, 64]), ALU.mult,
                           [po, rcp], [yb])
                        if "yb" in dbg_d and b == 0:
                            cp("dve", acc[:, 0:512], yb[:], [yb], [acc])
                            dbg_dump("yb", acc[:, 0:512], [acc], (slice(t0 + i * 128, t0 + (i + 1) * 128), slice(None)))
                        pm = pj.next()
                        pmb = pm[:].bitcast(BF16)
                        for kc in range(4):
                            tr(pmb[:, kc * 128:(kc + 1) * 128], yb[:, kc * 128:(kc + 1) * 128], identb[:], [yb, identb], [pm])
                        cp("act", ybt[:, :, tq], pmb[:, 0:512].rearrange("p (k t) -> p k t", t=128), [pm], [ybt])
                    dma(ybT_d[:, :, t0:t0 + NB], ybt[:], [ybt], [ybT_res], ybT_sem[blk_i % 2])
                    blk_i += 1
          S.barrier()

        if stop_after >= 3:
          with ExitStack() as es:
            WG = sb(es, "WG", (128, 8, 2048), BF16)
            wbr = sb(es, "wbr", (128, 8, 1024), BF16)
            wout = sb(es, "wout", (128, 8, 1024), BF16)
            wst = [sb(es, "wst3_%d" % i, (128, WSTN)) for i in range(2)]
            for kc in range(8):
                for hf in range(2):
                    def consG(st_, kc=kc, hf=hf):
                        ts("pool" if hf else "dve", WG[:, kc, hf * 1024:(hf + 1) * 1024], st_[:, :1024], gcol[:, 0, kc:kc + 1], None,
                           ALU.mult, None, [st_, gcol], [WG])
                    load_weight(wG_d[kc * 128:(kc + 1) * 128, hf * 1024:(hf + 1) * 1024], 1024, consG)

                def consB(st_, kc=kc):
                    cp("pool", wbr[:, kc, :], st_[:, :1024], [st_], [wbr])
                load_weight(wbr_d[kc * 128:(kc + 1) * 128, :], 1024, consB)

                def consO(st_, kc=kc):
                    cp("dve", wout[:, kc, :], st_[:, :1024], [st_], [wout])
                load_weight(wout_d[kc * 128:(kc + 1) * 128, :], 1024, consO)
            NB = MG_NB
            hT = sb(es, "hTm", (128, 8, NB), BF16)
            xk = sb(es, "xk", (128, 4, D))
            pTr = Pool([ps(es, "pTr3_%d" % i, (128, 8, 128), BF16) for i in range(1)])
            pj = Pool([ps(es, "pj3_%d" % i, (128, 512)) for i in range(6)])
            yaL = sb(es, "yaL", (128, 4, NB), BF16)
            ybL = sb(es, "ybL", (128, 4, NB), BF16)
            yl_sem = S.new_dsem()
            yl_sem2 = S.new_dsem()
            sgA = sb(es, "sgA", (128, NB))
            sgB = sb(es, "sgB", (128, NB))
            mA = sb(es, "mA", (128, NB))
            mB = sb(es, "mB", (128, NB))
            mgT = sb(es, "mgT", (128, 8, NB), BF16)
            x1t = [sb(es, "x1t%d" % i, (128, D)) for i in range(2)]
            x1_sem = [S.new_dsem() for _ in range(2)]
            n1 = 0
            for bi in range(NTOK // NB):
                t0 = bi * NB
                make_hT(pTr, t0, 4, hT, 0, x_d, keep=xk)
                dma(yaL[:], yaT_d[:, :, t0:t0 + NB], [yaT_res], [yaL], yl_sem)
                dma(ybL[:], ybT_d[:, :, t0:t0 + NB], [ybT_res], [ybL], yl_sem2)
                for dt_ in range(8):
                    ds_ = slice(dt_ * 128, (dt_ + 1) * 128)
                    pa = pj.next()
                    for kc in range(4):
                        mm(pa[:, :NB], wbr[:, kc, ds_], yaL[:, kc, :], kc == 0, kc == 3, [wbr, yaL], [pa])
                    pb = pj.next()
                    for kc in range(4):
                        mm(pb[:, :NB], wbr[:, 4 + kc, ds_], ybL[:, kc, :], kc == 0, kc == 3, [wbr, ybL], [pb])
                    g0 = pj.next()
                    for kc in range(8):
                        mm(g0[:, :NB], WG[:, kc, dt_ * 128:(dt_ + 1) * 128], hT[:, kc, :], kc == 0, kc == 7, [WG, hT], [g0])
                    g1 = pj.next()
                    for kc in range(8):
                        mm(g1[:, :NB], WG[:, kc, 1024 + dt_ * 128:1024 + (dt_ + 1) * 128], hT[:, kc, :], kc == 0, kc == 7,
                           [WG, hT], [g1])
                    act(sgA[:], g0[:, :NB], AF.Sigmoid, [g0], [sgA])
                    act(sgB[:], g1[:, :NB], AF.Sigmoid, [g1], [sgB])
                    tt("dve", mA[:], pa[:, :NB], sgA[:], ALU.mult, [pa, sgA], [mA])
                    tt("dve", mB[:], pb[:, :NB], sgB[:], ALU.mult, [pb, sgB], [mB])
                    tt("pool", mgT[:, dt_, :], mA[:], mB[:], ALU.add, [mA, mB], [mgT])
                for tt_ in range(4):
                    x1 = x1t[n1 % 2]
                    for hf in range(2):
                        po = pj.next()
                        for kc in range(8):
                            mm(po[:, :], mgT[:, kc, tt_ * 128:(tt_ + 1) * 128], wout[:, kc, hf * 512:(hf + 1) * 512],
                               kc == 0, kc == 7, [mgT, wout], [po])
                        tt("dve", x1[:, hf * 512:(hf + 1) * 512], po[:, :], xk[:, tt_, hf * 512:(hf + 1) * 512], ALU.add,
                           [po, xk], [x1])
                    dma(x1_d[t0 + tt_ * 128:t0 + (tt_ + 1) * 128, :], x1[:], [x1], [x1_res[bi]], x1_sem[n1 % 2])
                    if t0 < T:
                        dbg_dump("x1", x1[:], [x1], (slice(t0 + tt_ * 128, t0 + (tt_ + 1) * 128), slice(None)))
                    n1 += 1
          S.barrier()

        if stop_after >= 4:
          with ExitStack() as es:
            wup = sb(es, "wup", (128, 8, FFH), BF16)
            wdn = sb(es, "wdn", (128, 32, D), BF16)
            gzb = sb(es, "gzb", (128, D))
            wst = [sb(es, "wst4_%d" % i, (128, 1024)) for i in range(2)]
            dma(gzb[:], gz_d.partition_broadcast(128), [], [gzb], d0())
            for kc in range(8):
                for q4 in range(4):
                    def consU(st_, kc=kc, q4=q4):
                        if True:
                            ts("pool" if q4 % 2 else "dve", wup[:, kc, q4 * 1024:(q4 + 1) * 1024], st_[:, 0:1024], gcol[:, 1, kc:kc + 1], None,
                               ALU.mult, None, [st_, gcol], [wup])
                    load_weight(wup_d[kc * 128:(kc + 1) * 128, q4 * 1024:(q4 + 1) * 1024], 1024, consU)
            for g in range(32):
                def consDn(st_, g=g):
                    cp("pool" if g % 2 else "dve", wdn[:, g, :], st_[:, 0:1024], [st_], [wdn])
                load_weight(wdn_d[g * 128:(g + 1) * 128, :], 1024, consDn)
            NB = FF_NB
            NT4 = NB // 128
            hT = sb(es, "hTf", (128, 8, NB), BF16)
            xk = sb(es, "xkf", (128, NT4, D))
            pTr = Pool([ps(es, "pTr4_%d" % i, (128, 8, 128), BF16) for i in range(1)])
            pj = Pool([ps(es, "pj4_%d" % i, (128, 512)) for i in range(6)])
            aT = sb(es, "aT", (128, 32, NB), BF16)
            rl = [sb(es, "rl%d" % i, (128, NB), BF16) for i in range(2)]
            xx = sb(es, "x2", (128, D))
            ot = [sb(es, "ot%d" % i, (128, D)) for i in range(2)]
            o_sem = [S.new_dsem() for _ in range(2)]
            st2 = [sb(es, "st2_%d" % i, (128, 4)) for i in range(2)]
            n2 = 0
            for bi in range(NTOK // NB):
                t0 = bi * NB
                make_hT(pTr, t0, NT4, hT, 0, x1_d, keep=xk, src_res=[x1_res[t0 // 512]])
                for ht in range(32):
                    pu = pj.next()
                    for kc in range(8):
                        mm(pu[:, :NB], wup[:, kc, ht * 128:(ht + 1) * 128], hT[:, kc, :], kc == 0, kc == 7, [wup, hT], [pu])
                    r_ = rl[ht % 2]
                    act(r_[:], pu[:, :NB], AF.Relu, [pu], [r_])
                    tt("pool" if ht % 2 else "dve", aT[:, ht, :], r_[:], r_[:], ALU.mult, [r_], [aT])
                for tt_ in range(NT4):
                    oo = ot[n2 % 2]
                    s2 = st2[n2 % 2]
                    for hf in range(2):
                        pd = pj.next()
                        for ht in range(32):
                            mm(pd[:, :], aT[:, ht, tt_ * 128:(tt_ + 1) * 128], wdn[:, ht, hf * 512:(hf + 1) * 512],
                               ht == 0, ht == 31, [aT, wdn], [pd])
                        tt("dve", xx[:, hf * 512:(hf + 1) * 512], pd[:, :], xk[:, tt_, hf * 512:(hf + 1) * 512], ALU.add,
                           [pd, xk], [xx])
                    act(oo[:], xx[:], AF.Square, [xx], [oo, s2], accum_out=s2[:, 0:1])
                    ts("dve", s2[:, 1:2], s2[:, 0:1], 1.0 / D, 1e-6, ALU.mult, ALU.add, [s2], [s2])
                    rsqrt(s2[:, 2:3], s2[:, 1:2], [s2], [s2])
                    stt(oo[:], xx[:], s2[:, 2:3], gzb[:], ALU.mult, ALU.mult, [xx, s2, gzb], [oo])
                    out_dmas.append(dma(out_d[t0 + tt_ * 128:t0 + (tt_ + 1) * 128, :], oo[:], [oo], [], o_sem[n2 % 2]))
                    n2 += 1

        S.finish(out_dmas)
        S.emit()
    return nc


def _swap_halves(cols):
    c = np.asarray(cols).reshape(-1, 2, 32)
    return c[:, ::-1, :].reshape(-1)


def _layout_inputs(inp):
    f = lambda a: np.ascontiguousarray(np.asarray(a, dtype=np.float32))
    w_in = f(inp["w_in"])[0]
    mu = f(inp["mu_shift"])[0]
    colsA = np.concatenate([np.arange(0, 1024), np.arange(1536, 1824)])
    colsV = np.arange(1024, 1536)
    base = 1824
    q = base + np.arange(512)
    k = base + 512 + np.arange(64)
    v = base + 576 + np.arange(64)
    qi = base + 640 + np.arange(512)
    ki = base + 1152 + np.arange(64)
    wi = base + 1216 + np.arange(8)
    colsD = np.concatenate([q, _swap_halves(q), k, k, _swap_halves(k), _swap_halves(k),
                            qi, _swap_halves(qi), ki, ki, _swap_halves(ki), _swap_halves(ki)])
    colsT = np.concatenate([v, wi])
    colsG = 1824 + 1224 + np.arange(2048)
    per_ch = lambda a: f(a)[0].reshape(4, 128).T
    pp = np.concatenate([per_ch(inp["decay_bias"]), per_ch(inp["iclr_bias"]), per_ch(inp["k_k"]),
                         per_ch(inp["k_a"]), per_ch(inp["r_k"])], axis=1)
    shared = {
        "wA": f(w_in[:, colsA]), "wV": f(w_in[:, colsV]), "muA": f(mu[colsA]), "muV": f(mu[colsV]),
        "wD": f(w_in[:, colsD]), "wT": f(w_in[:, colsT]), "wG": f(w_in[:, colsG]),
        "gcols": f(np.concatenate([f(inp["g_mix"])[0].reshape(8, 128).T, f(inp["g_ffn"])[0].reshape(8, 128).T], axis=1)),
        "gfin": f(inp["g_final"]),
        "wlora": f(np.concatenate([f(inp["w_decay_up"])[0], f(inp["w_iclr_up"])[0]], axis=0)),
        "wgate": f(inp["w_gate_up"])[0], "pp": f(pp),
        "gnw": f(inp["gn_w"])[0], "gnb": f(inp["gn_b"])[0],
        "wbr": f(f(inp["w_branch"])[0].reshape(1024, 1024)), "wout": f(inp["w_out"])[0],
        "wup": f(inp["w_ffn_up"])[0], "wdn": f(inp["w_ffn_down"])[0], "cstA": _CSTA, "cst1": _CST1, "cst2": _CST2,
    }
    x = f(inp["x"])
    maps = []
    for c in range(NCORES):
        m = dict(shared)
        m["x"] = np.ascontiguousarray(x[c * NSEQ:(c + 1) * NSEQ].reshape(NTOK, D))
        maps.append(m)
    return maps


def kernel(**inputs):
    maps = _layout_inputs(inputs)
    nc = build_nc()
    res = run_bass_kernel_spmd(nc, maps, core_ids=list(range(NCORES)))
    outs = [np.asarray(r["out"], dtype=np.float32).reshape(NSEQ, T, D) for r in res.results]
    return np.concatenate(outs, axis=0)
```

```python
import os
from contextlib import ExitStack

import numpy as np
import concourse.bass as bass
import concourse.mybir as mybir
from concourse.bass_utils import run_bass_kernel_spmd

F32 = mybir.dt.float32
BF16 = mybir.dt.bfloat16
ALU = mybir.AluOpType
AF = mybir.ActivationFunctionType
AX = mybir.AxisListType

NCORES = 8
T = 2048
D = 1024
NSEQ = 2
NTOK = NSEQ * T
C = 64
C0 = float(np.exp(-0.5))
NEG = -1.0e30
RW_NB = 128
DS_NB = 512
MG_NB = 512
FF_NB = 256
FFH = 4096


class Res:
    __slots__ = ("name", "w", "rd", "rd_dma")

    def __init__(self, name):
        self.name = name
        self.w = None
        self.rd = {}
        self.rd_dma = []


class DmaSem:
    def __init__(self, sem):
        self.sem = sem
        self.count = 0


class _Op:
    __slots__ = ("id", "eng", "fn", "deps", "dsem", "val", "signal")


class Sched:
    ENGS = ("pe", "act", "dve", "pool", "sp")

    def __init__(self, nc, es):
        self.nc = nc
        self.es = es
        self.ops = []
        self.per = {e: [] for e in self.ENGS}
        self.sem = {e: es.enter_context(nc.semaphore("s_" + e)) for e in self.ENGS}
        self.n_dsem = 0
        self.last = {e: None for e in self.ENGS}
        self.dma_since_barrier = []

    def new_dsem(self):
        self.n_dsem += 1
        return DmaSem(self.es.enter_context(self.nc.semaphore("d%d" % self.n_dsem)))

    def op(self, eng, fn, reads=(), writes=(), dsem=None):
        o = _Op()
        o.id = len(self.ops)
        o.eng = eng
        o.fn = fn
        o.dsem = dsem
        o.signal = False
        o.val = None
        deps = {}

        def add(d, kind):
            if d is None:
                return
            if kind == "raw" or d not in deps:
                deps[d] = kind

        for r in reads:
            add(r.w, "raw")
        for w in writes:
            add(w.w, "waw")
            for d in w.rd.values():
                add(d, "war")
            for d in w.rd_dma:
                add(d, "war")
        o.deps = deps
        for r in reads:
            if dsem is not None:
                r.rd_dma.append(o.id)
            else:
                r.rd[eng] = o.id
        for w in writes:
            w.w = o.id
            w.rd = {}
            w.rd_dma = []
        if dsem is not None:
            dsem.count += 16
            o.val = dsem.count
            self.dma_since_barrier.append(o.id)
        self.ops.append(o)
        self.per[eng].append(o)
        self.last[eng] = o.id
        return o

    def barrier(self):
        lasts = [v for v in self.last.values() if v is not None]
        dmas = list(self.dma_since_barrier)
        self.dma_since_barrier = []
        for e in self.ENGS:
            o = self.op(e, lambda en: en.nop())
            for d in lasts + dmas:
                if d != o.id:
                    o.deps[d] = "raw"

    def finish(self, dma_ops):
        o = self.op("sp", lambda en: en.nop())
        for d in dma_ops:
            o.deps[d.id] = "raw"

    def emit(self):
        ops = self.ops
        for o in ops:
            for d, kind in o.deps.items():
                p = ops[d]
                if p.dsem is not None:
                    continue
                if p.eng == o.eng and o.dsem is None and o.eng in ("pe", "sp"):
                    continue
                p.signal = True
        cnt = {e: 0 for e in self.ENGS}
        for o in ops:
            if o.dsem is None and o.signal:
                cnt[o.eng] += 1
                o.val = cnt[o.eng]
        sem = self.sem

        def run(eng, en):
            known = {}
            for o in self.per[eng]:
                need = {}
                for d, kind in o.deps.items():
                    p = ops[d]
                    if p.dsem is not None:
                        key, s, v = ("d", id(p.dsem)), p.dsem.sem, p.val
                    else:
                        if not p.signal:
                            continue
                        if p.eng == eng and o.dsem is None and eng in ("pe", "sp"):
                            continue
                        key, s, v = ("e", p.eng), sem[p.eng], p.val
                    if known.get(key, 0) >= v:
                        continue
                    if key not in need or need[key][1] < v:
                        need[key] = (s, v)
                for key, (s, v) in need.items():
                    en.wait_ge(s, v)
                    known[key] = v
                ins = o.fn(en)
                if o.dsem is not None:
                    ins.then_inc(o.dsem.sem, 16)
                elif o.signal:
                    ins.then_inc(sem[eng], 1)

        with self.nc.Block() as block:
            @block.tensor
            def _(en):
                run("pe", en)

            @block.scalar
            def _(en):
                run("act", en)

            @block.vector
            def _(en):
                run("dve", en)

            @block.gpsimd
            def _(en):
                run("pool", en)

            @block.sync
            def _(en):
                run("sp", en)


class Tl:
    def __init__(self, h, name, nres=1):
        self.h = h
        self.name = name
        self.rs = [Res("%s.%d" % (name, i)) for i in range(nres)]

    @property
    def r(self):
        return self.rs[0]

    def __getitem__(self, k):
        return self.h[k]


class Pool:
    def __init__(self, tiles):
        self.tiles = tiles
        self.i = 0

    def next(self):
        t = self.tiles[self.i % len(self.tiles)]
        self.i += 1
        return t


class _CB:
    def __init__(self):
        self.cols = {}
        self.parts = []
        self.off = 0

    def put(self, name, arr):
        a = np.zeros((128, arr.shape[1]), np.float32)
        a[: arr.shape[0]] = arr
        self.cols[name] = (self.off, arr.shape[1], arr.shape[0])
        self.parts.append(a)
        self.off += arr.shape[1]

    def arr(self):
        return np.ascontiguousarray(np.concatenate(self.parts, axis=1))


def _const_f32():
    A, B1, B2 = _CB(), _CB(), _CB()
    A.put("ident", np.eye(128, dtype=np.float32))
    s = np.arange(64)[:, None]
    t = np.arange(64)[None, :]
    m1 = np.concatenate([(s < t), (s <= t)], axis=1).astype(np.float32)
    B1.put("mask1", m1)
    B1.put("maskL", (s > t).astype(np.float32))
    B1.put("eye8", np.eye(64, dtype=np.float32))
    rm = np.ones((128, RW_NB), np.float32)
    rm[:, ::C] = 0.0
    B1.put("reset", rm)
    bo = np.zeros((128, 128), np.float32)
    bo[:64, :64] = 1.0
    bo[64:, 64:] = 1.0
    A.put("blockones", bo)
    hi = np.zeros((128, 2), np.float32)
    hi[:64, 0] = 1.0
    hi[64:, 1] = 1.0
    A.put("headind", hi)
    tq = np.arange(128)[:, None]
    kk = np.arange(128)[None, :]
    A.put("causal_bias", np.where(kk <= tq, 0.0, NEG).astype(np.float32))
    A.put("causalT", (tq <= kk).astype(np.float32))
    inv = (1.0 / (10000.0 ** (np.arange(0, 64, 2, dtype=np.float32) / np.float32(64)))).astype(np.float32)
    ang = (np.arange(T, dtype=np.float32)[:, None] * inv[None, :]).astype(np.float32)
    cs = np.cos(ang).astype(np.float32).T
    sn = np.sin(ang).astype(np.float32).T
    d = np.arange(128) % 64
    B2.put("ropeC", cs[d % 32])
    sg = np.where(d < 32, -1.0, 1.0).astype(np.float32)[:, None]
    B2.put("ropeS", sn[d % 32] * sg)
    return A, B1, B2


_CA, _C1, _C2 = _const_f32()
_CSTA, _CST1, _CST2 = _CA.arr(), _C1.arr(), _C2.arr()


def build_nc(debug=None):
    nc = bass.Bass("TRN2", target_bir_lowering=False)
    dt_in = lambda name, shape: nc.dram_tensor(name, list(shape), F32, kind="ExternalInput").ap()
    x_d = dt_in("x", (NTOK, D))
    wA_d = dt_in("wA", (D, 1312))
    wV_d = dt_in("wV", (D, 512))
    muA_d = dt_in("muA", (1312,))
    muV_d = dt_in("muV", (512,))
    wD_d = dt_in("wD", (D, 2560))
    wT_d = dt_in("wT", (D, 72))
    wG_d = dt_in("wG", (D, 2048))
    gcol_d = dt_in("gcols", (128, 16))
    gz_d = dt_in("gfin", (D,))
    wlora_d = dt_in("wlora", (128, 512))
    wgate_d = dt_in("wgate", (160, 512))
    pp_d = dt_in("pp", (128, 20))
    gnw_d = dt_in("gnw", (512,))
    gnb_d = dt_in("gnb", (512,))
    wbr_d = dt_in("wbr", (1024, 1024))
    wout_d = dt_in("wout", (D, D))
    wup_d = dt_in("wup", (D, FFH))
    wdn_d = dt_in("wdn", (FFH, D))
    cstA_d = dt_in("cstA", _CSTA.shape)
    cst1_d = dt_in("cst1", _CST1.shape)
    cst2_d = dt_in("cst2", _CST2.shape)
    out_d = nc.dram_tensor("out", [NTOK, D], F32, kind="ExternalOutput").ap()
    yaT_d = nc.dram_tensor("yaT_scr", [128, 4, NTOK], BF16, kind="Internal").ap()
    ybT_d = nc.dram_tensor("ybT_scr", [128, 4, NTOK], BF16, kind="Internal").ap()
    x1_d = nc.dram_tensor("x1_scr", [NTOK, D], F32, kind="Internal").ap()
    dbg_d = {}
    dbg_sem = {}
    if debug:
        for name, shape in debug.items():
            dbg_d[name] = nc.dram_tensor("dbg_" + name, list(shape), F32, kind="ExternalOutput").ap()

    top = ExitStack()
    with top:
        S = Sched(nc, top)
        out_dmas = []
        yaT_res = Res("yaT_scr")
        ybT_res = Res("ybT_scr")
        x1_res = [Res("x1_scr%d" % i) for i in range(NTOK // 512)]

        uid = [0]

        def sb(es, name, shape, dt=F32, nres=1):
            uid[0] += 1
            return Tl(es.enter_context(nc.sbuf_tensor("sb%d_%s" % (uid[0], name), list(shape), dt)), name, nres)

        def ps(es, name, shape, dt=F32):
            uid[0] += 1
            return Tl(es.enter_context(nc.psum_tensor("ps%d_%s" % (uid[0], name), list(shape), dt)), name)

        def rr(*xs):
            out = []
            for x in xs:
                if isinstance(x, Tl):
                    out.extend(x.rs)
                elif isinstance(x, Res):
                    out.append(x)
                else:
                    out.extend(x)
            return out

        def dma(out_ap, in_ap, reads, writes, dsem):
            return S.op("sp", lambda en: en.dma_start(out=out_ap, in_=in_ap),
                        reads=rr(*reads), writes=rr(*writes), dsem=dsem)

        def mm(out_ap, lhsT, rhs, start, stop, reads, writes, skip=False):
            return S.op("pe", lambda en: en.matmul(out_ap, lhsT, rhs, start=start, stop=stop, skip_group_check=skip),
                        reads=rr(*reads), writes=rr(*writes))

        def tr(out_ap, in_ap, ident, reads, writes):
            return S.op("pe", lambda en: en.transpose(out_ap, in_ap, ident),
                        reads=rr(*reads), writes=rr(*writes))

        def act(out_ap, in_ap, func, reads, writes, bias=0.0, scale=1.0, accum_out=None):
            return S.op("act", lambda en: en.activation(out_ap, in_ap, func, bias=bias, scale=scale,
                                                        accum_out=accum_out),
                        reads=rr(*reads), writes=rr(*writes))

        def tt(eng, out_ap, a, b, op, reads, writes):
            return S.op(eng, lambda en: en.tensor_tensor(out_ap, a, b, op), reads=rr(*reads), writes=rr(*writes))

        def ts(eng, out_ap, a, s1, s2, op0, op1, reads, writes):
            if op1 is None:
                return S.op(eng, lambda en: en.tensor_scalar(out_ap, a, s1, None, op0),
                            reads=rr(*reads), writes=rr(*writes))
            return S.op(eng, lambda en: en.tensor_scalar(out_ap, a, s1, s2, op0, op1),
                        reads=rr(*reads), writes=rr(*writes))

        def stt(out_ap, a, sc, b, op0, op1, reads, writes):
            return S.op("dve", lambda en: en.scalar_tensor_tensor(out_ap, a, sc, b, op0, op1),
                        reads=rr(*reads), writes=rr(*writes))

        def rsqrt(out_ap, in_ap, reads, writes):
            act(out_ap, in_ap, AF.Sqrt, reads, writes)
            S.op("dve", lambda en: en.reciprocal(out_ap, out_ap), reads=rr(*writes), writes=rr(*writes))

        def cp(eng, out_ap, in_ap, reads, writes):
            if eng == "act":
                return S.op("act", lambda en: en.copy(out_ap, in_ap), reads=rr(*reads), writes=rr(*writes))
            return S.op(eng, lambda en: en.tensor_copy(out_ap, in_ap), reads=rr(*reads), writes=rr(*writes))

        def memset(eng, ap, val, writes):
            return S.op(eng, lambda en: en.memset(ap, val), writes=rr(*writes))

        def dbg_dump(name, src_ap, reads, dst_slice=None):
            if name not in dbg_d:
                return
            dst = dbg_d[name] if dst_slice is None else dbg_d[name][dst_slice]
            if name not in dbg_sem:
                dbg_sem[name] = S.new_dsem()
            out_dmas.append(dma(dst, src_ap, reads, [], dbg_sem[name]))

        def run_rr2(gens):
            gens = list(gens)
            while gens:
                for g_ in list(gens):
                    try:
                        next(g_)
                    except StopIteration:
                        gens.remove(g_)

        cst = Tl(None, "cstgroup", 0)
        ctiles = {}

        def load_const(es_, key, arr, src_d, cb):
            t_ = sb(es_, "cs_sb" + key, arr.shape)
            dma(t_[:], src_d, [], [t_], S.new_dsem())
            cst.rs.extend(t_.rs)
            for nm in cb.cols:
                ctiles[nm] = (t_, cb.cols[nm])

        load_const(top, "A", _CSTA, cstA_d, _CA)

        def cc(name, rows=None):
            t_, (o, n, r0) = ctiles[name]
            return t_[: (rows or r0), o:o + n]

        def d0():
            return S.new_dsem()

        nhalf = sb(top, "nhalf", (128, 256))
        memset("pool", nhalf[:], -0.5, [nhalf])
        identb = sb(top, "identb", (128, 128), BF16)
        cp("dve", identb[:], cc("ident"), [cst], [identb])
        gcol = sb(top, "gcol", (128, 2, 8))
        dma(gcol[:].rearrange("p a k -> p (a k)"), gcol_d, [], [gcol], d0())
        pp = sb(top, "pp", (128, 20))
        dma(pp[:], pp_d, [], [pp], d0())

        WSTN = 1312
        wst = None
        wst_sem = [S.new_dsem() for _ in range(2)]
        wst_i = [0]

        def load_weight(src_ap, ncols, consume):
            i = wst_i[0] % 2
            wst_i[0] += 1
            dma(wst[i][:, :ncols], src_ap, [], [wst[i]], wst_sem[i])
            consume(wst[i])

        xt_sem = [S.new_dsem() for _ in range(2)]
        xt_i = [0]
        xs_bf = [sb(top, "xsbf%d" % i, (128, D), BF16) for i in range(2)]
        stat = [sb(top, "stat%d" % i, (128, 4)) for i in range(2)]

        def make_hT(es_ps, tok0, ntile, hT, col0, src_d, xt=None, keep=None, src_res=()):
            for i in range(ntile):
                k = xt_i[0] % 2
                xt_i[0] += 1
                if keep is not None:
                    xin = keep
                    xap = keep[:, i, :]
                    dma(xap, src_d[tok0 + i * 128: tok0 + (i + 1) * 128, :], src_res, [keep], xt_sem[k])
                else:
                    xin = xt[k % len(xt)]
                    xap = xin[:]
                    dma(xap, src_d[tok0 + i * 128: tok0 + (i + 1) * 128, :], src_res, [xin], xt_sem[k % len(xt)])
                st = stat[k]
                act(xs_bf[k][:], xap, AF.Square, [xin], [xs_bf[k], st], accum_out=st[:, 0:1])
                ts("dve", st[:, 1:2], st[:, 0:1], 1.0 / D, 1e-6, ALU.mult, ALU.add, [st], [st])
                rsqrt(st[:, 2:3], st[:, 1:2], [st], [st])
                ts("dve", xs_bf[k][:], xap, st[:, 2:3], None, ALU.mult, None, [xin, st], [xs_bf[k]])
                pt = es_ps.next()
                for kc in range(8):
                    tr(pt[:, kc, :], xs_bf[k][:, kc * 128:(kc + 1) * 128], identb[:], [xs_bf[k], identb], [pt])
                cp("act" if i % 2 else "dve", hT[:, :, col0 + i * 128: col0 + (i + 1) * 128], pt[:, :, :], [pt], [hT])

        with ExitStack() as es:
            W1A = sb(es, "W1A", (128, 8, 1312), BF16)
            W2A = sb(es, "W2A", (128, 8, 1312), BF16)
            W1V = sb(es, "W1V", (128, 8, 512), BF16)
            W2V = sb(es, "W2V", (128, 8, 512), BF16)
            with ExitStack() as es_w:
                wst = [sb(es_w, "wst1_%d" % i, (128, WSTN)) for i in range(2)]
                mub = sb(es_w, "mub", (128, 1824))
                omb = sb(es_w, "omb", (128, 1824))
                dma(mub[:, 0:1312], muA_d.partition_broadcast(128), [], [mub], d0())
                dma(mub[:, 1312:1824], muV_d.partition_broadcast(128), [], [mub], d0())
                ts("pool", omb[:], mub[:], -1.0, 1.0, ALU.mult, ALU.add, [mub], [omb])
                for kc in range(8):
                    def consA(st_, kc=kc):
                        stt(W1A[:, kc, :], st_[:, :1312], gcol[:, 0, kc:kc + 1], omb[:, 0:1312], ALU.mult, ALU.mult,
                            [st_, gcol, omb], [W1A])
                        stt(W2A[:, kc, :], st_[:, :1312], gcol[:, 0, kc:kc + 1], mub[:, 0:1312], ALU.mult, ALU.mult,
                            [st_, gcol, mub], [W2A])
                    load_weight(wA_d[kc * 128:(kc + 1) * 128, :], 1312, consA)

                    def consV(st_, kc=kc):
                        stt(W1V[:, kc, :], st_[:, :512], gcol[:, 0, kc:kc + 1], omb[:, 1312:1824], ALU.mult, ALU.mult,
                            [st_, gcol, omb], [W1V])
                        stt(W2V[:, kc, :], st_[:, :512], gcol[:, 0, kc:kc + 1], mub[:, 1312:1824], ALU.mult, ALU.mult,
                            [st_, gcol, mub], [W2V])
                    load_weight(wV_d[kc * 128:(kc + 1) * 128, :], 512, consV)
            S.barrier()
            load_const(es, "1", _CST1, cst1_d, _C1)
            xt = [sb(es, "xt%d" % i, (128, D)) for i in range(1)]
            wlora = sb(es, "wlora", (128, 512))
            wg0f = sb(es, "wg0f", (128, 512))
            wg1f = sb(es, "wg1f", (32, 512))
            wg0 = sb(es, "wg0", (128, 512), BF16)
            wg1 = sb(es, "wg1", (32, 512), BF16)
            gnwb = sb(es, "gnwb", (64, 512))
            gnbb = sb(es, "gnbb", (64, 512))
            dma(wlora[:], wlora_d, [], [wlora], d0())
            dma(wg0f[:], wgate_d[0:128, :], [], [wg0f], d0())
            dma(wg1f[:], wgate_d[128:160, :], [], [wg1f], d0())
            cp("dve", wg0[:], wg0f[:], [wg0f], [wg0])
            cp("dve", wg1[:], wg1f[:], [wg1f], [wg1])
            dma(gnwb[:], gnw_d.partition_broadcast(64), [], [gnwb], d0())
            dma(gnbb[:], gnb_d.partition_broadcast(64), [], [gnbb], d0())

            NB = RW_NB
            NCH = NB // C
            G = T // C
            hT = sb(es, "hT", (128, 8, NB + 2), BF16)
            pTr = Pool([ps(es, "pTr", (128, 8, 128), BF16)])
            pjp = ps(es, "pjp", (128, 512))
            pbon = ps(es, "pbon", (128, 512))
            pI = ps(es, "pI", (128, 2, 512))
            pI3 = ps(es, "pI3", (128, 512))
            pSA = ps(es, "pSA", (128, 512))
            pSB = ps(es, "pSB", (128, 512))
            r_sb = sb(es, "r_sb", (128, 4, NB))
            k_sb = sb(es, "k_sb", (128, 4, NB))
            wa_sb = sb(es, "wa_sb", (128, NB))
            tmp = [sb(es, "rt%d" % i, (128, NB)) for i in range(10)]

            class PSet:
                pass
            psets = []
            for i in range(3):
                P_ = PSet()
                P_.sg0 = sb(es, "sg0_%d" % i, (128, NB), BF16)
                P_.sg1 = sb(es, "sg1_%d" % i, (32, NB), BF16)
                P_.v = sb(es, "v_sb%d" % i, (64, NCH, 512), BF16)
                P_.AR = [sb(es, "AR%d_%d" % (h, i), (128, NCH, 2, C), BF16) for h in range(4)]
                P_.Bt = [sb(es, "Bt%d_%d" % (h, i), (128, NCH, C), BF16) for h in range(4)]
                P_.Kt = [sb(es, "Kt%d_%d" % (h, i), (128, NCH, C), BF16) for h in range(4)]
                P_.BKh = [sb(es, "BKh%d_%d" % (h, i), (128, NCH, 2, C), BF16) for h in range(4)]
                P_.wC = sb(es, "wC%d" % i, (128, 4, NCH))
                P_.bon = sb(es, "bon%d" % i, (64, NCH, 8))
                P_.ya = sb(es, "yaT%d" % i, (128, 4, NB), BF16)
                P_.ya_sem = S.new_dsem()
                psets.append(P_)
            csets = []
            for i in range(2):
                Q_ = PSet()
                Q_.MA = sb(es, "MA%d" % i, (64, 8, 2 * C), BF16)
                Q_.KA = sb(es, "KA%d" % i, (64, 8, 2 * C), BF16)
                Q_.Tf = sb(es, "Tf%d" % i, (64, 8, C), BF16)
                Q_.BKtok = sb(es, "BKtok%d" % i, (64, 4, 2, 128), BF16)
                Q_.y = sb(es, "y_sb%d" % i, (64, 512))
                csets.append(Q_)
            ML = [sb(es, "ML%d" % i, (64, 8, 2, C), BF16) for i in range(2)]
            TT = [sb(es, "TT%d" % i, (64, 8, C), BF16) for i in range(2)]
            ST = sb(es, "ST", (128, 4, C))
            X_sb = sb(es, "X_sb", (64, 8, C), BF16)
            X32 = sb(es, "X32", (64, 8, C))
            STb = sb(es, "STb", (128, 4, C), BF16)
            U_sb = sb(es, "U_sb", (64, 512), BF16)
            ysq = sb(es, "ysq", (64, 512))
            ytmp = sb(es, "ytmp", (64, 512))
            gst = sb(es, "gst", (64, 6, 8))
            ident = cc("ident")
            m1b = cc("mask1").unsqueeze(1).to_broadcast([64, 8, 2 * C])
            mLb = cc("maskL").unsqueeze(1).to_broadcast([64, 8, C])
            eyb = cc("eye8").unsqueeze(1).to_broadcast([64, 8, C])
            hrow = lambda h: slice((h % 2) * 64, (h % 2) * 64 + 64)
            HORD = [0, 2, 4, 6, 1, 3, 5, 7]
            dbg_chunks = int(os.environ.get("MK_CHUNKS", "999")) if debug else 999

            def prep_task(b, blk, P_):
                t0 = b * T + blk * NB
                if blk == 0:
                    memset("pool", hT[:, :, 0:2], 0.0, [hT])
                else:
                    cp("dve", hT[:, :, 1:2], hT[:, :, NB + 1:NB + 2], [hT], [hT])
                make_hT(pTr, t0, NB // 128, hT, 2, x_d, xt=xt)
                yield
                for ct in range(11):
                    rows = 32 if ct == 10 else 128
                    c0 = ct * 128
                    p_ = pjp
                    for kc in range(8):
                        mm(p_[:rows, :NB], W1A[:, kc, c0:c0 + rows], hT[:, kc, 2:NB + 2], kc == 0, False, [W1A, hT], [p_])
                        mm(p_[:rows, :NB], W2A[:, kc, c0:c0 + rows], hT[:, kc, 1:NB + 1], False, kc == 7, [W2A, hT], [p_])
                    if ct < 4:
                        cp("act", r_sb[:, ct, :], p_[:, :NB], [p_], [r_sb])
                    elif ct < 8:
                        cp("dve", k_sb[:, ct - 4, :], p_[:, :NB], [p_], [k_sb])
                    elif ct == 8:
                        act(wa_sb[0:64, :], p_[0:64, :NB], AF.Tanh, [p_], [wa_sb])
                        cp("dve", wa_sb[64:128, :], p_[64:128, :NB], [p_], [wa_sb])
                    elif ct == 9:
                        act(P_.sg0[:], p_[:, :NB], AF.Sigmoid, [p_], [P_.sg0])
                    else:
                        act(P_.sg1[:], p_[0:32, :NB], AF.Sigmoid, [p_], [P_.sg1])
                    if ct % 3 == 2:
                        yield
                for c in range(NCH):
                    p_ = pjp
                    for kc in range(8):
                        mm(p_[0:64, :], hT[:, kc, 2 + c * C:2 + (c + 1) * C], W1V[:, kc, :], kc == 0, False, [W1V, hT], [p_])
                        mm(p_[0:64, :], hT[:, kc, 1 + c * C:1 + (c + 1) * C], W2V[:, kc, :], False, kc == 7, [W2V, hT], [p_])
                    cp("act", P_.v[:, c, :], p_[0:64, :], [p_], [P_.v])
                yield
                v4 = lambda a: a[:].rearrange("p (c t) -> p c t", t=C)
                for hp in range(4):
                    cs_ = slice(hp * 128, (hp + 1) * 128)
                    ppc = lambda j, hp=hp: pp[:, j * 4 + hp: j * 4 + hp + 1]
                    sgd, icl, cum, e_in, e_ng, e_ex, e_rm, kkn, kmod, t9 = tmp
                    AR, Bt, Kt, BKh = P_.AR, P_.Bt, P_.Kt, P_.BKh
                    p_ = pjp
                    mm(p_[:, :NB], wlora[0:64, cs_], wa_sb[0:64, :], True, True, [wlora, wa_sb], [p_])
                    act(sgd[:], p_[:, :NB], AF.Sigmoid, [p_, pp], [sgd], bias=ppc(0))
                    p_ = pbon
                    mm(p_[:, 256:256 + NB], wlora[64:128, cs_], wa_sb[64:128, :], True, True, [wlora, wa_sb], [p_])
                    act(icl[:], p_[:, 256:256 + NB], AF.Sigmoid, [p_, pp], [icl], bias=ppc(1))
                    S.op("dve", lambda en, cum=cum, sgd=sgd: en.tensor_tensor_scan(
                        cum[:], cc("reset"), sgd[:], 0.0, ALU.mult, ALU.add), reads=rr(cst, sgd), writes=rr(cum))
                    yield
                    act(e_in[:], cum[:], AF.Exp, [cum], [e_in], scale=-C0)
                    act(e_ng[:], cum[:], AF.Exp, [cum], [e_ng], scale=C0)
                    tt("dve", t9[:], cum[:], sgd[:], ALU.subtract, [cum, sgd], [t9])
                    act(e_ex[:], t9[:], AF.Exp, [t9], [e_ex], scale=-C0)
                    cum3 = cum[:].rearrange("p (c t) -> p c t", t=C)
                    tt("dve", t9[:].rearrange("p (c t) -> p c t", t=C),
                       cum3[:, :, C - 1:C].to_broadcast([128, NCH, C]), cum3, ALU.subtract, [cum], [t9])
                    act(e_rm[:], t9[:], AF.Exp, [t9], [e_rm], scale=-C0)
                    cp("dve", P_.wC[:, hp, :], e_in[:].rearrange("p (c t) -> p c t", t=C)[:, :, C - 1], [e_in], [P_.wC])
                    kx = k_sb[:, hp, :]
                    ts("dve", kkn[:], kx, ppc(2), None, ALU.mult, None, [k_sb, pp], [kkn])
                    tt("dve", t9[:], kkn[:], kkn[:], ALU.mult, [kkn], [t9])
                    p_ = pjp
                    mm(p_[:, :NB], cc("blockones"), t9[:], True, True, [cst, t9], [p_])
                    ts("dve", t9[:], p_[:, :NB], 1e-24, None, ALU.max, None, [p_], [t9])
                    yield
                    rsqrt(t9[:], t9[:], [t9], [t9])
                    tt("dve", kkn[:], kkn[:], t9[:], ALU.mult, [kkn, t9], [kkn])
                    ts("dve", t9[:], icl[:], -1.0, ppc(3), ALU.add, ALU.mult, [icl, pp], [t9])
                    stt(kmod[:], t9[:], 1.0, kx, ALU.add, ALU.mult, [t9, k_sb], [kmod])
                    tt("dve", icl[:], icl[:], kkn[:], ALU.mult, [icl, kkn], [icl])
                    stt(AR[hp][:, :, 0, :], v4(kkn), -1.0, v4(e_ex), ALU.mult, ALU.mult, [kkn, e_ex], [AR[hp]])
                    tt("dve", AR[hp][:, :, 1, :], r_sb[:, hp, :].rearrange("p (c t) -> p c t", t=C), v4(e_in),
                       ALU.mult, [r_sb, e_in], [AR[hp]])
                    yield
                    tt("dve", Bt[hp][:], v4(icl), v4(e_ng), ALU.mult, [icl, e_ng], [Bt[hp]])
                    tt("dve", Kt[hp][:], v4(kmod), v4(e_ng), ALU.mult, [kmod, e_ng], [Kt[hp]])
                    tt("dve", BKh[hp][:, :, 0, :], v4(icl), v4(e_rm), ALU.mult, [icl, e_rm], [BKh[hp]])
                    tt("dve", BKh[hp][:, :, 1, :], v4(kmod), v4(e_rm), ALU.mult, [kmod, e_rm], [BKh[hp]])
                    stt(t9[:], r_sb[:, hp, :], ppc(4), kmod[:], ALU.mult, ALU.mult, [r_sb, pp, kmod], [t9])
                    for c in range(NCH):
                        mm(pbon[0:64, c * 8 + hp * 2: c * 8 + hp * 2 + 2], t9[:, c * C:(c + 1) * C],
                           cc("headind"), True, True, [t9, cst], [pbon])
                    yield
                cp("act", P_.bon[:].rearrange("p c h -> p (c h)"), pbon[0:64, 0:NCH * 8], [pbon], [P_.bon])

            def inv_task(c, P_, Q_):
                AR, Bt, Kt, BKh = P_.AR, P_.Bt, P_.Kt, P_.BKh
                MA, KA = Q_.MA, Q_.KA
                for h in HORD:
                    hp, rs_ = h // 2, hrow(h)
                    arh = AR[hp][rs_, c, :, :].rearrange("p a t -> p (a t)")
                    mm(pI[0:64, h % 2, hp * 128:hp * 128 + 128], Bt[hp][rs_, c, :], arh, True, True,
                       [Bt[hp], AR[hp]], [pI])
                m1p = cc("mask1").unsqueeze(1).unsqueeze(1).to_broadcast([64, 2, 4, 2 * C])
                tt("dve", MA[:].rearrange("p (hp par) m -> p par hp m", par=2),
                   pI[0:64, :, :].rearrange("p par (hp m) -> p par hp m", m=2 * C), m1p, ALU.mult, [pI, cst], [MA])
                mlc = ML[0]
                cp("dve", mlc[:, :, 0, :], MA[:, :, 0:C], [MA], [mlc])
                tcur = TT[0]
                tt("dve", tcur[:], MA[:, :, 0:C], eyb, ALU.add, [MA, cst], [tcur])
                yield
                for h in HORD:
                    hp, rs_ = h // 2, hrow(h)
                    mm(pI[0:64, h % 2, hp * C:(hp + 1) * C], AR[hp][rs_, c, 0, :], Bt[hp][rs_, c, :], True, True,
                       [AR[hp], Bt[hp]], [pI])
                tt("dve", mlc[:, :, 1, :].rearrange("p (hp par) s -> p par hp s", par=2),
                   pI[0:64, :, 0:4 * C].rearrange("p par (hp s) -> p par hp s", s=C),
                   cc("maskL").unsqueeze(1).unsqueeze(1).to_broadcast([64, 2, 4, C]), ALU.mult, [pI, cst], [mlc])
                yield
                for h in HORD:
                    hp, rs_ = h // 2, hrow(h)
                    arh = AR[hp][rs_, c, :, :].rearrange("p a t -> p (a t)")
                    mm(pI[0:64, h % 2, hp * 128:hp * 128 + 128], Kt[hp][rs_, c, :], arh, True, True,
                       [Kt[hp], AR[hp]], [pI])
                tt("dve", KA[:].rearrange("p (hp par) m -> p par hp m", par=2),
                   pI[0:64, :, :].rearrange("p par (hp m) -> p par hp m", m=2 * C), m1p, ALU.mult, [pI, cst], [KA])
                yield

                def squares(mlc, mln, lev):
                    for h in range(8):
                        if lev < 5:
                            mm(pI[0:64, h // 4, (h % 4) * 128:(h % 4) * 128 + C], mlc[:, h, 1, :], mlc[:, h, 0, :],
                               True, True, [mlc], [pI])
                        mm(pI[0:64, h // 4, (h % 4) * 128 + C:(h % 4) * 128 + 2 * C], mlc[:, h, 0, :],
                           mlc[:, h, 1, :], True, True, [mlc], [pI])
                    if lev < 5:
                        cp("act", mln[:].rearrange("p (a h) x s -> p a (h x s)", a=2), pI[0:64, :, :], [pI], [mln])
                    else:
                        cp("act", mln[:, :, 1, :].rearrange("p (a h) s -> p a h s", a=2),
                           pI[0:64, :, :].rearrange("p a (h x s) -> p a h x s", h=4, x=2)[:, :, :, 1, :], [pI], [mln])

                def tupdate(mln, tcur, tnew):
                    for h in range(8):
                        mm(pI3[0:64, h * C:(h + 1) * C], mln[:, h, 1, :], tcur[:, h, :], True, True, [mln, tcur], [pI3])
                    tt("dve", tnew[:], pI3[0:64, :].rearrange("p (h s) -> p h s", h=8), tcur[:], ALU.add,
                       [pI3, tcur], [tnew])

                for lev in range(1, 6):
                    mln = ML[lev % 2]
                    squares(mlc, mln, lev)
                    tnew = Q_.Tf if lev == 5 else TT[lev % 2]
                    tupdate(mln, tcur, tnew)
                    mlc, tcur = mln, tnew
                    yield
                pIb = pI[:, 0, :].bitcast(BF16)
                for hp in range(4):
                    for a in range(2):
                        tr(pIb[0:64, hp * 256 + a * 128:hp * 256 + a * 128 + 128], BKh[hp][:, c, a, :], identb[:],
                           [BKh[hp], identb], [pI])
                cp("act", Q_.BKtok[:].rearrange("p h x m -> p (h x m)"), pIb[0:64, :], [pI], [Q_.BKtok])
                yield

            def state_task(c, P_, Q_):
                AR, v_sb = P_.AR, P_.v
                MA, KA, tcur, BKtok, y_sb = Q_.MA, Q_.KA, Q_.Tf, Q_.BKtok, Q_.y
                bank = lambda h: (pSA if h % 2 == 0 else pSB)
                bank2 = lambda h: (pSA if h < 4 else pSB)
                for h in HORD:
                    hp, rs_ = h // 2, hrow(h)
                    mm(bank(h)[0:64, hp * C:(hp + 1) * C], AR[hp][rs_, c, 0, :], STb[rs_, hp, :], True, True,
                       [AR[hp], STb], [bank(h)])
                for h in range(8):
                    mm(bank2(h)[0:64, 256 + (h % 4) * C:256 + (h % 4 + 1) * C], KA[:, h, 0:C], v_sb[:, c, h * C:(h + 1) * C],
                       True, True, [KA, v_sb], [bank2(h)])
                X4 = X32[:].rearrange("p (hp par) s -> p par hp s", par=2)
                cp("act", X4[:, 0, :, :], pSA[0:64, 0:4 * C].rearrange("p (hp s) -> p hp s", s=C), [pSA], [X32])
                cp("act", X4[:, 1, :, :], pSB[0:64, 0:4 * C].rearrange("p (hp s) -> p hp s", s=C), [pSB], [X32])
                tt("dve", X_sb[:, 0:4, :], X32[:, 0:4, :], pSA[0:64, 256:512].rearrange("p (h s) -> p h s", s=C), ALU.add,
                   [X32, pSA], [X_sb])
                tt("dve", X_sb[:, 4:8, :], X32[:, 4:8, :], pSB[0:64, 256:512].rearrange("p (h s) -> p h s", s=C), ALU.add,
                   [X32, pSB], [X_sb])
                yield
                for h in range(8):
                    mm(pSA[0:64, h * C:(h + 1) * C], tcur[:, h, :], X_sb[:, h, :], True, True, [tcur, X_sb], [pSA])
                cp("dve", U_sb[:], pSA[0:64, :], [pSA], [U_sb])
                yield
                for h in HORD:
                    hp, rs_ = h // 2, hrow(h)
                    mm(bank(h)[0:64, hp * C:(hp + 1) * C], AR[hp][rs_, c, 1, :], STb[rs_, hp, :], True, True,
                       [AR[hp], STb], [bank(h)])
                for h in range(8):
                    o_ = bank2(h)[0:64, 256 + (h % 4) * C:256 + (h % 4 + 1) * C]
                    mm(o_, MA[:, h, C:2 * C], U_sb[:, h * C:(h + 1) * C], True, False, [MA, U_sb], [bank2(h)])
                    mm(o_, KA[:, h, C:2 * C], v_sb[:, c, h * C:(h + 1) * C], False, True, [KA, v_sb], [bank2(h)])
                Y4 = y_sb[:].rearrange("p (hp par s) -> p par hp s", par=2, s=C)
                cp("act", Y4[:, 0, :, :], pSA[0:64, 0:4 * C].rearrange("p (hp s) -> p hp s", s=C), [pSA], [y_sb])
                cp("act", Y4[:, 1, :, :], pSB[0:64, 0:4 * C].rearrange("p (hp s) -> p hp s", s=C), [pSB], [y_sb])
                tt("dve", y_sb[:, 0:256], y_sb[:, 0:256], pSA[0:64, 256:512], ALU.add, [y_sb, pSA], [y_sb])
                tt("dve", y_sb[:, 256:512], y_sb[:, 256:512], pSB[0:64, 256:512], ALU.add, [y_sb, pSB], [y_sb])
                yield
                pS = pSB
                for hp in range(4):
                    mm(pS[:, hp * 128:(hp + 1) * 128], BKtok[:, hp, 0, :], U_sb[:, hp * 128:(hp + 1) * 128],
                       True, False, [BKtok, U_sb], [pS])
                    mm(pS[:, hp * 128:(hp + 1) * 128], BKtok[:, hp, 1, :], v_sb[:, c, hp * 128:(hp + 1) * 128],
                       False, True, [BKtok, v_sb], [pS])
                for hp in range(4):
                    for hh in range(2):
                        rs_ = slice(hh * 64, hh * 64 + 64)
                        stt(ST[rs_, hp, :], ST[rs_, hp, :], P_.wC[rs_, hp, c:c + 1],
                            pS[rs_, hp * 128 + hh * 64: hp * 128 + hh * 64 + 64], ALU.mult, ALU.add, [ST, P_.wC, pS], [ST])
                cp("act", STb[:], ST[:], [ST], [STb])
                yield

            def ypost_task(b, g, c, P_, Q_):
                y_sb, v_sb = Q_.y, P_.v
                t0c = b * T + g * C
                y3 = y_sb[:].rearrange("p (h i) -> p h i", h=8)
                S.op("dve", lambda en: en.tensor_reduce(gst[:, 0, :], y3, AX.X, ALU.add), reads=rr(y_sb), writes=rr(gst))
                act(ysq[:], y_sb[:], AF.Square, [y_sb], [ysq])
                S.op("dve", lambda en: en.tensor_reduce(gst[:, 1, :], ysq[:].rearrange("p (h i) -> p h i", h=8),
                                                        AX.X, ALU.add), reads=rr(ysq), writes=rr(gst))
                ts("dve", gst[:, 2, :], gst[:, 0, :], 1.0 / 64, None, ALU.mult, None, [gst], [gst])
                tt("dve", gst[:, 3, :], gst[:, 2, :], gst[:, 2, :], ALU.mult, [gst], [gst])
                stt(gst[:, 4, :], gst[:, 1, :], 1.0 / 64, gst[:, 3, :], ALU.mult, ALU.subtract, [gst], [gst])
                ts("dve", gst[:, 4, :], gst[:, 4, :], 64e-5, None, ALU.add, None, [gst], [gst])
                rsqrt(gst[:, 5, :], gst[:, 4, :], [gst], [gst])
                yield
                bc = lambda a: a.unsqueeze(2).to_broadcast([64, 8, 64])
                yt3 = ytmp[:].rearrange("p (h i) -> p h i", h=8)
                tt("dve", yt3, y3, bc(gst[:, 2, :]), ALU.subtract, [y_sb, gst], [ytmp])
                tt("dve", yt3, yt3, bc(gst[:, 5, :]), ALU.mult, [ytmp, gst], [ytmp])
                tt("dve", ytmp[:], ytmp[:], gnwb[:], ALU.mult, [ytmp, gnwb], [ytmp])
                tt("dve", ytmp[:], ytmp[:], gnbb[:], ALU.add, [ytmp, gnbb], [ytmp])
                ys3 = ysq[:].rearrange("p (h i) -> p h i", h=8)
                tt("dve", ys3, v_sb[:, c, :].rearrange("p (h i) -> p h i", h=8), bc(P_.bon[:, c, :]), ALU.mult,
                   [v_sb, P_.bon], [ysq])
                tt("dve", ytmp[:], ytmp[:], ysq[:], ALU.add, [ytmp, ysq], [ytmp])
                pg = pjp
                mm(pg[0:64, :], P_.sg0[:, c * C:(c + 1) * C], wg0[:], True, False, [P_.sg0, wg0], [pg])
                mm(pg[0:64, :], P_.sg1[:, c * C:(c + 1) * C], wg1[:], False, True, [P_.sg1, wg1], [pg])
                tt("dve", ytmp[:], ytmp[:], pg[0:64, :], ALU.mult, [ytmp, pg], [ytmp])
                if b == 0:
                    dbg_dump("ya", ytmp[:], [ytmp], (slice(t0c, t0c + C), slice(None)))
                yield
                pq = pjp
                for kc in range(4):
                    tr(pq[:, kc * C:(kc + 1) * C], ytmp[:, kc * 128:(kc + 1) * 128], ident[0:64, 0:64], [ytmp, cst], [pq])
                cp("act", P_.ya[:, :, c * C:(c + 1) * C], pq[:, 0:4 * C].rearrange("p (k t) -> p k t", k=4), [pq], [P_.ya])
                if c == NCH - 1:
                    tb = b * T + (g // NCH) * NB
                    dma(yaT_d[:, :, tb:tb + NB], P_.ya[:], [P_.ya], [yaT_res], P_.ya_sem)
                yield

            def pgen_slice(pg_, j, n):
                cnt = 0
                while True:
                    if j < n - 1 and cnt >= (24 // n):
                        return
                    try:
                        next(pg_)
                    except StopIteration:
                        return
                    cnt += 1
                    yield

            def run_rr(gens):
                run_rr2(gens)

                gens = list(gens)
                while gens:
                    for g_ in list(gens):
                        try:
                            next(g_)
                        except StopIteration:
                            gens.remove(g_)

            for b in range(NSEQ):
                memset("dve", ST[:], 0.0, [ST])
                memset("dve", STb[:], 0.0, [STb])
                Gn = min(G, dbg_chunks)
                run_rr([prep_task(b, 0, psets[0])])
                for k in range(Gn + 2):
                    gens = []
                    if k < Gn:
                        gens.append(inv_task(k % NCH, psets[(k // NCH) % 3], csets[k % 2]))
                    if 1 <= k <= Gn:
                        g = k - 1
                        gens.append(state_task(g % NCH, psets[(g // NCH) % 3], csets[g % 2]))
                    if 2 <= k <= Gn + 1:
                        g = k - 2
                        gens.append(ypost_task(b, g, g % NCH, psets[(g // NCH) % 3], csets[g % 2]))
                    if k % NCH == 0:
                        nb_ = k // NCH + 1
                        pgen = prep_task(b, nb_, psets[nb_ % 3]) if nb_ * NCH < Gn else None
                    if pgen is not None:
                        gens.append(pgen_slice(pgen, k % NCH, NCH))
                    run_rr(gens)

        S.barrier()
        stop_after = int(os.environ.get("MK_STOP", "99")) if debug else 99

        if stop_after >= 2:
          with ExitStack() as es:
            WD = sb(es, "WD", (128, 8, 2560), BF16)
            WT = sb(es, "WT", (128, 8, 72), BF16)
            load_const(es, "2", _CST2, cst2_d, _C2)
            xt = [sb(es, "xt2_%d" % i, (128, D)) for i in range(2)]
            wst = [sb(es, "wst2_%d" % i, (128, WSTN)) for i in range(2)]
            for kc in range(8):
                for hf in range(2):
                    def consD(st_, kc=kc, hf=hf):
                        ts("pool" if hf else "dve", WD[:, kc, hf * 1280:(hf + 1) * 1280], st_[:, :1280], gcol[:, 0, kc:kc + 1], None,
                           ALU.mult, None, [st_, gcol], [WD])
                    load_weight(wD_d[kc * 128:(kc + 1) * 128, hf * 1280:(hf + 1) * 1280], 1280, consD)

                def consT(st_, kc=kc):
                    ts("dve", WT[:, kc, :], st_[:, :72], gcol[:, 0, kc:kc + 1], None, ALU.mult, None, [st_, gcol], [WT])
                load_weight(wT_d[kc * 128:(kc + 1) * 128, :], 72, consT)
            NB = DS_NB
            hT = sb(es, "hTd", (128, 8, NB), BF16)
            pTr = Pool([ps(es, "pTr2_%d" % i, (128, 8, 128), BF16) for i in range(1)])
            pj = Pool([ps(es, "pj2_%d" % i, (128, 512)) for i in range(3)])
            pW = Pool([ps(es, "pW2_%d" % i, (128, 2, 512)) for i in range(1)])
            po = ps(es, "po2", (128, 2, 512))
            qT = sb(es, "qT", (128, 4, NB), BF16)
            qiT = sb(es, "qiT", (128, 4, NB), BF16)
            kT_all = sb(es, "kT_all", (128, T), BF16)
            kiT_all = sb(es, "kiT_all", (128, T), BF16)
            vones = sb(es, "vones", (128, 16, 65), BF16)
            wi_sb = sb(es, "wi_sb", (128, 4, 8))
            rt1 = sb(es, "rt1", (128, NB))
            rt2 = sb(es, "rt2", (128, NB))
            acc = sb(es, "acc", (128, T))
            work = sb(es, "work", (128, T))
            relu_t = [sb(es, "relu%d" % i, (128, 512)) for i in range(2)]
            mx8 = sb(es, "mx8", (128, 8))
            maskb = sb(es, "maskb", (128, T), BF16)
            maskT = sb(es, "maskT", (128, 16, 128), BF16)
            causT = sb(es, "causT", (128, 128), BF16)
            eT = [sb(es, "eT%d" % i, (128, 8, 128), BF16) for i in range(2)]
            rcp = sb(es, "rcp", (128, 8))
            yb = sb(es, "yb", (128, 512), BF16)
            ybT = [sb(es, "ybT%d" % i, (128, 4, NB), BF16) for i in range(2)]
            ybT_sem = [S.new_dsem() for _ in range(2)]
            cp("dve", causT[:], cc("causalT"), [cst], [causT])
            memset("pool", vones[:, :, 64:65], 1.0, [vones])
            WI_SCALE = float(8 ** -0.5 * 64 ** -0.5)
            MBIG = 30000.0
            cbT = sb(es, "cbT", (128, 128), BF16)
            ts("dve", cbT[:], cc("causalT"), MBIG, -MBIG, ALU.mult, ALU.add, [cst], [cbT])
            qTs = [qT, sb(es, "qT_b", (128, 4, NB), BF16)]
            maskTs = [maskT] + [sb(es, "maskT_%d" % i, (128, 16, 128), BF16) for i in range(3)]
            accs = [acc, sb(es, "acc_b", (128, T))]
            works = [work, sb(es, "work_b", (128, T))]
            mx8s = [mx8, sb(es, "mx8_b", (128, 8))]
            maskbs = [maskb, sb(es, "maskb_b", (128, T), BF16)]
            relus = [relu_t, [sb(es, "relub%d" % i, (128, 512)) for i in range(2)]]
            ybs = [yb, sb(es, "yb_b", (128, 512), BF16)]
            dbg_qt = int(os.environ.get("MK_QT", "999")) if debug else 999

            def proj_block(b, blk, qT_):
                tl0 = blk * NB
                t0 = b * T + tl0
                make_hT(pTr, t0, NB // 128, hT, 0, x_d, xt=xt)
                ropeC = cc("ropeC")[:, tl0:tl0 + NB]
                ropeS = cc("ropeS")[:, tl0:tl0 + NB]

                def proj(ct):
                    p_ = pj.next()
                    for kc in range(8):
                        mm(p_[:, :NB], WD[:, kc, ct * 128:(ct + 1) * 128], hT[:, kc, :], kc == 0, kc == 7, [WD, hT], [p_])
                    return p_

                def rope(ct_a, ct_b, dst_ap, dst_tl):
                    pa = proj(ct_a)
                    tt("dve", rt1[:], pa[:, :NB], ropeC, ALU.mult, [pa, cst], [rt1])
                    pb = proj(ct_b)
                    tt("dve", rt2[:], pb[:, :NB], ropeS, ALU.mult, [pb, cst], [rt2])
                    tt("pool", dst_ap, rt1[:], rt2[:], ALU.add, [rt1, rt2], [dst_tl])

                for i in range(4):
                    rope(i, 4 + i, qT_[:, i, :], qT_)
                    yield
                rope(8, 9, kT_all[:, tl0:tl0 + NB], kT_all)
                yield
                for i in range(4):
                    rope(10 + i, 14 + i, qiT[:, i, :], qiT)
                    yield
                rope(18, 19, kiT_all[:, tl0:tl0 + NB], kiT_all)
                for i in range(NB // 128):
                    p_ = pj.next()
                    for kc in range(8):
                        mm(p_[:, 0:72], hT[:, kc, i * 128:(i + 1) * 128], WT[:, kc, :], kc == 0, kc == 7, [WT, hT], [p_])
                    cp("dve", vones[:, blk * 4 + i, 0:64], p_[:, 0:64], [p_], [vones])
                    ts("dve", wi_sb[:, i, :], p_[:, 64:72], WI_SCALE, None, ALU.mult, None, [p_], [wi_sb])
                yield

            def topk_task(qt, i, mT):
                if qt < 2:
                    return
                acc, work, mx8, maskb, relu_t = accs[qt % 2], works[qt % 2], mx8s[qt % 2], maskbs[qt % 2], relus[qt % 2]
                Sk = (qt + 1) * 128
                tq = slice(i * 128, (i + 1) * 128)
                nseg = (Sk + 511) // 512
                for sg in range(nseg):
                    s0 = sg * 512
                    sn = min(512, Sk - s0)
                    for h in range(8):
                        rs_ = slice((h % 2) * 64, (h % 2) * 64 + 64)
                        p_ = pj.next()
                        mm(p_[:, :sn], qiT[rs_, h // 2, tq], kiT_all[rs_, s0:s0 + sn], True, True, [qiT, kiT_all], [p_])
                        rl = relu_t[h % 2]
                        act(rl[:, :sn], p_[:, :sn], AF.Relu, [p_], [rl])
                        if h == 0:
                            ts("dve", acc[:, s0:s0 + sn], rl[:, :sn], wi_sb[:, i, 0:1], None, ALU.mult, None, [rl, wi_sb], [acc])
                        else:
                            stt(acc[:, s0:s0 + sn], rl[:, :sn], wi_sb[:, i, h:h + 1], acc[:, s0:s0 + sn],
                                ALU.mult, ALU.add, [rl, wi_sb, acc], [acc])
                        yield
                tt("dve", acc[:, Sk - 128:Sk], acc[:, Sk - 128:Sk], cc("causal_bias"), ALU.add, [acc, cst], [acc])
                src = acc
                for rnd in range(32):
                    S.op("dve", lambda en, src=src, Sk=Sk: en.max(out=mx8[:], in_=src[:, :Sk]), reads=rr(src), writes=rr(mx8))
                    yield
                    if rnd < 31:
                        S.op("dve", lambda en, src=src, Sk=Sk: en.match_replace(
                            out=work[:, :Sk], in_to_replace=mx8[:], in_values=src[:, :Sk], imm_value=NEG),
                            reads=rr(src, mx8), writes=rr(work))
                        src = work
                        yield
                ts("dve", maskb[:, :Sk], acc[:, :Sk], mx8[:, 7:8], None, ALU.is_ge, None, [acc, mx8], [maskb])
                for g in range((qt + 1 + 3) // 4):
                    pm = pj.next()
                    pmb = pm[:].bitcast(BF16)
                    nk = min(4, qt + 1 - g * 4)
                    for j in range(nk):
                        kt = g * 4 + j
                        tr(pmb[:, j * 128:(j + 1) * 128], maskb[:, kt * 128:(kt + 1) * 128], identb[:], [maskb, identb], [pm])
                    act(mT[:, g * 4:g * 4 + nk, :], pmb[:, 0:nk * 128].rearrange("p (k t) -> p k t", t=128), AF.Identity,
                        [pm], [mT], bias=-MBIG, scale=MBIG)
                yield

            def attn_task(b, blk, qt, i, mT, qT_, ybt, yb_):
                tq = slice(i * 128, (i + 1) * 128)
                t0 = b * T + blk * NB
                for kt in range(qt + 1):
                    psc = pW.next()
                    need_bias = (qt >= 2) or (kt == qt)
                    brhs = mT[:, kt, :] if qt >= 2 else cbT[:]
                    bres = mT if qt >= 2 else cbT
                    for h in [0, 2, 4, 6, 1, 3, 5, 7]:
                        rs_ = slice((h % 2) * 64, (h % 2) * 64 + 64)
                        o_ = psc[:, h % 2, (h // 2) * 128:(h // 2) * 128 + 128]
                        mm(o_, kT_all[rs_, kt * 128:(kt + 1) * 128], qT_[rs_, h // 2, tq], True, not need_bias, [kT_all, qT_], [psc])
                        if need_bias:
                            mm(o_, identb[:], brhs, False, True, [identb, bres], [psc])
                    e_ = eT[kt % 2]
                    act(e_[:].rearrange("p (a h) t -> p a (h t)", a=2), psc[:, :, :], AF.Exp, [psc], [e_], scale=0.125)
                    for h in range(8):
                        mm(po[:, h // 4, (h % 4) * 65:(h % 4) * 65 + 65], e_[:, (h % 2) * 4 + h // 2, :], vones[:, kt, :],
                           kt == 0 and h % 4 == 0, kt == qt, [e_, vones], [po], skip=True)
                    if kt % 2 == 1:
                        yield
                pov = po[:, :, 0:260].rearrange("p a (h e) -> p a h e", e=65)
                S.op("dve", lambda en: en.reciprocal(rcp[:].rearrange("p (a h) -> p a h", a=2), pov[:, :, :, 64]),
                     reads=rr(po), writes=rr(rcp))
                tt("dve", yb_[:].rearrange("p (a h e) -> p a h e", a=2, h=4), pov[:, :, :, 0:64],
                   rcp[:].rearrange("p (a h) -> p a h", a=2).unsqueeze(3).to_broadcast([128, 2, 4, 64]), ALU.mult,
                   [po, rcp], [yb_])
                if "yb" in dbg_d and b == 0:
                    cp("dve", rt1[:, 0:512], yb_[:], [yb_], [rt1])
                    dbg_dump("yb", rt1[:, 0:512], [rt1], (slice(t0 + i * 128, t0 + (i + 1) * 128), slice(None)))
                pm = pj.next()
                pmb = pm[:].bitcast(BF16)
                for kc in range(4):
                    tr(pmb[:, kc * 128:(kc + 1) * 128], yb_[:, kc * 128:(kc + 1) * 128], identb[:], [yb_, identb], [pm])
                cp("act", ybt[:, :, tq], pmb[:, 0:512].rearrange("p (k t) -> p k t", t=128), [pm], [ybt])
                if i == NB // 128 - 1:
                    dma(ybT_d[:, :, t0:t0 + NB], ybt[:], [ybt], [ybT_res], ybT_sem[(b * (T // NB) + blk) % 2])
                yield

            def chain(*gs):
                for g_ in gs:
                    yield from g_

            for b in range(NSEQ):
                NT = min(T // 128, dbg_qt)
                prev = []
                for p in range(NT // 2 + 1):
                    gens = []
                    cur = []
                    for j in (2 * p, 2 * p + 1):
                        if j < NT:
                            blk, i = j // 4, j % 4
                            if i == 0:
                                run_rr2([proj_block(b, blk, qTs[blk % 2])])
                            gens.append(topk_task(j, i, maskTs[j % 4]))
                            cur.append((b, blk, j, i, maskTs[j % 4], qTs[blk % 2], ybT[(b * (T // NB) + blk) % 2], ybs[j % 2]))
                    if prev:
                        gens.append(chain(*[attn_task(*a_) for a_ in prev]))
                    prev = cur
                    run_rr2(gens)
          S.barrier()

        if stop_after >= 3:
          with ExitStack() as es:
            WG = sb(es, "WG", (128, 8, 2048), BF16)
            wbr = sb(es, "wbr", (128, 8, 1024), BF16)
            wout = sb(es, "wout", (128, 8, 1024), BF16)
            wst = [sb(es, "wst3_%d" % i, (128, WSTN)) for i in range(2)]
            for kc in range(8):
                for hf in range(2):
                    def consG(st_, kc=kc, hf=hf):
                        ts("pool" if hf else "dve", WG[:, kc, hf * 1024:(hf + 1) * 1024], st_[:, :1024], gcol[:, 0, kc:kc + 1], None,
                           ALU.mult, None, [st_, gcol], [WG])
                    load_weight(wG_d[kc * 128:(kc + 1) * 128, hf * 1024:(hf + 1) * 1024], 1024, consG)

                def consB(st_, kc=kc):
                    cp("pool", wbr[:, kc, :], st_[:, :1024], [st_], [wbr])
                load_weight(wbr_d[kc * 128:(kc + 1) * 128, :], 1024, consB)

                def consO(st_, kc=kc):
                    cp("dve", wout[:, kc, :], st_[:, :1024], [st_], [wout])
                load_weight(wout_d[kc * 128:(kc + 1) * 128, :], 1024, consO)
            NB = MG_NB
            hT = sb(es, "hTm", (128, 8, NB), BF16)
            xk = sb(es, "xk", (128, 4, D))
            pTr = Pool([ps(es, "pTr3_%d" % i, (128, 8, 128), BF16) for i in range(1)])
            pj = Pool([ps(es, "pj3_%d" % i, (128, 512)) for i in range(6)])
            yaL = sb(es, "yaL", (128, 4, NB), BF16)
            ybL = sb(es, "ybL", (128, 4, NB), BF16)
            yl_sem = S.new_dsem()
            yl_sem2 = S.new_dsem()
            sgA = sb(es, "sgA", (128, NB))
            sgB = sb(es, "sgB", (128, NB))
            mA = sb(es, "mA", (128, NB))
            mB = sb(es, "mB", (128, NB))
            mgT = sb(es, "mgT", (128, 8, NB), BF16)
            x1t = [sb(es, "x1t%d" % i, (128, D)) for i in range(2)]
            x1_sem = [S.new_dsem() for _ in range(2)]
            n1 = 0
            for bi in range(NTOK // NB):
                t0 = bi * NB
                make_hT(pTr, t0, 4, hT, 0, x_d, keep=xk)
                dma(yaL[:], yaT_d[:, :, t0:t0 + NB], [yaT_res], [yaL], yl_sem)
                dma(ybL[:], ybT_d[:, :, t0:t0 + NB], [ybT_res], [ybL], yl_sem2)
                for dt_ in range(8):
                    ds_ = slice(dt_ * 128, (dt_ + 1) * 128)
                    pa = pj.next()
                    for kc in range(4):
                        mm(pa[:, :NB], wbr[:, kc, ds_], yaL[:, kc, :], kc == 0, kc == 3, [wbr, yaL], [pa])
                    pb = pj.next()
                    for kc in range(4):
                        mm(pb[:, :NB], wbr[:, 4 + kc, ds_], ybL[:, kc, :], kc == 0, kc == 3, [wbr, ybL], [pb])
                    g0 = pj.next()
                    for kc in range(8):
                        mm(g0[:, :NB], WG[:, kc, dt_ * 128:(dt_ + 1) * 128], hT[:, kc, :], kc == 0, kc == 7, [WG, hT], [g0])
                    g1 = pj.next()
                    for kc in range(8):
                        mm(g1[:, :NB], WG[:, kc, 1024 + dt_ * 128:1024 + (dt_ + 1) * 128], hT[:, kc, :], kc == 0, kc == 7,
                           [WG, hT], [g1])
                    act(sgA[:], g0[:, :NB], AF.Sigmoid, [g0], [sgA])
                    act(sgB[:], g1[:, :NB], AF.Sigmoid, [g1], [sgB])
                    tt("dve", mA[:], pa[:, :NB], sgA[:], ALU.mult, [pa, sgA], [mA])
                    tt("dve", mB[:], pb[:, :NB], sgB[:], ALU.mult, [pb, sgB], [mB])
                    tt("pool", mgT[:, dt_, :], mA[:], mB[:], ALU.add, [mA, mB], [mgT])
                for tt_ in range(4):
                    x1 = x1t[n1 % 2]
                    for hf in range(2):
                        po = pj.next()
                        for kc in range(8):
                            mm(po[:, :], mgT[:, kc, tt_ * 128:(tt_ + 1) * 128], wout[:, kc, hf * 512:(hf + 1) * 512],
                               kc == 0, kc == 7, [mgT, wout], [po])
                        tt("dve", x1[:, hf * 512:(hf + 1) * 512], po[:, :], xk[:, tt_, hf * 512:(hf + 1) * 512], ALU.add,
                           [po, xk], [x1])
                    dma(x1_d[t0 + tt_ * 128:t0 + (tt_ + 1) * 128, :], x1[:], [x1], [x1_res[bi]], x1_sem[n1 % 2])
                    if t0 < T:
                        dbg_dump("x1", x1[:], [x1], (slice(t0 + tt_ * 128, t0 + (tt_ + 1) * 128), slice(None)))
                    n1 += 1
          S.barrier()

        if stop_after >= 4:
          with ExitStack() as es:
            wup = sb(es, "wup", (128, 8, FFH), BF16)
            wdn = sb(es, "wdn", (128, 32, D), BF16)
            gzb = sb(es, "gzb", (128, D))
            wst = [sb(es, "wst4_%d" % i, (128, 1024)) for i in range(2)]
            dma(gzb[:], gz_d.partition_broadcast(128), [], [gzb], d0())
            for kc in range(8):
                for q4 in range(4):
                    def consU(st_, kc=kc, q4=q4):
                        if True:
                            ts("pool" if q4 % 2 else "dve", wup[:, kc, q4 * 1024:(q4 + 1) * 1024], st_[:, 0:1024], gcol[:, 1, kc:kc + 1], None,
                               ALU.mult, None, [st_, gcol], [wup])
                    load_weight(wup_d[kc * 128:(kc + 1) * 128, q4 * 1024:(q4 + 1) * 1024], 1024, consU)
            for g in range(32):
                def consDn(st_, g=g):
                    cp("pool" if g % 2 else "dve", wdn[:, g, :], st_[:, 0:1024], [st_], [wdn])
                load_weight(wdn_d[g * 128:(g + 1) * 128, :], 1024, consDn)
            NB = FF_NB
            NT4 = NB // 128
            hT = sb(es, "hTf", (128, 8, NB), BF16)
            xk = sb(es, "xkf", (128, NT4, D))
            pTr = Pool([ps(es, "pTr4_%d" % i, (128, 8, 128), BF16) for i in range(1)])
            pj = Pool([ps(es, "pj4_%d" % i, (128, 512)) for i in range(6)])
            aT = sb(es, "aT", (128, 32, NB), BF16)
            rl = [sb(es, "rl%d" % i, (128, NB), BF16) for i in range(2)]
            xx = sb(es, "x2", (128, D))
            ot = [sb(es, "ot%d" % i, (128, D)) for i in range(2)]
            o_sem = [S.new_dsem() for _ in range(2)]
            st2 = [sb(es, "st2_%d" % i, (128, 4)) for i in range(2)]
            n2 = 0
            for bi in range(NTOK // NB):
                t0 = bi * NB
                make_hT(pTr, t0, NT4, hT, 0, x1_d, keep=xk, src_res=[x1_res[t0 // 512]])
                for ht in range(32):
                    pu = pj.next()
                    for kc in range(8):
                        mm(pu[:, :NB], wup[:, kc, ht * 128:(ht + 1) * 128], hT[:, kc, :], kc == 0, kc == 7, [wup, hT], [pu])
                    r_ = rl[ht % 2]
                    act(r_[:], pu[:, :NB], AF.Relu, [pu], [r_])
                    tt("pool" if ht % 2 else "dve", aT[:, ht, :], r_[:], r_[:], ALU.mult, [r_], [aT])
                for tt_ in range(NT4):
                    oo = ot[n2 % 2]
                    s2 = st2[n2 % 2]
                    for hf in range(2):
                        pd = pj.next()
                        for ht in range(32):
                            mm(pd[:, :], aT[:, ht, tt_ * 128:(tt_ + 1) * 128], wdn[:, ht, hf * 512:(hf + 1) * 512],
                               ht == 0, ht == 31, [aT, wdn], [pd])
                        tt("dve", xx[:, hf * 512:(hf + 1) * 512], pd[:, :], xk[:, tt_, hf * 512:(hf + 1) * 512], ALU.add,
                           [pd, xk], [xx])
                    act(oo[:], xx[:], AF.Square, [xx], [oo, s2], accum_out=s2[:, 0:1])
                    ts("dve", s2[:, 1:2], s2[:, 0:1], 1.0 / D, 1e-6, ALU.mult, ALU.add, [s2], [s2])
                    rsqrt(s2[:, 2:3], s2[:, 1:2], [s2], [s2])
                    stt(oo[:], xx[:], s2[:, 2:3], gzb[:], ALU.mult, ALU.mult, [xx, s2, gzb], [oo])
                    out_dmas.append(dma(out_d[t0 + tt_ * 128:t0 + (tt_ + 1) * 128, :], oo[:], [oo], [], o_sem[n2 % 2]))
                    n2 += 1

        S.finish(out_dmas)
        S.emit()
    return nc


def _swap_halves(cols):
    c = np.asarray(cols).reshape(-1, 2, 32)
    return c[:, ::-1, :].reshape(-1)


def _layout_inputs(inp):
    f = lambda a: np.ascontiguousarray(np.asarray(a, dtype=np.float32))
    w_in = f(inp["w_in"])[0]
    mu = f(inp["mu_shift"])[0]
    colsA = np.concatenate([np.arange(0, 1024), np.arange(1536, 1824)])
    colsV = np.arange(1024, 1536)
    base = 1824
    q = base + np.arange(512)
    k = base + 512 + np.arange(64)
    v = base + 576 + np.arange(64)
    qi = base + 640 + np.arange(512)
    ki = base + 1152 + np.arange(64)
    wi = base + 1216 + np.arange(8)
    colsD = np.concatenate([q, _swap_halves(q), k, k, _swap_halves(k), _swap_halves(k),
                            qi, _swap_halves(qi), ki, ki, _swap_halves(ki), _swap_halves(ki)])
    colsT = np.concatenate([v, wi])
    colsG = 1824 + 1224 + np.arange(2048)
    per_ch = lambda a: f(a)[0].reshape(4, 128).T
    pp = np.concatenate([per_ch(inp["decay_bias"]), per_ch(inp["iclr_bias"]), per_ch(inp["k_k"]),
                         per_ch(inp["k_a"]), per_ch(inp["r_k"])], axis=1)
    shared = {
        "wA": f(w_in[:, colsA]), "wV": f(w_in[:, colsV]), "muA": f(mu[colsA]), "muV": f(mu[colsV]),
        "wD": f(w_in[:, colsD]), "wT": f(w_in[:, colsT]), "wG": f(w_in[:, colsG]),
        "gcols": f(np.concatenate([f(inp["g_mix"])[0].reshape(8, 128).T, f(inp["g_ffn"])[0].reshape(8, 128).T], axis=1)),
        "gfin": f(inp["g_final"]),
        "wlora": f(np.concatenate([f(inp["w_decay_up"])[0], f(inp["w_iclr_up"])[0]], axis=0)),
        "wgate": f(inp["w_gate_up"])[0], "pp": f(pp),
        "gnw": f(inp["gn_w"])[0], "gnb": f(inp["gn_b"])[0],
        "wbr": f(f(inp["w_branch"])[0].reshape(1024, 1024)), "wout": f(inp["w_out"])[0],
        "wup": f(inp["w_ffn_up"])[0], "wdn": f(inp["w_ffn_down"])[0], "cstA": _CSTA, "cst1": _CST1, "cst2": _CST2,
    }
    x = f(inp["x"])
    maps = []
    for c in range(NCORES):
        m = dict(shared)
        m["x"] = np.ascontiguousarray(x[c * NSEQ:(c + 1) * NSEQ].reshape(NTOK, D))
        maps.append(m)
    return maps


def kernel(**inputs):
    maps = _layout_inputs(inputs)
    nc = build_nc()
    res = run_bass_kernel_spmd(nc, maps, core_ids=list(range(NCORES)))
    outs = [np.asarray(r["out"], dtype=np.float32).reshape(NSEQ, T, D) for r in res.results]
    return np.concatenate(outs, axis=0)
```

```python
import os
from contextlib import ExitStack

import numpy as np
import concourse.bass as bass
import concourse.mybir as mybir
from concourse.bass_utils import run_bass_kernel_spmd

F32 = mybir.dt.float32
BF16 = mybir.dt.bfloat16
ALU = mybir.AluOpType
AF = mybir.ActivationFunctionType
AX = mybir.AxisListType

NCORES = 8
T = 2048
D = 1024
NSEQ = 2
NTOK = NSEQ * T
C = 64
C0 = float(np.exp(-0.5))
NEG = -1.0e30
RW_NB = 128
DS_NB = 512
MG_NB = 512
FF_NB = 256
FFH = 4096


class Res:
    __slots__ = ("name", "w", "rd", "rd_dma")

    def __init__(self, name):
        self.name = name
        self.w = None
        self.rd = {}
        self.rd_dma = []


class DmaSem:
    def __init__(self, sem):
        self.sem = sem
        self.count = 0


class _Op:
    __slots__ = ("id", "eng", "fn", "deps", "dsem", "val", "signal")


class Sched:
    ENGS = ("pe", "act", "dve", "pool", "sp")

    def __init__(self, nc, es):
        self.nc = nc
        self.es = es
        self.ops = []
        self.per = {e: [] for e in self.ENGS}
        self.sem = {e: es.enter_context(nc.semaphore("s_" + e)) for e in self.ENGS}
        self.n_dsem = 0
        self.last = {e: None for e in self.ENGS}
        self.dma_since_barrier = []

    def new_dsem(self):
        self.n_dsem += 1
        return DmaSem(self.es.enter_context(self.nc.semaphore("d%d" % self.n_dsem)))

    def op(self, eng, fn, reads=(), writes=(), dsem=None):
        o = _Op()
        o.id = len(self.ops)
        o.eng = eng
        o.fn = fn
        o.dsem = dsem
        o.signal = False
        o.val = None
        deps = {}

        def add(d, kind):
            if d is None:
                return
            if kind == "raw" or d not in deps:
                deps[d] = kind

        for r in reads:
            add(r.w, "raw")
        for w in writes:
            add(w.w, "waw")
            for d in w.rd.values():
                add(d, "war")
            for d in w.rd_dma:
                add(d, "war")
        o.deps = deps
        for r in reads:
            if dsem is not None:
                r.rd_dma.append(o.id)
            else:
                r.rd[eng] = o.id
        for w in writes:
            w.w = o.id
            w.rd = {}
            w.rd_dma = []
        if dsem is not None:
            dsem.count += 16
            o.val = dsem.count
            self.dma_since_barrier.append(o.id)
        self.ops.append(o)
        self.per[eng].append(o)
        self.last[eng] = o.id
        return o

    def barrier(self):
        lasts = [v for v in self.last.values() if v is not None]
        dmas = list(self.dma_since_barrier)
        self.dma_since_barrier = []
        for e in self.ENGS:
            o = self.op(e, lambda en: en.nop())
            for d in lasts + dmas:
                if d != o.id:
                    o.deps[d] = "raw"

    def finish(self, dma_ops):
        o = self.op("sp", lambda en: en.nop())
        for d in dma_ops:
            o.deps[d.id] = "raw"

    def emit(self):
        ops = self.ops
        for o in ops:
            for d, kind in o.deps.items():
                p = ops[d]
                if p.dsem is not None:
                    continue
                if p.eng == o.eng and o.dsem is None and o.eng in ("pe", "sp"):
                    continue
                p.signal = True
        cnt = {e: 0 for e in self.ENGS}
        for o in ops:
            if o.dsem is None and o.signal:
                cnt[o.eng] += 1
                o.val = cnt[o.eng]
        sem = self.sem

        def run(eng, en):
            known = {}
            for o in self.per[eng]:
                need = {}
                for d, kind in o.deps.items():
                    p = ops[d]
                    if p.dsem is not None:
                        key, s, v = ("d", id(p.dsem)), p.dsem.sem, p.val
                    else:
                        if not p.signal:
                            continue
                        if p.eng == eng and o.dsem is None and eng in ("pe", "sp"):
                            continue
                        key, s, v = ("e", p.eng), sem[p.eng], p.val
                    if known.get(key, 0) >= v:
                        continue
                    if key not in need or need[key][1] < v:
                        need[key] = (s, v)
                for key, (s, v) in need.items():
                    en.wait_ge(s, v)
                    known[key] = v
                ins = o.fn(en)
                if o.dsem is not None:
                    ins.then_inc(o.dsem.sem, 16)
                elif o.signal:
                    ins.then_inc(sem[eng], 1)

        with self.nc.Block() as block:
            @block.tensor
            def _(en):
                run("pe", en)

            @block.scalar
            def _(en):
                run("act", en)

            @block.vector
            def _(en):
                run("dve", en)

            @block.gpsimd
            def _(en):
                run("pool", en)

            @block.sync
            def _(en):
                run("sp", en)


class Tl:
    def __init__(self, h, name, nres=1):
        self.h = h
        self.name = name
        self.rs = [Res("%s.%d" % (name, i)) for i in range(nres)]

    @property
    def r(self):
        return self.rs[0]

    def __getitem__(self, k):
        return self.h[k]


class Pool:
    def __init__(self, tiles):
        self.tiles = tiles
        self.i = 0

    def next(self):
        t = self.tiles[self.i % len(self.tiles)]
        self.i += 1
        return t


class _CB:
    def __init__(self):
        self.cols = {}
        self.parts = []
        self.off = 0

    def put(self, name, arr):
        a = np.zeros((128, arr.shape[1]), np.float32)
        a[: arr.shape[0]] = arr
        self.cols[name] = (self.off, arr.shape[1], arr.shape[0])
        self.parts.append(a)
        self.off += arr.shape[1]

    def arr(self):
        return np.ascontiguousarray(np.concatenate(self.parts, axis=1))


def _const_f32():
    A, B1, B2 = _CB(), _CB(), _CB()
    A.put("ident", np.eye(128, dtype=np.float32))
    s = np.arange(64)[:, None]
    t = np.arange(64)[None, :]
    m1 = np.concatenate([(s < t), (s <= t)], axis=1).astype(np.float32)
    B1.put("mask1", m1)
    B1.put("maskL", (s > t).astype(np.float32))
    B1.put("eye8", np.eye(64, dtype=np.float32))
    rm = np.ones((128, RW_NB), np.float32)
    rm[:, ::C] = 0.0
    B1.put("reset", rm)
    bo = np.zeros((128, 128), np.float32)
    bo[:64, :64] = 1.0
    bo[64:, 64:] = 1.0
    A.put("blockones", bo)
    hi = np.zeros((128, 2), np.float32)
    hi[:64, 0] = 1.0
    hi[64:, 1] = 1.0
    A.put("headind", hi)
    tq = np.arange(128)[:, None]
    kk = np.arange(128)[None, :]
    A.put("causal_bias", np.where(kk <= tq, 0.0, NEG).astype(np.float32))
    A.put("causalT", (tq <= kk).astype(np.float32))
    inv = (1.0 / (10000.0 ** (np.arange(0, 64, 2, dtype=np.float32) / np.float32(64)))).astype(np.float32)
    ang = (np.arange(T, dtype=np.float32)[:, None] * inv[None, :]).astype(np.float32)
    cs = np.cos(ang).astype(np.float32).T
    sn = np.sin(ang).astype(np.float32).T
    d = np.arange(128) % 64
    B2.put("ropeC", cs[d % 32])
    sg = np.where(d < 32, -1.0, 1.0).astype(np.float32)[:, None]
    B2.put("ropeS", sn[d % 32] * sg)
    return A, B1, B2


_CA, _C1, _C2 = _const_f32()
_CSTA, _CST1, _CST2 = _CA.arr(), _C1.arr(), _C2.arr()


def build_nc(debug=None):
    nc = bass.Bass("TRN2", target_bir_lowering=False)
    dt_in = lambda name, shape: nc.dram_tensor(name, list(shape), F32, kind="ExternalInput").ap()
    x_d = dt_in("x", (NTOK, D))
    wA_d = dt_in("wA", (D, 1312))
    wV_d = dt_in("wV", (D, 512))
    muA_d = dt_in("muA", (1312,))
    muV_d = dt_in("muV", (512,))
    wD_d = dt_in("wD", (D, 2560))
    wT_d = dt_in("wT", (D, 72))
    wG_d = dt_in("wG", (D, 2048))
    gcol_d = dt_in("gcols", (128, 16))
    gz_d = dt_in("gfin", (D,))
    wlora_d = dt_in("wlora", (128, 512))
    wgate_d = dt_in("wgate", (160, 512))
    pp_d = dt_in("pp", (128, 20))
    gnw_d = dt_in("gnw", (512,))
    gnb_d = dt_in("gnb", (512,))
    wbr_d = dt_in("wbr", (1024, 1024))
    wout_d = dt_in("wout", (D, D))
    wup_d = dt_in("wup", (D, FFH))
    wdn_d = dt_in("wdn", (FFH, D))
    cstA_d = dt_in("cstA", _CSTA.shape)
    cst1_d = dt_in("cst1", _CST1.shape)
    cst2_d = dt_in("cst2", _CST2.shape)
    out_d = nc.dram_tensor("out", [NTOK, D], F32, kind="ExternalOutput").ap()
    yaT_d = nc.dram_tensor("yaT_scr", [128, 4, NTOK], BF16, kind="Internal").ap()
    ybT_d = nc.dram_tensor("ybT_scr", [128, 4, NTOK], BF16, kind="Internal").ap()
    x1_d = nc.dram_tensor("x1_scr", [NTOK, D], F32, kind="Internal").ap()
    dbg_d = {}
    dbg_sem = {}
    if debug:
        for name, shape in debug.items():
            dbg_d[name] = nc.dram_tensor("dbg_" + name, list(shape), F32, kind="ExternalOutput").ap()

    top = ExitStack()
    with top:
        S = Sched(nc, top)
        out_dmas = []
        yaT_res = Res("yaT_scr")
        ybT_res = Res("ybT_scr")
        x1_res = [Res("x1_scr%d" % i) for i in range(NTOK // 512)]

        uid = [0]

        def sb(es, name, shape, dt=F32, nres=1):
            uid[0] += 1
            return Tl(es.enter_context(nc.sbuf_tensor("sb%d_%s" % (uid[0], name), list(shape), dt)), name, nres)

        def ps(es, name, shape, dt=F32):
            uid[0] += 1
            return Tl(es.enter_context(nc.psum_tensor("ps%d_%s" % (uid[0], name), list(shape), dt)), name)

        def rr(*xs):
            out = []
            for x in xs:
                if isinstance(x, Tl):
                    out.extend(x.rs)
                elif isinstance(x, Res):
                    out.append(x)
                else:
                    out.extend(x)
            return out

        def dma(out_ap, in_ap, reads, writes, dsem):
            return S.op("sp", lambda en: en.dma_start(out=out_ap, in_=in_ap),
                        reads=rr(*reads), writes=rr(*writes), dsem=dsem)

        def mm(out_ap, lhsT, rhs, start, stop, reads, writes, skip=False):
            return S.op("pe", lambda en: en.matmul(out_ap, lhsT, rhs, start=start, stop=stop, skip_group_check=skip),
                        reads=rr(*reads), writes=rr(*writes))

        def tr(out_ap, in_ap, ident, reads, writes):
            return S.op("pe", lambda en: en.transpose(out_ap, in_ap, ident),
                        reads=rr(*reads), writes=rr(*writes))

        def act(out_ap, in_ap, func, reads, writes, bias=0.0, scale=1.0, accum_out=None):
            return S.op("act", lambda en: en.activation(out_ap, in_ap, func, bias=bias, scale=scale,
                                                        accum_out=accum_out),
                        reads=rr(*reads), writes=rr(*writes))

        def tt(eng, out_ap, a, b, op, reads, writes):
            return S.op(eng, lambda en: en.tensor_tensor(out_ap, a, b, op), reads=rr(*reads), writes=rr(*writes))

        def ts(eng, out_ap, a, s1, s2, op0, op1, reads, writes):
            if op1 is None:
                return S.op(eng, lambda en: en.tensor_scalar(out_ap, a, s1, None, op0),
                            reads=rr(*reads), writes=rr(*writes))
            return S.op(eng, lambda en: en.tensor_scalar(out_ap, a, s1, s2, op0, op1),
                        reads=rr(*reads), writes=rr(*writes))

        def stt(out_ap, a, sc, b, op0, op1, reads, writes):
            return S.op("dve", lambda en: en.scalar_tensor_tensor(out_ap, a, sc, b, op0, op1),
                        reads=rr(*reads), writes=rr(*writes))

        def rsqrt(out_ap, in_ap, reads, writes):
            act(out_ap, in_ap, AF.Sqrt, reads, writes)
            S.op("dve", lambda en: en.reciprocal(out_ap, out_ap), reads=rr(*writes), writes=rr(*writes))

        def cp(eng, out_ap, in_ap, reads, writes):
            if eng == "act":
                return S.op("act", lambda en: en.copy(out_ap, in_ap), reads=rr(*reads), writes=rr(*writes))
            return S.op(eng, lambda en: en.tensor_copy(out_ap, in_ap), reads=rr(*reads), writes=rr(*writes))

        def memset(eng, ap, val, writes):
            return S.op(eng, lambda en: en.memset(ap, val), writes=rr(*writes))

        def dbg_dump(name, src_ap, reads, dst_slice=None):
            if name not in dbg_d:
                return
            dst = dbg_d[name] if dst_slice is None else dbg_d[name][dst_slice]
            if name not in dbg_sem:
                dbg_sem[name] = S.new_dsem()
            out_dmas.append(dma(dst, src_ap, reads, [], dbg_sem[name]))

        def run_rr2(gens):
            gens = list(gens)
            while gens:
                for g_ in list(gens):
                    try:
                        next(g_)
                    except StopIteration:
                        gens.remove(g_)

        cst = Tl(None, "cstgroup", 0)
        ctiles = {}

        def load_const(es_, key, arr, src_d, cb):
            t_ = sb(es_, "cs_sb" + key, arr.shape)
            dma(t_[:], src_d, [], [t_], S.new_dsem())
            cst.rs.extend(t_.rs)
            for nm in cb.cols:
                ctiles[nm] = (t_, cb.cols[nm])

        load_const(top, "A", _CSTA, cstA_d, _CA)

        def cc(name, rows=None):
            t_, (o, n, r0) = ctiles[name]
            return t_[: (rows or r0), o:o + n]

        def d0():
            return S.new_dsem()

        nhalf = sb(top, "nhalf", (128, 256))
        memset("pool", nhalf[:], -0.5, [nhalf])
        identb = sb(top, "identb", (128, 128), BF16)
        cp("dve", identb[:], cc("ident"), [cst], [identb])
        gcol = sb(top, "gcol", (128, 2, 8))
        dma(gcol[:].rearrange("p a k -> p (a k)"), gcol_d, [], [gcol], d0())
        pp = sb(top, "pp", (128, 20))
        dma(pp[:], pp_d, [], [pp], d0())

        WSTN = 1312
        wst = None
        wst_sem = [S.new_dsem() for _ in range(2)]
        wst_i = [0]

        def load_weight(src_ap, ncols, consume):
            i = wst_i[0] % 2
            wst_i[0] += 1
            dma(wst[i][:, :ncols], src_ap, [], [wst[i]], wst_sem[i])
            consume(wst[i])

        xt_sem = [S.new_dsem() for _ in range(2)]
        xt_i = [0]
        xs_bf = [sb(top, "xsbf%d" % i, (128, D), BF16) for i in range(2)]
        stat = [sb(top, "stat%d" % i, (128, 4)) for i in range(2)]

        def make_hT(es_ps, tok0, ntile, hT, col0, src_d, xt=None, keep=None, src_res=()):
            for i in range(ntile):
                k = xt_i[0] % 2
                xt_i[0] += 1
                if keep is not None:
                    xin = keep
                    xap = keep[:, i, :]
                    dma(xap, src_d[tok0 + i * 128: tok0 + (i + 1) * 128, :], src_res, [keep], xt_sem[k])
                else:
                    xin = xt[k % len(xt)]
                    xap = xin[:]
                    dma(xap, src_d[tok0 + i * 128: tok0 + (i + 1) * 128, :], src_res, [xin], xt_sem[k % len(xt)])
                st = stat[k]
                act(xs_bf[k][:], xap, AF.Square, [xin], [xs_bf[k], st], accum_out=st[:, 0:1])
                ts("dve", st[:, 1:2], st[:, 0:1], 1.0 / D, 1e-6, ALU.mult, ALU.add, [st], [st])
                rsqrt(st[:, 2:3], st[:, 1:2], [st], [st])
                ts("dve", xs_bf[k][:], xap, st[:, 2:3], None, ALU.mult, None, [xin, st], [xs_bf[k]])
                pt = es_ps.next()
                for kc in range(8):
                    tr(pt[:, kc, :], xs_bf[k][:, kc * 128:(kc + 1) * 128], identb[:], [xs_bf[k], identb], [pt])
                cp("act" if i % 2 else "dve", hT[:, :, col0 + i * 128: col0 + (i + 1) * 128], pt[:, :, :], [pt], [hT])

        with ExitStack() as es:
            W1A = sb(es, "W1A", (128, 8, 1312), BF16)
            W2A = sb(es, "W2A", (128, 8, 1312), BF16)
            W1V = sb(es, "W1V", (128, 8, 512), BF16)
            W2V = sb(es, "W2V", (128, 8, 512), BF16)
            with ExitStack() as es_w:
                wst = [sb(es_w, "wst1_%d" % i, (128, WSTN)) for i in range(2)]
                mub = sb(es_w, "mub", (128, 1824))
                omb = sb(es_w, "omb", (128, 1824))
                dma(mub[:, 0:1312], muA_d.partition_broadcast(128), [], [mub], d0())
                dma(mub[:, 1312:1824], muV_d.partition_broadcast(128), [], [mub], d0())
                ts("pool", omb[:], mub[:], -1.0, 1.0, ALU.mult, ALU.add, [mub], [omb])
                for kc in range(8):
                    def consA(st_, kc=kc):
                        stt(W1A[:, kc, :], st_[:, :1312], gcol[:, 0, kc:kc + 1], omb[:, 0:1312], ALU.mult, ALU.mult,
                            [st_, gcol, omb], [W1A])
                        stt(W2A[:, kc, :], st_[:, :1312], gcol[:, 0, kc:kc + 1], mub[:, 0:1312], ALU.mult, ALU.mult,
                            [st_, gcol, mub], [W2A])
                    load_weight(wA_d[kc * 128:(kc + 1) * 128, :], 1312, consA)

                    def consV(st_, kc=kc):
                        stt(W1V[:, kc, :], st_[:, :512], gcol[:, 0, kc:kc + 1], omb[:, 1312:1824], ALU.mult, ALU.mult,
                            [st_, gcol, omb], [W1V])
                        stt(W2V[:, kc, :], st_[:, :512], gcol[:, 0, kc:kc + 1], mub[:, 1312:1824], ALU.mult, ALU.mult,
                            [st_, gcol, mub], [W2V])
                    load_weight(wV_d[kc * 128:(kc + 1) * 128, :], 512, consV)
            S.barrier()
            load_const(es, "1", _CST1, cst1_d, _C1)
            xt = [sb(es, "xt%d" % i, (128, D)) for i in range(1)]
            wlora = sb(es, "wlora", (128, 512))
            wg0f = sb(es, "wg0f", (128, 512))
            wg1f = sb(es, "wg1f", (32, 512))
            wg0 = sb(es, "wg0", (128, 512), BF16)
            wg1 = sb(es, "wg1", (32, 512), BF16)
            gnwb = sb(es, "gnwb", (64, 512))
            gnbb = sb(es, "gnbb", (64, 512))
            dma(wlora[:], wlora_d, [], [wlora], d0())
            dma(wg0f[:], wgate_d[0:128, :], [], [wg0f], d0())
            dma(wg1f[:], wgate_d[128:160, :], [], [wg1f], d0())
            cp("dve", wg0[:], wg0f[:], [wg0f], [wg0])
            cp("dve", wg1[:], wg1f[:], [wg1f], [wg1])
            dma(gnwb[:], gnw_d.partition_broadcast(64), [], [gnwb], d0())
            dma(gnbb[:], gnb_d.partition_broadcast(64), [], [gnbb], d0())

            NB = RW_NB
            NCH = NB // C
            G = T // C
            hT = sb(es, "hT", (128, 8, NB + 2), BF16)
            pTr = Pool([ps(es, "pTr", (128, 8, 128), BF16)])
            pjp = ps(es, "pjp", (128, 512))
            pbon = ps(es, "pbon", (128, 512))
            pI = ps(es, "pI", (128, 2, 512))
            pI3 = ps(es, "pI3", (128, 512))
            pSA = ps(es, "pSA", (128, 512))
            pSB = ps(es, "pSB", (128, 512))
            r_sb = sb(es, "r_sb", (128, 4, NB))
            k_sb = sb(es, "k_sb", (128, 4, NB))
            wa_sb = sb(es, "wa_sb", (128, NB))
            tmps = [[sb(es, "rt%d_%d" % (i, q), (128, NB)) for i in range(10)] for q in range(4)]

            class PSet:
                pass
            psets = []
            for i in range(3):
                P_ = PSet()
                P_.sg0 = sb(es, "sg0_%d" % i, (128, NB), BF16)
                P_.sg1 = sb(es, "sg1_%d" % i, (32, NB), BF16)
                P_.v = sb(es, "v_sb%d" % i, (64, NCH, 512), BF16)
                P_.AR = [sb(es, "AR%d_%d" % (h, i), (128, NCH, 2, C), BF16) for h in range(4)]
                P_.Bt = [sb(es, "Bt%d_%d" % (h, i), (128, NCH, C), BF16) for h in range(4)]
                P_.Kt = [sb(es, "Kt%d_%d" % (h, i), (128, NCH, C), BF16) for h in range(4)]
                P_.BKh = [sb(es, "BKh%d_%d" % (h, i), (128, NCH, 2, C), BF16) for h in range(4)]
                P_.wC = sb(es, "wC%d" % i, (128, 4, NCH))
                P_.bon = sb(es, "bon%d" % i, (64, NCH, 8))
                P_.ya = sb(es, "yaT%d" % i, (128, 4, NB), BF16)
                P_.ya_sem = S.new_dsem()
                psets.append(P_)
            csets = []
            for i in range(2):
                Q_ = PSet()
                Q_.MA = sb(es, "MA%d" % i, (64, 8, 2 * C), BF16)
                Q_.KA = sb(es, "KA%d" % i, (64, 8, 2 * C), BF16)
                Q_.Tf = sb(es, "Tf%d" % i, (64, 8, C), BF16)
                Q_.BKtok = sb(es, "BKtok%d" % i, (64, 4, 2, 128), BF16)
                Q_.y = sb(es, "y_sb%d" % i, (64, 512))
                csets.append(Q_)
            ML = [sb(es, "ML%d" % i, (64, 8, 2, C), BF16) for i in range(2)]
            TT = [sb(es, "TT%d" % i, (64, 8, C), BF16) for i in range(2)]
            ST = sb(es, "ST", (128, 4, C))
            X_sb = sb(es, "X_sb", (64, 8, C), BF16)
            X32 = sb(es, "X32", (64, 8, C))
            STb = sb(es, "STb", (128, 4, C), BF16)
            U_sb = sb(es, "U_sb", (64, 512), BF16)
            ysq = sb(es, "ysq", (64, 512))
            ytmp = sb(es, "ytmp", (64, 512))
            gst = sb(es, "gst", (64, 6, 8))
            ident = cc("ident")
            m1b = cc("mask1").unsqueeze(1).to_broadcast([64, 8, 2 * C])
            mLb = cc("maskL").unsqueeze(1).to_broadcast([64, 8, C])
            eyb = cc("eye8").unsqueeze(1).to_broadcast([64, 8, C])
            hrow = lambda h: slice((h % 2) * 64, (h % 2) * 64 + 64)
            HORD = [0, 2, 4, 6, 1, 3, 5, 7]
            dbg_chunks = int(os.environ.get("MK_CHUNKS", "999")) if debug else 999

            def prep_task(b, blk, P_):
                t0 = b * T + blk * NB
                if blk == 0:
                    memset("pool", hT[:, :, 0:2], 0.0, [hT])
                else:
                    cp("dve", hT[:, :, 1:2], hT[:, :, NB + 1:NB + 2], [hT], [hT])
                make_hT(pTr, t0, NB // 128, hT, 2, x_d, xt=xt)
                yield
                for ct in range(11):
                    rows = 32 if ct == 10 else 128
                    c0 = ct * 128
                    p_ = pjp
                    for kc in range(8):
                        mm(p_[:rows, :NB], W1A[:, kc, c0:c0 + rows], hT[:, kc, 2:NB + 2], kc == 0, False, [W1A, hT], [p_])
                        mm(p_[:rows, :NB], W2A[:, kc, c0:c0 + rows], hT[:, kc, 1:NB + 1], False, kc == 7, [W2A, hT], [p_])
                    if ct < 4:
                        cp("act", r_sb[:, ct, :], p_[:, :NB], [p_], [r_sb])
                    elif ct < 8:
                        cp("dve", k_sb[:, ct - 4, :], p_[:, :NB], [p_], [k_sb])
                    elif ct == 8:
                        act(wa_sb[0:64, :], p_[0:64, :NB], AF.Tanh, [p_], [wa_sb])
                        cp("dve", wa_sb[64:128, :], p_[64:128, :NB], [p_], [wa_sb])
                    elif ct == 9:
                        act(P_.sg0[:], p_[:, :NB], AF.Sigmoid, [p_], [P_.sg0])
                    else:
                        act(P_.sg1[:], p_[0:32, :NB], AF.Sigmoid, [p_], [P_.sg1])
                    if ct % 3 == 2:
                        yield
                for c in range(NCH):
                    p_ = pjp
                    for kc in range(8):
                        mm(p_[0:64, :], hT[:, kc, 2 + c * C:2 + (c + 1) * C], W1V[:, kc, :], kc == 0, False, [W1V, hT], [p_])
                        mm(p_[0:64, :], hT[:, kc, 1 + c * C:1 + (c + 1) * C], W2V[:, kc, :], False, kc == 7, [W2V, hT], [p_])
                    cp("act", P_.v[:, c, :], p_[0:64, :], [p_], [P_.v])
                yield
                v4 = lambda a: a[:].rearrange("p (c t) -> p c t", t=C)

                def hp_task(hp, tmp):
                    cs_ = slice(hp * 128, (hp + 1) * 128)
                    ppc = lambda j, hp=hp: pp[:, j * 4 + hp: j * 4 + hp + 1]
                    sgd, icl, cum, e_in, e_ng, e_ex, e_rm, kkn, kmod, t9 = tmp
                    AR, Bt, Kt, BKh = P_.AR, P_.Bt, P_.Kt, P_.BKh
                    p_ = pjp
                    mm(p_[:, :NB], wlora[0:64, cs_], wa_sb[0:64, :], True, True, [wlora, wa_sb], [p_])
                    act(sgd[:], p_[:, :NB], AF.Sigmoid, [p_, pp], [sgd], bias=ppc(0))
                    p_ = pbon
                    mm(p_[:, 256:256 + NB], wlora[64:128, cs_], wa_sb[64:128, :], True, True, [wlora, wa_sb], [p_])
                    act(icl[:], p_[:, 256:256 + NB], AF.Sigmoid, [p_, pp], [icl], bias=ppc(1))
                    S.op("dve", lambda en, cum=cum, sgd=sgd: en.tensor_tensor_scan(
                        cum[:], cc("reset"), sgd[:], 0.0, ALU.mult, ALU.add), reads=rr(cst, sgd), writes=rr(cum))
                    yield
                    act(e_in[:], cum[:], AF.Exp, [cum], [e_in], scale=-C0)
                    act(e_ng[:], cum[:], AF.Exp, [cum], [e_ng], scale=C0)
                    tt("dve", t9[:], cum[:], sgd[:], ALU.subtract, [cum, sgd], [t9])
                    act(e_ex[:], t9[:], AF.Exp, [t9], [e_ex], scale=-C0)
                    cum3 = cum[:].rearrange("p (c t) -> p c t", t=C)
                    tt("dve", t9[:].rearrange("p (c t) -> p c t", t=C),
                       cum3[:, :, C - 1:C].to_broadcast([128, NCH, C]), cum3, ALU.subtract, [cum], [t9])
                    act(e_rm[:], t9[:], AF.Exp, [t9], [e_rm], scale=-C0)
                    cp("dve", P_.wC[:, hp, :], e_in[:].rearrange("p (c t) -> p c t", t=C)[:, :, C - 1], [e_in], [P_.wC])
                    kx = k_sb[:, hp, :]
                    ts("dve", kkn[:], kx, ppc(2), None, ALU.mult, None, [k_sb, pp], [kkn])
                    tt("dve", t9[:], kkn[:], kkn[:], ALU.mult, [kkn], [t9])
                    p_ = pjp
                    mm(p_[:, :NB], cc("blockones"), t9[:], True, True, [cst, t9], [p_])
                    ts("dve", t9[:], p_[:, :NB], 1e-24, None, ALU.max, None, [p_], [t9])
                    yield
                    rsqrt(t9[:], t9[:], [t9], [t9])
                    tt("dve", kkn[:], kkn[:], t9[:], ALU.mult, [kkn, t9], [kkn])
                    ts("dve", t9[:], icl[:], -1.0, ppc(3), ALU.add, ALU.mult, [icl, pp], [t9])
                    stt(kmod[:], t9[:], 1.0, kx, ALU.add, ALU.mult, [t9, k_sb], [kmod])
                    tt("dve", icl[:], icl[:], kkn[:], ALU.mult, [icl, kkn], [icl])
                    stt(AR[hp][:, :, 0, :], v4(kkn), -1.0, v4(e_ex), ALU.mult, ALU.mult, [kkn, e_ex], [AR[hp]])
                    tt("dve", AR[hp][:, :, 1, :], r_sb[:, hp, :].rearrange("p (c t) -> p c t", t=C), v4(e_in),
                       ALU.mult, [r_sb, e_in], [AR[hp]])
                    yield
                    tt("dve", Bt[hp][:], v4(icl), v4(e_ng), ALU.mult, [icl, e_ng], [Bt[hp]])
                    tt("dve", Kt[hp][:], v4(kmod), v4(e_ng), ALU.mult, [kmod, e_ng], [Kt[hp]])
                    tt("dve", BKh[hp][:, :, 0, :], v4(icl), v4(e_rm), ALU.mult, [icl, e_rm], [BKh[hp]])
                    tt("dve", BKh[hp][:, :, 1, :], v4(kmod), v4(e_rm), ALU.mult, [kmod, e_rm], [BKh[hp]])
                    stt(t9[:], r_sb[:, hp, :], ppc(4), kmod[:], ALU.mult, ALU.mult, [r_sb, pp, kmod], [t9])
                    for c in range(NCH):
                        mm(pbon[0:64, c * 8 + hp * 2: c * 8 + hp * 2 + 2], t9[:, c * C:(c + 1) * C],
                           cc("headind"), True, True, [t9, cst], [pbon])
                    yield

                subs = [hp_task(hp, tmps[hp]) for hp in range(4)]
                while subs:
                    for g_ in list(subs):
                        try:
                            next(g_)
                        except StopIteration:
                            subs.remove(g_)
                    yield
                cp("act", P_.bon[:].rearrange("p c h -> p (c h)"), pbon[0:64, 0:NCH * 8], [pbon], [P_.bon])

            def inv_task(c, P_, Q_):
                AR, Bt, Kt, BKh = P_.AR, P_.Bt, P_.Kt, P_.BKh
                MA, KA = Q_.MA, Q_.KA
                for h in HORD:
                    hp, rs_ = h // 2, hrow(h)
                    arh = AR[hp][rs_, c, :, :].rearrange("p a t -> p (a t)")
                    mm(pI[0:64, h % 2, hp * 128:hp * 128 + 128], Bt[hp][rs_, c, :], arh, True, True,
                       [Bt[hp], AR[hp]], [pI])
                m1p = cc("mask1").unsqueeze(1).unsqueeze(1).to_broadcast([64, 2, 4, 2 * C])
                tt("dve", MA[:].rearrange("p (hp par) m -> p par hp m", par=2),
                   pI[0:64, :, :].rearrange("p par (hp m) -> p par hp m", m=2 * C), m1p, ALU.mult, [pI, cst], [MA])
                mlc = ML[0]
                cp("dve", mlc[:, :, 0, :], MA[:, :, 0:C], [MA], [mlc])
                tcur = TT[0]
                tt("dve", tcur[:], MA[:, :, 0:C], eyb, ALU.add, [MA, cst], [tcur])
                yield
                for h in HORD:
                    hp, rs_ = h // 2, hrow(h)
                    mm(pI[0:64, h % 2, hp * C:(hp + 1) * C], AR[hp][rs_, c, 0, :], Bt[hp][rs_, c, :], True, True,
                       [AR[hp], Bt[hp]], [pI])
                tt("dve", mlc[:, :, 1, :].rearrange("p (hp par) s -> p par hp s", par=2),
                   pI[0:64, :, 0:4 * C].rearrange("p par (hp s) -> p par hp s", s=C),
                   cc("maskL").unsqueeze(1).unsqueeze(1).to_broadcast([64, 2, 4, C]), ALU.mult, [pI, cst], [mlc])
                yield
                for h in HORD:
                    hp, rs_ = h // 2, hrow(h)
                    arh = AR[hp][rs_, c, :, :].rearrange("p a t -> p (a t)")
                    mm(pI[0:64, h % 2, hp * 128:hp * 128 + 128], Kt[hp][rs_, c, :], arh, True, True,
                       [Kt[hp], AR[hp]], [pI])
                tt("dve", KA[:].rearrange("p (hp par) m -> p par hp m", par=2),
                   pI[0:64, :, :].rearrange("p par (hp m) -> p par hp m", m=2 * C), m1p, ALU.mult, [pI, cst], [KA])
                yield

                def squares(mlc, mln, lev):
                    for h in range(8):
                        if lev < 5:
                            mm(pI[0:64, h // 4, (h % 4) * 128:(h % 4) * 128 + C], mlc[:, h, 1, :], mlc[:, h, 0, :],
                               True, True, [mlc], [pI])
                        mm(pI[0:64, h // 4, (h % 4) * 128 + C:(h % 4) * 128 + 2 * C], mlc[:, h, 0, :],
                           mlc[:, h, 1, :], True, True, [mlc], [pI])
                    if lev < 5:
                        cp("act", mln[:].rearrange("p (a h) x s -> p a (h x s)", a=2), pI[0:64, :, :], [pI], [mln])
                    else:
                        cp("act", mln[:, :, 1, :].rearrange("p (a h) s -> p a h s", a=2),
                           pI[0:64, :, :].rearrange("p a (h x s) -> p a h x s", h=4, x=2)[:, :, :, 1, :], [pI], [mln])

                def tupdate(mln, tcur, tnew):
                    for h in range(8):
                        mm(pI3[0:64, h * C:(h + 1) * C], mln[:, h, 1, :], tcur[:, h, :], True, True, [mln, tcur], [pI3])
                    tt("dve", tnew[:], pI3[0:64, :].rearrange("p (h s) -> p h s", h=8), tcur[:], ALU.add,
                       [pI3, tcur], [tnew])

                for lev in range(1, 6):
                    mln = ML[lev % 2]
                    squares(mlc, mln, lev)
                    tnew = Q_.Tf if lev == 5 else TT[lev % 2]
                    tupdate(mln, tcur, tnew)
                    mlc, tcur = mln, tnew
                    yield
                pIb = pI[:, 0, :].bitcast(BF16)
                for hp in range(4):
                    for a in range(2):
                        tr(pIb[0:64, hp * 256 + a * 128:hp * 256 + a * 128 + 128], BKh[hp][:, c, a, :], identb[:],
                           [BKh[hp], identb], [pI])
                cp("act", Q_.BKtok[:].rearrange("p h x m -> p (h x m)"), pIb[0:64, :], [pI], [Q_.BKtok])
                yield

            def state_task(c, P_, Q_):
                AR, v_sb = P_.AR, P_.v
                MA, KA, tcur, BKtok, y_sb = Q_.MA, Q_.KA, Q_.Tf, Q_.BKtok, Q_.y
                bank = lambda h: (pSA if h % 2 == 0 else pSB)
                bank2 = lambda h: (pSA if h < 4 else pSB)
                for h in HORD:
                    hp, rs_ = h // 2, hrow(h)
                    mm(bank(h)[0:64, hp * C:(hp + 1) * C], AR[hp][rs_, c, 0, :], STb[rs_, hp, :], True, True,
                       [AR[hp], STb], [bank(h)])
                for h in range(8):
                    mm(bank2(h)[0:64, 256 + (h % 4) * C:256 + (h % 4 + 1) * C], KA[:, h, 0:C], v_sb[:, c, h * C:(h + 1) * C],
                       True, True, [KA, v_sb], [bank2(h)])
                X4 = X32[:].rearrange("p (hp par) s -> p par hp s", par=2)
                cp("act", X4[:, 0, :, :], pSA[0:64, 0:4 * C].rearrange("p (hp s) -> p hp s", s=C), [pSA], [X32])
                cp("act", X4[:, 1, :, :], pSB[0:64, 0:4 * C].rearrange("p (hp s) -> p hp s", s=C), [pSB], [X32])
                tt("dve", X_sb[:, 0:4, :], X32[:, 0:4, :], pSA[0:64, 256:512].rearrange("p (h s) -> p h s", s=C), ALU.add,
                   [X32, pSA], [X_sb])
                tt("dve", X_sb[:, 4:8, :], X32[:, 4:8, :], pSB[0:64, 256:512].rearrange("p (h s) -> p h s", s=C), ALU.add,
                   [X32, pSB], [X_sb])
                yield
                for h in range(8):
                    mm(pSA[0:64, h * C:(h + 1) * C], tcur[:, h, :], X_sb[:, h, :], True, True, [tcur, X_sb], [pSA])
                cp("dve", U_sb[:], pSA[0:64, :], [pSA], [U_sb])
                yield
                for h in HORD:
                    hp, rs_ = h // 2, hrow(h)
                    mm(bank(h)[0:64, hp * C:(hp + 1) * C], AR[hp][rs_, c, 1, :], STb[rs_, hp, :], True, True,
                       [AR[hp], STb], [bank(h)])
                for h in range(8):
                    o_ = bank2(h)[0:64, 256 + (h % 4) * C:256 + (h % 4 + 1) * C]
                    mm(o_, MA[:, h, C:2 * C], U_sb[:, h * C:(h + 1) * C], True, False, [MA, U_sb], [bank2(h)])
                    mm(o_, KA[:, h, C:2 * C], v_sb[:, c, h * C:(h + 1) * C], False, True, [KA, v_sb], [bank2(h)])
                Y4 = y_sb[:].rearrange("p (hp par s) -> p par hp s", par=2, s=C)
                cp("act", Y4[:, 0, :, :], pSA[0:64, 0:4 * C].rearrange("p (hp s) -> p hp s", s=C), [pSA], [y_sb])
                cp("act", Y4[:, 1, :, :], pSB[0:64, 0:4 * C].rearrange("p (hp s) -> p hp s", s=C), [pSB], [y_sb])
                tt("dve", y_sb[:, 0:256], y_sb[:, 0:256], pSA[0:64, 256:512], ALU.add, [y_sb, pSA], [y_sb])
                tt("dve", y_sb[:, 256:512], y_sb[:, 256:512], pSB[0:64, 256:512], ALU.add, [y_sb, pSB], [y_sb])
                yield
                pS = pSB
                for hp in range(4):
                    mm(pS[:, hp * 128:(hp + 1) * 128], BKtok[:, hp, 0, :], U_sb[:, hp * 128:(hp + 1) * 128],
                       True, False, [BKtok, U_sb], [pS])
                    mm(pS[:, hp * 128:(hp + 1) * 128], BKtok[:, hp, 1, :], v_sb[:, c, hp * 128:(hp + 1) * 128],
                       False, True, [BKtok, v_sb], [pS])
                for hp in range(4):
                    for hh in range(2):
                        rs_ = slice(hh * 64, hh * 64 + 64)
                        stt(ST[rs_, hp, :], ST[rs_, hp, :], P_.wC[rs_, hp, c:c + 1],
                            pS[rs_, hp * 128 + hh * 64: hp * 128 + hh * 64 + 64], ALU.mult, ALU.add, [ST, P_.wC, pS], [ST])
                cp("act", STb[:], ST[:], [ST], [STb])
                yield

            def ypost_task(b, g, c, P_, Q_):
                y_sb, v_sb = Q_.y, P_.v
                t0c = b * T + g * C
                y3 = y_sb[:].rearrange("p (h i) -> p h i", h=8)
                S.op("dve", lambda en: en.tensor_reduce(gst[:, 0, :], y3, AX.X, ALU.add), reads=rr(y_sb), writes=rr(gst))
                act(ysq[:], y_sb[:], AF.Square, [y_sb], [ysq])
                S.op("dve", lambda en: en.tensor_reduce(gst[:, 1, :], ysq[:].rearrange("p (h i) -> p h i", h=8),
                                                        AX.X, ALU.add), reads=rr(ysq), writes=rr(gst))
                ts("dve", gst[:, 2, :], gst[:, 0, :], 1.0 / 64, None, ALU.mult, None, [gst], [gst])
                tt("dve", gst[:, 3, :], gst[:, 2, :], gst[:, 2, :], ALU.mult, [gst], [gst])
                stt(gst[:, 4, :], gst[:, 1, :], 1.0 / 64, gst[:, 3, :], ALU.mult, ALU.subtract, [gst], [gst])
                ts("dve", gst[:, 4, :], gst[:, 4, :], 64e-5, None, ALU.add, None, [gst], [gst])
                rsqrt(gst[:, 5, :], gst[:, 4, :], [gst], [gst])
                yield
                bc = lambda a: a.unsqueeze(2).to_broadcast([64, 8, 64])
                yt3 = ytmp[:].rearrange("p (h i) -> p h i", h=8)
                tt("pool", yt3, y3, bc(gst[:, 2, :]), ALU.subtract, [y_sb, gst], [ytmp])
                tt("pool", yt3, yt3, bc(gst[:, 5, :]), ALU.mult, [ytmp, gst], [ytmp])
                tt("pool", ytmp[:], ytmp[:], gnwb[:], ALU.mult, [ytmp, gnwb], [ytmp])
                tt("pool", ytmp[:], ytmp[:], gnbb[:], ALU.add, [ytmp, gnbb], [ytmp])
                ys3 = ysq[:].rearrange("p (h i) -> p h i", h=8)
                tt("dve", ys3, v_sb[:, c, :].rearrange("p (h i) -> p h i", h=8), bc(P_.bon[:, c, :]), ALU.mult,
                   [v_sb, P_.bon], [ysq])
                tt("dve", ytmp[:], ytmp[:], ysq[:], ALU.add, [ytmp, ysq], [ytmp])
                pg = pjp
                mm(pg[0:64, :], P_.sg0[:, c * C:(c + 1) * C], wg0[:], True, False, [P_.sg0, wg0], [pg])
                mm(pg[0:64, :], P_.sg1[:, c * C:(c + 1) * C], wg1[:], False, True, [P_.sg1, wg1], [pg])
                tt("dve", ytmp[:], ytmp[:], pg[0:64, :], ALU.mult, [ytmp, pg], [ytmp])
                if b == 0:
                    dbg_dump("ya", ytmp[:], [ytmp], (slice(t0c, t0c + C), slice(None)))
                yield
                pq = pjp
                for kc in range(4):
                    tr(pq[:, kc * C:(kc + 1) * C], ytmp[:, kc * 128:(kc + 1) * 128], ident[0:64, 0:64], [ytmp, cst], [pq])
                cp("act", P_.ya[:, :, c * C:(c + 1) * C], pq[:, 0:4 * C].rearrange("p (k t) -> p k t", k=4), [pq], [P_.ya])
                if c == NCH - 1:
                    tb = b * T + (g // NCH) * NB
                    dma(yaT_d[:, :, tb:tb + NB], P_.ya[:], [P_.ya], [yaT_res], P_.ya_sem)
                yield

            def pgen_slice(pg_, j, n):
                cnt = 0
                while True:
                    if j < n - 1 and cnt >= (12 // n):
                        return
                    try:
                        next(pg_)
                    except StopIteration:
                        return
                    cnt += 1
                    yield

            def run_rr(gens):
                run_rr2(gens)

                gens = list(gens)
                while gens:
                    for g_ in list(gens):
                        try:
                            next(g_)
                        except StopIteration:
                            gens.remove(g_)

            for b in range(NSEQ):
                memset("dve", ST[:], 0.0, [ST])
                memset("dve", STb[:], 0.0, [STb])
                Gn = min(G, dbg_chunks)
                run_rr([prep_task(b, 0, psets[0])])
                for k in range(Gn + 2):
                    gens = []
                    if k < Gn:
                        gens.append(inv_task(k % NCH, psets[(k // NCH) % 3], csets[k % 2]))
                    if 1 <= k <= Gn:
                        g = k - 1
                        gens.append(state_task(g % NCH, psets[(g // NCH) % 3], csets[g % 2]))
                    if 2 <= k <= Gn + 1:
                        g = k - 2
                        gens.append(ypost_task(b, g, g % NCH, psets[(g // NCH) % 3], csets[g % 2]))
                    if k % NCH == 0:
                        nb_ = k // NCH + 1
                        pgen = prep_task(b, nb_, psets[nb_ % 3]) if nb_ * NCH < Gn else None
                    if pgen is not None:
                        gens.append(pgen_slice(pgen, k % NCH, NCH))
                    run_rr(gens)

        S.barrier()
        stop_after = int(os.environ.get("MK_STOP", "99")) if debug else 99

        if stop_after >= 2:
          with ExitStack() as es:
            WD = sb(es, "WD", (128, 8, 2560), BF16)
            WT = sb(es, "WT", (128, 8, 72), BF16)
            load_const(es, "2", _CST2, cst2_d, _C2)
            xt = [sb(es, "xt2_%d" % i, (128, D)) for i in range(2)]
            wst = [sb(es, "wst2_%d" % i, (128, WSTN)) for i in range(2)]
            for kc in range(8):
                for hf in range(2):
                    def consD(st_, kc=kc, hf=hf):
                        ts("dve", WD[:, kc, hf * 1280:(hf + 1) * 1280], st_[:, :1280], gcol[:, 0, kc:kc + 1], None,
                           ALU.mult, None, [st_, gcol], [WD])
                    load_weight(wD_d[kc * 128:(kc + 1) * 128, hf * 1280:(hf + 1) * 1280], 1280, consD)

                def consT(st_, kc=kc):
                    ts("dve", WT[:, kc, :], st_[:, :72], gcol[:, 0, kc:kc + 1], None, ALU.mult, None, [st_, gcol], [WT])
                load_weight(wT_d[kc * 128:(kc + 1) * 128, :], 72, consT)
            NB = DS_NB
            hT = sb(es, "hTd", (128, 8, NB), BF16)
            pTr = Pool([ps(es, "pTr2_%d" % i, (128, 8, 128), BF16) for i in range(1)])
            pj = Pool([ps(es, "pj2_%d" % i, (128, 512)) for i in range(3)])
            pW = Pool([ps(es, "pW2_%d" % i, (128, 2, 512)) for i in range(1)])
            po = ps(es, "po2", (128, 2, 512))
            qT = sb(es, "qT", (128, 4, NB), BF16)
            qiT = sb(es, "qiT", (128, 4, NB), BF16)
            kT_all = sb(es, "kT_all", (128, T), BF16)
            kiT_all = sb(es, "kiT_all", (128, T), BF16)
            vones = sb(es, "vones", (128, 16, 65), BF16)
            wi_sb = sb(es, "wi_sb", (128, 4, 8))
            rt1 = sb(es, "rt1", (128, NB))
            rt2 = sb(es, "rt2", (128, NB))
            acc = sb(es, "acc", (128, T))
            work = sb(es, "work", (128, T))
            relu_t = [sb(es, "relu%d" % i, (128, 512)) for i in range(2)]
            mx8 = sb(es, "mx8", (128, 8))
            maskb = sb(es, "maskb", (128, T), BF16)
            maskT = sb(es, "maskT", (128, 16, 128), BF16)
            causT = sb(es, "causT", (128, 128), BF16)
            eT = [sb(es, "eT%d" % i, (128, 8, 128), BF16) for i in range(2)]
            rcp = sb(es, "rcp", (128, 8))
            yb = sb(es, "yb", (128, 512), BF16)
            ybT = [sb(es, "ybT%d" % i, (128, 4, NB), BF16) for i in range(2)]
            ybT_sem = [S.new_dsem() for _ in range(2)]
            cp("dve", causT[:], cc("causalT"), [cst], [causT])
            memset("pool", vones[:, :, 64:65], 1.0, [vones])
            WI_SCALE = float(8 ** -0.5 * 64 ** -0.5)
            MBIG = 30000.0
            cbT = sb(es, "cbT", (128, 128), BF16)
            ts("dve", cbT[:], cc("causalT"), MBIG, -MBIG, ALU.mult, ALU.add, [cst], [cbT])
            qTs = [qT, sb(es, "qT_b", (128, 4, NB), BF16), sb(es, "qT_c", (128, 4, NB), BF16)]
            qiTs = [qiT, sb(es, "qiT_b", (128, 4, NB), BF16)]
            wis = [wi_sb, sb(es, "wi_sb_b", (128, 4, 8))]
            NBLK = T // NB
            kT_res = [Res("kT%d" % i) for i in range(NBLK)]
            kiT_res = [Res("kiT%d" % i) for i in range(NBLK)]
            von_res = [Res("von%d" % i) for i in range(NBLK)]
            maskTs = [maskT] + [sb(es, "maskT_%d" % i, (128, 16, 128), BF16) for i in range(3)]
            accs = [acc, sb(es, "acc_b", (128, T))]
            works = [work, sb(es, "work_b", (128, T))]
            mx8s = [mx8, sb(es, "mx8_b", (128, 8))]
            maskbs = [maskb, sb(es, "maskb_b", (128, T), BF16)]
            relus = [relu_t, [sb(es, "relub%d" % i, (128, 512)) for i in range(2)]]
            ybs = [yb, sb(es, "yb_b", (128, 512), BF16)]
            dbg_qt = int(os.environ.get("MK_QT", "999")) if debug else 999

            def proj_block(b, blk, qT_, qiT, wi_sb):
                tl0 = blk * NB
                t0 = b * T + tl0
                make_hT(pTr, t0, NB // 128, hT, 0, x_d, xt=xt)
                ropeC = cc("ropeC")[:, tl0:tl0 + NB]
                ropeS = cc("ropeS")[:, tl0:tl0 + NB]

                def proj(ct):
                    p_ = pj.next()
                    for kc in range(8):
                        mm(p_[:, :NB], WD[:, kc, ct * 128:(ct + 1) * 128], hT[:, kc, :], kc == 0, kc == 7, [WD, hT], [p_])
                    return p_

                def rope(ct_a, ct_b, dst_ap, dst_tl):
                    pa = proj(ct_a)
                    tt("dve", rt1[:], pa[:, :NB], ropeC, ALU.mult, [pa, cst], [rt1])
                    pb = proj(ct_b)
                    tt("dve", rt2[:], pb[:, :NB], ropeS, ALU.mult, [pb, cst], [rt2])
                    tt("pool", dst_ap, rt1[:], rt2[:], ALU.add, [rt1, rt2], [dst_tl])

                for i in range(4):
                    rope(i, 4 + i, qT_[:, i, :], qT_)
                    yield
                rope(8, 9, kT_all[:, tl0:tl0 + NB], kT_res[blk])
                yield
                for i in range(4):
                    rope(10 + i, 14 + i, qiT[:, i, :], qiT)
                    yield
                rope(18, 19, kiT_all[:, tl0:tl0 + NB], kiT_res[blk])
                for i in range(NB // 128):
                    p_ = pj.next()
                    for kc in range(8):
                        mm(p_[:, 0:72], hT[:, kc, i * 128:(i + 1) * 128], WT[:, kc, :], kc == 0, kc == 7, [WT, hT], [p_])
                    cp("dve", vones[:, blk * 4 + i, 0:64], p_[:, 0:64], [p_], [von_res[blk]])
                    ts("dve", wi_sb[:, i, :], p_[:, 64:72], WI_SCALE, None, ALU.mult, None, [p_], [wi_sb])
                yield

            def topk_task(qt, i, mT, qiT, wi_sb):
                if qt < 2:
                    return
                acc, work, mx8, maskb, relu_t = accs[qt % 2], works[qt % 2], mx8s[qt % 2], maskbs[qt % 2], relus[qt % 2]
                Sk = (qt + 1) * 128
                tq = slice(i * 128, (i + 1) * 128)
                nseg = (Sk + 511) // 512
                for sg in range(nseg):
                    s0 = sg * 512
                    sn = min(512, Sk - s0)
                    for h in range(8):
                        rs_ = slice((h % 2) * 64, (h % 2) * 64 + 64)
                        p_ = pj.next()
                        mm(p_[:, :sn], qiT[rs_, h // 2, tq], kiT_all[rs_, s0:s0 + sn], True, True, [qiT, kiT_res[sg]], [p_])
                        rl = relu_t[h % 2]
                        act(rl[:, :sn], p_[:, :sn], AF.Relu, [p_], [rl])
                        if h == 0:
                            ts("dve", acc[:, s0:s0 + sn], rl[:, :sn], wi_sb[:, i, 0:1], None, ALU.mult, None, [rl, wi_sb], [acc])
                        else:
                            stt(acc[:, s0:s0 + sn], rl[:, :sn], wi_sb[:, i, h:h + 1], acc[:, s0:s0 + sn],
                                ALU.mult, ALU.add, [rl, wi_sb, acc], [acc])
                        yield
                tt("dve", acc[:, Sk - 128:Sk], acc[:, Sk - 128:Sk], cc("causal_bias"), ALU.add, [acc, cst], [acc])
                src = acc
                for rnd in range(32):
                    S.op("dve", lambda en, src=src, Sk=Sk: en.max(out=mx8[:], in_=src[:, :Sk]), reads=rr(src), writes=rr(mx8))
                    yield
                    if rnd < 31:
                        S.op("dve", lambda en, src=src, Sk=Sk: en.match_replace(
                            out=work[:, :Sk], in_to_replace=mx8[:], in_values=src[:, :Sk], imm_value=NEG),
                            reads=rr(src, mx8), writes=rr(work))
                        src = work
                        yield
                ts("dve", maskb[:, :Sk], acc[:, :Sk], mx8[:, 7:8], None, ALU.is_ge, None, [acc, mx8], [maskb])
                for g in range((qt + 1 + 3) // 4):
                    pm = pj.next()
                    pmb = pm[:].bitcast(BF16)
                    nk = min(4, qt + 1 - g * 4)
                    for j in range(nk):
                        kt = g * 4 + j
                        tr(pmb[:, j * 128:(j + 1) * 128], maskb[:, kt * 128:(kt + 1) * 128], identb[:], [maskb, identb], [pm])
                    act(mT[:, g * 4:g * 4 + nk, :], pmb[:, 0:nk * 128].rearrange("p (k t) -> p k t", t=128), AF.Identity,
                        [pm], [mT], bias=-MBIG, scale=MBIG)
                yield

            def attn_task(b, blk, qt, i, mT, qT_, ybt, yb_):
                tq = slice(i * 128, (i + 1) * 128)
                t0 = b * T + blk * NB
                for kt in range(qt + 1):
                    psc = pW.next()
                    need_bias = (qt >= 2) or (kt == qt)
                    brhs = mT[:, kt, :] if qt >= 2 else cbT[:]
                    bres = mT if qt >= 2 else cbT
                    for h in [0, 2, 4, 6, 1, 3, 5, 7]:
                        rs_ = slice((h % 2) * 64, (h % 2) * 64 + 64)
                        o_ = psc[:, h % 2, (h // 2) * 128:(h // 2) * 128 + 128]
                        mm(o_, kT_all[rs_, kt * 128:(kt + 1) * 128], qT_[rs_, h // 2, tq], True, not need_bias,
                           [kT_res[kt // 4], qT_], [psc])
                        if need_bias:
                            mm(o_, identb[:], brhs, False, True, [identb, bres], [psc])
                    e_ = eT[kt % 2]
                    act(e_[:].rearrange("p (a h) t -> p a (h t)", a=2), psc[:, :, :], AF.Exp, [psc], [e_], scale=0.125)
                    for h in range(8):
                        mm(po[:, h // 4, (h % 4) * 65:(h % 4) * 65 + 65], e_[:, (h % 2) * 4 + h // 2, :], vones[:, kt, :],
                           kt == 0 and h % 4 == 0, kt == qt, [e_, von_res[kt // 4], vones], [po], skip=True)
                    if kt % 2 == 1:
                        yield
                pov = po[:, :, 0:260].rearrange("p a (h e) -> p a h e", e=65)
                S.op("dve", lambda en: en.reciprocal(rcp[:].rearrange("p (a h) -> p a h", a=2), pov[:, :, :, 64]),
                     reads=rr(po), writes=rr(rcp))
                tt("dve", yb_[:].rearrange("p (a h e) -> p a h e", a=2, h=4), pov[:, :, :, 0:64],
                   rcp[:].rearrange("p (a h) -> p a h", a=2).unsqueeze(3).to_broadcast([128, 2, 4, 64]), ALU.mult,
                   [po, rcp], [yb_])
                if "yb" in dbg_d and b == 0:
                    cp("dve", rt1[:, 0:512], yb_[:], [yb_], [rt1])
                    dbg_dump("yb", rt1[:, 0:512], [rt1], (slice(t0 + i * 128, t0 + (i + 1) * 128), slice(None)))
                pm = pj.next()
                pmb = pm[:].bitcast(BF16)
                for kc in range(4):
                    tr(pmb[:, kc * 128:(kc + 1) * 128], yb_[:, kc * 128:(kc + 1) * 128], identb[:], [yb_, identb], [pm])
                cp("act", ybt[:, :, tq], pmb[:, 0:512].rearrange("p (k t) -> p k t", t=128), [pm], [ybt])
                if i == NB // 128 - 1:
                    dma(ybT_d[:, :, t0:t0 + NB], ybt[:], [ybt], [ybT_res], ybT_sem[(b * (T // NB) + blk) % 2])
                yield

            def chain(*gs):
                for g_ in gs:
                    yield from g_

            def gslice(pg_, last, nmax):
                cnt = 0
                while True:
                    if not last and cnt >= nmax:
                        return
                    try:
                        next(pg_)
                    except StopIteration:
                        return
                    cnt += 1
                    yield

            for b in range(NSEQ):
                NT = min(T // 128, dbg_qt)
                prev = []
                pgen = None
                run_rr2([proj_block(b, 0, qTs[0], qiTs[0], wis[0])])
                for p in range(NT // 2 + 1):
                    gens = []
                    cur = []
                    for j in (2 * p, 2 * p + 1):
                        if j < NT:
                            blk, i = j // 4, j % 4
                            gens.append(topk_task(j, i, maskTs[j % 4], qiTs[blk % 2], wis[blk % 2]))
                            cur.append((b, blk, j, i, maskTs[j % 4], qTs[blk % 3], ybT[(b * NBLK + blk) % 2], ybs[j % 2]))
                    if prev:
                        gens.append(chain(*[attn_task(*a_) for a_ in prev]))
                    if p % 2 == 0:
                        nb_ = p // 2 + 1
                        pgen = proj_block(b, nb_, qTs[nb_ % 3], qiTs[nb_ % 2], wis[nb_ % 2]) if nb_ * 4 < NT else None
                    if pgen is not None:
                        gens.append(gslice(pgen, p % 2 == 1, 5))
                    prev = cur
                    run_rr2(gens)
          S.barrier()

        if stop_after >= 3:
          with ExitStack() as es:
            WG = sb(es, "WG", (128, 8, 2048), BF16)
            wbr = sb(es, "wbr", (128, 8, 1024), BF16)
            wout = sb(es, "wout", (128, 8, 1024), BF16)
            wst = [sb(es, "wst3_%d" % i, (128, WSTN)) for i in range(2)]
            for kc in range(8):
                for hf in range(2):
                    def consG(st_, kc=kc, hf=hf):
                        ts("dve", WG[:, kc, hf * 1024:(hf + 1) * 1024], st_[:, :1024], gcol[:, 0, kc:kc + 1], None,
                           ALU.mult, None, [st_, gcol], [WG])
                    load_weight(wG_d[kc * 128:(kc + 1) * 128, hf * 1024:(hf + 1) * 1024], 1024, consG)

                def consB(st_, kc=kc):
                    cp("act", wbr[:, kc, :], st_[:, :1024], [st_], [wbr])
                load_weight(wbr_d[kc * 128:(kc + 1) * 128, :], 1024, consB)

                def consO(st_, kc=kc):
                    cp("dve", wout[:, kc, :], st_[:, :1024], [st_], [wout])
                load_weight(wout_d[kc * 128:(kc + 1) * 128, :], 1024, consO)
            NB = MG_NB
            hT = sb(es, "hTm", (128, 8, NB), BF16)
            xk = sb(es, "xk", (128, 4, D))
            pTr = Pool([ps(es, "pTr3_%d" % i, (128, 8, 128), BF16) for i in range(1)])
            pj = Pool([ps(es, "pj3_%d" % i, (128, 512)) for i in range(6)])
            yaL = sb(es, "yaL", (128, 4, NB), BF16)
            ybL = sb(es, "ybL", (128, 4, NB), BF16)
            yl_sem = S.new_dsem()
            yl_sem2 = S.new_dsem()
            sgA = sb(es, "sgA", (128, NB))
            sgB = sb(es, "sgB", (128, NB))
            mA = sb(es, "mA", (128, NB))
            mB = sb(es, "mB", (128, NB))
            mgT = sb(es, "mgT", (128, 8, NB), BF16)
            x1t = [sb(es, "x1t%d" % i, (128, D)) for i in range(2)]
            x1_sem = [S.new_dsem() for _ in range(2)]
            n1 = 0
            for bi in range(NTOK // NB):
                t0 = bi * NB
                make_hT(pTr, t0, 4, hT, 0, x_d, keep=xk)
                dma(yaL[:], yaT_d[:, :, t0:t0 + NB], [yaT_res], [yaL], yl_sem)
                dma(ybL[:], ybT_d[:, :, t0:t0 + NB], [ybT_res], [ybL], yl_sem2)
                for dt_ in range(8):
                    ds_ = slice(dt_ * 128, (dt_ + 1) * 128)
                    pa = pj.next()
                    for kc in range(4):
                        mm(pa[:, :NB], wbr[:, kc, ds_], yaL[:, kc, :], kc == 0, kc == 3, [wbr, yaL], [pa])
                    pb = pj.next()
                    for kc in range(4):
                        mm(pb[:, :NB], wbr[:, 4 + kc, ds_], ybL[:, kc, :], kc == 0, kc == 3, [wbr, ybL], [pb])
                    g0 = pj.next()
                    for kc in range(8):
                        mm(g0[:, :NB], WG[:, kc, dt_ * 128:(dt_ + 1) * 128], hT[:, kc, :], kc == 0, kc == 7, [WG, hT], [g0])
                    g1 = pj.next()
                    for kc in range(8):
                        mm(g1[:, :NB], WG[:, kc, 1024 + dt_ * 128:1024 + (dt_ + 1) * 128], hT[:, kc, :], kc == 0, kc == 7,
                           [WG, hT], [g1])
                    act(sgA[:], g0[:, :NB], AF.Sigmoid, [g0], [sgA])
                    act(sgB[:], g1[:, :NB], AF.Sigmoid, [g1], [sgB])
                    tt("dve", mA[:], pa[:, :NB], sgA[:], ALU.mult, [pa, sgA], [mA])
                    tt("dve", mB[:], pb[:, :NB], sgB[:], ALU.mult, [pb, sgB], [mB])
                    tt("pool", mgT[:, dt_, :], mA[:], mB[:], ALU.add, [mA, mB], [mgT])
                for tt_ in range(4):
                    x1 = x1t[n1 % 2]
                    for hf in range(2):
                        po = pj.next()
                        for kc in range(8):
                            mm(po[:, :], mgT[:, kc, tt_ * 128:(tt_ + 1) * 128], wout[:, kc, hf * 512:(hf + 1) * 512],
                               kc == 0, kc == 7, [mgT, wout], [po])
                        tt("dve", x1[:, hf * 512:(hf + 1) * 512], po[:, :], xk[:, tt_, hf * 512:(hf + 1) * 512], ALU.add,
                           [po, xk], [x1])
                    dma(x1_d[t0 + tt_ * 128:t0 + (tt_ + 1) * 128, :], x1[:], [x1], [x1_res[bi]], x1_sem[n1 % 2])
                    if t0 < T:
                        dbg_dump("x1", x1[:], [x1], (slice(t0 + tt_ * 128, t0 + (tt_ + 1) * 128), slice(None)))
                    n1 += 1
          S.barrier()

        if stop_after >= 4:
          with ExitStack() as es:
            wup = sb(es, "wup", (128, 8, FFH), BF16)
            wdn = sb(es, "wdn", (128, 32, D), BF16)
            gzb = sb(es, "gzb", (128, D))
            wst = [sb(es, "wst4_%d" % i, (128, 1024)) for i in range(2)]
            dma(gzb[:], gz_d.partition_broadcast(128), [], [gzb], d0())
            for kc in range(8):
                for q4 in range(4):
                    def consU(st_, kc=kc, q4=q4):
                        if True:
                            ts("dve", wup[:, kc, q4 * 1024:(q4 + 1) * 1024], st_[:, 0:1024], gcol[:, 1, kc:kc + 1], None,
                               ALU.mult, None, [st_, gcol], [wup])
                    load_weight(wup_d[kc * 128:(kc + 1) * 128, q4 * 1024:(q4 + 1) * 1024], 1024, consU)
            for g in range(32):
                def consDn(st_, g=g):
                    cp("act" if g % 2 else "dve", wdn[:, g, :], st_[:, 0:1024], [st_], [wdn])
                load_weight(wdn_d[g * 128:(g + 1) * 128, :], 1024, consDn)
            NB = FF_NB
            NT4 = NB // 128
            hT = sb(es, "hTf", (128, 8, NB), BF16)
            xk = sb(es, "xkf", (128, NT4, D))
            pTr = Pool([ps(es, "pTr4_%d" % i, (128, 8, 128), BF16) for i in range(1)])
            pj = Pool([ps(es, "pj4_%d" % i, (128, 512)) for i in range(6)])
            aT = sb(es, "aT", (128, 32, NB), BF16)
            rl = [sb(es, "rl%d" % i, (128, NB), BF16) for i in range(2)]
            xx = sb(es, "x2", (128, D))
            ot = [sb(es, "ot%d" % i, (128, D)) for i in range(2)]
            o_sem = [S.new_dsem() for _ in range(2)]
            st2 = [sb(es, "st2_%d" % i, (128, 4)) for i in range(2)]
            n2 = 0
            for bi in range(NTOK // NB):
                t0 = bi * NB
                make_hT(pTr, t0, NT4, hT, 0, x1_d, keep=xk, src_res=[x1_res[t0 // 512]])
                for ht in range(32):
                    pu = pj.next()
                    for kc in range(8):
                        mm(pu[:, :NB], wup[:, kc, ht * 128:(ht + 1) * 128], hT[:, kc, :], kc == 0, kc == 7, [wup, hT], [pu])
                    r_ = rl[ht % 2]
                    act(r_[:], pu[:, :NB], AF.Relu, [pu], [r_])
                    tt("pool" if ht % 2 else "dve", aT[:, ht, :], r_[:], r_[:], ALU.mult, [r_], [aT])
                for tt_ in range(NT4):
                    oo = ot[n2 % 2]
                    s2 = st2[n2 % 2]
                    for hf in range(2):
                        pd = pj.next()
                        for ht in range(32):
                            mm(pd[:, :], aT[:, ht, tt_ * 128:(tt_ + 1) * 128], wdn[:, ht, hf * 512:(hf + 1) * 512],
                               ht == 0, ht == 31, [aT, wdn], [pd])
                        tt("dve", xx[:, hf * 512:(hf + 1) * 512], pd[:, :], xk[:, tt_, hf * 512:(hf + 1) * 512], ALU.add,
                           [pd, xk], [xx])
                    act(oo[:], xx[:], AF.Square, [xx], [oo, s2], accum_out=s2[:, 0:1])
                    ts("dve", s2[:, 1:2], s2[:, 0:1], 1.0 / D, 1e-6, ALU.mult, ALU.add, [s2], [s2])
                    rsqrt(s2[:, 2:3], s2[:, 1:2], [s2], [s2])
                    stt(oo[:], xx[:], s2[:, 2:3], gzb[:], ALU.mult, ALU.mult, [xx, s2, gzb], [oo])
                    out_dmas.append(dma(out_d[t0 + tt_ * 128:t0 + (tt_ + 1) * 128, :], oo[:], [oo], [], o_sem[n2 % 2]))
                    n2 += 1

        S.finish(out_dmas)
        S.emit()
    return nc


def _swap_halves(cols):
    c = np.asarray(cols).reshape(-1, 2, 32)
    return c[:, ::-1, :].reshape(-1)


def _layout_inputs(inp):
    f = lambda a: np.ascontiguousarray(np.asarray(a, dtype=np.float32))
    w_in = f(inp["w_in"])[0]
    mu = f(inp["mu_shift"])[0]
    colsA = np.concatenate([np.arange(0, 1024), np.arange(1536, 1824)])
    colsV = np.arange(1024, 1536)
    base = 1824
    q = base + np.arange(512)
    k = base + 512 + np.arange(64)
    v = base + 576 + np.arange(64)
    qi = base + 640 + np.arange(512)
    ki = base + 1152 + np.arange(64)
    wi = base + 1216 + np.arange(8)
    colsD = np.concatenate([q, _swap_halves(q), k, k, _swap_halves(k), _swap_halves(k),
                            qi, _swap_halves(qi), ki, ki, _swap_halves(ki), _swap_halves(ki)])
    colsT = np.concatenate([v, wi])
    colsG = 1824 + 1224 + np.arange(2048)
    per_ch = lambda a: f(a)[0].reshape(4, 128).T
    pp = np.concatenate([per_ch(inp["decay_bias"]), per_ch(inp["iclr_bias"]), per_ch(inp["k_k"]),
                         per_ch(inp["k_a"]), per_ch(inp["r_k"])], axis=1)
    shared = {
        "wA": f(w_in[:, colsA]), "wV": f(w_in[:, colsV]), "muA": f(mu[colsA]), "muV": f(mu[colsV]),
        "wD": f(w_in[:, colsD]), "wT": f(w_in[:, colsT]), "wG": f(w_in[:, colsG]),
        "gcols": f(np.concatenate([f(inp["g_mix"])[0].reshape(8, 128).T, f(inp["g_ffn"])[0].reshape(8, 128).T], axis=1)),
        "gfin": f(inp["g_final"]),
        "wlora": f(np.concatenate([f(inp["w_decay_up"])[0], f(inp["w_iclr_up"])[0]], axis=0)),
        "wgate": f(inp["w_gate_up"])[0], "pp": f(pp),
        "gnw": f(inp["gn_w"])[0], "gnb": f(inp["gn_b"])[0],
        "wbr": f(f(inp["w_branch"])[0].reshape(1024, 1024)), "wout": f(inp["w_out"])[0],
        "wup": f(inp["w_ffn_up"])[0], "wdn": f(inp["w_ffn_down"])[0], "cstA": _CSTA, "cst1": _CST1, "cst2": _CST2,
    }
    x = f(inp["x"])
    maps = []
    for c in range(NCORES):
        m = dict(shared)
        m["x"] = np.ascontiguousarray(x[c * NSEQ:(c + 1) * NSEQ].reshape(NTOK, D))
        maps.append(m)
    return maps


def kernel(**inputs):
    maps = _layout_inputs(inputs)
    nc = build_nc()
    res = run_bass_kernel_spmd(nc, maps, core_ids=list(range(NCORES)))
    outs = [np.asarray(r["out"], dtype=np.float32).reshape(NSEQ, T, D) for r in res.results]
    return np.concatenate(outs, axis=0)
```

```python
import os
from contextlib import ExitStack

import numpy as np
import concourse.bass as bass
import concourse.mybir as mybir
from concourse.bass_utils import run_bass_kernel_spmd

F32 = mybir.dt.float32
BF16 = mybir.dt.bfloat16
ALU = mybir.AluOpType
AF = mybir.ActivationFunctionType
AX = mybir.AxisListType

NCORES = 8
T = 2048
D = 1024
NSEQ = 2
NTOK = NSEQ * T
C = 64
C0 = float(np.exp(-0.5))
NEG = -1.0e30
RW_NB = 128
DS_NB = 512
MG_NB = 512
FF_NB = 256
FFH = 4096


class Res:
    __slots__ = ("name", "w", "rd", "rd_dma")

    def __init__(self, name):
        self.name = name
        self.w = None
        self.rd = {}
        self.rd_dma = []


class DmaSem:
    def __init__(self, sem):
        self.sem = sem
        self.count = 0


class _Op:
    __slots__ = ("id", "eng", "fn", "deps", "dsem", "val", "signal")


class Sched:
    ENGS = ("pe", "act", "dve", "pool", "sp")

    def __init__(self, nc, es):
        self.nc = nc
        self.es = es
        self.ops = []
        self.per = {e: [] for e in self.ENGS}
        self.sem = {e: es.enter_context(nc.semaphore("s_" + e)) for e in self.ENGS}
        self.n_dsem = 0
        self.last = {e: None for e in self.ENGS}
        self.dma_since_barrier = []

    def new_dsem(self):
        self.n_dsem += 1
        return DmaSem(self.es.enter_context(self.nc.semaphore("d%d" % self.n_dsem)))

    def op(self, eng, fn, reads=(), writes=(), dsem=None):
        o = _Op()
        o.id = len(self.ops)
        o.eng = eng
        o.fn = fn
        o.dsem = dsem
        o.signal = False
        o.val = None
        deps = {}

        def add(d, kind):
            if d is None:
                return
            if kind == "raw" or d not in deps:
                deps[d] = kind

        for r in reads:
            add(r.w, "raw")
        for w in writes:
            add(w.w, "waw")
            for d in w.rd.values():
                add(d, "war")
            for d in w.rd_dma:
                add(d, "war")
        o.deps = deps
        for r in reads:
            if dsem is not None:
                r.rd_dma.append(o.id)
            else:
                r.rd[eng] = o.id
        for w in writes:
            w.w = o.id
            w.rd = {}
            w.rd_dma = []
        if dsem is not None:
            dsem.count += 16
            o.val = dsem.count
            self.dma_since_barrier.append(o.id)
        self.ops.append(o)
        self.per[eng].append(o)
        self.last[eng] = o.id
        return o

    def barrier(self):
        lasts = [v for v in self.last.values() if v is not None]
        dmas = list(self.dma_since_barrier)
        self.dma_since_barrier = []
        for e in self.ENGS:
            o = self.op(e, lambda en: en.nop())
            for d in lasts + dmas:
                if d != o.id:
                    o.deps[d] = "raw"

    def finish(self, dma_ops):
        o = self.op("sp", lambda en: en.nop())
        for d in dma_ops:
            o.deps[d.id] = "raw"

    def emit(self):
        ops = self.ops
        for o in ops:
            for d, kind in o.deps.items():
                p = ops[d]
                if p.dsem is not None:
                    continue
                if p.eng == o.eng and o.dsem is None and o.eng in ("pe", "sp"):
                    continue
                p.signal = True
        cnt = {e: 0 for e in self.ENGS}
        for o in ops:
            if o.dsem is None and o.signal:
                cnt[o.eng] += 1
                o.val = cnt[o.eng]
        sem = self.sem

        def run(eng, en):
            known = {}
            for o in self.per[eng]:
                need = {}
                for d, kind in o.deps.items():
                    p = ops[d]
                    if p.dsem is not None:
                        key, s, v = ("d", id(p.dsem)), p.dsem.sem, p.val
                    else:
                        if not p.signal:
                            continue
                        if p.eng == eng and o.dsem is None and eng in ("pe", "sp"):
                            continue
                        key, s, v = ("e", p.eng), sem[p.eng], p.val
                    if known.get(key, 0) >= v:
                        continue
                    if key not in need or need[key][1] < v:
                        need[key] = (s, v)
                for key, (s, v) in need.items():
                    en.wait_ge(s, v)
                    known[key] = v
                ins = o.fn(en)
                if o.dsem is not None:
                    ins.then_inc(o.dsem.sem, 16)
                elif o.signal:
                    ins.then_inc(sem[eng], 1)

        with self.nc.Block() as block:
            @block.tensor
            def _(en):
                run("pe", en)

            @block.scalar
            def _(en):
                run("act", en)

            @block.vector
            def _(en):
                run("dve", en)

            @block.gpsimd
            def _(en):
                run("pool", en)

            @block.sync
            def _(en):
                run("sp", en)


class Tl:
    def __init__(self, h, name, nres=1):
        self.h = h
        self.name = name
        self.rs = [Res("%s.%d" % (name, i)) for i in range(nres)]

    @property
    def r(self):
        return self.rs[0]

    def __getitem__(self, k):
        return self.h[k]


class Pool:
    def __init__(self, tiles):
        self.tiles = tiles
        self.i = 0

    def next(self):
        t = self.tiles[self.i % len(self.tiles)]
        self.i += 1
        return t


class _CB:
    def __init__(self):
        self.cols = {}
        self.parts = []
        self.off = 0

    def put(self, name, arr):
        a = np.zeros((128, arr.shape[1]), np.float32)
        a[: arr.shape[0]] = arr
        self.cols[name] = (self.off, arr.shape[1], arr.shape[0])
        self.parts.append(a)
        self.off += arr.shape[1]

    def arr(self):
        return np.ascontiguousarray(np.concatenate(self.parts, axis=1))


def _const_f32():
    A, B1, B2 = _CB(), _CB(), _CB()
    A.put("ident", np.eye(128, dtype=np.float32))
    s = np.arange(64)[:, None]
    t = np.arange(64)[None, :]
    m1 = np.concatenate([(s < t), (s <= t)], axis=1).astype(np.float32)
    B1.put("mask1", m1)
    B1.put("maskL", (s > t).astype(np.float32))
    B1.put("eye8", np.eye(64, dtype=np.float32))
    rm = np.ones((128, RW_NB), np.float32)
    rm[:, ::C] = 0.0
    B1.put("reset", rm)
    bo = np.zeros((128, 128), np.float32)
    bo[:64, :64] = 1.0
    bo[64:, 64:] = 1.0
    A.put("blockones", bo)
    hi = np.zeros((128, 2), np.float32)
    hi[:64, 0] = 1.0
    hi[64:, 1] = 1.0
    A.put("headind", hi)
    tq = np.arange(128)[:, None]
    kk = np.arange(128)[None, :]
    A.put("causal_bias", np.where(kk <= tq, 0.0, NEG).astype(np.float32))
    A.put("causalT", (tq <= kk).astype(np.float32))
    inv = (1.0 / (10000.0 ** (np.arange(0, 64, 2, dtype=np.float32) / np.float32(64)))).astype(np.float32)
    ang = (np.arange(T, dtype=np.float32)[:, None] * inv[None, :]).astype(np.float32)
    cs = np.cos(ang).astype(np.float32).T
    sn = np.sin(ang).astype(np.float32).T
    d = np.arange(128) % 64
    B2.put("ropeC", cs[d % 32])
    sg = np.where(d < 32, -1.0, 1.0).astype(np.float32)[:, None]
    B2.put("ropeS", sn[d % 32] * sg)
    return A, B1, B2


_CA, _C1, _C2 = _const_f32()
_CSTA, _CST1, _CST2 = _CA.arr(), _C1.arr(), _C2.arr()


def build_nc(debug=None):
    nc = bass.Bass("TRN2", target_bir_lowering=False)
    dt_in = lambda name, shape: nc.dram_tensor(name, list(shape), F32, kind="ExternalInput").ap()
    x_d = dt_in("x", (NTOK, D))
    wA_d = dt_in("wA", (D, 1312))
    wV_d = dt_in("wV", (D, 512))
    muA_d = dt_in("muA", (1312,))
    muV_d = dt_in("muV", (512,))
    wD_d = dt_in("wD", (D, 2560))
    wT_d = dt_in("wT", (D, 72))
    wG_d = dt_in("wG", (D, 2048))
    gcol_d = dt_in("gcols", (128, 16))
    gz_d = dt_in("gfin", (D,))
    wlora_d = dt_in("wlora", (128, 512))
    wgate_d = dt_in("wgate", (160, 512))
    pp_d = dt_in("pp", (128, 20))
    gnw_d = dt_in("gnw", (512,))
    gnb_d = dt_in("gnb", (512,))
    wbr_d = dt_in("wbr", (1024, 1024))
    wout_d = dt_in("wout", (D, D))
    wup_d = dt_in("wup", (D, FFH))
    wdn_d = dt_in("wdn", (FFH, D))
    cstA_d = dt_in("cstA", _CSTA.shape)
    cst1_d = dt_in("cst1", _CST1.shape)
    cst2_d = dt_in("cst2", _CST2.shape)
    out_d = nc.dram_tensor("out", [NTOK, D], F32, kind="ExternalOutput").ap()
    yaT_d = nc.dram_tensor("yaT_scr", [128, 4, NTOK], BF16, kind="Internal").ap()
    ybT_d = nc.dram_tensor("ybT_scr", [128, 4, NTOK], BF16, kind="Internal").ap()
    x1_d = nc.dram_tensor("x1_scr", [NTOK, D], F32, kind="Internal").ap()
    dbg_d = {}
    dbg_sem = {}
    if debug:
        for name, shape in debug.items():
            dbg_d[name] = nc.dram_tensor("dbg_" + name, list(shape), F32, kind="ExternalOutput").ap()

    top = ExitStack()
    with top:
        S = Sched(nc, top)
        out_dmas = []
        yaT_res = Res("yaT_scr")
        ybT_res = Res("ybT_scr")
        x1_res = [Res("x1_scr%d" % i) for i in range(NTOK // 512)]

        uid = [0]

        def sb(es, name, shape, dt=F32, nres=1):
            uid[0] += 1
            return Tl(es.enter_context(nc.sbuf_tensor("sb%d_%s" % (uid[0], name), list(shape), dt)), name, nres)

        def ps(es, name, shape, dt=F32):
            uid[0] += 1
            return Tl(es.enter_context(nc.psum_tensor("ps%d_%s" % (uid[0], name), list(shape), dt)), name)

        def rr(*xs):
            out = []
            for x in xs:
                if isinstance(x, Tl):
                    out.extend(x.rs)
                elif isinstance(x, Res):
                    out.append(x)
                else:
                    out.extend(x)
            return out

        def dma(out_ap, in_ap, reads, writes, dsem):
            return S.op("sp", lambda en: en.dma_start(out=out_ap, in_=in_ap),
                        reads=rr(*reads), writes=rr(*writes), dsem=dsem)

        def mm(out_ap, lhsT, rhs, start, stop, reads, writes, skip=False):
            return S.op("pe", lambda en: en.matmul(out_ap, lhsT, rhs, start=start, stop=stop, skip_group_check=skip),
                        reads=rr(*reads), writes=rr(*writes))

        def tr(out_ap, in_ap, ident, reads, writes):
            return S.op("pe", lambda en: en.transpose(out_ap, in_ap, ident),
                        reads=rr(*reads), writes=rr(*writes))

        def act(out_ap, in_ap, func, reads, writes, bias=0.0, scale=1.0, accum_out=None):
            return S.op("act", lambda en: en.activation(out_ap, in_ap, func, bias=bias, scale=scale,
                                                        accum_out=accum_out),
                        reads=rr(*reads), writes=rr(*writes))

        def tt(eng, out_ap, a, b, op, reads, writes):
            return S.op(eng, lambda en: en.tensor_tensor(out_ap, a, b, op), reads=rr(*reads), writes=rr(*writes))

        def ts(eng, out_ap, a, s1, s2, op0, op1, reads, writes):
            if op1 is None:
                return S.op(eng, lambda en: en.tensor_scalar(out_ap, a, s1, None, op0),
                            reads=rr(*reads), writes=rr(*writes))
            return S.op(eng, lambda en: en.tensor_scalar(out_ap, a, s1, s2, op0, op1),
                        reads=rr(*reads), writes=rr(*writes))

        def stt(out_ap, a, sc, b, op0, op1, reads, writes):
            return S.op("dve", lambda en: en.scalar_tensor_tensor(out_ap, a, sc, b, op0, op1),
                        reads=rr(*reads), writes=rr(*writes))

        def rsqrt(out_ap, in_ap, reads, writes):
            act(out_ap, in_ap, AF.Sqrt, reads, writes)
            S.op("dve", lambda en: en.reciprocal(out_ap, out_ap), reads=rr(*writes), writes=rr(*writes))

        def cp(eng, out_ap, in_ap, reads, writes):
            if eng == "act":
                return S.op("act", lambda en: en.copy(out_ap, in_ap), reads=rr(*reads), writes=rr(*writes))
            return S.op(eng, lambda en: en.tensor_copy(out_ap, in_ap), reads=rr(*reads), writes=rr(*writes))

        def memset(eng, ap, val, writes):
            return S.op(eng, lambda en: en.memset(ap, val), writes=rr(*writes))

        def dbg_dump(name, src_ap, reads, dst_slice=None):
            if name not in dbg_d:
                return
            dst = dbg_d[name] if dst_slice is None else dbg_d[name][dst_slice]
            if name not in dbg_sem:
                dbg_sem[name] = S.new_dsem()
            out_dmas.append(dma(dst, src_ap, reads, [], dbg_sem[name]))

        def run_rr2(gens):
            gens = list(gens)
            while gens:
                for g_ in list(gens):
                    try:
                        next(g_)
                    except StopIteration:
                        gens.remove(g_)

        cst = Tl(None, "cstgroup", 0)
        ctiles = {}

        def load_const(es_, key, arr, src_d, cb):
            t_ = sb(es_, "cs_sb" + key, arr.shape)
            dma(t_[:], src_d, [], [t_], S.new_dsem())
            cst.rs.extend(t_.rs)
            for nm in cb.cols:
                ctiles[nm] = (t_, cb.cols[nm])

        load_const(top, "A", _CSTA, cstA_d, _CA)

        def cc(name, rows=None):
            t_, (o, n, r0) = ctiles[name]
            return t_[: (rows or r0), o:o + n]

        def d0():
            return S.new_dsem()

        nhalf = sb(top, "nhalf", (128, 256))
        memset("pool", nhalf[:], -0.5, [nhalf])
        identb = sb(top, "identb", (128, 128), BF16)
        cp("dve", identb[:], cc("ident"), [cst], [identb])
        gcol = sb(top, "gcol", (128, 2, 8))
        dma(gcol[:].rearrange("p a k -> p (a k)"), gcol_d, [], [gcol], d0())
        pp = sb(top, "pp", (128, 20))
        dma(pp[:], pp_d, [], [pp], d0())

        WSTN = 1312
        wst = None
        wst_sem = [S.new_dsem() for _ in range(2)]
        wst_i = [0]

        def load_weight(src_ap, ncols, consume):
            i = wst_i[0] % 2
            wst_i[0] += 1
            dma(wst[i][:, :ncols], src_ap, [], [wst[i]], wst_sem[i])
            consume(wst[i])

        xt_sem = [S.new_dsem() for _ in range(2)]
        xt_i = [0]
        xs_bf = [sb(top, "xsbf%d" % i, (128, D), BF16) for i in range(2)]
        stat = [sb(top, "stat%d" % i, (128, 4)) for i in range(2)]

        def make_hT(es_ps, tok0, ntile, hT, col0, src_d, xt=None, keep=None, src_res=()):
            for i in range(ntile):
                k = xt_i[0] % 2
                xt_i[0] += 1
                if keep is not None:
                    xin = keep
                    xap = keep[:, i, :]
                    dma(xap, src_d[tok0 + i * 128: tok0 + (i + 1) * 128, :], src_res, [keep], xt_sem[k])
                else:
                    xin = xt[k % len(xt)]
                    xap = xin[:]
                    dma(xap, src_d[tok0 + i * 128: tok0 + (i + 1) * 128, :], src_res, [xin], xt_sem[k % len(xt)])
                st = stat[k]
                act(xs_bf[k][:], xap, AF.Square, [xin], [xs_bf[k], st], accum_out=st[:, 0:1])
                ts("dve", st[:, 1:2], st[:, 0:1], 1.0 / D, 1e-6, ALU.mult, ALU.add, [st], [st])
                rsqrt(st[:, 2:3], st[:, 1:2], [st], [st])
                ts("dve", xs_bf[k][:], xap, st[:, 2:3], None, ALU.mult, None, [xin, st], [xs_bf[k]])
                pt = es_ps.next()
                for kc in range(8):
                    tr(pt[:, kc, :], xs_bf[k][:, kc * 128:(kc + 1) * 128], identb[:], [xs_bf[k], identb], [pt])
                cp("act" if i % 2 else "dve", hT[:, :, col0 + i * 128: col0 + (i + 1) * 128], pt[:, :, :], [pt], [hT])

        with ExitStack() as es:
            W1A = sb(es, "W1A", (128, 8, 1312), BF16)
            W2A = sb(es, "W2A", (128, 8, 1312), BF16)
            W1V = sb(es, "W1V", (128, 8, 512), BF16)
            W2V = sb(es, "W2V", (128, 8, 512), BF16)
            with ExitStack() as es_w:
                wst = [sb(es_w, "wst1_%d" % i, (128, WSTN)) for i in range(2)]
                mub = sb(es_w, "mub", (128, 1824))
                omb = sb(es_w, "omb", (128, 1824))
                dma(mub[:, 0:1312], muA_d.partition_broadcast(128), [], [mub], d0())
                dma(mub[:, 1312:1824], muV_d.partition_broadcast(128), [], [mub], d0())
                ts("pool", omb[:], mub[:], -1.0, 1.0, ALU.mult, ALU.add, [mub], [omb])
                for kc in range(8):
                    def consA(st_, kc=kc):
                        stt(W1A[:, kc, :], st_[:, :1312], gcol[:, 0, kc:kc + 1], omb[:, 0:1312], ALU.mult, ALU.mult,
                            [st_, gcol, omb], [W1A])
                        stt(W2A[:, kc, :], st_[:, :1312], gcol[:, 0, kc:kc + 1], mub[:, 0:1312], ALU.mult, ALU.mult,
                            [st_, gcol, mub], [W2A])
                    load_weight(wA_d[kc * 128:(kc + 1) * 128, :], 1312, consA)

                    def consV(st_, kc=kc):
                        stt(W1V[:, kc, :], st_[:, :512], gcol[:, 0, kc:kc + 1], omb[:, 1312:1824], ALU.mult, ALU.mult,
                            [st_, gcol, omb], [W1V])
                        stt(W2V[:, kc, :], st_[:, :512], gcol[:, 0, kc:kc + 1], mub[:, 1312:1824], ALU.mult, ALU.mult,
                            [st_, gcol, mub], [W2V])
                    load_weight(wV_d[kc * 128:(kc + 1) * 128, :], 512, consV)
            S.barrier()
            load_const(es, "1", _CST1, cst1_d, _C1)
            xt = [sb(es, "xt%d" % i, (128, D)) for i in range(1)]
            wlora = sb(es, "wlora", (128, 512))
            wg0f = sb(es, "wg0f", (128, 512))
            wg1f = sb(es, "wg1f", (32, 512))
            wg0 = sb(es, "wg0", (128, 512), BF16)
            wg1 = sb(es, "wg1", (32, 512), BF16)
            gnwb = sb(es, "gnwb", (64, 512))
            gnbb = sb(es, "gnbb", (64, 512))
            dma(wlora[:], wlora_d, [], [wlora], d0())
            dma(wg0f[:], wgate_d[0:128, :], [], [wg0f], d0())
            dma(wg1f[:], wgate_d[128:160, :], [], [wg1f], d0())
            cp("dve", wg0[:], wg0f[:], [wg0f], [wg0])
            cp("dve", wg1[:], wg1f[:], [wg1f], [wg1])
            dma(gnwb[:], gnw_d.partition_broadcast(64), [], [gnwb], d0())
            dma(gnbb[:], gnb_d.partition_broadcast(64), [], [gnbb], d0())

            NB = RW_NB
            NCH = NB // C
            G = T // C
            hT = sb(es, "hT", (128, 8, NB + 2), BF16)
            pTr = Pool([ps(es, "pTr", (128, 8, 128), BF16)])
            pjp = ps(es, "pjp", (128, 512))
            pbon = ps(es, "pbon", (128, 512))
            pI = ps(es, "pI", (128, 2, 512))
            pI3 = ps(es, "pI3", (128, 512))
            pSA = ps(es, "pSA", (128, 512))
            pSB = ps(es, "pSB", (128, 512))
            r_sb = sb(es, "r_sb", (128, 4, NB))
            k_sb = sb(es, "k_sb", (128, 4, NB))
            wa_sb = sb(es, "wa_sb", (128, NB))
            tmps = [[sb(es, "rt%d_%d" % (i, q), (128, NB)) for i in range(10)] for q in range(4)]

            class PSet:
                pass
            psets = []
            for i in range(3):
                P_ = PSet()
                P_.sg0 = sb(es, "sg0_%d" % i, (128, NB), BF16)
                P_.sg1 = sb(es, "sg1_%d" % i, (32, NB), BF16)
                P_.v = sb(es, "v_sb%d" % i, (64, NCH, 512), BF16)
                P_.AR = [sb(es, "AR%d_%d" % (h, i), (128, NCH, 2, C), BF16) for h in range(4)]
                P_.Bt = [sb(es, "Bt%d_%d" % (h, i), (128, NCH, C), BF16) for h in range(4)]
                P_.Kt = [sb(es, "Kt%d_%d" % (h, i), (128, NCH, C), BF16) for h in range(4)]
                P_.BKh = [sb(es, "BKh%d_%d" % (h, i), (128, NCH, 2, C), BF16) for h in range(4)]
                P_.wC = sb(es, "wC%d" % i, (128, 4, NCH))
                P_.bon = sb(es, "bon%d" % i, (64, NCH, 8))
                P_.ya = sb(es, "yaT%d" % i, (128, 4, NB), BF16)
                P_.ya_sem = S.new_dsem()
                psets.append(P_)
            csets = []
            for i in range(2):
                Q_ = PSet()
                Q_.MA = sb(es, "MA%d" % i, (64, 8, 2 * C), BF16)
                Q_.KA = sb(es, "KA%d" % i, (64, 8, 2 * C), BF16)
                Q_.Tf = sb(es, "Tf%d" % i, (64, 8, C), BF16)
                Q_.BKtok = sb(es, "BKtok%d" % i, (64, 4, 2, 128), BF16)
                Q_.y = sb(es, "y_sb%d" % i, (64, 512))
                csets.append(Q_)
            ML = [sb(es, "ML%d" % i, (64, 8, 2, C), BF16) for i in range(2)]
            TT = [sb(es, "TT%d" % i, (64, 8, C), BF16) for i in range(2)]
            ST = sb(es, "ST", (128, 4, C))
            X_sb = sb(es, "X_sb", (64, 8, C), BF16)
            X32 = sb(es, "X32", (64, 8, C))
            STb = sb(es, "STb", (128, 4, C), BF16)
            U_sb = sb(es, "U_sb", (64, 512), BF16)
            ysq = sb(es, "ysq", (64, 512))
            ytmp = sb(es, "ytmp", (64, 512))
            gst = sb(es, "gst", (64, 6, 8))
            ident = cc("ident")
            m1b = cc("mask1").unsqueeze(1).to_broadcast([64, 8, 2 * C])
            mLb = cc("maskL").unsqueeze(1).to_broadcast([64, 8, C])
            eyb = cc("eye8").unsqueeze(1).to_broadcast([64, 8, C])
            hrow = lambda h: slice((h % 2) * 64, (h % 2) * 64 + 64)
            HORD = [0, 2, 4, 6, 1, 3, 5, 7]
            dbg_chunks = int(os.environ.get("MK_CHUNKS", "999")) if debug else 999

            def prep_task(b, blk, P_):
                t0 = b * T + blk * NB
                if blk == 0:
                    memset("pool", hT[:, :, 0:2], 0.0, [hT])
                else:
                    cp("dve", hT[:, :, 1:2], hT[:, :, NB + 1:NB + 2], [hT], [hT])
                make_hT(pTr, t0, NB // 128, hT, 2, x_d, xt=xt)
                yield
                for ct in range(11):
                    rows = 32 if ct == 10 else 128
                    c0 = ct * 128
                    p_ = pjp
                    for kc in range(8):
                        mm(p_[:rows, :NB], W1A[:, kc, c0:c0 + rows], hT[:, kc, 2:NB + 2], kc == 0, False, [W1A, hT], [p_])
                        mm(p_[:rows, :NB], W2A[:, kc, c0:c0 + rows], hT[:, kc, 1:NB + 1], False, kc == 7, [W2A, hT], [p_])
                    if ct < 4:
                        cp("act", r_sb[:, ct, :], p_[:, :NB], [p_], [r_sb])
                    elif ct < 8:
                        cp("dve", k_sb[:, ct - 4, :], p_[:, :NB], [p_], [k_sb])
                    elif ct == 8:
                        act(wa_sb[0:64, :], p_[0:64, :NB], AF.Tanh, [p_], [wa_sb])
                        cp("dve", wa_sb[64:128, :], p_[64:128, :NB], [p_], [wa_sb])
                    elif ct == 9:
                        act(P_.sg0[:], p_[:, :NB], AF.Sigmoid, [p_], [P_.sg0])
                    else:
                        act(P_.sg1[:], p_[0:32, :NB], AF.Sigmoid, [p_], [P_.sg1])
                    if ct % 3 == 2:
                        yield
                for c in range(NCH):
                    p_ = pjp
                    for kc in range(8):
                        mm(p_[0:64, :], hT[:, kc, 2 + c * C:2 + (c + 1) * C], W1V[:, kc, :], kc == 0, False, [W1V, hT], [p_])
                        mm(p_[0:64, :], hT[:, kc, 1 + c * C:1 + (c + 1) * C], W2V[:, kc, :], False, kc == 7, [W2V, hT], [p_])
                    cp("act", P_.v[:, c, :], p_[0:64, :], [p_], [P_.v])
                yield
                v4 = lambda a: a[:].rearrange("p (c t) -> p c t", t=C)

                def hp_task(hp, tmp):
                    cs_ = slice(hp * 128, (hp + 1) * 128)
                    ppc = lambda j, hp=hp: pp[:, j * 4 + hp: j * 4 + hp + 1]
                    sgd, icl, cum, e_in, e_ng, e_ex, e_rm, kkn, kmod, t9 = tmp
                    AR, Bt, Kt, BKh = P_.AR, P_.Bt, P_.Kt, P_.BKh
                    p_ = pjp
                    mm(p_[:, :NB], wlora[0:64, cs_], wa_sb[0:64, :], True, True, [wlora, wa_sb], [p_])
                    act(sgd[:], p_[:, :NB], AF.Sigmoid, [p_, pp], [sgd], bias=ppc(0))
                    p_ = pbon
                    mm(p_[:, 256:256 + NB], wlora[64:128, cs_], wa_sb[64:128, :], True, True, [wlora, wa_sb], [p_])
                    act(icl[:], p_[:, 256:256 + NB], AF.Sigmoid, [p_, pp], [icl], bias=ppc(1))
                    S.op("dve", lambda en, cum=cum, sgd=sgd: en.tensor_tensor_scan(
                        cum[:], cc("reset"), sgd[:], 0.0, ALU.mult, ALU.add), reads=rr(cst, sgd), writes=rr(cum))
                    yield
                    act(e_in[:], cum[:], AF.Exp, [cum], [e_in], scale=-C0)
                    act(e_ng[:], cum[:], AF.Exp, [cum], [e_ng], scale=C0)
                    tt("dve", t9[:], cum[:], sgd[:], ALU.subtract, [cum, sgd], [t9])
                    act(e_ex[:], t9[:], AF.Exp, [t9], [e_ex], scale=-C0)
                    cum3 = cum[:].rearrange("p (c t) -> p c t", t=C)
                    tt("dve", t9[:].rearrange("p (c t) -> p c t", t=C),
                       cum3[:, :, C - 1:C].to_broadcast([128, NCH, C]), cum3, ALU.subtract, [cum], [t9])
                    act(e_rm[:], t9[:], AF.Exp, [t9], [e_rm], scale=-C0)
                    cp("dve", P_.wC[:, hp, :], e_in[:].rearrange("p (c t) -> p c t", t=C)[:, :, C - 1], [e_in], [P_.wC])
                    kx = k_sb[:, hp, :]
                    ts("dve", kkn[:], kx, ppc(2), None, ALU.mult, None, [k_sb, pp], [kkn])
                    tt("dve", t9[:], kkn[:], kkn[:], ALU.mult, [kkn], [t9])
                    p_ = pjp
                    mm(p_[:, :NB], cc("blockones"), t9[:], True, True, [cst, t9], [p_])
                    ts("dve", t9[:], p_[:, :NB], 1e-24, None, ALU.max, None, [p_], [t9])
                    yield
                    rsqrt(t9[:], t9[:], [t9], [t9])
                    tt("dve", kkn[:], kkn[:], t9[:], ALU.mult, [kkn, t9], [kkn])
                    ts("dve", t9[:], icl[:], -1.0, ppc(3), ALU.add, ALU.mult, [icl, pp], [t9])
                    stt(kmod[:], t9[:], 1.0, kx, ALU.add, ALU.mult, [t9, k_sb], [kmod])
                    tt("dve", icl[:], icl[:], kkn[:], ALU.mult, [icl, kkn], [icl])
                    stt(AR[hp][:, :, 0, :], v4(kkn), -1.0, v4(e_ex), ALU.mult, ALU.mult, [kkn, e_ex], [AR[hp]])
                    tt("dve", AR[hp][:, :, 1, :], r_sb[:, hp, :].rearrange("p (c t) -> p c t", t=C), v4(e_in),
                       ALU.mult, [r_sb, e_in], [AR[hp]])
                    yield
                    tt("dve", Bt[hp][:], v4(icl), v4(e_ng), ALU.mult, [icl, e_ng], [Bt[hp]])
                    tt("dve", Kt[hp][:], v4(kmod), v4(e_ng), ALU.mult, [kmod, e_ng], [Kt[hp]])
                    tt("dve", BKh[hp][:, :, 0, :], v4(icl), v4(e_rm), ALU.mult, [icl, e_rm], [BKh[hp]])
                    tt("dve", BKh[hp][:, :, 1, :], v4(kmod), v4(e_rm), ALU.mult, [kmod, e_rm], [BKh[hp]])
                    stt(t9[:], r_sb[:, hp, :], ppc(4), kmod[:], ALU.mult, ALU.mult, [r_sb, pp, kmod], [t9])
                    for c in range(NCH):
                        mm(pbon[0:64, c * 8 + hp * 2: c * 8 + hp * 2 + 2], t9[:, c * C:(c + 1) * C],
                           cc("headind"), True, True, [t9, cst], [pbon])
                    yield

                subs = [hp_task(hp, tmps[hp]) for hp in range(4)]
                while subs:
                    for g_ in list(subs):
                        try:
                            next(g_)
                        except StopIteration:
                            subs.remove(g_)
                    yield
                cp("act", P_.bon[:].rearrange("p c h -> p (c h)"), pbon[0:64, 0:NCH * 8], [pbon], [P_.bon])

            def inv_task(c, P_, Q_):
                AR, Bt, Kt, BKh = P_.AR, P_.Bt, P_.Kt, P_.BKh
                MA, KA = Q_.MA, Q_.KA
                for h in HORD:
                    hp, rs_ = h // 2, hrow(h)
                    arh = AR[hp][rs_, c, :, :].rearrange("p a t -> p (a t)")
                    mm(pI[0:64, h % 2, hp * 128:hp * 128 + 128], Bt[hp][rs_, c, :], arh, True, True,
                       [Bt[hp], AR[hp]], [pI])
                m1p = cc("mask1").unsqueeze(1).unsqueeze(1).to_broadcast([64, 2, 4, 2 * C])
                tt("dve", MA[:].rearrange("p (hp par) m -> p par hp m", par=2),
                   pI[0:64, :, :].rearrange("p par (hp m) -> p par hp m", m=2 * C), m1p, ALU.mult, [pI, cst], [MA])
                mlc = ML[0]
                cp("dve", mlc[:, :, 0, :], MA[:, :, 0:C], [MA], [mlc])
                tcur = TT[0]
                tt("dve", tcur[:], MA[:, :, 0:C], eyb, ALU.add, [MA, cst], [tcur])
                yield
                for h in HORD:
                    hp, rs_ = h // 2, hrow(h)
                    mm(pI[0:64, h % 2, hp * C:(hp + 1) * C], AR[hp][rs_, c, 0, :], Bt[hp][rs_, c, :], True, True,
                       [AR[hp], Bt[hp]], [pI])
                tt("dve", mlc[:, :, 1, :].rearrange("p (hp par) s -> p par hp s", par=2),
                   pI[0:64, :, 0:4 * C].rearrange("p par (hp s) -> p par hp s", s=C),
                   cc("maskL").unsqueeze(1).unsqueeze(1).to_broadcast([64, 2, 4, C]), ALU.mult, [pI, cst], [mlc])
                yield
                for h in HORD:
                    hp, rs_ = h // 2, hrow(h)
                    arh = AR[hp][rs_, c, :, :].rearrange("p a t -> p (a t)")
                    mm(pI[0:64, h % 2, hp * 128:hp * 128 + 128], Kt[hp][rs_, c, :], arh, True, True,
                       [Kt[hp], AR[hp]], [pI])
                tt("dve", KA[:].rearrange("p (hp par) m -> p par hp m", par=2),
                   pI[0:64, :, :].rearrange("p par (hp m) -> p par hp m", m=2 * C), m1p, ALU.mult, [pI, cst], [KA])
                yield

                def squares(mlc, mln, lev):
                    for h in range(8):
                        if lev < 5:
                            mm(pI[0:64, h // 4, (h % 4) * 128:(h % 4) * 128 + C], mlc[:, h, 1, :], mlc[:, h, 0, :],
                               True, True, [mlc], [pI])
                        mm(pI[0:64, h // 4, (h % 4) * 128 + C:(h % 4) * 128 + 2 * C], mlc[:, h, 0, :],
                           mlc[:, h, 1, :], True, True, [mlc], [pI])
                    if lev < 5:
                        cp("act", mln[:].rearrange("p (a h) x s -> p a (h x s)", a=2), pI[0:64, :, :], [pI], [mln])
                    else:
                        cp("act", mln[:, :, 1, :].rearrange("p (a h) s -> p a h s", a=2),
                           pI[0:64, :, :].rearrange("p a (h x s) -> p a h x s", h=4, x=2)[:, :, :, 1, :], [pI], [mln])

                def tupdate(mln, tcur, tnew):
                    for h in range(8):
                        mm(pI3[0:64, h * C:(h + 1) * C], mln[:, h, 1, :], tcur[:, h, :], True, True, [mln, tcur], [pI3])
                    tt("dve", tnew[:], pI3[0:64, :].rearrange("p (h s) -> p h s", h=8), tcur[:], ALU.add,
                       [pI3, tcur], [tnew])

                for lev in range(1, 6):
                    mln = ML[lev % 2]
                    squares(mlc, mln, lev)
                    tnew = Q_.Tf if lev == 5 else TT[lev % 2]
                    tupdate(mln, tcur, tnew)
                    mlc, tcur = mln, tnew
                    yield
                pIb = pI[:, 0, :].bitcast(BF16)
                for hp in range(4):
                    for a in range(2):
                        tr(pIb[0:64, hp * 256 + a * 128:hp * 256 + a * 128 + 128], BKh[hp][:, c, a, :], identb[:],
                           [BKh[hp], identb], [pI])
                cp("act", Q_.BKtok[:].rearrange("p h x m -> p (h x m)"), pIb[0:64, :], [pI], [Q_.BKtok])
                yield

            def state_task(c, P_, Q_):
                AR, v_sb = P_.AR, P_.v
                MA, KA, tcur, BKtok, y_sb = Q_.MA, Q_.KA, Q_.Tf, Q_.BKtok, Q_.y
                bank = lambda h: (pSA if h % 2 == 0 else pSB)
                bank2 = lambda h: (pSA if h < 4 else pSB)
                for h in HORD:
                    hp, rs_ = h // 2, hrow(h)
                    mm(bank(h)[0:64, hp * C:(hp + 1) * C], AR[hp][rs_, c, 0, :], STb[rs_, hp, :], True, True,
                       [AR[hp], STb], [bank(h)])
                for h in range(8):
                    mm(bank2(h)[0:64, 256 + (h % 4) * C:256 + (h % 4 + 1) * C], KA[:, h, 0:C], v_sb[:, c, h * C:(h + 1) * C],
                       True, True, [KA, v_sb], [bank2(h)])
                X4 = X32[:].rearrange("p (hp par) s -> p par hp s", par=2)
                cp("act", X4[:, 0, :, :], pSA[0:64, 0:4 * C].rearrange("p (hp s) -> p hp s", s=C), [pSA], [X32])
                cp("act", X4[:, 1, :, :], pSB[0:64, 0:4 * C].rearrange("p (hp s) -> p hp s", s=C), [pSB], [X32])
                tt("dve", X_sb[:, 0:4, :], X32[:, 0:4, :], pSA[0:64, 256:512].rearrange("p (h s) -> p h s", s=C), ALU.add,
                   [X32, pSA], [X_sb])
                tt("dve", X_sb[:, 4:8, :], X32[:, 4:8, :], pSB[0:64, 256:512].rearrange("p (h s) -> p h s", s=C), ALU.add,
                   [X32, pSB], [X_sb])
                yield
                for h in range(8):
                    mm(pSA[0:64, h * C:(h + 1) * C], tcur[:, h, :], X_sb[:, h, :], True, True, [tcur, X_sb], [pSA])
                cp("dve", U_sb[:], pSA[0:64, :], [pSA], [U_sb])
                yield
                for h in HORD:
                    hp, rs_ = h // 2, hrow(h)
                    mm(bank(h)[0:64, hp * C:(hp + 1) * C], AR[hp][rs_, c, 1, :], STb[rs_, hp, :], True, True,
                       [AR[hp], STb], [bank(h)])
                for h in range(8):
                    o_ = bank2(h)[0:64, 256 + (h % 4) * C:256 + (h % 4 + 1) * C]
                    mm(o_, MA[:, h, C:2 * C], U_sb[:, h * C:(h + 1) * C], True, False, [MA, U_sb], [bank2(h)])
                    mm(o_, KA[:, h, C:2 * C], v_sb[:, c, h * C:(h + 1) * C], False, True, [KA, v_sb], [bank2(h)])
                Y4 = y_sb[:].rearrange("p (hp par s) -> p par hp s", par=2, s=C)
                cp("act", Y4[:, 0, :, :], pSA[0:64, 0:4 * C].rearrange("p (hp s) -> p hp s", s=C), [pSA], [y_sb])
                cp("act", Y4[:, 1, :, :], pSB[0:64, 0:4 * C].rearrange("p (hp s) -> p hp s", s=C), [pSB], [y_sb])
                tt("dve", y_sb[:, 0:256], y_sb[:, 0:256], pSA[0:64, 256:512], ALU.add, [y_sb, pSA], [y_sb])
                tt("dve", y_sb[:, 256:512], y_sb[:, 256:512], pSB[0:64, 256:512], ALU.add, [y_sb, pSB], [y_sb])
                yield
                pS = pSB
                for hp in range(4):
                    mm(pS[:, hp * 128:(hp + 1) * 128], BKtok[:, hp, 0, :], U_sb[:, hp * 128:(hp + 1) * 128],
                       True, False, [BKtok, U_sb], [pS])
                    mm(pS[:, hp * 128:(hp + 1) * 128], BKtok[:, hp, 1, :], v_sb[:, c, hp * 128:(hp + 1) * 128],
                       False, True, [BKtok, v_sb], [pS])
                for hp in range(4):
                    for hh in range(2):
                        rs_ = slice(hh * 64, hh * 64 + 64)
                        stt(ST[rs_, hp, :], ST[rs_, hp, :], P_.wC[rs_, hp, c:c + 1],
                            pS[rs_, hp * 128 + hh * 64: hp * 128 + hh * 64 + 64], ALU.mult, ALU.add, [ST, P_.wC, pS], [ST])
                cp("act", STb[:], ST[:], [ST], [STb])
                yield

            def ypost_task(b, g, c, P_, Q_):
                y_sb, v_sb = Q_.y, P_.v
                t0c = b * T + g * C
                y3 = y_sb[:].rearrange("p (h i) -> p h i", h=8)
                S.op("dve", lambda en: en.tensor_reduce(gst[:, 0, :], y3, AX.X, ALU.add), reads=rr(y_sb), writes=rr(gst))
                act(ysq[:], y_sb[:], AF.Square, [y_sb], [ysq])
                S.op("dve", lambda en: en.tensor_reduce(gst[:, 1, :], ysq[:].rearrange("p (h i) -> p h i", h=8),
                                                        AX.X, ALU.add), reads=rr(ysq), writes=rr(gst))
                ts("dve", gst[:, 2, :], gst[:, 0, :], 1.0 / 64, None, ALU.mult, None, [gst], [gst])
                tt("dve", gst[:, 3, :], gst[:, 2, :], gst[:, 2, :], ALU.mult, [gst], [gst])
                stt(gst[:, 4, :], gst[:, 1, :], 1.0 / 64, gst[:, 3, :], ALU.mult, ALU.subtract, [gst], [gst])
                ts("dve", gst[:, 4, :], gst[:, 4, :], 64e-5, None, ALU.add, None, [gst], [gst])
                rsqrt(gst[:, 5, :], gst[:, 4, :], [gst], [gst])
                yield
                bc = lambda a: a.unsqueeze(2).to_broadcast([64, 8, 64])
                yt3 = ytmp[:].rearrange("p (h i) -> p h i", h=8)
                tt("pool", yt3, y3, bc(gst[:, 2, :]), ALU.subtract, [y_sb, gst], [ytmp])
                tt("pool", yt3, yt3, bc(gst[:, 5, :]), ALU.mult, [ytmp, gst], [ytmp])
                tt("pool", ytmp[:], ytmp[:], gnwb[:], ALU.mult, [ytmp, gnwb], [ytmp])
                tt("pool", ytmp[:], ytmp[:], gnbb[:], ALU.add, [ytmp, gnbb], [ytmp])
                ys3 = ysq[:].rearrange("p (h i) -> p h i", h=8)
                tt("dve", ys3, v_sb[:, c, :].rearrange("p (h i) -> p h i", h=8), bc(P_.bon[:, c, :]), ALU.mult,
                   [v_sb, P_.bon], [ysq])
                tt("dve", ytmp[:], ytmp[:], ysq[:], ALU.add, [ytmp, ysq], [ytmp])
                pg = pjp
                mm(pg[0:64, :], P_.sg0[:, c * C:(c + 1) * C], wg0[:], True, False, [P_.sg0, wg0], [pg])
                mm(pg[0:64, :], P_.sg1[:, c * C:(c + 1) * C], wg1[:], False, True, [P_.sg1, wg1], [pg])
                tt("dve", ytmp[:], ytmp[:], pg[0:64, :], ALU.mult, [ytmp, pg], [ytmp])
                if b == 0:
                    dbg_dump("ya", ytmp[:], [ytmp], (slice(t0c, t0c + C), slice(None)))
                yield
                pq = pjp
                for kc in range(4):
                    tr(pq[:, kc * C:(kc + 1) * C], ytmp[:, kc * 128:(kc + 1) * 128], ident[0:64, 0:64], [ytmp, cst], [pq])
                cp("act", P_.ya[:, :, c * C:(c + 1) * C], pq[:, 0:4 * C].rearrange("p (k t) -> p k t", k=4), [pq], [P_.ya])
                if c == NCH - 1:
                    tb = b * T + (g // NCH) * NB
                    dma(yaT_d[:, :, tb:tb + NB], P_.ya[:], [P_.ya], [yaT_res], P_.ya_sem)
                yield

            def pgen_slice(pg_, j, n):
                cnt = 0
                while True:
                    if j < n - 1 and cnt >= (12 // n):
                        return
                    try:
                        next(pg_)
                    except StopIteration:
                        return
                    cnt += 1
                    yield

            def run_rr(gens):
                run_rr2(gens)

                gens = list(gens)
                while gens:
                    for g_ in list(gens):
                        try:
                            next(g_)
                        except StopIteration:
                            gens.remove(g_)

            for b in range(NSEQ):
                memset("dve", ST[:], 0.0, [ST])
                memset("dve", STb[:], 0.0, [STb])
                Gn = min(G, dbg_chunks)
                run_rr([prep_task(b, 0, psets[0])])
                for k in range(Gn + 2):
                    gens = []
                    if k < Gn:
                        gens.append(inv_task(k % NCH, psets[(k // NCH) % 3], csets[k % 2]))
                    if 1 <= k <= Gn:
                        g = k - 1
                        gens.append(state_task(g % NCH, psets[(g // NCH) % 3], csets[g % 2]))
                    if 2 <= k <= Gn + 1:
                        g = k - 2
                        gens.append(ypost_task(b, g, g % NCH, psets[(g // NCH) % 3], csets[g % 2]))
                    if k % NCH == 0:
                        nb_ = k // NCH + 1
                        pgen = prep_task(b, nb_, psets[nb_ % 3]) if nb_ * NCH < Gn else None
                    if pgen is not None:
                        gens.append(pgen_slice(pgen, k % NCH, NCH))
                    run_rr(gens)

        S.barrier()
        stop_after = int(os.environ.get("MK_STOP", "99")) if debug else 99

        if stop_after >= 2:
          with ExitStack() as es:
            WD = sb(es, "WD", (128, 8, 2560), BF16)
            WT = sb(es, "WT", (128, 8, 72), BF16)
            load_const(es, "2", _CST2, cst2_d, _C2)
            xt = [sb(es, "xt2_%d" % i, (128, D)) for i in range(2)]
            wst = [sb(es, "wst2_%d" % i, (128, WSTN)) for i in range(2)]
            for kc in range(8):
                for hf in range(2):
                    def consD(st_, kc=kc, hf=hf):
                        ts("dve", WD[:, kc, hf * 1280:(hf + 1) * 1280], st_[:, :1280], gcol[:, 0, kc:kc + 1], None,
                           ALU.mult, None, [st_, gcol], [WD])
                    load_weight(wD_d[kc * 128:(kc + 1) * 128, hf * 1280:(hf + 1) * 1280], 1280, consD)

                def consT(st_, kc=kc):
                    ts("dve", WT[:, kc, :], st_[:, :72], gcol[:, 0, kc:kc + 1], None, ALU.mult, None, [st_, gcol], [WT])
                load_weight(wT_d[kc * 128:(kc + 1) * 128, :], 72, consT)
            NB = DS_NB
            hT = sb(es, "hTd", (128, 8, NB), BF16)
            pTr = Pool([ps(es, "pTr2_%d" % i, (128, 8, 128), BF16) for i in range(1)])
            pj = Pool([ps(es, "pj2_%d" % i, (128, 512)) for i in range(3)])
            pW = Pool([ps(es, "pW2_%d" % i, (128, 2, 512)) for i in range(1)])
            po = ps(es, "po2", (128, 2, 512))
            qT = sb(es, "qT", (128, 4, NB), BF16)
            qiT = sb(es, "qiT", (128, 4, NB), BF16)
            kT_all = sb(es, "kT_all", (128, T), BF16)
            kiT_all = sb(es, "kiT_all", (128, T), BF16)
            vones = sb(es, "vones", (128, 16, 65), BF16)
            wi_sb = sb(es, "wi_sb", (128, 4, 8))
            rt1 = sb(es, "rt1", (128, NB))
            rt2 = sb(es, "rt2", (128, NB))
            acc = sb(es, "acc", (128, T))
            work = sb(es, "work", (128, T))
            relu_t = [sb(es, "relu%d" % i, (128, 512)) for i in range(2)]
            mx8 = sb(es, "mx8", (128, 8))
            maskb = sb(es, "maskb", (128, T), BF16)
            maskT = sb(es, "maskT", (128, 16, 128), BF16)
            causT = sb(es, "causT", (128, 128), BF16)
            eT = [sb(es, "eT%d" % i, (128, 8, 128), BF16) for i in range(2)]
            rcp = sb(es, "rcp", (128, 8))
            yb = sb(es, "yb", (128, 512), BF16)
            ybT = [sb(es, "ybT%d" % i, (128, 4, NB), BF16) for i in range(2)]
            ybT_sem = [S.new_dsem() for _ in range(2)]
            cp("dve", causT[:], cc("causalT"), [cst], [causT])
            memset("pool", vones[:, :, 64:65], 1.0, [vones])
            WI_SCALE = float(8 ** -0.5 * 64 ** -0.5)
            MBIG = 30000.0
            cbT = sb(es, "cbT", (128, 128), BF16)
            ts("dve", cbT[:], cc("causalT"), MBIG, -MBIG, ALU.mult, ALU.add, [cst], [cbT])
            qTs = [qT, sb(es, "qT_b", (128, 4, NB), BF16), sb(es, "qT_c", (128, 4, NB), BF16)]
            qiTs = [qiT, sb(es, "qiT_b", (128, 4, NB), BF16)]
            wis = [wi_sb, sb(es, "wi_sb_b", (128, 4, 8))]
            NBLK = T // NB
            kT_res = [Res("kT%d" % i) for i in range(NBLK)]
            kiT_res = [Res("kiT%d" % i) for i in range(NBLK)]
            von_res = [Res("von%d" % i) for i in range(NBLK)]
            maskTs = [maskT] + [sb(es, "maskT_%d" % i, (128, 16, 128), BF16) for i in range(3)]
            accs = [acc, sb(es, "acc_b", (128, T))]
            works = [work, sb(es, "work_b", (128, T))]
            mx8s = [mx8, sb(es, "mx8_b", (128, 8))]
            maskbs = [maskb, sb(es, "maskb_b", (128, T), BF16)]
            relus = [relu_t, [sb(es, "relub%d" % i, (128, 512)) for i in range(2)]]
            ybs = [yb, sb(es, "yb_b", (128, 512), BF16)]
            dbg_qt = int(os.environ.get("MK_QT", "999")) if debug else 999
            NIT = 40
            pow2 = sb(es, "pow2", (128, 2, NIT + 1))
            for k_ in range(NIT + 1):
                memset("pool", pow2[:, 0, k_:k_ + 1], float(2.0 ** -(k_ + 1)), [pow2])
                memset("pool", pow2[:, 1, k_:k_ + 1], float(-(2.0 ** -(k_ + 1))), [pow2])
            wtabs = [sb(es, "wtab%d" % i, (128, 2, NIT + 1)) for i in range(2)]

            def proj_block(b, blk, qT_, qiT, wi_sb):
                tl0 = blk * NB
                t0 = b * T + tl0
                make_hT(pTr, t0, NB // 128, hT, 0, x_d, xt=xt)
                ropeC = cc("ropeC")[:, tl0:tl0 + NB]
                ropeS = cc("ropeS")[:, tl0:tl0 + NB]

                def proj(ct):
                    p_ = pj.next()
                    for kc in range(8):
                        mm(p_[:, :NB], WD[:, kc, ct * 128:(ct + 1) * 128], hT[:, kc, :], kc == 0, kc == 7, [WD, hT], [p_])
                    return p_

                def rope(ct_a, ct_b, dst_ap, dst_tl):
                    pa = proj(ct_a)
                    tt("dve", rt1[:], pa[:, :NB], ropeC, ALU.mult, [pa, cst], [rt1])
                    pb = proj(ct_b)
                    tt("dve", rt2[:], pb[:, :NB], ropeS, ALU.mult, [pb, cst], [rt2])
                    tt("pool", dst_ap, rt1[:], rt2[:], ALU.add, [rt1, rt2], [dst_tl])

                for i in range(4):
                    rope(i, 4 + i, qT_[:, i, :], qT_)
                    yield
                rope(8, 9, kT_all[:, tl0:tl0 + NB], kT_res[blk])
                yield
                for i in range(4):
                    rope(10 + i, 14 + i, qiT[:, i, :], qiT)
                    yield
                rope(18, 19, kiT_all[:, tl0:tl0 + NB], kiT_res[blk])
                for i in range(NB // 128):
                    p_ = pj.next()
                    for kc in range(8):
                        mm(p_[:, 0:72], hT[:, kc, i * 128:(i + 1) * 128], WT[:, kc, :], kc == 0, kc == 7, [WT, hT], [p_])
                    cp("dve", vones[:, blk * 4 + i, 0:64], p_[:, 0:64], [p_], [von_res[blk]])
                    ts("dve", wi_sb[:, i, :], p_[:, 64:72], WI_SCALE, None, ALU.mult, None, [p_], [wi_sb])
                yield

            def topk_task(qt, i, mT, qiT, wi_sb):
                if qt < 2:
                    return
                acc, work, mx8, maskb, relu_t = accs[qt % 2], works[qt % 2], mx8s[qt % 2], maskbs[qt % 2], relus[qt % 2]
                Sk = (qt + 1) * 128
                tq = slice(i * 128, (i + 1) * 128)
                nseg = (Sk + 511) // 512
                for sg in range(nseg):
                    s0 = sg * 512
                    sn = min(512, Sk - s0)
                    for h in range(8):
                        rs_ = slice((h % 2) * 64, (h % 2) * 64 + 64)
                        p_ = pj.next()
                        mm(p_[:, :sn], qiT[rs_, h // 2, tq], kiT_all[rs_, s0:s0 + sn], True, True, [qiT, kiT_res[sg]], [p_])
                        rl = relu_t[h % 2]
                        act(rl[:, :sn], p_[:, :sn], AF.Relu, [p_], [rl])
                        if h == 0:
                            ts("dve", acc[:, s0:s0 + sn], rl[:, :sn], wi_sb[:, i, 0:1], None, ALU.mult, None, [rl, wi_sb], [acc])
                        else:
                            stt(acc[:, s0:s0 + sn], rl[:, :sn], wi_sb[:, i, h:h + 1], acc[:, s0:s0 + sn],
                                ALU.mult, ALU.add, [rl, wi_sb, acc], [acc])
                        yield
                tt("dve", acc[:, Sk - 128:Sk], acc[:, Sk - 128:Sk], cc("causal_bias"), ALU.add, [acc, cst], [acc])
                bs = mx8
                Wt = wtabs[qt % 2]
                S.op("dve", lambda en: en.tensor_reduce(bs[:, 0:1], acc[:, :Sk - 128], AX.X, ALU.min), reads=rr(acc), writes=rr(bs))
                S.op("dve", lambda en: en.tensor_reduce(bs[:, 5:6], acc[:, :Sk], AX.X, ALU.max), reads=rr(acc), writes=rr(bs))
                yield
                tt("dve", bs[:, 1:2], bs[:, 5:6], bs[:, 0:1], ALU.subtract, [bs], [bs])
                ts("dve", bs[:, 1:2], bs[:, 1:2], 1.0001, 1e-6, ALU.mult, ALU.add, [bs], [bs])
                ts("dve", Wt[:, 0, :], pow2[:, 0, :], bs[:, 1:2], None, ALU.mult, None, [pow2, bs], [Wt])
                ts("dve", Wt[:, 1, :], pow2[:, 1, :], bs[:, 1:2], None, ALU.mult, None, [pow2, bs], [Wt])
                stt(bs[:, 2:3], bs[:, 0:1], -1.0, Wt[:, 1, 0:1], ALU.mult, ALU.add, [bs, Wt], [bs])
                c0 = float(2 * 256 - Sk)
                for it_ in range(NIT):
                    yield
                    act(work[:, :Sk], acc[:, :Sk], AF.Sign, [acc, bs], [work, bs], bias=bs[:, 2:3], accum_out=bs[:, 3:4])
                    yield
                    ts("dve", bs[:, 4:5], bs[:, 3:4], c0, Wt[:, 1, it_:it_ + 1], ALU.is_ge, ALU.mult, [bs, Wt], [bs])
                    stt(bs[:, 2:3], bs[:, 4:5], Wt[:, 0, it_ + 1:it_ + 2], bs[:, 2:3], ALU.add, ALU.add, [bs, Wt], [bs])
                stt(bs[:, 6:7], bs[:, 2:3], Wt[:, 0, NIT:NIT + 1], bs[:, 2:3], ALU.add, ALU.bypass, [bs, Wt], [bs]) if False else None
                ts("dve", bs[:, 6:7], bs[:, 2:3], Wt[:, 0, NIT:NIT + 1], -1.0, ALU.add, ALU.mult, [bs, Wt], [bs])
                yield
                ts("dve", maskb[:, :Sk], acc[:, :Sk], bs[:, 6:7], None, ALU.is_ge, None, [acc, bs], [maskb])
                for g in range((qt + 1 + 3) // 4):
                    pm = pj.next()
                    pmb = pm[:].bitcast(BF16)
                    nk = min(4, qt + 1 - g * 4)
                    for j in range(nk):
                        kt = g * 4 + j
                        tr(pmb[:, j * 128:(j + 1) * 128], maskb[:, kt * 128:(kt + 1) * 128], identb[:], [maskb, identb], [pm])
                    act(mT[:, g * 4:g * 4 + nk, :], pmb[:, 0:nk * 128].rearrange("p (k t) -> p k t", t=128), AF.Identity,
                        [pm], [mT], bias=-MBIG, scale=MBIG)
                yield

            def attn_task(b, blk, qt, i, mT, qT_, ybt, yb_):
                tq = slice(i * 128, (i + 1) * 128)
                t0 = b * T + blk * NB
                for kt in range(qt + 1):
                    psc = pW.next()
                    need_bias = (qt >= 2) or (kt == qt)
                    brhs = mT[:, kt, :] if qt >= 2 else cbT[:]
                    bres = mT if qt >= 2 else cbT
                    for h in [0, 2, 4, 6, 1, 3, 5, 7]:
                        rs_ = slice((h % 2) * 64, (h % 2) * 64 + 64)
                        o_ = psc[:, h % 2, (h // 2) * 128:(h // 2) * 128 + 128]
                        mm(o_, kT_all[rs_, kt * 128:(kt + 1) * 128], qT_[rs_, h // 2, tq], True, not need_bias,
                           [kT_res[kt // 4], qT_], [psc])
                        if need_bias:
                            mm(o_, identb[:], brhs, False, True, [identb, bres], [psc])
                    e_ = eT[kt % 2]
                    act(e_[:].rearrange("p (a h) t -> p a (h t)", a=2), psc[:, :, :], AF.Exp, [psc], [e_], scale=0.125)
                    for h in range(8):
                        mm(po[:, h // 4, (h % 4) * 65:(h % 4) * 65 + 65], e_[:, (h % 2) * 4 + h // 2, :], vones[:, kt, :],
                           kt == 0 and h % 4 == 0, kt == qt, [e_, von_res[kt // 4], vones], [po], skip=True)
                    if kt % 2 == 1:
                        yield
                pov = po[:, :, 0:260].rearrange("p a (h e) -> p a h e", e=65)
                S.op("dve", lambda en: en.reciprocal(rcp[:].rearrange("p (a h) -> p a h", a=2), pov[:, :, :, 64]),
                     reads=rr(po), writes=rr(rcp))
                tt("dve", yb_[:].rearrange("p (a h e) -> p a h e", a=2, h=4), pov[:, :, :, 0:64],
                   rcp[:].rearrange("p (a h) -> p a h", a=2).unsqueeze(3).to_broadcast([128, 2, 4, 64]), ALU.mult,
                   [po, rcp], [yb_])
                if "yb" in dbg_d and b == 0:
                    cp("dve", rt1[:, 0:512], yb_[:], [yb_], [rt1])
                    dbg_dump("yb", rt1[:, 0:512], [rt1], (slice(t0 + i * 128, t0 + (i + 1) * 128), slice(None)))
                pm = pj.next()
                pmb = pm[:].bitcast(BF16)
                for kc in range(4):
                    tr(pmb[:, kc * 128:(kc + 1) * 128], yb_[:, kc * 128:(kc + 1) * 128], identb[:], [yb_, identb], [pm])
                cp("act", ybt[:, :, tq], pmb[:, 0:512].rearrange("p (k t) -> p k t", t=128), [pm], [ybt])
                if i == NB // 128 - 1:
                    dma(ybT_d[:, :, t0:t0 + NB], ybt[:], [ybt], [ybT_res], ybT_sem[(b * (T // NB) + blk) % 2])
                yield

            def chain(*gs):
                for g_ in gs:
                    yield from g_

            def gslice(pg_, last, nmax):
                cnt = 0
                while True:
                    if not last and cnt >= nmax:
                        return
                    try:
                        next(pg_)
                    except StopIteration:
                        return
                    cnt += 1
                    yield

            for b in range(NSEQ):
                NT = min(T // 128, dbg_qt)
                prev = []
                pgen = None
                run_rr2([proj_block(b, 0, qTs[0], qiTs[0], wis[0])])
                for p in range(NT // 2 + 1):
                    gens = []
                    cur = []
                    for j in (2 * p, 2 * p + 1):
                        if j < NT:
                            blk, i = j // 4, j % 4
                            gens.append(topk_task(j, i, maskTs[j % 4], qiTs[blk % 2], wis[blk % 2]))
                            cur.append((b, blk, j, i, maskTs[j % 4], qTs[blk % 3], ybT[(b * NBLK + blk) % 2], ybs[j % 2]))
                    if prev:
                        gens.append(chain(*[attn_task(*a_) for a_ in prev]))
                    if p % 2 == 0:
                        nb_ = p // 2 + 1
                        pgen = proj_block(b, nb_, qTs[nb_ % 3], qiTs[nb_ % 2], wis[nb_ % 2]) if nb_ * 4 < NT else None
                    if pgen is not None:
                        gens.append(gslice(pgen, p % 2 == 1, 5))
                    prev = cur
                    run_rr2(gens)
          S.barrier()

        if stop_after >= 3:
          with ExitStack() as es:
            WG = sb(es, "WG", (128, 8, 2048), BF16)
            wbr = sb(es, "wbr", (128, 8, 1024), BF16)
            wout = sb(es, "wout", (128, 8, 1024), BF16)
            wst = [sb(es, "wst3_%d" % i, (128, WSTN)) for i in range(2)]
            for kc in range(8):
                for hf in range(2):
                    def consG(st_, kc=kc, hf=hf):
                        ts("dve", WG[:, kc, hf * 1024:(hf + 1) * 1024], st_[:, :1024], gcol[:, 0, kc:kc + 1], None,
                           ALU.mult, None, [st_, gcol], [WG])
                    load_weight(wG_d[kc * 128:(kc + 1) * 128, hf * 1024:(hf + 1) * 1024], 1024, consG)

                def consB(st_, kc=kc):
                    cp("act", wbr[:, kc, :], st_[:, :1024], [st_], [wbr])
                load_weight(wbr_d[kc * 128:(kc + 1) * 128, :], 1024, consB)

                def consO(st_, kc=kc):
                    cp("dve", wout[:, kc, :], st_[:, :1024], [st_], [wout])
                load_weight(wout_d[kc * 128:(kc + 1) * 128, :], 1024, consO)
            NB = MG_NB
            hT = sb(es, "hTm", (128, 8, NB), BF16)
            xk = sb(es, "xk", (128, 4, D))
            pTr = Pool([ps(es, "pTr3_%d" % i, (128, 8, 128), BF16) for i in range(1)])
            pj = Pool([ps(es, "pj3_%d" % i, (128, 512)) for i in range(6)])
            yaL = sb(es, "yaL", (128, 4, NB), BF16)
            ybL = sb(es, "ybL", (128, 4, NB), BF16)
            yl_sem = S.new_dsem()
            yl_sem2 = S.new_dsem()
            sgA = sb(es, "sgA", (128, NB))
            sgB = sb(es, "sgB", (128, NB))
            mA = sb(es, "mA", (128, NB))
            mB = sb(es, "mB", (128, NB))
            mgT = sb(es, "mgT", (128, 8, NB), BF16)
            x1t = [sb(es, "x1t%d" % i, (128, D)) for i in range(2)]
            x1_sem = [S.new_dsem() for _ in range(2)]
            n1 = 0
            for bi in range(NTOK // NB):
                t0 = bi * NB
                make_hT(pTr, t0, 4, hT, 0, x_d, keep=xk)
                dma(yaL[:], yaT_d[:, :, t0:t0 + NB], [yaT_res], [yaL], yl_sem)
                dma(ybL[:], ybT_d[:, :, t0:t0 + NB], [ybT_res], [ybL], yl_sem2)
                for dt_ in range(8):
                    ds_ = slice(dt_ * 128, (dt_ + 1) * 128)
                    pa = pj.next()
                    for kc in range(4):
                        mm(pa[:, :NB], wbr[:, kc, ds_], yaL[:, kc, :], kc == 0, kc == 3, [wbr, yaL], [pa])
                    pb = pj.next()
                    for kc in range(4):
                        mm(pb[:, :NB], wbr[:, 4 + kc, ds_], ybL[:, kc, :], kc == 0, kc == 3, [wbr, ybL], [pb])
                    g0 = pj.next()
                    for kc in range(8):
                        mm(g0[:, :NB], WG[:, kc, dt_ * 128:(dt_ + 1) * 128], hT[:, kc, :], kc == 0, kc == 7, [WG, hT], [g0])
                    g1 = pj.next()
                    for kc in range(8):
                        mm(g1[:, :NB], WG[:, kc, 1024 + dt_ * 128:1024 + (dt_ + 1) * 128], hT[:, kc, :], kc == 0, kc == 7,
                           [WG, hT], [g1])
                    act(sgA[:], g0[:, :NB], AF.Sigmoid, [g0], [sgA])
                    act(sgB[:], g1[:, :NB], AF.Sigmoid, [g1], [sgB])
                    tt("dve", mA[:], pa[:, :NB], sgA[:], ALU.mult, [pa, sgA], [mA])
                    tt("dve", mB[:], pb[:, :NB], sgB[:], ALU.mult, [pb, sgB], [mB])
                    tt("pool", mgT[:, dt_, :], mA[:], mB[:], ALU.add, [mA, mB], [mgT])
                for tt_ in range(4):
                    x1 = x1t[n1 % 2]
                    for hf in range(2):
                        po = pj.next()
                        for kc in range(8):
                            mm(po[:, :], mgT[:, kc, tt_ * 128:(tt_ + 1) * 128], wout[:, kc, hf * 512:(hf + 1) * 512],
                               kc == 0, kc == 7, [mgT, wout], [po])
                        tt("dve", x1[:, hf * 512:(hf + 1) * 512], po[:, :], xk[:, tt_, hf * 512:(hf + 1) * 512], ALU.add,
                           [po, xk], [x1])
                    dma(x1_d[t0 + tt_ * 128:t0 + (tt_ + 1) * 128, :], x1[:], [x1], [x1_res[bi]], x1_sem[n1 % 2])
                    if t0 < T:
                        dbg_dump("x1", x1[:], [x1], (slice(t0 + tt_ * 128, t0 + (tt_ + 1) * 128), slice(None)))
                    n1 += 1
          S.barrier()

        if stop_after >= 4:
          with ExitStack() as es:
            wup = sb(es, "wup", (128, 8, FFH), BF16)
            wdn = sb(es, "wdn", (128, 32, D), BF16)
            gzb = sb(es, "gzb", (128, D))
            wst = [sb(es, "wst4_%d" % i, (128, 1024)) for i in range(2)]
            dma(gzb[:], gz_d.partition_broadcast(128), [], [gzb], d0())
            for kc in range(8):
                for q4 in range(4):
                    def consU(st_, kc=kc, q4=q4):
                        if True:
                            ts("dve", wup[:, kc, q4 * 1024:(q4 + 1) * 1024], st_[:, 0:1024], gcol[:, 1, kc:kc + 1], None,
                               ALU.mult, None, [st_, gcol], [wup])
                    load_weight(wup_d[kc * 128:(kc + 1) * 128, q4 * 1024:(q4 + 1) * 1024], 1024, consU)
            for g in range(32):
                def consDn(st_, g=g):
                    cp("act" if g % 2 else "dve", wdn[:, g, :], st_[:, 0:1024], [st_], [wdn])
                load_weight(wdn_d[g * 128:(g + 1) * 128, :], 1024, consDn)
            NB = FF_NB
            NT4 = NB // 128
            hT = sb(es, "hTf", (128, 8, NB), BF16)
            xk = sb(es, "xkf", (128, NT4, D))
            pTr = Pool([ps(es, "pTr4_%d" % i, (128, 8, 128), BF16) for i in range(1)])
            pj = Pool([ps(es, "pj4_%d" % i, (128, 512)) for i in range(6)])
            aT = sb(es, "aT", (128, 32, NB), BF16)
            rl = [sb(es, "rl%d" % i, (128, NB), BF16) for i in range(2)]
            xx = sb(es, "x2", (128, D))
            ot = [sb(es, "ot%d" % i, (128, D)) for i in range(2)]
            o_sem = [S.new_dsem() for _ in range(2)]
            st2 = [sb(es, "st2_%d" % i, (128, 4)) for i in range(2)]
            n2 = 0
            for bi in range(NTOK // NB):
                t0 = bi * NB
                make_hT(pTr, t0, NT4, hT, 0, x1_d, keep=xk, src_res=[x1_res[t0 // 512]])
                for ht in range(32):
                    pu = pj.next()
                    for kc in range(8):
                        mm(pu[:, :NB], wup[:, kc, ht * 128:(ht + 1) * 128], hT[:, kc, :], kc == 0, kc == 7, [wup, hT], [pu])
                    r_ = rl[ht % 2]
                    act(r_[:], pu[:, :NB], AF.Relu, [pu], [r_])
                    tt("pool" if ht % 2 else "dve", aT[:, ht, :], r_[:], r_[:], ALU.mult, [r_], [aT])
                for tt_ in range(NT4):
                    oo = ot[n2 % 2]
                    s2 = st2[n2 % 2]
                    for hf in range(2):
                        pd = pj.next()
                        for ht in range(32):
                            mm(pd[:, :], aT[:, ht, tt_ * 128:(tt_ + 1) * 128], wdn[:, ht, hf * 512:(hf + 1) * 512],
                               ht == 0, ht == 31, [aT, wdn], [pd])
                        tt("dve", xx[:, hf * 512:(hf + 1) * 512], pd[:, :], xk[:, tt_, hf * 512:(hf + 1) * 512], ALU.add,
                           [pd, xk], [xx])
                    act(oo[:], xx[:], AF.Square, [xx], [oo, s2], accum_out=s2[:, 0:1])
                    ts("dve", s2[:, 1:2], s2[:, 0:1], 1.0 / D, 1e-6, ALU.mult, ALU.add, [s2], [s2])
                    rsqrt(s2[:, 2:3], s2[:, 1:2], [s2], [s2])
                    stt(oo[:], xx[:], s2[:, 2:3], gzb[:], ALU.mult, ALU.mult, [xx, s2, gzb], [oo])
                    out_dmas.append(dma(out_d[t0 + tt_ * 128:t0 + (tt_ + 1) * 128, :], oo[:], [oo], [], o_sem[n2 % 2]))
                    n2 += 1

        S.finish(out_dmas)
        S.emit()
    return nc


def _swap_halves(cols):
    c = np.asarray(cols).reshape(-1, 2, 32)
    return c[:, ::-1, :].reshape(-1)


def _layout_inputs(inp):
    f = lambda a: np.ascontiguousarray(np.asarray(a, dtype=np.float32))
    w_in = f(inp["w_in"])[0]
    mu = f(inp["mu_shift"])[0]
    colsA = np.concatenate([np.arange(0, 1024), np.arange(1536, 1824)])
    colsV = np.arange(1024, 1536)
    base = 1824
    q = base + np.arange(512)
    k = base + 512 + np.arange(64)
    v = base + 576 + np.arange(64)
    qi = base + 640 + np.arange(512)
    ki = base + 1152 + np.arange(64)
    wi = base + 1216 + np.arange(8)
    colsD = np.concatenate([q, _swap_halves(q), k, k, _swap_halves(k), _swap_halves(k),
                            qi, _swap_halves(qi), ki, ki, _swap_halves(ki), _swap_halves(ki)])
    colsT = np.concatenate([v, wi])
    colsG = 1824 + 1224 + np.arange(2048)
    per_ch = lambda a: f(a)[0].reshape(4, 128).T
    pp = np.concatenate([per_ch(inp["decay_bias"]), per_ch(inp["iclr_bias"]), per_ch(inp["k_k"]),
                         per_ch(inp["k_a"]), per_ch(inp["r_k"])], axis=1)
    shared = {
        "wA": f(w_in[:, colsA]), "wV": f(w_in[:, colsV]), "muA": f(mu[colsA]), "muV": f(mu[colsV]),
        "wD": f(w_in[:, colsD]), "wT": f(w_in[:, colsT]), "wG": f(w_in[:, colsG]),
        "gcols": f(np.concatenate([f(inp["g_mix"])[0].reshape(8, 128).T, f(inp["g_ffn"])[0].reshape(8, 128).T], axis=1)),
        "gfin": f(inp["g_final"]),
        "wlora": f(np.concatenate([f(inp["w_decay_up"])[0], f(inp["w_iclr_up"])[0]], axis=0)),
        "wgate": f(inp["w_gate_up"])[0], "pp": f(pp),
        "gnw": f(inp["gn_w"])[0], "gnb": f(inp["gn_b"])[0],
        "wbr": f(f(inp["w_branch"])[0].reshape(1024, 1024)), "wout": f(inp["w_out"])[0],
        "wup": f(inp["w_ffn_up"])[0], "wdn": f(inp["w_ffn_down"])[0], "cstA": _CSTA, "cst1": _CST1, "cst2": _CST2,
    }
    x = f(inp["x"])
    maps = []
    for c in range(NCORES):
        m = dict(shared)
        m["x"] = np.ascontiguousarray(x[c * NSEQ:(c + 1) * NSEQ].reshape(NTOK, D))
        maps.append(m)
    return maps


def kernel(**inputs):
    maps = _layout_inputs(inputs)
    nc = build_nc()
    res = run_bass_kernel_spmd(nc, maps, core_ids=list(range(NCORES)))
    outs = [np.asarray(r["out"], dtype=np.float32).reshape(NSEQ, T, D) for r in res.results]
    return np.concatenate(outs, axis=0)
```

```python
import os
from contextlib import ExitStack

import numpy as np
import concourse.bass as bass
import concourse.mybir as mybir
from concourse.bass_utils import run_bass_kernel_spmd

F32 = mybir.dt.float32
BF16 = mybir.dt.bfloat16
ALU = mybir.AluOpType
AF = mybir.ActivationFunctionType
AX = mybir.AxisListType

NCORES = 8
T = 2048
D = 1024
NSEQ = 2
NTOK = NSEQ * T
C = 64
C0 = float(np.exp(-0.5))
NEG = -1.0e30
RW_NB = 128
DS_NB = 512
MG_NB = 512
FF_NB = 256
FFH = 4096


class Res:
    __slots__ = ("name", "w", "rd", "rd_dma")

    def __init__(self, name):
        self.name = name
        self.w = None
        self.rd = {}
        self.rd_dma = []


class DmaSem:
    def __init__(self, sem):
        self.sem = sem
        self.count = 0


class _Op:
    __slots__ = ("id", "eng", "fn", "deps", "dsem", "val", "signal")


class Sched:
    ENGS = ("pe", "act", "dve", "pool", "sp")

    def __init__(self, nc, es):
        self.nc = nc
        self.es = es
        self.ops = []
        self.per = {e: [] for e in self.ENGS}
        self.sem = {e: es.enter_context(nc.semaphore("s_" + e)) for e in self.ENGS}
        self.n_dsem = 0
        self.last = {e: None for e in self.ENGS}
        self.dma_since_barrier = []

    def new_dsem(self):
        self.n_dsem += 1
        return DmaSem(self.es.enter_context(self.nc.semaphore("d%d" % self.n_dsem)))

    def op(self, eng, fn, reads=(), writes=(), dsem=None):
        o = _Op()
        o.id = len(self.ops)
        o.eng = eng
        o.fn = fn
        o.dsem = dsem
        o.signal = False
        o.val = None
        deps = {}

        def add(d, kind):
            if d is None:
                return
            if kind == "raw" or d not in deps:
                deps[d] = kind

        for r in reads:
            add(r.w, "raw")
        for w in writes:
            add(w.w, "waw")
            for d in w.rd.values():
                add(d, "war")
            for d in w.rd_dma:
                add(d, "war")
        o.deps = deps
        for r in reads:
            if dsem is not None:
                r.rd_dma.append(o.id)
            else:
                r.rd[eng] = o.id
        for w in writes:
            w.w = o.id
            w.rd = {}
            w.rd_dma = []
        if dsem is not None:
            dsem.count += 16
            o.val = dsem.count
            self.dma_since_barrier.append(o.id)
        self.ops.append(o)
        self.per[eng].append(o)
        self.last[eng] = o.id
        return o

    def barrier(self):
        lasts = [v for v in self.last.values() if v is not None]
        dmas = list(self.dma_since_barrier)
        self.dma_since_barrier = []
        for e in self.ENGS:
            o = self.op(e, lambda en: en.nop())
            for d in lasts + dmas:
                if d != o.id:
                    o.deps[d] = "raw"

    def finish(self, dma_ops):
        o = self.op("sp", lambda en: en.nop())
        for d in dma_ops:
            o.deps[d.id] = "raw"

    def emit(self):
        ops = self.ops
        for o in ops:
            for d, kind in o.deps.items():
                p = ops[d]
                if p.dsem is not None:
                    continue
                if p.eng == o.eng and o.dsem is None and o.eng in ("pe", "sp"):
                    continue
                p.signal = True
        cnt = {e: 0 for e in self.ENGS}
        for o in ops:
            if o.dsem is None and o.signal:
                cnt[o.eng] += 1
                o.val = cnt[o.eng]
        sem = self.sem

        def run(eng, en):
            known = {}
            for o in self.per[eng]:
                need = {}
                for d, kind in o.deps.items():
                    p = ops[d]
                    if p.dsem is not None:
                        key, s, v = ("d", id(p.dsem)), p.dsem.sem, p.val
                    else:
                        if not p.signal:
                            continue
                        if p.eng == eng and o.dsem is None and eng in ("pe", "sp"):
                            continue
                        key, s, v = ("e", p.eng), sem[p.eng], p.val
                    if known.get(key, 0) >= v:
                        continue
                    if key not in need or need[key][1] < v:
                        need[key] = (s, v)
                for key, (s, v) in need.items():
                    en.wait_ge(s, v)
                    known[key] = v
                ins = o.fn(en)
                if o.dsem is not None:
                    ins.then_inc(o.dsem.sem, 16)
                elif o.signal:
                    ins.then_inc(sem[eng], 1)

        with self.nc.Block() as block:
            @block.tensor
            def _(en):
                run("pe", en)

            @block.scalar
            def _(en):
                run("act", en)

            @block.vector
            def _(en):
                run("dve", en)

            @block.gpsimd
            def _(en):
                run("pool", en)

            @block.sync
            def _(en):
                run("sp", en)


class Tl:
    def __init__(self, h, name, nres=1):
        self.h = h
        self.name = name
        self.rs = [Res("%s.%d" % (name, i)) for i in range(nres)]

    @property
    def r(self):
        return self.rs[0]

    def __getitem__(self, k):
        return self.h[k]


class Pool:
    def __init__(self, tiles):
        self.tiles = tiles
        self.i = 0

    def next(self):
        t = self.tiles[self.i % len(self.tiles)]
        self.i += 1
        return t


class _CB:
    def __init__(self):
        self.cols = {}
        self.parts = []
        self.off = 0

    def put(self, name, arr):
        a = np.zeros((128, arr.shape[1]), np.float32)
        a[: arr.shape[0]] = arr
        self.cols[name] = (self.off, arr.shape[1], arr.shape[0])
        self.parts.append(a)
        self.off += arr.shape[1]

    def arr(self):
        return np.ascontiguousarray(np.concatenate(self.parts, axis=1))


def _const_f32():
    A, B1, B2 = _CB(), _CB(), _CB()
    A.put("ident", np.eye(128, dtype=np.float32))
    s = np.arange(64)[:, None]
    t = np.arange(64)[None, :]
    m1 = np.concatenate([(s < t), (s <= t)], axis=1).astype(np.float32)
    B1.put("mask1", m1)
    B1.put("maskL", (s > t).astype(np.float32))
    B1.put("eye8", np.eye(64, dtype=np.float32))
    rm = np.ones((128, RW_NB), np.float32)
    rm[:, ::C] = 0.0
    B1.put("reset", rm)
    bo = np.zeros((128, 128), np.float32)
    bo[:64, :64] = 1.0
    bo[64:, 64:] = 1.0
    A.put("blockones", bo)
    hi = np.zeros((128, 2), np.float32)
    hi[:64, 0] = 1.0
    hi[64:, 1] = 1.0
    A.put("headind", hi)
    tq = np.arange(128)[:, None]
    kk = np.arange(128)[None, :]
    A.put("causal_bias", np.where(kk <= tq, 0.0, NEG).astype(np.float32))
    A.put("causalT", (tq <= kk).astype(np.float32))
    inv = (1.0 / (10000.0 ** (np.arange(0, 64, 2, dtype=np.float32) / np.float32(64)))).astype(np.float32)
    ang = (np.arange(T, dtype=np.float32)[:, None] * inv[None, :]).astype(np.float32)
    cs = np.cos(ang).astype(np.float32).T
    sn = np.sin(ang).astype(np.float32).T
    d = np.arange(128) % 64
    B2.put("ropeC", cs[d % 32])
    sg = np.where(d < 32, -1.0, 1.0).astype(np.float32)[:, None]
    B2.put("ropeS", sn[d % 32] * sg)
    return A, B1, B2


_CA, _C1, _C2 = _const_f32()
_CSTA, _CST1, _CST2 = _CA.arr(), _C1.arr(), _C2.arr()


def build_nc(debug=None):
    nc = bass.Bass("TRN2", target_bir_lowering=False)
    dt_in = lambda name, shape: nc.dram_tensor(name, list(shape), F32, kind="ExternalInput").ap()
    x_d = dt_in("x", (NTOK, D))
    wA_d = dt_in("wA", (D, 1312))
    wV_d = dt_in("wV", (D, 512))
    muA_d = dt_in("muA", (1312,))
    muV_d = dt_in("muV", (512,))
    wD_d = dt_in("wD", (D, 2560))
    wT_d = dt_in("wT", (D, 72))
    wG_d = dt_in("wG", (D, 2048))
    gcol_d = dt_in("gcols", (128, 16))
    gz_d = dt_in("gfin", (D,))
    wlora_d = dt_in("wlora", (128, 512))
    wgate_d = dt_in("wgate", (160, 512))
    pp_d = dt_in("pp", (128, 20))
    gnw_d = dt_in("gnw", (512,))
    gnb_d = dt_in("gnb", (512,))
    wbr_d = dt_in("wbr", (1024, 1024))
    wout_d = dt_in("wout", (D, D))
    wup_d = dt_in("wup", (D, FFH))
    wdn_d = dt_in("wdn", (FFH, D))
    cstA_d = dt_in("cstA", _CSTA.shape)
    cst1_d = dt_in("cst1", _CST1.shape)
    cst2_d = dt_in("cst2", _CST2.shape)
    out_d = nc.dram_tensor("out", [NTOK, D], F32, kind="ExternalOutput").ap()
    yaT_d = nc.dram_tensor("yaT_scr", [128, 4, NTOK], BF16, kind="Internal").ap()
    ybT_d = nc.dram_tensor("ybT_scr", [128, 4, NTOK], BF16, kind="Internal").ap()
    x1_d = nc.dram_tensor("x1_scr", [NTOK, D], F32, kind="Internal").ap()
    dbg_d = {}
    dbg_sem = {}
    if debug:
        for name, shape in debug.items():
            dbg_d[name] = nc.dram_tensor("dbg_" + name, list(shape), F32, kind="ExternalOutput").ap()

    top = ExitStack()
    with top:
        S = Sched(nc, top)
        out_dmas = []
        yaT_res = Res("yaT_scr")
        ybT_res = Res("ybT_scr")
        x1_res = [Res("x1_scr%d" % i) for i in range(NTOK // 512)]

        uid = [0]

        def sb(es, name, shape, dt=F32, nres=1):
            uid[0] += 1
            return Tl(es.enter_context(nc.sbuf_tensor("sb%d_%s" % (uid[0], name), list(shape), dt)), name, nres)

        def ps(es, name, shape, dt=F32):
            uid[0] += 1
            return Tl(es.enter_context(nc.psum_tensor("ps%d_%s" % (uid[0], name), list(shape), dt)), name)

        def rr(*xs):
            out = []
            for x in xs:
                if isinstance(x, Tl):
                    out.extend(x.rs)
                elif isinstance(x, Res):
                    out.append(x)
                else:
                    out.extend(x)
            return out

        def dma(out_ap, in_ap, reads, writes, dsem):
            return S.op("sp", lambda en: en.dma_start(out=out_ap, in_=in_ap),
                        reads=rr(*reads), writes=rr(*writes), dsem=dsem)

        def mm(out_ap, lhsT, rhs, start, stop, reads, writes, skip=False):
            return S.op("pe", lambda en: en.matmul(out_ap, lhsT, rhs, start=start, stop=stop, skip_group_check=skip),
                        reads=rr(*reads), writes=rr(*writes))

        def tr(out_ap, in_ap, ident, reads, writes):
            return S.op("pe", lambda en: en.transpose(out_ap, in_ap, ident),
                        reads=rr(*reads), writes=rr(*writes))

        def act(out_ap, in_ap, func, reads, writes, bias=0.0, scale=1.0, accum_out=None):
            return S.op("act", lambda en: en.activation(out_ap, in_ap, func, bias=bias, scale=scale,
                                                        accum_out=accum_out),
                        reads=rr(*reads), writes=rr(*writes))

        def tt(eng, out_ap, a, b, op, reads, writes):
            return S.op(eng, lambda en: en.tensor_tensor(out_ap, a, b, op), reads=rr(*reads), writes=rr(*writes))

        def ts(eng, out_ap, a, s1, s2, op0, op1, reads, writes):
            if op1 is None:
                return S.op(eng, lambda en: en.tensor_scalar(out_ap, a, s1, None, op0),
                            reads=rr(*reads), writes=rr(*writes))
            return S.op(eng, lambda en: en.tensor_scalar(out_ap, a, s1, s2, op0, op1),
                        reads=rr(*reads), writes=rr(*writes))

        def stt(out_ap, a, sc, b, op0, op1, reads, writes):
            return S.op("dve", lambda en: en.scalar_tensor_tensor(out_ap, a, sc, b, op0, op1),
                        reads=rr(*reads), writes=rr(*writes))

        def rsqrt(out_ap, in_ap, reads, writes):
            act(out_ap, in_ap, AF.Sqrt, reads, writes)
            S.op("dve", lambda en: en.reciprocal(out_ap, out_ap), reads=rr(*writes), writes=rr(*writes))

        def cp(eng, out_ap, in_ap, reads, writes):
            if eng == "act":
                return S.op("act", lambda en: en.copy(out_ap, in_ap), reads=rr(*reads), writes=rr(*writes))
            return S.op(eng, lambda en: en.tensor_copy(out_ap, in_ap), reads=rr(*reads), writes=rr(*writes))

        def memset(eng, ap, val, writes):
            return S.op(eng, lambda en: en.memset(ap, val), writes=rr(*writes))

        def dbg_dump(name, src_ap, reads, dst_slice=None):
            if name not in dbg_d:
                return
            dst = dbg_d[name] if dst_slice is None else dbg_d[name][dst_slice]
            if name not in dbg_sem:
                dbg_sem[name] = S.new_dsem()
            out_dmas.append(dma(dst, src_ap, reads, [], dbg_sem[name]))

        def run_rr2(gens):
            gens = list(gens)
            while gens:
                for g_ in list(gens):
                    try:
                        next(g_)
                    except StopIteration:
                        gens.remove(g_)

        cst = Tl(None, "cstgroup", 0)
        ctiles = {}

        def load_const(es_, key, arr, src_d, cb):
            t_ = sb(es_, "cs_sb" + key, arr.shape)
            dma(t_[:], src_d, [], [t_], S.new_dsem())
            cst.rs.extend(t_.rs)
            for nm in cb.cols:
                ctiles[nm] = (t_, cb.cols[nm])

        load_const(top, "A", _CSTA, cstA_d, _CA)

        def cc(name, rows=None):
            t_, (o, n, r0) = ctiles[name]
            return t_[: (rows or r0), o:o + n]

        def d0():
            return S.new_dsem()

        nhalf = sb(top, "nhalf", (128, 256))
        memset("pool", nhalf[:], -0.5, [nhalf])
        identb = sb(top, "identb", (128, 128), BF16)
        cp("dve", identb[:], cc("ident"), [cst], [identb])
        gcol = sb(top, "gcol", (128, 2, 8))
        dma(gcol[:].rearrange("p a k -> p (a k)"), gcol_d, [], [gcol], d0())
        pp = sb(top, "pp", (128, 20))
        dma(pp[:], pp_d, [], [pp], d0())

        WSTN = 1312
        wst = None
        wst_sem = [S.new_dsem() for _ in range(4)]
        wst_i = [0]

        def load_weight(src_ap, ncols, consume):
            i = wst_i[0] % len(wst)
            wst_i[0] += 1
            dma(wst[i][:, :ncols], src_ap, [], [wst[i]], wst_sem[i])
            consume(wst[i])

        xt_sem = [S.new_dsem() for _ in range(2)]
        xt_i = [0]
        xs_bf = [sb(top, "xsbf%d" % i, (128, D), BF16) for i in range(2)]
        stat = [sb(top, "stat%d" % i, (128, 4)) for i in range(2)]

        def make_hT(es_ps, tok0, ntile, hT, col0, src_d, xt=None, keep=None, src_res=()):
            for i in range(ntile):
                k = xt_i[0] % 2
                xt_i[0] += 1
                if keep is not None:
                    xin = keep
                    xap = keep[:, i, :]
                    dma(xap, src_d[tok0 + i * 128: tok0 + (i + 1) * 128, :], src_res, [keep], xt_sem[k])
                else:
                    xin = xt[k % len(xt)]
                    xap = xin[:]
                    dma(xap, src_d[tok0 + i * 128: tok0 + (i + 1) * 128, :], src_res, [xin], xt_sem[k % len(xt)])
                st = stat[k]
                act(xs_bf[k][:], xap, AF.Square, [xin], [xs_bf[k], st], accum_out=st[:, 0:1])
                ts("dve", st[:, 1:2], st[:, 0:1], 1.0 / D, 1e-6, ALU.mult, ALU.add, [st], [st])
                rsqrt(st[:, 2:3], st[:, 1:2], [st], [st])
                ts("dve", xs_bf[k][:], xap, st[:, 2:3], None, ALU.mult, None, [xin, st], [xs_bf[k]])
                pt = es_ps.next()
                for kc in range(8):
                    tr(pt[:, kc, :], xs_bf[k][:, kc * 128:(kc + 1) * 128], identb[:], [xs_bf[k], identb], [pt])
                cp("act" if i % 2 else "dve", hT[:, :, col0 + i * 128: col0 + (i + 1) * 128], pt[:, :, :], [pt], [hT])

        with ExitStack() as es:
            W1A = sb(es, "W1A", (128, 8, 1312), BF16)
            W2A = sb(es, "W2A", (128, 8, 1312), BF16)
            W1V = sb(es, "W1V", (128, 8, 512), BF16)
            W2V = sb(es, "W2V", (128, 8, 512), BF16)
            with ExitStack() as es_w:
                wst = [sb(es_w, "wst1_%d" % i, (128, WSTN)) for i in range(4)]
                mub = sb(es_w, "mub", (128, 1824))
                omb = sb(es_w, "omb", (128, 1824))
                dma(mub[:, 0:1312], muA_d.partition_broadcast(128), [], [mub], d0())
                dma(mub[:, 1312:1824], muV_d.partition_broadcast(128), [], [mub], d0())
                ts("pool", omb[:], mub[:], -1.0, 1.0, ALU.mult, ALU.add, [mub], [omb])
                for kc in range(8):
                    def consA(st_, kc=kc):
                        stt(W1A[:, kc, :], st_[:, :1312], gcol[:, 0, kc:kc + 1], omb[:, 0:1312], ALU.mult, ALU.mult,
                            [st_, gcol, omb], [W1A])
                        stt(W2A[:, kc, :], st_[:, :1312], gcol[:, 0, kc:kc + 1], mub[:, 0:1312], ALU.mult, ALU.mult,
                            [st_, gcol, mub], [W2A])
                    load_weight(wA_d[kc * 128:(kc + 1) * 128, :], 1312, consA)

                    def consV(st_, kc=kc):
                        stt(W1V[:, kc, :], st_[:, :512], gcol[:, 0, kc:kc + 1], omb[:, 1312:1824], ALU.mult, ALU.mult,
                            [st_, gcol, omb], [W1V])
                        stt(W2V[:, kc, :], st_[:, :512], gcol[:, 0, kc:kc + 1], mub[:, 1312:1824], ALU.mult, ALU.mult,
                            [st_, gcol, mub], [W2V])
                    load_weight(wV_d[kc * 128:(kc + 1) * 128, :], 512, consV)
            S.barrier()
            load_const(es, "1", _CST1, cst1_d, _C1)
            xt = [sb(es, "xt%d" % i, (128, D)) for i in range(1)]
            wlora = sb(es, "wlora", (128, 512))
            wg0f = sb(es, "wg0f", (128, 512))
            wg1f = sb(es, "wg1f", (32, 512))
            wg0 = sb(es, "wg0", (128, 512), BF16)
            wg1 = sb(es, "wg1", (32, 512), BF16)
            gnwb = sb(es, "gnwb", (64, 512))
            gnbb = sb(es, "gnbb", (64, 512))
            dma(wlora[:], wlora_d, [], [wlora], d0())
            dma(wg0f[:], wgate_d[0:128, :], [], [wg0f], d0())
            dma(wg1f[:], wgate_d[128:160, :], [], [wg1f], d0())
            cp("dve", wg0[:], wg0f[:], [wg0f], [wg0])
            cp("dve", wg1[:], wg1f[:], [wg1f], [wg1])
            dma(gnwb[:], gnw_d.partition_broadcast(64), [], [gnwb], d0())
            dma(gnbb[:], gnb_d.partition_broadcast(64), [], [gnbb], d0())

            NB = RW_NB
            NCH = NB // C
            G = T // C
            hT = sb(es, "hT", (128, 8, NB + 2), BF16)
            pTr = Pool([ps(es, "pTr", (128, 8, 128), BF16)])
            pjp = ps(es, "pjp", (128, 512))
            pbon = ps(es, "pbon", (128, 512))
            pI = ps(es, "pI", (128, 2, 512))
            pI3 = ps(es, "pI3", (128, 512))
            pSA = ps(es, "pSA", (128, 512))
            pSB = ps(es, "pSB", (128, 512))
            r_sb = sb(es, "r_sb", (128, 4, NB))
            k_sb = sb(es, "k_sb", (128, 4, NB))
            wa_sb = sb(es, "wa_sb", (128, NB))
            tmps = [[sb(es, "rt%d_%d" % (i, q), (128, NB)) for i in range(10)] for q in range(4)]

            class PSet:
                pass
            psets = []
            for i in range(3):
                P_ = PSet()
                P_.sg0 = sb(es, "sg0_%d" % i, (128, NB), BF16)
                P_.sg1 = sb(es, "sg1_%d" % i, (32, NB), BF16)
                P_.v = sb(es, "v_sb%d" % i, (64, NCH, 512), BF16)
                P_.AR = [sb(es, "AR%d_%d" % (h, i), (128, NCH, 2, C), BF16) for h in range(4)]
                P_.Bt = [sb(es, "Bt%d_%d" % (h, i), (128, NCH, C), BF16) for h in range(4)]
                P_.Kt = [sb(es, "Kt%d_%d" % (h, i), (128, NCH, C), BF16) for h in range(4)]
                P_.BKh = [sb(es, "BKh%d_%d" % (h, i), (128, NCH, 2, C), BF16) for h in range(4)]
                P_.wC = sb(es, "wC%d" % i, (128, 4, NCH))
                P_.bon = sb(es, "bon%d" % i, (64, NCH, 8))
                P_.ya = sb(es, "yaT%d" % i, (128, 4, NB), BF16)
                P_.ya_sem = S.new_dsem()
                psets.append(P_)
            csets = []
            for i in range(2):
                Q_ = PSet()
                Q_.MA = sb(es, "MA%d" % i, (64, 8, 2 * C), BF16)
                Q_.KA = sb(es, "KA%d" % i, (64, 8, 2 * C), BF16)
                Q_.Tf = sb(es, "Tf%d" % i, (64, 8, C), BF16)
                Q_.BKtok = sb(es, "BKtok%d" % i, (64, 4, 2, 128), BF16)
                Q_.y = sb(es, "y_sb%d" % i, (64, 512))
                csets.append(Q_)
            ML = [sb(es, "ML%d" % i, (64, 8, 2, C), BF16) for i in range(2)]
            TT = [sb(es, "TT%d" % i, (64, 8, C), BF16) for i in range(2)]
            ST = sb(es, "ST", (128, 4, C))
            X_sb = sb(es, "X_sb", (64, 8, C), BF16)
            X32 = sb(es, "X32", (64, 8, C))
            STb = sb(es, "STb", (128, 4, C), BF16)
            U_sb = sb(es, "U_sb", (64, 512), BF16)
            ysq = sb(es, "ysq", (64, 512))
            ytmp = sb(es, "ytmp", (64, 512))
            gst = sb(es, "gst", (64, 6, 8))
            ident = cc("ident")
            m1b = cc("mask1").unsqueeze(1).to_broadcast([64, 8, 2 * C])
            mLb = cc("maskL").unsqueeze(1).to_broadcast([64, 8, C])
            eyb = cc("eye8").unsqueeze(1).to_broadcast([64, 8, C])
            hrow = lambda h: slice((h % 2) * 64, (h % 2) * 64 + 64)
            HORD = [0, 2, 4, 6, 1, 3, 5, 7]
            dbg_chunks = int(os.environ.get("MK_CHUNKS", "999")) if debug else 999

            def prep_task(b, blk, P_):
                t0 = b * T + blk * NB
                if blk == 0:
                    memset("pool", hT[:, :, 0:2], 0.0, [hT])
                else:
                    cp("dve", hT[:, :, 1:2], hT[:, :, NB + 1:NB + 2], [hT], [hT])
                make_hT(pTr, t0, NB // 128, hT, 2, x_d, xt=xt)
                yield
                for ct in range(11):
                    rows = 32 if ct == 10 else 128
                    c0 = ct * 128
                    p_ = pjp
                    for kc in range(8):
                        mm(p_[:rows, :NB], W1A[:, kc, c0:c0 + rows], hT[:, kc, 2:NB + 2], kc == 0, False, [W1A, hT], [p_])
                        mm(p_[:rows, :NB], W2A[:, kc, c0:c0 + rows], hT[:, kc, 1:NB + 1], False, kc == 7, [W2A, hT], [p_])
                    if ct < 4:
                        cp("act", r_sb[:, ct, :], p_[:, :NB], [p_], [r_sb])
                    elif ct < 8:
                        cp("dve", k_sb[:, ct - 4, :], p_[:, :NB], [p_], [k_sb])
                    elif ct == 8:
                        act(wa_sb[0:64, :], p_[0:64, :NB], AF.Tanh, [p_], [wa_sb])
                        cp("dve", wa_sb[64:128, :], p_[64:128, :NB], [p_], [wa_sb])
                    elif ct == 9:
                        act(P_.sg0[:], p_[:, :NB], AF.Sigmoid, [p_], [P_.sg0])
                    else:
                        act(P_.sg1[:], p_[0:32, :NB], AF.Sigmoid, [p_], [P_.sg1])
                    if ct % 3 == 2:
                        yield
                for c in range(NCH):
                    p_ = pjp
                    for kc in range(8):
                        mm(p_[0:64, :], hT[:, kc, 2 + c * C:2 + (c + 1) * C], W1V[:, kc, :], kc == 0, False, [W1V, hT], [p_])
                        mm(p_[0:64, :], hT[:, kc, 1 + c * C:1 + (c + 1) * C], W2V[:, kc, :], False, kc == 7, [W2V, hT], [p_])
                    cp("act", P_.v[:, c, :], p_[0:64, :], [p_], [P_.v])
                yield
                v4 = lambda a: a[:].rearrange("p (c t) -> p c t", t=C)

                def hp_task(hp, tmp):
                    cs_ = slice(hp * 128, (hp + 1) * 128)
                    ppc = lambda j, hp=hp: pp[:, j * 4 + hp: j * 4 + hp + 1]
                    sgd, icl, cum, e_in, e_ng, e_ex, e_rm, kkn, kmod, t9 = tmp
                    AR, Bt, Kt, BKh = P_.AR, P_.Bt, P_.Kt, P_.BKh
                    p_ = pjp
                    mm(p_[:, :NB], wlora[0:64, cs_], wa_sb[0:64, :], True, True, [wlora, wa_sb], [p_])
                    act(sgd[:], p_[:, :NB], AF.Sigmoid, [p_, pp], [sgd], bias=ppc(0))
                    p_ = pbon
                    mm(p_[:, 256:256 + NB], wlora[64:128, cs_], wa_sb[64:128, :], True, True, [wlora, wa_sb], [p_])
                    act(icl[:], p_[:, 256:256 + NB], AF.Sigmoid, [p_, pp], [icl], bias=ppc(1))
                    S.op("dve", lambda en, cum=cum, sgd=sgd: en.tensor_tensor_scan(
                        cum[:], cc("reset"), sgd[:], 0.0, ALU.mult, ALU.add), reads=rr(cst, sgd), writes=rr(cum))
                    yield
                    act(e_in[:], cum[:], AF.Exp, [cum], [e_in], scale=-C0)
                    act(e_ng[:], cum[:], AF.Exp, [cum], [e_ng], scale=C0)
                    tt("dve", t9[:], cum[:], sgd[:], ALU.subtract, [cum, sgd], [t9])
                    act(e_ex[:], t9[:], AF.Exp, [t9], [e_ex], scale=-C0)
                    cum3 = cum[:].rearrange("p (c t) -> p c t", t=C)
                    tt("dve", t9[:].rearrange("p (c t) -> p c t", t=C),
                       cum3[:, :, C - 1:C].to_broadcast([128, NCH, C]), cum3, ALU.subtract, [cum], [t9])
                    act(e_rm[:], t9[:], AF.Exp, [t9], [e_rm], scale=-C0)
                    cp("dve", P_.wC[:, hp, :], e_in[:].rearrange("p (c t) -> p c t", t=C)[:, :, C - 1], [e_in], [P_.wC])
                    kx = k_sb[:, hp, :]
                    ts("dve", kkn[:], kx, ppc(2), None, ALU.mult, None, [k_sb, pp], [kkn])
                    tt("dve", t9[:], kkn[:], kkn[:], ALU.mult, [kkn], [t9])
                    p_ = pjp
                    mm(p_[:, :NB], cc("blockones"), t9[:], True, True, [cst, t9], [p_])
                    ts("dve", t9[:], p_[:, :NB], 1e-24, None, ALU.max, None, [p_], [t9])
                    yield
                    rsqrt(t9[:], t9[:], [t9], [t9])
                    tt("dve", kkn[:], kkn[:], t9[:], ALU.mult, [kkn, t9], [kkn])
                    ts("dve", t9[:], icl[:], -1.0, ppc(3), ALU.add, ALU.mult, [icl, pp], [t9])
                    stt(kmod[:], t9[:], 1.0, kx, ALU.add, ALU.mult, [t9, k_sb], [kmod])
                    tt("dve", icl[:], icl[:], kkn[:], ALU.mult, [icl, kkn], [icl])
                    stt(AR[hp][:, :, 0, :], v4(kkn), -1.0, v4(e_ex), ALU.mult, ALU.mult, [kkn, e_ex], [AR[hp]])
                    tt("dve", AR[hp][:, :, 1, :], r_sb[:, hp, :].rearrange("p (c t) -> p c t", t=C), v4(e_in),
                       ALU.mult, [r_sb, e_in], [AR[hp]])
                    yield
                    tt("dve", Bt[hp][:], v4(icl), v4(e_ng), ALU.mult, [icl, e_ng], [Bt[hp]])
                    tt("dve", Kt[hp][:], v4(kmod), v4(e_ng), ALU.mult, [kmod, e_ng], [Kt[hp]])
                    tt("dve", BKh[hp][:, :, 0, :], v4(icl), v4(e_rm), ALU.mult, [icl, e_rm], [BKh[hp]])
                    tt("dve", BKh[hp][:, :, 1, :], v4(kmod), v4(e_rm), ALU.mult, [kmod, e_rm], [BKh[hp]])
                    stt(t9[:], r_sb[:, hp, :], ppc(4), kmod[:], ALU.mult, ALU.mult, [r_sb, pp, kmod], [t9])
                    for c in range(NCH):
                        mm(pbon[0:64, c * 8 + hp * 2: c * 8 + hp * 2 + 2], t9[:, c * C:(c + 1) * C],
                           cc("headind"), True, True, [t9, cst], [pbon])
                    yield

                subs = [hp_task(hp, tmps[hp]) for hp in range(4)]
                while subs:
                    for g_ in list(subs):
                        try:
                            next(g_)
                        except StopIteration:
                            subs.remove(g_)
                    yield
                cp("act", P_.bon[:].rearrange("p c h -> p (c h)"), pbon[0:64, 0:NCH * 8], [pbon], [P_.bon])

            def inv_task(c, P_, Q_):
                AR, Bt, Kt, BKh = P_.AR, P_.Bt, P_.Kt, P_.BKh
                MA, KA = Q_.MA, Q_.KA
                for h in HORD:
                    hp, rs_ = h // 2, hrow(h)
                    arh = AR[hp][rs_, c, :, :].rearrange("p a t -> p (a t)")
                    mm(pI[0:64, h % 2, hp * 128:hp * 128 + 128], Bt[hp][rs_, c, :], arh, True, True,
                       [Bt[hp], AR[hp]], [pI])
                m1p = cc("mask1").unsqueeze(1).unsqueeze(1).to_broadcast([64, 2, 4, 2 * C])
                tt("dve", MA[:].rearrange("p (hp par) m -> p par hp m", par=2),
                   pI[0:64, :, :].rearrange("p par (hp m) -> p par hp m", m=2 * C), m1p, ALU.mult, [pI, cst], [MA])
                mlc = ML[0]
                cp("dve", mlc[:, :, 0, :], MA[:, :, 0:C], [MA], [mlc])
                tcur = TT[0]
                tt("dve", tcur[:], MA[:, :, 0:C], eyb, ALU.add, [MA, cst], [tcur])
                yield
                for h in HORD:
                    hp, rs_ = h // 2, hrow(h)
                    mm(pI[0:64, h % 2, hp * C:(hp + 1) * C], AR[hp][rs_, c, 0, :], Bt[hp][rs_, c, :], True, True,
                       [AR[hp], Bt[hp]], [pI])
                tt("dve", mlc[:, :, 1, :].rearrange("p (hp par) s -> p par hp s", par=2),
                   pI[0:64, :, 0:4 * C].rearrange("p par (hp s) -> p par hp s", s=C),
                   cc("maskL").unsqueeze(1).unsqueeze(1).to_broadcast([64, 2, 4, C]), ALU.mult, [pI, cst], [mlc])
                yield
                for h in HORD:
                    hp, rs_ = h // 2, hrow(h)
                    arh = AR[hp][rs_, c, :, :].rearrange("p a t -> p (a t)")
                    mm(pI[0:64, h % 2, hp * 128:hp * 128 + 128], Kt[hp][rs_, c, :], arh, True, True,
                       [Kt[hp], AR[hp]], [pI])
                tt("dve", KA[:].rearrange("p (hp par) m -> p par hp m", par=2),
                   pI[0:64, :, :].rearrange("p par (hp m) -> p par hp m", m=2 * C), m1p, ALU.mult, [pI, cst], [KA])
                yield

                def squares(mlc, mln, lev):
                    for h in range(8):
                        if lev < 5:
                            mm(pI[0:64, h // 4, (h % 4) * 128:(h % 4) * 128 + C], mlc[:, h, 1, :], mlc[:, h, 0, :],
                               True, True, [mlc], [pI])
                        mm(pI[0:64, h // 4, (h % 4) * 128 + C:(h % 4) * 128 + 2 * C], mlc[:, h, 0, :],
                           mlc[:, h, 1, :], True, True, [mlc], [pI])
                    if lev < 5:
                        cp("act", mln[:].rearrange("p (a h) x s -> p a (h x s)", a=2), pI[0:64, :, :], [pI], [mln])
                    else:
                        cp("act", mln[:, :, 1, :].rearrange("p (a h) s -> p a h s", a=2),
                           pI[0:64, :, :].rearrange("p a (h x s) -> p a h x s", h=4, x=2)[:, :, :, 1, :], [pI], [mln])

                def tupdate(mln, tcur, tnew):
                    for h in range(8):
                        mm(pI3[0:64, h * C:(h + 1) * C], mln[:, h, 1, :], tcur[:, h, :], True, True, [mln, tcur], [pI3])
                    tt("dve", tnew[:], pI3[0:64, :].rearrange("p (h s) -> p h s", h=8), tcur[:], ALU.add,
                       [pI3, tcur], [tnew])

                for lev in range(1, 6):
                    mln = ML[lev % 2]
                    squares(mlc, mln, lev)
                    tnew = Q_.Tf if lev == 5 else TT[lev % 2]
                    tupdate(mln, tcur, tnew)
                    mlc, tcur = mln, tnew
                    yield
                pIb = pI[:, 0, :].bitcast(BF16)
                for hp in range(4):
                    for a in range(2):
                        tr(pIb[0:64, hp * 256 + a * 128:hp * 256 + a * 128 + 128], BKh[hp][:, c, a, :], identb[:],
                           [BKh[hp], identb], [pI])
                cp("act", Q_.BKtok[:].rearrange("p h x m -> p (h x m)"), pIb[0:64, :], [pI], [Q_.BKtok])
                yield

            def state_task(c, P_, Q_):
                AR, v_sb = P_.AR, P_.v
                MA, KA, tcur, BKtok, y_sb = Q_.MA, Q_.KA, Q_.Tf, Q_.BKtok, Q_.y
                bank = lambda h: (pSA if h % 2 == 0 else pSB)
                bank2 = lambda h: (pSA if h < 4 else pSB)
                for h in HORD:
                    hp, rs_ = h // 2, hrow(h)
                    mm(bank(h)[0:64, hp * C:(hp + 1) * C], AR[hp][rs_, c, 0, :], STb[rs_, hp, :], True, True,
                       [AR[hp], STb], [bank(h)])
                for h in range(8):
                    mm(bank2(h)[0:64, 256 + (h % 4) * C:256 + (h % 4 + 1) * C], KA[:, h, 0:C], v_sb[:, c, h * C:(h + 1) * C],
                       True, True, [KA, v_sb], [bank2(h)])
                X4 = X32[:].rearrange("p (hp par) s -> p par hp s", par=2)
                cp("act", X4[:, 0, :, :], pSA[0:64, 0:4 * C].rearrange("p (hp s) -> p hp s", s=C), [pSA], [X32])
                cp("act", X4[:, 1, :, :], pSB[0:64, 0:4 * C].rearrange("p (hp s) -> p hp s", s=C), [pSB], [X32])
                tt("dve", X_sb[:, 0:4, :], X32[:, 0:4, :], pSA[0:64, 256:512].rearrange("p (h s) -> p h s", s=C), ALU.add,
                   [X32, pSA], [X_sb])
                tt("dve", X_sb[:, 4:8, :], X32[:, 4:8, :], pSB[0:64, 256:512].rearrange("p (h s) -> p h s", s=C), ALU.add,
                   [X32, pSB], [X_sb])
                yield
                for h in range(8):
                    mm(pSA[0:64, h * C:(h + 1) * C], tcur[:, h, :], X_sb[:, h, :], True, True, [tcur, X_sb], [pSA])
                cp("dve", U_sb[:], pSA[0:64, :], [pSA], [U_sb])
                yield
                for h in HORD:
                    hp, rs_ = h // 2, hrow(h)
                    mm(bank(h)[0:64, hp * C:(hp + 1) * C], AR[hp][rs_, c, 1, :], STb[rs_, hp, :], True, True,
                       [AR[hp], STb], [bank(h)])
                for h in range(8):
                    o_ = bank2(h)[0:64, 256 + (h % 4) * C:256 + (h % 4 + 1) * C]
                    mm(o_, MA[:, h, C:2 * C], U_sb[:, h * C:(h + 1) * C], True, False, [MA, U_sb], [bank2(h)])
                    mm(o_, KA[:, h, C:2 * C], v_sb[:, c, h * C:(h + 1) * C], False, True, [KA, v_sb], [bank2(h)])
                Y4 = y_sb[:].rearrange("p (hp par s) -> p par hp s", par=2, s=C)
                cp("act", Y4[:, 0, :, :], pSA[0:64, 0:4 * C].rearrange("p (hp s) -> p hp s", s=C), [pSA], [y_sb])
                cp("act", Y4[:, 1, :, :], pSB[0:64, 0:4 * C].rearrange("p (hp s) -> p hp s", s=C), [pSB], [y_sb])
                tt("dve", y_sb[:, 0:256], y_sb[:, 0:256], pSA[0:64, 256:512], ALU.add, [y_sb, pSA], [y_sb])
                tt("dve", y_sb[:, 256:512], y_sb[:, 256:512], pSB[0:64, 256:512], ALU.add, [y_sb, pSB], [y_sb])
                yield
                pS = pSB
                for hp in range(4):
                    mm(pS[:, hp * 128:(hp + 1) * 128], BKtok[:, hp, 0, :], U_sb[:, hp * 128:(hp + 1) * 128],
                       True, False, [BKtok, U_sb], [pS])
                    mm(pS[:, hp * 128:(hp + 1) * 128], BKtok[:, hp, 1, :], v_sb[:, c, hp * 128:(hp + 1) * 128],
                       False, True, [BKtok, v_sb], [pS])
                for hp in range(4):
                    for hh in range(2):
                        rs_ = slice(hh * 64, hh * 64 + 64)
                        stt(ST[rs_, hp, :], ST[rs_, hp, :], P_.wC[rs_, hp, c:c + 1],
                            pS[rs_, hp * 128 + hh * 64: hp * 128 + hh * 64 + 64], ALU.mult, ALU.add, [ST, P_.wC, pS], [ST])
                cp("act", STb[:], ST[:], [ST], [STb])
                yield

            def ypost_task(b, g, c, P_, Q_):
                y_sb, v_sb = Q_.y, P_.v
                t0c = b * T + g * C
                y3 = y_sb[:].rearrange("p (h i) -> p h i", h=8)
                S.op("dve", lambda en: en.tensor_reduce(gst[:, 0, :], y3, AX.X, ALU.add), reads=rr(y_sb), writes=rr(gst))
                act(ysq[:], y_sb[:], AF.Square, [y_sb], [ysq])
                S.op("dve", lambda en: en.tensor_reduce(gst[:, 1, :], ysq[:].rearrange("p (h i) -> p h i", h=8),
                                                        AX.X, ALU.add), reads=rr(ysq), writes=rr(gst))
                ts("dve", gst[:, 2, :], gst[:, 0, :], 1.0 / 64, None, ALU.mult, None, [gst], [gst])
                tt("dve", gst[:, 3, :], gst[:, 2, :], gst[:, 2, :], ALU.mult, [gst], [gst])
                stt(gst[:, 4, :], gst[:, 1, :], 1.0 / 64, gst[:, 3, :], ALU.mult, ALU.subtract, [gst], [gst])
                ts("dve", gst[:, 4, :], gst[:, 4, :], 64e-5, None, ALU.add, None, [gst], [gst])
                rsqrt(gst[:, 5, :], gst[:, 4, :], [gst], [gst])
                yield
                bc = lambda a: a.unsqueeze(2).to_broadcast([64, 8, 64])
                yt3 = ytmp[:].rearrange("p (h i) -> p h i", h=8)
                tt("pool", yt3, y3, bc(gst[:, 2, :]), ALU.subtract, [y_sb, gst], [ytmp])
                tt("pool", yt3, yt3, bc(gst[:, 5, :]), ALU.mult, [ytmp, gst], [ytmp])
                tt("pool", ytmp[:], ytmp[:], gnwb[:], ALU.mult, [ytmp, gnwb], [ytmp])
                tt("pool", ytmp[:], ytmp[:], gnbb[:], ALU.add, [ytmp, gnbb], [ytmp])
                ys3 = ysq[:].rearrange("p (h i) -> p h i", h=8)
                tt("dve", ys3, v_sb[:, c, :].rearrange("p (h i) -> p h i", h=8), bc(P_.bon[:, c, :]), ALU.mult,
                   [v_sb, P_.bon], [ysq])
                tt("dve", ytmp[:], ytmp[:], ysq[:], ALU.add, [ytmp, ysq], [ytmp])
                pg = pjp
                mm(pg[0:64, :], P_.sg0[:, c * C:(c + 1) * C], wg0[:], True, False, [P_.sg0, wg0], [pg])
                mm(pg[0:64, :], P_.sg1[:, c * C:(c + 1) * C], wg1[:], False, True, [P_.sg1, wg1], [pg])
                tt("dve", ytmp[:], ytmp[:], pg[0:64, :], ALU.mult, [ytmp, pg], [ytmp])
                if b == 0:
                    dbg_dump("ya", ytmp[:], [ytmp], (slice(t0c, t0c + C), slice(None)))
                yield
                pq = pjp
                for kc in range(4):
                    tr(pq[:, kc * C:(kc + 1) * C], ytmp[:, kc * 128:(kc + 1) * 128], ident[0:64, 0:64], [ytmp, cst], [pq])
                cp("act", P_.ya[:, :, c * C:(c + 1) * C], pq[:, 0:4 * C].rearrange("p (k t) -> p k t", k=4), [pq], [P_.ya])
                if c == NCH - 1:
                    tb = b * T + (g // NCH) * NB
                    dma(yaT_d[:, :, tb:tb + NB], P_.ya[:], [P_.ya], [yaT_res], P_.ya_sem)
                yield

            def pgen_slice(pg_, j, n):
                cnt = 0
                while True:
                    if j < n - 1 and cnt >= (12 // n):
                        return
                    try:
                        next(pg_)
                    except StopIteration:
                        return
                    cnt += 1
                    yield

            def run_rr(gens):
                run_rr2(gens)

                gens = list(gens)
                while gens:
                    for g_ in list(gens):
                        try:
                            next(g_)
                        except StopIteration:
                            gens.remove(g_)

            for b in range(NSEQ):
                memset("dve", ST[:], 0.0, [ST])
                memset("dve", STb[:], 0.0, [STb])
                Gn = min(G, dbg_chunks)
                run_rr([prep_task(b, 0, psets[0])])
                for k in range(Gn + 2):
                    gens = []
                    if k < Gn:
                        gens.append(inv_task(k % NCH, psets[(k // NCH) % 3], csets[k % 2]))
                    if 1 <= k <= Gn:
                        g = k - 1
                        gens.append(state_task(g % NCH, psets[(g // NCH) % 3], csets[g % 2]))
                    if 2 <= k <= Gn + 1:
                        g = k - 2
                        gens.append(ypost_task(b, g, g % NCH, psets[(g // NCH) % 3], csets[g % 2]))
                    if k % NCH == 0:
                        nb_ = k // NCH + 1
                        pgen = prep_task(b, nb_, psets[nb_ % 3]) if nb_ * NCH < Gn else None
                    if pgen is not None:
                        gens.append(pgen_slice(pgen, k % NCH, NCH))
                    run_rr(gens)

        S.barrier()
        stop_after = int(os.environ.get("MK_STOP", "99")) if debug else 99

        if stop_after >= 2:
          with ExitStack() as es:
            WD = sb(es, "WD", (128, 8, 2560), BF16)
            WT = sb(es, "WT", (128, 8, 72), BF16)
            load_const(es, "2", _CST2, cst2_d, _C2)
            xt = [sb(es, "xt2_%d" % i, (128, D)) for i in range(2)]
            wst = [sb(es, "wst2_%d" % i, (128, WSTN)) for i in range(2)]
            for kc in range(8):
                for hf in range(2):
                    def consD(st_, kc=kc, hf=hf):
                        ts("dve", WD[:, kc, hf * 1280:(hf + 1) * 1280], st_[:, :1280], gcol[:, 0, kc:kc + 1], None,
                           ALU.mult, None, [st_, gcol], [WD])
                    load_weight(wD_d[kc * 128:(kc + 1) * 128, hf * 1280:(hf + 1) * 1280], 1280, consD)

                def consT(st_, kc=kc):
                    ts("dve", WT[:, kc, :], st_[:, :72], gcol[:, 0, kc:kc + 1], None, ALU.mult, None, [st_, gcol], [WT])
                load_weight(wT_d[kc * 128:(kc + 1) * 128, :], 72, consT)
            NB = DS_NB
            hT = sb(es, "hTd", (128, 8, NB), BF16)
            pTr = Pool([ps(es, "pTr2_%d" % i, (128, 8, 128), BF16) for i in range(1)])
            pj = Pool([ps(es, "pj2_%d" % i, (128, 512)) for i in range(3)])
            pW = Pool([ps(es, "pW2_%d" % i, (128, 2, 512)) for i in range(1)])
            po = ps(es, "po2", (128, 2, 512))
            qT = sb(es, "qT", (128, 4, NB), BF16)
            qiT = sb(es, "qiT", (128, 4, NB), BF16)
            kT_all = sb(es, "kT_all", (128, T), BF16)
            kiT_all = sb(es, "kiT_all", (128, T), BF16)
            vones = sb(es, "vones", (128, 16, 65), BF16)
            wi_sb = sb(es, "wi_sb", (128, 4, 8))
            rt1 = sb(es, "rt1", (128, NB))
            rt2 = sb(es, "rt2", (128, NB))
            acc = sb(es, "acc", (128, T))
            work = sb(es, "work", (128, T))
            relu_t = [sb(es, "relu%d" % i, (128, 512)) for i in range(2)]
            mx8 = sb(es, "mx8", (128, 8))
            maskb = sb(es, "maskb", (128, T), BF16)
            maskT = sb(es, "maskT", (128, 16, 128), BF16)
            causT = sb(es, "causT", (128, 128), BF16)
            eT = [sb(es, "eT%d" % i, (128, 8, 128), BF16) for i in range(2)]
            rcp = sb(es, "rcp", (128, 8))
            yb = sb(es, "yb", (128, 512), BF16)
            ybT = [sb(es, "ybT%d" % i, (128, 4, NB), BF16) for i in range(2)]
            ybT_sem = [S.new_dsem() for _ in range(2)]
            cp("dve", causT[:], cc("causalT"), [cst], [causT])
            memset("pool", vones[:, :, 64:65], 1.0, [vones])
            WI_SCALE = float(8 ** -0.5 * 64 ** -0.5)
            MBIG = 30000.0
            cbT = sb(es, "cbT", (128, 128), BF16)
            ts("dve", cbT[:], cc("causalT"), MBIG, -MBIG, ALU.mult, ALU.add, [cst], [cbT])
            qTs = [qT, sb(es, "qT_b", (128, 4, NB), BF16), sb(es, "qT_c", (128, 4, NB), BF16)]
            qiTs = [qiT, sb(es, "qiT_b", (128, 4, NB), BF16)]
            wis = [wi_sb, sb(es, "wi_sb_b", (128, 4, 8))]
            NBLK = T // NB
            kT_res = [Res("kT%d" % i) for i in range(NBLK)]
            kiT_res = [Res("kiT%d" % i) for i in range(NBLK)]
            von_res = [Res("von%d" % i) for i in range(NBLK)]
            maskTs = [maskT] + [sb(es, "maskT_%d" % i, (128, 16, 128), BF16) for i in range(3)]
            accs = [acc, sb(es, "acc_b", (128, T))]
            works = [work, sb(es, "work_b", (128, T))]
            mx8s = [mx8, sb(es, "mx8_b", (128, 8))]
            maskbs = [maskb, sb(es, "maskb_b", (128, T), BF16)]
            relus = [relu_t, [sb(es, "relub%d" % i, (128, 512)) for i in range(2)]]
            ybs = [yb, sb(es, "yb_b", (128, 512), BF16)]
            dbg_qt = int(os.environ.get("MK_QT", "999")) if debug else 999
            NIT = 40
            pow2 = sb(es, "pow2", (128, 2, NIT + 1))
            for k_ in range(NIT + 1):
                memset("pool", pow2[:, 0, k_:k_ + 1], float(2.0 ** -(k_ + 1)), [pow2])
                memset("pool", pow2[:, 1, k_:k_ + 1], float(-(2.0 ** -(k_ + 1))), [pow2])
            wtabs = [sb(es, "wtab%d" % i, (128, 2, NIT + 1)) for i in range(2)]

            def proj_block(b, blk, qT_, qiT, wi_sb):
                tl0 = blk * NB
                t0 = b * T + tl0
                make_hT(pTr, t0, NB // 128, hT, 0, x_d, xt=xt)
                ropeC = cc("ropeC")[:, tl0:tl0 + NB]
                ropeS = cc("ropeS")[:, tl0:tl0 + NB]

                def proj(ct):
                    p_ = pj.next()
                    for kc in range(8):
                        mm(p_[:, :NB], WD[:, kc, ct * 128:(ct + 1) * 128], hT[:, kc, :], kc == 0, kc == 7, [WD, hT], [p_])
                    return p_

                def rope(ct_a, ct_b, dst_ap, dst_tl):
                    pa = proj(ct_a)
                    tt("dve", rt1[:], pa[:, :NB], ropeC, ALU.mult, [pa, cst], [rt1])
                    pb = proj(ct_b)
                    tt("dve", rt2[:], pb[:, :NB], ropeS, ALU.mult, [pb, cst], [rt2])
                    tt("pool", dst_ap, rt1[:], rt2[:], ALU.add, [rt1, rt2], [dst_tl])

                for i in range(4):
                    rope(i, 4 + i, qT_[:, i, :], qT_)
                    yield
                rope(8, 9, kT_all[:, tl0:tl0 + NB], kT_res[blk])
                yield
                for i in range(4):
                    rope(10 + i, 14 + i, qiT[:, i, :], qiT)
                    yield
                rope(18, 19, kiT_all[:, tl0:tl0 + NB], kiT_res[blk])
                for i in range(NB // 128):
                    p_ = pj.next()
                    for kc in range(8):
                        mm(p_[:, 0:72], hT[:, kc, i * 128:(i + 1) * 128], WT[:, kc, :], kc == 0, kc == 7, [WT, hT], [p_])
                    cp("dve", vones[:, blk * 4 + i, 0:64], p_[:, 0:64], [p_], [von_res[blk]])
                    ts("dve", wi_sb[:, i, :], p_[:, 64:72], WI_SCALE, None, ALU.mult, None, [p_], [wi_sb])
                yield

            def topk_task(qt, i, mT, qiT, wi_sb):
                if qt < 2:
                    return
                acc, work, mx8, maskb, relu_t = accs[qt % 2], works[qt % 2], mx8s[qt % 2], maskbs[qt % 2], relus[qt % 2]
                Sk = (qt + 1) * 128
                tq = slice(i * 128, (i + 1) * 128)
                nseg = (Sk + 511) // 512
                for sg in range(nseg):
                    s0 = sg * 512
                    sn = min(512, Sk - s0)
                    for h in range(8):
                        rs_ = slice((h % 2) * 64, (h % 2) * 64 + 64)
                        p_ = pj.next()
                        mm(p_[:, :sn], qiT[rs_, h // 2, tq], kiT_all[rs_, s0:s0 + sn], True, True, [qiT, kiT_res[sg]], [p_])
                        rl = relu_t[h % 2]
                        act(rl[:, :sn], p_[:, :sn], AF.Relu, [p_], [rl])
                        if h == 0:
                            ts("dve", acc[:, s0:s0 + sn], rl[:, :sn], wi_sb[:, i, 0:1], None, ALU.mult, None, [rl, wi_sb], [acc])
                        else:
                            stt(acc[:, s0:s0 + sn], rl[:, :sn], wi_sb[:, i, h:h + 1], acc[:, s0:s0 + sn],
                                ALU.mult, ALU.add, [rl, wi_sb, acc], [acc])
                        yield
                tt("dve", acc[:, Sk - 128:Sk], acc[:, Sk - 128:Sk], cc("causal_bias"), ALU.add, [acc, cst], [acc])
                bs = mx8
                Wt = wtabs[qt % 2]
                S.op("dve", lambda en: en.tensor_reduce(bs[:, 0:1], acc[:, :Sk - 128], AX.X, ALU.min), reads=rr(acc), writes=rr(bs))
                S.op("dve", lambda en: en.tensor_reduce(bs[:, 5:6], acc[:, :Sk], AX.X, ALU.max), reads=rr(acc), writes=rr(bs))
                yield
                tt("dve", bs[:, 1:2], bs[:, 5:6], bs[:, 0:1], ALU.subtract, [bs], [bs])
                ts("dve", bs[:, 1:2], bs[:, 1:2], 1.0001, 1e-6, ALU.mult, ALU.add, [bs], [bs])
                ts("dve", Wt[:, 0, :], pow2[:, 0, :], bs[:, 1:2], None, ALU.mult, None, [pow2, bs], [Wt])
                ts("dve", Wt[:, 1, :], pow2[:, 1, :], bs[:, 1:2], None, ALU.mult, None, [pow2, bs], [Wt])
                stt(bs[:, 2:3], bs[:, 0:1], -1.0, Wt[:, 1, 0:1], ALU.mult, ALU.add, [bs, Wt], [bs])
                c0 = float(2 * 256 - Sk)
                for it_ in range(NIT):
                    yield
                    act(work[:, :Sk], acc[:, :Sk], AF.Sign, [acc, bs], [work, bs], bias=bs[:, 2:3], accum_out=bs[:, 3:4])
                    yield
                    ts("dve", bs[:, 4:5], bs[:, 3:4], c0, Wt[:, 1, it_:it_ + 1], ALU.is_ge, ALU.mult, [bs, Wt], [bs])
                    stt(bs[:, 2:3], bs[:, 4:5], Wt[:, 0, it_ + 1:it_ + 2], bs[:, 2:3], ALU.add, ALU.add, [bs, Wt], [bs])
                stt(bs[:, 6:7], bs[:, 2:3], Wt[:, 0, NIT:NIT + 1], bs[:, 2:3], ALU.add, ALU.bypass, [bs, Wt], [bs]) if False else None
                ts("dve", bs[:, 6:7], bs[:, 2:3], Wt[:, 0, NIT:NIT + 1], -1.0, ALU.add, ALU.mult, [bs, Wt], [bs])
                yield
                ts("dve", maskb[:, :Sk], acc[:, :Sk], bs[:, 6:7], None, ALU.is_ge, None, [acc, bs], [maskb])
                for g in range((qt + 1 + 3) // 4):
                    pm = pj.next()
                    pmb = pm[:].bitcast(BF16)
                    nk = min(4, qt + 1 - g * 4)
                    for j in range(nk):
                        kt = g * 4 + j
                        tr(pmb[:, j * 128:(j + 1) * 128], maskb[:, kt * 128:(kt + 1) * 128], identb[:], [maskb, identb], [pm])
                    act(mT[:, g * 4:g * 4 + nk, :], pmb[:, 0:nk * 128].rearrange("p (k t) -> p k t", t=128), AF.Identity,
                        [pm], [mT], bias=-MBIG, scale=MBIG)
                yield

            def attn_task(b, blk, qt, i, mT, qT_, ybt, yb_):
                tq = slice(i * 128, (i + 1) * 128)
                t0 = b * T + blk * NB
                for kt in range(qt + 1):
                    psc = pW.next()
                    need_bias = (qt >= 2) or (kt == qt)
                    brhs = mT[:, kt, :] if qt >= 2 else cbT[:]
                    bres = mT if qt >= 2 else cbT
                    for h in [0, 2, 4, 6, 1, 3, 5, 7]:
                        rs_ = slice((h % 2) * 64, (h % 2) * 64 + 64)
                        o_ = psc[:, h % 2, (h // 2) * 128:(h // 2) * 128 + 128]
                        mm(o_, kT_all[rs_, kt * 128:(kt + 1) * 128], qT_[rs_, h // 2, tq], True, not need_bias,
                           [kT_res[kt // 4], qT_], [psc])
                        if need_bias:
                            mm(o_, identb[:], brhs, False, True, [identb, bres], [psc])
                    e_ = eT[kt % 2]
                    act(e_[:].rearrange("p (a h) t -> p a (h t)", a=2), psc[:, :, :], AF.Exp, [psc], [e_], scale=0.125)
                    for h in range(8):
                        mm(po[:, h // 4, (h % 4) * 65:(h % 4) * 65 + 65], e_[:, (h % 2) * 4 + h // 2, :], vones[:, kt, :],
                           kt == 0 and h % 4 == 0, kt == qt, [e_, von_res[kt // 4], vones], [po], skip=True)
                    if kt % 2 == 1:
                        yield
                pov = po[:, :, 0:260].rearrange("p a (h e) -> p a h e", e=65)
                S.op("dve", lambda en: en.reciprocal(rcp[:].rearrange("p (a h) -> p a h", a=2), pov[:, :, :, 64]),
                     reads=rr(po), writes=rr(rcp))
                tt("dve", yb_[:].rearrange("p (a h e) -> p a h e", a=2, h=4), pov[:, :, :, 0:64],
                   rcp[:].rearrange("p (a h) -> p a h", a=2).unsqueeze(3).to_broadcast([128, 2, 4, 64]), ALU.mult,
                   [po, rcp], [yb_])
                if "yb" in dbg_d and b == 0:
                    cp("dve", rt1[:, 0:512], yb_[:], [yb_], [rt1])
                    dbg_dump("yb", rt1[:, 0:512], [rt1], (slice(t0 + i * 128, t0 + (i + 1) * 128), slice(None)))
                pm = pj.next()
                pmb = pm[:].bitcast(BF16)
                for kc in range(4):
                    tr(pmb[:, kc * 128:(kc + 1) * 128], yb_[:, kc * 128:(kc + 1) * 128], identb[:], [yb_, identb], [pm])
                cp("act", ybt[:, :, tq], pmb[:, 0:512].rearrange("p (k t) -> p k t", t=128), [pm], [ybt])
                if i == NB // 128 - 1:
                    dma(ybT_d[:, :, t0:t0 + NB], ybt[:], [ybt], [ybT_res], ybT_sem[(b * (T // NB) + blk) % 2])
                yield

            def chain(*gs):
                for g_ in gs:
                    yield from g_

            def gslice(pg_, last, nmax):
                cnt = 0
                while True:
                    if not last and cnt >= nmax:
                        return
                    try:
                        next(pg_)
                    except StopIteration:
                        return
                    cnt += 1
                    yield

            for b in range(NSEQ):
                NT = min(T // 128, dbg_qt)
                prev = []
                pgen = None
                run_rr2([proj_block(b, 0, qTs[0], qiTs[0], wis[0])])
                for p in range(NT // 2 + 1):
                    gens = []
                    cur = []
                    for j in (2 * p, 2 * p + 1):
                        if j < NT:
                            blk, i = j // 4, j % 4
                            gens.append(topk_task(j, i, maskTs[j % 4], qiTs[blk % 2], wis[blk % 2]))
                            cur.append((b, blk, j, i, maskTs[j % 4], qTs[blk % 3], ybT[(b * NBLK + blk) % 2], ybs[j % 2]))
                    if prev:
                        gens.append(chain(*[attn_task(*a_) for a_ in prev]))
                    if p % 2 == 0:
                        nb_ = p // 2 + 1
                        pgen = proj_block(b, nb_, qTs[nb_ % 3], qiTs[nb_ % 2], wis[nb_ % 2]) if nb_ * 4 < NT else None
                    if pgen is not None:
                        gens.append(gslice(pgen, p % 2 == 1, 5))
                    prev = cur
                    run_rr2(gens)
          S.barrier()

        if stop_after >= 3:
          with ExitStack() as es:
            WG = sb(es, "WG", (128, 8, 2048), BF16)
            wbr = sb(es, "wbr", (128, 8, 1024), BF16)
            wout = sb(es, "wout", (128, 8, 1024), BF16)
            wst = [sb(es, "wst3_%d" % i, (128, 1024)) for i in range(4)]
            for kc in range(8):
                for hf in range(2):
                    def consG(st_, kc=kc, hf=hf):
                        ts("dve", WG[:, kc, hf * 1024:(hf + 1) * 1024], st_[:, :1024], gcol[:, 0, kc:kc + 1], None,
                           ALU.mult, None, [st_, gcol], [WG])
                    load_weight(wG_d[kc * 128:(kc + 1) * 128, hf * 1024:(hf + 1) * 1024], 1024, consG)

                def consB(st_, kc=kc):
                    cp("act", wbr[:, kc, :], st_[:, :1024], [st_], [wbr])
                load_weight(wbr_d[kc * 128:(kc + 1) * 128, :], 1024, consB)

                def consO(st_, kc=kc):
                    cp("dve", wout[:, kc, :], st_[:, :1024], [st_], [wout])
                load_weight(wout_d[kc * 128:(kc + 1) * 128, :], 1024, consO)
            NB = MG_NB
            hT = sb(es, "hTm", (128, 8, NB), BF16)
            xk = sb(es, "xk", (128, 4, D))
            pTr = Pool([ps(es, "pTr3_%d" % i, (128, 8, 128), BF16) for i in range(1)])
            pj = Pool([ps(es, "pj3_%d" % i, (128, 512)) for i in range(6)])
            yaL = sb(es, "yaL", (128, 4, NB), BF16)
            ybL = sb(es, "ybL", (128, 4, NB), BF16)
            yl_sem = S.new_dsem()
            yl_sem2 = S.new_dsem()
            sgA = sb(es, "sgA", (128, NB))
            sgB = sb(es, "sgB", (128, NB))
            mA = sb(es, "mA", (128, NB))
            mB = sb(es, "mB", (128, NB))
            mgT = sb(es, "mgT", (128, 8, NB), BF16)
            x1t = [sb(es, "x1t%d" % i, (128, D)) for i in range(2)]
            x1_sem = [S.new_dsem() for _ in range(2)]
            n1 = 0
            for bi in range(NTOK // NB):
                t0 = bi * NB
                make_hT(pTr, t0, 4, hT, 0, x_d, keep=xk)
                dma(yaL[:], yaT_d[:, :, t0:t0 + NB], [yaT_res], [yaL], yl_sem)
                dma(ybL[:], ybT_d[:, :, t0:t0 + NB], [ybT_res], [ybL], yl_sem2)
                for dt_ in range(8):
                    ds_ = slice(dt_ * 128, (dt_ + 1) * 128)
                    pa = pj.next()
                    for kc in range(4):
                        mm(pa[:, :NB], wbr[:, kc, ds_], yaL[:, kc, :], kc == 0, kc == 3, [wbr, yaL], [pa])
                    pb = pj.next()
                    for kc in range(4):
                        mm(pb[:, :NB], wbr[:, 4 + kc, ds_], ybL[:, kc, :], kc == 0, kc == 3, [wbr, ybL], [pb])
                    g0 = pj.next()
                    for kc in range(8):
                        mm(g0[:, :NB], WG[:, kc, dt_ * 128:(dt_ + 1) * 128], hT[:, kc, :], kc == 0, kc == 7, [WG, hT], [g0])
                    g1 = pj.next()
                    for kc in range(8):
                        mm(g1[:, :NB], WG[:, kc, 1024 + dt_ * 128:1024 + (dt_ + 1) * 128], hT[:, kc, :], kc == 0, kc == 7,
                           [WG, hT], [g1])
                    act(sgA[:], g0[:, :NB], AF.Sigmoid, [g0], [sgA])
                    act(sgB[:], g1[:, :NB], AF.Sigmoid, [g1], [sgB])
                    tt("dve", mA[:], pa[:, :NB], sgA[:], ALU.mult, [pa, sgA], [mA])
                    tt("dve", mB[:], pb[:, :NB], sgB[:], ALU.mult, [pb, sgB], [mB])
                    tt("pool", mgT[:, dt_, :], mA[:], mB[:], ALU.add, [mA, mB], [mgT])
                for tt_ in range(4):
                    x1 = x1t[n1 % 2]
                    for hf in range(2):
                        po = pj.next()
                        for kc in range(8):
                            mm(po[:, :], mgT[:, kc, tt_ * 128:(tt_ + 1) * 128], wout[:, kc, hf * 512:(hf + 1) * 512],
                               kc == 0, kc == 7, [mgT, wout], [po])
                        tt("dve", x1[:, hf * 512:(hf + 1) * 512], po[:, :], xk[:, tt_, hf * 512:(hf + 1) * 512], ALU.add,
                           [po, xk], [x1])
                    dma(x1_d[t0 + tt_ * 128:t0 + (tt_ + 1) * 128, :], x1[:], [x1], [x1_res[bi]], x1_sem[n1 % 2])
                    if t0 < T:
                        dbg_dump("x1", x1[:], [x1], (slice(t0 + tt_ * 128, t0 + (tt_ + 1) * 128), slice(None)))
                    n1 += 1
          S.barrier()

        if stop_after >= 4:
          with ExitStack() as es:
            wup = sb(es, "wup", (128, 8, FFH), BF16)
            wdn = sb(es, "wdn", (128, 32, D), BF16)
            gzb = sb(es, "gzb", (128, D))
            wst = [sb(es, "wst4_%d" % i, (128, 1024)) for i in range(4)]
            dma(gzb[:], gz_d.partition_broadcast(128), [], [gzb], d0())
            for kc in range(8):
                for q4 in range(4):
                    def consU(st_, kc=kc, q4=q4):
                        if True:
                            ts("dve", wup[:, kc, q4 * 1024:(q4 + 1) * 1024], st_[:, 0:1024], gcol[:, 1, kc:kc + 1], None,
                               ALU.mult, None, [st_, gcol], [wup])
                    load_weight(wup_d[kc * 128:(kc + 1) * 128, q4 * 1024:(q4 + 1) * 1024], 1024, consU)
            for g in range(32):
                def consDn(st_, g=g):
                    cp("act" if g % 2 else "dve", wdn[:, g, :], st_[:, 0:1024], [st_], [wdn])
                load_weight(wdn_d[g * 128:(g + 1) * 128, :], 1024, consDn)
            NB = FF_NB
            NT4 = NB // 128
            hT = sb(es, "hTf", (128, 8, NB), BF16)
            xk = sb(es, "xkf", (128, NT4, D))
            pTr = Pool([ps(es, "pTr4_%d" % i, (128, 8, 128), BF16) for i in range(1)])
            pj = Pool([ps(es, "pj4_%d" % i, (128, 512)) for i in range(6)])
            aT = sb(es, "aT", (128, 32, NB), BF16)
            rl = [sb(es, "rl%d" % i, (128, NB), BF16) for i in range(2)]
            xx = sb(es, "x2", (128, D))
            ot = [sb(es, "ot%d" % i, (128, D)) for i in range(2)]
            o_sem = [S.new_dsem() for _ in range(2)]
            st2 = [sb(es, "st2_%d" % i, (128, 4)) for i in range(2)]
            n2 = 0
            for bi in range(NTOK // NB):
                t0 = bi * NB
                make_hT(pTr, t0, NT4, hT, 0, x1_d, keep=xk, src_res=[x1_res[t0 // 512]])
                for ht in range(32):
                    pu = pj.next()
                    for kc in range(8):
                        mm(pu[:, :NB], wup[:, kc, ht * 128:(ht + 1) * 128], hT[:, kc, :], kc == 0, kc == 7, [wup, hT], [pu])
                    r_ = rl[ht % 2]
                    act(r_[:], pu[:, :NB], AF.Relu, [pu], [r_])
                    tt("pool" if ht % 2 else "dve", aT[:, ht, :], r_[:], r_[:], ALU.mult, [r_], [aT])
                for tt_ in range(NT4):
                    oo = ot[n2 % 2]
                    s2 = st2[n2 % 2]
                    for hf in range(2):
                        pd = pj.next()
                        for ht in range(32):
                            mm(pd[:, :], aT[:, ht, tt_ * 128:(tt_ + 1) * 128], wdn[:, ht, hf * 512:(hf + 1) * 512],
                               ht == 0, ht == 31, [aT, wdn], [pd])
                        tt("dve", xx[:, hf * 512:(hf + 1) * 512], pd[:, :], xk[:, tt_, hf * 512:(hf + 1) * 512], ALU.add,
                           [pd, xk], [xx])
                    act(oo[:], xx[:], AF.Square, [xx], [oo, s2], accum_out=s2[:, 0:1])
                    ts("dve", s2[:, 1:2], s2[:, 0:1], 1.0 / D, 1e-6, ALU.mult, ALU.add, [s2], [s2])
                    rsqrt(s2[:, 2:3], s2[:, 1:2], [s2], [s2])
                    stt(oo[:], xx[:], s2[:, 2:3], gzb[:], ALU.mult, ALU.mult, [xx, s2, gzb], [oo])
                    out_dmas.append(dma(out_d[t0 + tt_ * 128:t0 + (tt_ + 1) * 128, :], oo[:], [oo], [], o_sem[n2 % 2]))
                    n2 += 1

        S.finish(out_dmas)
        S.emit()
    return nc


def _swap_halves(cols):
    c = np.asarray(cols).reshape(-1, 2, 32)
    return c[:, ::-1, :].reshape(-1)


def _layout_inputs(inp):
    f = lambda a: np.ascontiguousarray(np.asarray(a, dtype=np.float32))
    w_in = f(inp["w_in"])[0]
    mu = f(inp["mu_shift"])[0]
    colsA = np.concatenate([np.arange(0, 1024), np.arange(1536, 1824)])
    colsV = np.arange(1024, 1536)
    base = 1824
    q = base + np.arange(512)
    k = base + 512 + np.arange(64)
    v = base + 576 + np.arange(64)
    qi = base + 640 + np.arange(512)
    ki = base + 1152 + np.arange(64)
    wi = base + 1216 + np.arange(8)
    colsD = np.concatenate([q, _swap_halves(q), k, k, _swap_halves(k), _swap_halves(k),
                            qi, _swap_halves(qi), ki, ki, _swap_halves(ki), _swap_halves(ki)])
    colsT = np.concatenate([v, wi])
    colsG = 1824 + 1224 + np.arange(2048)
    per_ch = lambda a: f(a)[0].reshape(4, 128).T
    pp = np.concatenate([per_ch(inp["decay_bias"]), per_ch(inp["iclr_bias"]), per_ch(inp["k_k"]),
                         per_ch(inp["k_a"]), per_ch(inp["r_k"])], axis=1)
    shared = {
        "wA": f(w_in[:, colsA]), "wV": f(w_in[:, colsV]), "muA": f(mu[colsA]), "muV": f(mu[colsV]),
        "wD": f(w_in[:, colsD]), "wT": f(w_in[:, colsT]), "wG": f(w_in[:, colsG]),
        "gcols": f(np.concatenate([f(inp["g_mix"])[0].reshape(8, 128).T, f(inp["g_ffn"])[0].reshape(8, 128).T], axis=1)),
        "gfin": f(inp["g_final"]),
        "wlora": f(np.concatenate([f(inp["w_decay_up"])[0], f(inp["w_iclr_up"])[0]], axis=0)),
        "wgate": f(inp["w_gate_up"])[0], "pp": f(pp),
        "gnw": f(inp["gn_w"])[0], "gnb": f(inp["gn_b"])[0],
        "wbr": f(f(inp["w_branch"])[0].reshape(1024, 1024)), "wout": f(inp["w_out"])[0],
        "wup": f(inp["w_ffn_up"])[0], "wdn": f(inp["w_ffn_down"])[0], "cstA": _CSTA, "cst1": _CST1, "cst2": _CST2,
    }
    x = f(inp["x"])
    maps = []
    for c in range(NCORES):
        m = dict(shared)
        m["x"] = np.ascontiguousarray(x[c * NSEQ:(c + 1) * NSEQ].reshape(NTOK, D))
        maps.append(m)
    return maps


def kernel(**inputs):
    maps = _layout_inputs(inputs)
    nc = build_nc()
    res = run_bass_kernel_spmd(nc, maps, core_ids=list(range(NCORES)))
    outs = [np.asarray(r["out"], dtype=np.float32).reshape(NSEQ, T, D) for r in res.results]
    return np.concatenate(outs, axis=0)
```
